# Optimizing a Trainium2 kernel written in Bass

```python
import math
import jax
import jax.numpy as jnp
from jax import lax
import numpy as np

D_MODEL = 1024
BATCH = 16
SEQ = 2048
DEPTH = 4

CTX_LEN = 256
GRID_W = 64

CONV_W = 4
CONV_LEFT = 2

BRANCH_W = D_MODEL
N_BRANCH = 3

SSD_HEAD_DIM = 64
SSD_HEADS = BRANCH_W // SSD_HEAD_DIM
SSD_WIDTH = SSD_HEADS * SSD_HEAD_DIM
SSD_GROUPS = 4
SSD_STATE = 128
SSD_CONV_CH = SSD_WIDTH + 2 * SSD_GROUPS * SSD_STATE
SSD_CHUNK = 128
DT_MIN = 1e-3
DT_MAX = 1e-1

LRU_WIDTH = BRANCH_W
LRU_BLOCKS = 16
LRU_BLOCK = LRU_WIDTH // LRU_BLOCKS
LRU_C = 8.0

GLA_HEADS = 4
GLA_KEY = D_MODEL // 2
GLA_VAL = BRANCH_W
GLA_DK = GLA_KEY // GLA_HEADS
GLA_DV = GLA_VAL // GLA_HEADS
GLA_GATE_RANK = 16
GLA_TAU = 16.0
GLA_CHUNK = 64

D_FF = 2816
FFN_RES_W = 0.5

N_MOD = 9

ALPHA = (2.0 * DEPTH) ** 0.25
BETA = (8.0 * DEPTH) ** -0.25
NORM_EPS = 1e-5

IN_SIZES = (SSD_WIDTH, SSD_CONV_CH, 2 * SSD_HEADS,
            LRU_WIDTH, LRU_WIDTH,
            GLA_KEY, GLA_KEY, GLA_VAL, GLA_VAL, 2 * GLA_GATE_RANK,
            N_BRANCH * D_MODEL)
IN_TOTAL = sum(IN_SIZES)

kernel_name = 'hybrid_ssd_rglru_gla_prefix_dit'


def layer_norm(x, g, b):
    xf = x.astype(jnp.float32)
    mu = jnp.mean(xf, axis=-1, keepdims=True)
    var = jnp.mean(jnp.square(xf - mu), axis=-1, keepdims=True)
    return ((xf - mu) * lax.rsqrt(var + NORM_EPS)).astype(g.dtype) * g + b


def rms_norm(x, g):
    xf = x.astype(jnp.float32)
    return (xf * lax.rsqrt(jnp.mean(jnp.square(xf), axis=-1, keepdims=True) + NORM_EPS)).astype(g.dtype) * g


def modulate(h, shift, scale):
    return h * (1.0 + scale) + shift


def post_norm(h, y, gate, res_w, g, b):
    return layer_norm(ALPHA * h + res_w * gate * y, g, b)


def swiglu(u, w_up, w_down):
    a, v = jnp.split(u @ w_up, 2, axis=-1)
    return (jax.nn.silu(a) * v) @ w_down


def ffn_sublayer(h, m, j, w_up, w_down, g, b):
    u = modulate(h, m[:, :, 3 * j], m[:, :, 3 * j + 1])
    return post_norm(h, swiglu(u, w_up, w_down), m[:, :, 3 * j + 2], FFN_RES_W, g, b)


def line_conv(x, w, b, line):
    n_b, t, ch = x.shape
    xl = x.reshape(n_b, t // line, line, ch)
    xp = jnp.pad(xl, ((0, 0), (0, 0), (CONV_LEFT, CONV_W - 1 - CONV_LEFT), (0, 0)))
    y = b + xp[:, :, 0:line] * w[0]
    for k in range(1, CONV_W):
        y = y + xp[:, :, k:k + line] * w[k]
    return y.reshape(n_b, t, ch)


def raster_to_colmajor(x, rows):
    n_b, t, ch = x.shape
    return x.reshape(n_b, rows, GRID_W, ch).transpose(0, 2, 1, 3).reshape(n_b, t, ch)


def colmajor_to_raster(x, rows):
    n_b, t, ch = x.shape
    return x.reshape(n_b, GRID_W, rows, ch).transpose(0, 2, 1, 3).reshape(n_b, t, ch)


def direction(t, d):
    return jnp.flip(t, axis=1) if d else t


def linear_scan(a, b, h0):
    b = b.at[:, 0].add(a[:, 0] * h0)

    def combine(lhs, rhs):
        return lhs[0] * rhs[0], rhs[0] * lhs[1] + rhs[1]

    return lax.associative_scan(combine, (a, b), axis=1)[1]


def ssd_chunk_scan(x, dt, A, B, C, h0):
    n_b, t, n_h, p = x.shape
    n_g, n_s = B.shape[-2], B.shape[-1]
    q = SSD_CHUNK
    nc = t // q
    hg = n_h // n_g
    xs = (x * dt[..., None]).reshape(n_b, nc, q, n_g, hg, p)
    a = (dt * A).reshape(n_b, nc, q, n_g, hg).transpose(0, 1, 3, 4, 2)
    a_cs = jnp.cumsum(a, axis=-1)
    bc = B.reshape(n_b, nc, q, n_g, n_s)
    cc = C.reshape(n_b, nc, q, n_g, n_s)
    causal = jnp.tril(jnp.ones((q, q), dtype=bool))
    seg = a_cs[..., :, None] - a_cs[..., None, :]
    decay = jnp.exp(jnp.where(causal, seg, -jnp.inf))
    scores = jnp.einsum('bclgn,bcsgn->bcgls', cc, bc)
    y_diag = jnp.einsum('bcghls,bcsghp->bclghp', scores[:, :, :, None] * decay, xs)
    to_end = jnp.exp(a_cs[..., -1:] - a_cs).transpose(0, 1, 4, 2, 3)
    chunk_states = jnp.einsum('bclgn,bclghp->bcghpn', bc, xs * to_end[..., None])
    chunk_decay = jnp.exp(a_cs[..., -1])[..., None, None]
    h0g = h0.reshape(n_b, n_g, hg, p, n_s)
    s_after = linear_scan(chunk_decay, chunk_states, h0g)
    s_before = jnp.concatenate([h0g[:, None], s_after[:, :-1]], axis=1)
    from_start = jnp.exp(a_cs).transpose(0, 1, 4, 2, 3)[..., None]
    y_off = jnp.einsum('bclgn,bcghpn->bclghp', cc, s_before) * from_start
    return (y_diag + y_off).reshape(n_b, t, n_h, p), s_after[:, -1].reshape(n_b, n_h, p, n_s)


def gla_chunk_scan(q, k, v, log_a, s0):
    n_b, t, n_h, dk = q.shape
    dv = v.shape[-1]
    cl = GLA_CHUNK
    nc = t // cl
    q = q.reshape(n_b, nc, cl, n_h, dk)
    k = k.reshape(n_b, nc, cl, n_h, dk)
    v = v.reshape(n_b, nc, cl, n_h, dv)
    bcum = jnp.cumsum(log_a.reshape(n_b, nc, cl, n_h, dk), axis=2)
    b_last = bcum[:, :, -1]
    q_in = q * jnp.exp(bcum)
    k_in = k * jnp.exp(-bcum)
    causal = jnp.tril(jnp.ones((cl, cl), dtype=bool))
    att = jnp.where(causal, jnp.einsum('bclhd,bcshd->bchls', q_in, k_in), 0.0)
    o = jnp.einsum('bchls,bcshv->bclhv', att, v)
    k_st = k * jnp.exp(b_last[:, :, None] - bcum)
    d_state = jnp.einsum('bclhd,bclhv->bchdv', k_st, v)
    s_after = linear_scan(jnp.exp(b_last)[..., None], d_state, s0)
    s_before = jnp.concatenate([s0[:, None], s_after[:, :-1]], axis=1)
    o = o + jnp.einsum('bclhd,bchdv->bclhv', q_in, s_before)
    return o.reshape(n_b, t, n_h, dv), s_after[:, -1]


def ssd_mixer(pc, pl, line_lat, conv_w, conv_b, dt_bias, a_log, d_skip, norm_g):
    def prep(p, line):
        z, xbc, dt_raw = p
        n_b, t, _ = z.shape
        xbc = jax.nn.silu(line_conv(xbc, conv_w, conv_b, line)).astype(jnp.float32)
        xs, bs, cs = jnp.split(xbc, [SSD_WIDTH, SSD_WIDTH + SSD_GROUPS * SSD_STATE], axis=-1)
        return (z, xs.reshape(n_b, t, SSD_HEADS, SSD_HEAD_DIM),
                bs.reshape(n_b, t, SSD_GROUPS, SSD_STATE), cs.reshape(n_b, t, SSD_GROUPS, SSD_STATE),
                dt_raw.astype(jnp.float32))

    zc, xc, bc, cc, dtc = prep(pc, pc[0].shape[1])
    zl, xl, bl, cl, dtl = prep(pl, line_lat)
    h0 = jnp.zeros((xl.shape[0], SSD_HEADS, SSD_HEAD_DIM, SSD_STATE), jnp.float32)
    ys = []
    for d in range(2):
        A = -jnp.exp(a_log[d].astype(jnp.float32))
        cols = slice(d * SSD_HEADS, (d + 1) * SSD_HEADS)
        dtc_d = jax.nn.softplus(dtc[..., cols] + dt_bias[d])
        dtl_d = jax.nn.softplus(dtl[..., cols] + dt_bias[d])
        y_c, h_c = ssd_chunk_scan(direction(xc, d), direction(dtc_d, d), A, direction(bc, d), direction(cc, d), h0)
        y_l, _ = ssd_chunk_scan(direction(xl, d), direction(dtl_d, d), A, direction(bl, d), direction(cl, d), h_c)
        skip = d_skip[d].astype(jnp.float32)[:, None]
        ys.append((direction(y_c, d) + skip * xc, direction(y_l, d) + skip * xl))

    def out(y, z):
        n_b, t = z.shape[:2]
        return rms_norm(y.reshape(n_b, t, SSD_WIDTH) * jax.nn.silu(z.astype(jnp.float32)), norm_g)

    return out(ys[0][0] + ys[1][0], zc), out(ys[0][1] + ys[1][1], zl)


def rglru_mixer(pc, pl, line_lat, conv_w, conv_b, w_a, b_a, w_x, b_x, lam):
    xc_raw, gc = pc
    xl_raw, gl = pl
    xc = line_conv(xc_raw, conv_w, conv_b, xc_raw.shape[1]).astype(jnp.float32)
    xl = line_conv(xl_raw, conv_w, conv_b, line_lat).astype(jnp.float32)

    def coeffs(x, d):
        xb = x.reshape(x.shape[:-1] + (LRU_BLOCKS, LRU_BLOCK))
        r = jax.nn.sigmoid(jnp.einsum('btnk,nkj->btnj', xb, w_a[d]).reshape(x.shape) + b_a[d])
        i = jax.nn.sigmoid(jnp.einsum('btnk,nkj->btnj', xb, w_x[d]).reshape(x.shape) + b_x[d])
        log_a = -LRU_C * r * jax.nn.softplus(-lam[d])
        return jnp.exp(log_a), jnp.sqrt(-jnp.expm1(2.0 * log_a)) * (i * x)

    h0 = jnp.zeros((xl.shape[0], LRU_WIDTH), jnp.float32)
    hs = []
    for d in range(2):
        ac, bcoef = coeffs(direction(xc, d), d)
        al, blcoef = coeffs(direction(xl, d), d)
        h_c = linear_scan(ac, bcoef, h0)
        h_l = linear_scan(al, blcoef, h_c[:, -1])
        hs.append((direction(h_c, d), direction(h_l, d)))
    y_c = (hs[0][0] + hs[1][0]).astype(gc.dtype) * jax.nn.gelu(gc)
    y_l = (hs[0][1] + hs[1][1]).astype(gl.dtype) * jax.nn.gelu(gl)
    return y_c, y_l


def gla_mixer(pc, pl, w_gate, b_gate, norm_g):
    def prep(p):
        q, k, v, g, a_lr = p
        n_b, t, _ = q.shape
        qh = q.astype(jnp.float32).reshape(n_b, t, GLA_HEADS, GLA_DK) * (GLA_DK ** -0.5)
        kh = k.astype(jnp.float32).reshape(n_b, t, GLA_HEADS, GLA_DK)
        vh = v.astype(jnp.float32).reshape(n_b, t, GLA_HEADS, GLA_DV)
        return qh, kh, vh, g, a_lr

    def log_decay(a_lr, d):
        z = a_lr[..., d * GLA_GATE_RANK:(d + 1) * GLA_GATE_RANK] @ w_gate[d] + b_gate[d]
        return (jax.nn.log_sigmoid(z.astype(jnp.float32)) / GLA_TAU).reshape(a_lr.shape[:2] + (GLA_HEADS, GLA_DK))

    qc, kc, vc, gc, ac = prep(pc)
    ql, kl, vl, gl, al = prep(pl)
    s0 = jnp.zeros((ql.shape[0], GLA_HEADS, GLA_DK, GLA_DV), jnp.float32)
    os_ = []
    for d in range(2):
        o_c, s_c = gla_chunk_scan(direction(qc, d), direction(kc, d), direction(vc, d), direction(log_decay(ac, d), d), s0)
        o_l, _ = gla_chunk_scan(direction(ql, d), direction(kl, d), direction(vl, d), direction(log_decay(al, d), d), s_c)
        os_.append((direction(o_c, d), direction(o_l, d)))

    def out(o, g):
        n_b, t = g.shape[:2]
        return rms_norm(o, norm_g).reshape(n_b, t, GLA_VAL) * jax.nn.silu(g)

    return out(os_[0][0] + os_[1][0], gc), out(os_[0][1] + os_[1][1], gl)


def merge_branches(branches, gate_logits, w_branch, w_out):
    n_b, t, _ = gate_logits.shape
    gates = jax.nn.sigmoid(gate_logits.reshape(n_b, t, N_BRANCH, D_MODEL))
    m = gates[:, :, 0] * (branches[0] @ w_branch[0])
    for n in range(1, N_BRANCH):
        m = m + gates[:, :, n] * (branches[n] @ w_branch[n])
    return m @ w_out


def token_mixer(u_ctx, u_lat, line_lat, need_ctx, w_in, ssd_conv_w, ssd_conv_b, ssd_dt_bias, ssd_a_log, ssd_d,
                ssd_norm_g, lru_conv_w, lru_conv_b, lru_w_a, lru_b_a, lru_w_x, lru_b_x, lru_lam,
                gla_w_gate, gla_b_gate, gla_norm_g, w_branch, w_out):
    offsets = [int(o) for o in np.cumsum(IN_SIZES)[:-1]]
    pc = jnp.split(u_ctx @ w_in, offsets, axis=-1)
    pl = jnp.split(u_lat @ w_in, offsets, axis=-1)
    ssd_c, ssd_l = ssd_mixer(pc[0:3], pl[0:3], line_lat, ssd_conv_w, ssd_conv_b, ssd_dt_bias, ssd_a_log, ssd_d, ssd_norm_g)
    lru_c, lru_l = rglru_mixer(pc[3:5], pl[3:5], line_lat, lru_conv_w, lru_conv_b, lru_w_a, lru_b_a, lru_w_x, lru_b_x, lru_lam)
    gla_c, gla_l = gla_mixer(pc[5:10], pl[5:10], gla_w_gate, gla_b_gate, gla_norm_g)
    y_lat = merge_branches((ssd_l, lru_l, gla_l), pl[10], w_branch, w_out)
    y_ctx = merge_branches((ssd_c, lru_c, gla_c), pc[10], w_branch, w_out) if need_ctx else None
    return y_ctx, y_lat


def setup_inputs(seed: int = 0) -> dict:
    key = jax.random.key(seed)
    keys = list(jax.random.split(key, 40))

    def nrm(shape, scale):
        return jax.random.normal(keys.pop(), shape, jnp.float32) * scale

    def unif(shape, lo, hi):
        return jax.random.uniform(keys.pop(), shape, jnp.float32, lo, hi)

    L, D = DEPTH, D_MODEL
    dt = jnp.exp(unif((L, 2, SSD_HEADS), math.log(DT_MIN), math.log(DT_MAX)))
    a_pow = unif((L, 2, LRU_WIDTH), 0.9, 0.999)
    a_lru = a_pow ** (1.0 / LRU_C)
    return {
        'x': nrm((BATCH, SEQ, D), 1.0),
        'c': nrm((BATCH, D), 1.0),
        'ctx': nrm((BATCH, CTX_LEN, D), 1.0),
        'c_ctx': nrm((D,), 1.0),
        'w_ada': nrm((L, D, N_MOD * D), 0.5 * D ** -0.5),
        'b_ada': nrm((L, N_MOD * D), 0.01),
        'ln_g': 1.0 + nrm((L, 3, D), 0.02),
        'ln_b': nrm((L, 3, D), 0.02),
        'ffn_w_up': nrm((L, 2, D, 2 * D_FF), D ** -0.5),
        'ffn_w_down': nrm((L, 2, D_FF, D), BETA * D_FF ** -0.5),
        'w_in': nrm((L, D, IN_TOTAL), D ** -0.5),
        'ssd_conv_w': nrm((L, CONV_W, SSD_CONV_CH), CONV_W ** -0.5),
        'ssd_conv_b': nrm((L, SSD_CONV_CH), 0.02),
        'ssd_dt_bias': dt + jnp.log(-jnp.expm1(-dt)),
        'ssd_a_log': jnp.log(unif((L, 2, SSD_HEADS), 1.0, 16.0)),
        'ssd_d': 1.0 + nrm((L, 2, SSD_HEADS), 0.1),
        'ssd_norm_g': 1.0 + nrm((L, SSD_WIDTH), 0.02),
        'lru_conv_w': nrm((L, CONV_W, LRU_WIDTH), CONV_W ** -0.5),
        'lru_conv_b': nrm((L, LRU_WIDTH), 0.02),
        'lru_w_a': nrm((L, 2, LRU_BLOCKS, LRU_BLOCK, LRU_BLOCK), LRU_BLOCK ** -0.5),
        'lru_b_a': nrm((L, 2, LRU_WIDTH), 0.02),
        'lru_w_x': nrm((L, 2, LRU_BLOCKS, LRU_BLOCK, LRU_BLOCK), LRU_BLOCK ** -0.5),
        'lru_b_x': nrm((L, 2, LRU_WIDTH), 0.02),
        'lru_lam': jnp.log(a_lru) - jnp.log1p(-a_lru),
        'gla_w_gate': nrm((L, 2, GLA_GATE_RANK, GLA_KEY), GLA_GATE_RANK ** -0.5),
        'gla_b_gate': nrm((L, 2, GLA_KEY), 0.02),
        'gla_norm_g': 1.0 + nrm((L, GLA_DV), 0.02),
        'w_branch': nrm((L, N_BRANCH, BRANCH_W, D), BRANCH_W ** -0.5),
        'w_out': nrm((L, D, D), BETA * D ** -0.5),
    }


def reference(x, c, ctx, c_ctx, w_ada, b_ada, ln_g, ln_b, ffn_w_up, ffn_w_down, w_in, ssd_conv_w, ssd_conv_b,
              ssd_dt_bias, ssd_a_log, ssd_d, ssd_norm_g, lru_conv_w, lru_conv_b, lru_w_a, lru_b_a, lru_w_x, lru_b_x,
              lru_lam, gla_w_gate, gla_b_gate, gla_norm_g, w_branch, w_out):
    n_b, t_lat, _ = x.shape
    rows = t_lat // GRID_W
    s_lat = jax.nn.silu(c)
    s_ctx = jax.nn.silu(c_ctx)
    for l in range(DEPTH):
        last = l == DEPTH - 1
        m_lat = (s_lat @ w_ada[l] + b_ada[l]).reshape(n_b, 1, N_MOD, D_MODEL)
        m_ctx = (s_ctx @ w_ada[l] + b_ada[l]).reshape(1, 1, N_MOD, D_MODEL)
        x = ffn_sublayer(x, m_lat, 0, ffn_w_up[l, 0], ffn_w_down[l, 0], ln_g[l, 0], ln_b[l, 0])
        ctx = ffn_sublayer(ctx, m_ctx, 0, ffn_w_up[l, 0], ffn_w_down[l, 0], ln_g[l, 0], ln_b[l, 0])
        u_lat = modulate(x, m_lat[:, :, 3], m_lat[:, :, 4])
        u_ctx = modulate(ctx, m_ctx[:, :, 3], m_ctx[:, :, 4])
        col_major = l % 2 == 1
        line = rows if col_major else GRID_W
        if col_major:
            u_lat = raster_to_colmajor(u_lat, rows)
        y_ctx, y_lat = token_mixer(u_ctx, u_lat, line, not last, w_in[l], ssd_conv_w[l], ssd_conv_b[l], ssd_dt_bias[l],
                                   ssd_a_log[l], ssd_d[l], ssd_norm_g[l], lru_conv_w[l], lru_conv_b[l], lru_w_a[l],
                                   lru_b_a[l], lru_w_x[l], lru_b_x[l], lru_lam[l], gla_w_gate[l], gla_b_gate[l],
                                   gla_norm_g[l], w_branch[l], w_out[l])
        if col_major:
            y_lat = colmajor_to_raster(y_lat, rows)
        x = post_norm(x, y_lat, m_lat[:, :, 5], 1.0, ln_g[l, 1], ln_b[l, 1])
        if not last:
            ctx = post_norm(ctx, y_ctx, m_ctx[:, :, 5], 1.0, ln_g[l, 1], ln_b[l, 1])
            ctx = ffn_sublayer(ctx, m_ctx, 2, ffn_w_up[l, 1], ffn_w_down[l, 1], ln_g[l, 2], ln_b[l, 2])
        x = ffn_sublayer(x, m_lat, 2, ffn_w_up[l, 1], ffn_w_down[l, 1], ln_g[l, 2], ln_b[l, 2])
    return x
```

```python
import numpy as np
import concourse.bass as bass
import concourse.mybir as mybir
from concourse.ap import AP

F32 = mybir.dt.float32
BF16 = mybir.dt.bfloat16
AF = mybir.ActivationFunctionType
ALU = mybir.AluOpType
AX = mybir.AxisListType

PE, ACT, DVE, POOL, SP = "pe", "act", "dve", "pool", "sp"
COMPUTE = (PE, ACT, DVE, POOL)
NDMASEM = 6


class Buf:
    __slots__ = ("name", "last_w", "readers")

    def __init__(self, name=""):
        self.name = name
        self.last_w = None
        self.readers = {}


class Op:
    __slots__ = ("eng", "fn", "deps", "idx", "dma", "signal", "sigval", "sem", "semval", "tag")


class Sched:
    def __init__(self, nc):
        self.nc = nc
        self.ops = {e: [] for e in (PE, ACT, DVE, POOL, SP)}
        self.n_dma = {SP: 0, POOL: 0, ACT: 0}

    def add(self, eng, fn, reads=(), writes=(), dma=False, tag=None):
        op = Op()
        op.eng, op.fn, op.dma, op.signal, op.tag = eng, fn, dma, False, tag
        op.idx = len(self.ops[eng])
        deps = {}

        def dep(d, kind):
            if d is None or d is op:
                return
            if d.eng == eng and not d.dma:
                if eng == PE or eng == SP:
                    return
                if kind == "WAR":
                    return
            key = id(d) if d.dma else d.eng
            cur = deps.get(key)
            if cur is None or (not d.dma and d.idx > cur.idx):
                deps[key] = d

        for b in reads:
            dep(b.last_w, "RAW")
        for b in writes:
            dep(b.last_w, "WAW")
            for r in b.readers.values():
                if isinstance(r, list):
                    for rr in r:
                        dep(rr, "WAR")
                else:
                    dep(r, "WAR")
        for b in reads:
            if dma:
                b.readers.setdefault("dma", []).append(op)
            else:
                b.readers[eng] = op
        for b in writes:
            b.last_w = op
            b.readers = {}
        op.deps = list(deps.values())
        for d in op.deps:
            d.signal = True
        self.ops[eng].append(op)
        return op

    def pe(self, fn, reads=(), writes=()):
        return self.add(PE, fn, reads, writes)

    def act(self, fn, reads=(), writes=()):
        return self.add(ACT, fn, reads, writes)

    def dve(self, fn, reads=(), writes=()):
        return self.add(DVE, fn, reads, writes)

    def pool(self, fn, reads=(), writes=()):
        return self.add(POOL, fn, reads, writes)

    def dma(self, q, out, in_, reads=(), writes=(), **kw):
        return self.add(q, lambda e: e.dma_start(out=out, in_=in_, **kw), reads, writes, dma=True)

    def emit(self, final_wait_all_dma=True):
        nc = self.nc
        gs = GSYNC[0]
        esem, dsem = gs["esem"], gs["dsem"]
        for e in COMPUTE:
            c = gs["ebase"][e]
            for op in self.ops[e]:
                if op.dma:
                    continue
                if op.signal:
                    c += 1
                    op.sigval = c
            gs["ebase"][e] = c
        for q in (SP, POOL, ACT):
            k = gs["dk"][q]
            vals = gs["dvals"][q]
            for op in self.ops[q]:
                if op.dma:
                    s = k % NDMASEM
                    k += 1
                    vals[s] += 16
                    op.sem = dsem[q][s]
                    op.semval = vals[s]
            gs["dk"][q] = k
        engobj = {PE: "tensor", ACT: "scalar", DVE: "vector", POOL: "gpsimd", SP: "sync"}
        with nc.Block() as block:

            def run(e, eng):
                waited = {}

                def wait(sem, val):
                    k = id(sem)
                    if waited.get(k, 0) >= val:
                        return
                    waited[k] = val
                    eng.wait_ge(sem, val)

                for op in self.ops[e]:
                    for d in op.deps:
                        if d.dma:
                            wait(d.sem, d.semval)
                        else:
                            wait(esem[d.eng], d.sigval)
                    if op.dma:
                        if op.semval > 16:
                            wait(op.sem, op.semval - 16)
                        ins = op.fn(eng)
                        ins.then_inc(op.sem, 16)
                    else:
                        ins = op.fn(eng)
                        if op.signal:
                            ins.then_inc(esem[e], 1)
                if final_wait_all_dma:
                    last = {}
                    for op in self.ops[e]:
                        if op.dma:
                            last[id(op.sem)] = (op.sem, op.semval)
                    for sem, val in last.values():
                        wait(sem, val)

            for e in (PE, ACT, DVE, POOL, SP):
                if not self.ops[e]:
                    continue
                getattr(block, engobj[e])(lambda eng, e=e: run(e, eng))


GSYNC = [None]


def init_gsync(nc, st):
    gs = {"esem": {e: st.enter_context(nc.semaphore("s_" + e)) for e in COMPUTE}, "dsem": {},
          "ebase": {e: 0 for e in COMPUTE}, "dk": {}, "dvals": {}}
    for q in (SP, POOL, ACT):
        gs["dsem"][q] = [st.enter_context(nc.semaphore("d_%s%d" % (q, i))) for i in range(NDMASEM)]
        gs["dk"][q] = 0
        gs["dvals"][q] = [0] * NDMASEM
    GSYNC[0] = gs

import contextlib

NL, D = 4, 1024
NB = 2
CTX, SEQ = 256, 2048
TB = CTX + SEQ
TT_ = NB * TB
DFF = 2816
ALPHA = 8.0 ** 0.25
EPS = 1e-5
EPSP = EPS / (ALPHA * ALPHA)
IN_TOTAL = 11328
OFF_Z, OFF_XBC, OFF_DT, OFF_LX, OFF_LG = 0, 1024, 3072, 3104, 4128
OFF_Q, OFF_K, OFF_V, OFF_G, OFF_ALR, OFF_GATE = 5152, 5664, 6176, 7200, 8224, 8256
TS_ = 256
TILES = []
for _b in range(NB):
    TILES.append((_b, 0, 256, 2))
    for _i in range(SEQ // TS_):
        TILES.append((_b, CTX + _i * TS_, TS_, _b))


def MM(S, out, lhsT, rhs, start, stop, R, W):
    return S.pe(lambda e: e.matmul(out, lhsT, rhs, start=start, stop=stop), R, W)


def TR(S, out, in_, ident, R, W):
    return S.pe(lambda e: e.transpose(out, in_, ident), R, W)


def ACTV(S, out, in_, func, R, W, bias=None, scale=None):
    kw = {}
    if bias is not None:
        kw["bias"] = bias
    if scale is not None:
        kw["scale"] = scale
    return S.act(lambda e: e.activation(out=out, in_=in_, func=func, **kw), R, W)


def TT(S, eng, out, in0, in1, op, R, W):
    return S.add(eng, lambda e: e.tensor_tensor(out=out, in0=in0, in1=in1, op=op), R, W)


def TSC(S, eng, out, in0, s1, s2, op0, op1, R, W):
    if s2 is None:
        return S.add(eng, lambda e: e.tensor_scalar(out=out, in0=in0, scalar1=s1, scalar2=None, op0=op0), R, W)
    return S.add(eng, lambda e: e.tensor_scalar(out=out, in0=in0, scalar1=s1, scalar2=s2, op0=op0, op1=op1), R, W)


def STT(S, out, in0, scalar, in1, op0, op1, R, W):
    return S.dve(lambda e: e.scalar_tensor_tensor(out=out, in0=in0, scalar=scalar, in1=in1, op0=op0, op1=op1), R, W)


def COPY(S, eng, out, in_, R, W):
    if eng == ACT:
        return S.act(lambda e: e.activation(out=out, in_=in_, func=AF.Identity), R, W)
    return S.add(eng, lambda e: e.tensor_copy(out=out, in_=in_), R, W)


def bc(ap, shape):
    return ap.to_broadcast(list(shape))


DEBUG_OUT = [False]
_UID = [0]


def uname(n):
    _UID[0] += 1
    return '%s_u%d' % (n, _UID[0])


class G:
    pass


def declare_io(nc, g):
    def din(name, shape):
        return nc.dram_tensor(name, list(shape), F32, kind="ExternalInput").ap()
    g.x = din("x", [NB, SEQ, D])
    g.c = din("c", [NB, D])
    g.ctx = din("ctx", [NB, CTX, D])
    g.c_ctx = din("c_ctx", [1, D])
    g.w_ada = din("w_ada", [NL, D, 9 * D])
    g.b_ada = din("b_ada", [NL, 9 * D])
    g.ln_g = din("ln_g", [NL, 3, D])
    g.ln_b = din("ln_b", [NL, 3, D])
    g.ffn_w_up = din("ffn_w_up", [NL, 2, D, 2 * DFF])
    g.ffn_w_down = din("ffn_w_down", [NL, 2, DFF, D])
    g.w_in = din("w_in", [NL, D, IN_TOTAL])
    g.ssd_conv_w = din("ssd_conv_w", [NL, 4, 2048])
    g.ssd_conv_b = din("ssd_conv_b", [NL, 2048])
    g.ssd_dt_bias = din("ssd_dt_bias", [NL, 2, 16])
    g.ssd_a_log = din("ssd_a_log", [NL, 2, 16])
    g.ssd_d = din("ssd_d", [NL, 2, 16])
    g.ssd_norm_g = din("ssd_norm_g", [NL, 1024])
    g.lru_conv_w = din("lru_conv_w", [NL, 4, 1024])
    g.lru_conv_b = din("lru_conv_b", [NL, 1024])
    g.lru_w_a = din("lru_w_a", [NL, 2, 16, 64, 64])
    g.lru_b_a = din("lru_b_a", [NL, 2, 1024])
    g.lru_w_x = din("lru_w_x", [NL, 2, 16, 64, 64])
    g.lru_b_x = din("lru_b_x", [NL, 2, 1024])
    g.lru_lam = din("lru_lam", [NL, 2, 1024])
    g.gla_w_gate = din("gla_w_gate", [NL, 2, 16, 512])
    g.gla_b_gate = din("gla_b_gate", [NL, 2, 512])
    g.gla_norm_g = din("gla_norm_g", [NL, 256])
    g.w_branch = din("w_branch", [NL, 3, 1024, 1024])
    g.w_out = din("w_out", [NL, 1024, 1024])
    g.out = nc.dram_tensor("out", [NB, SEQ, D], F32, kind="ExternalOutput").ap()
    kd = "ExternalOutput" if DEBUG_OUT[0] else "Internal"
    g.HT = nc.dram_tensor("HT", [128, 8, TT_], F32, kind=kd).ap()
    g.Y = [nc.dram_tensor("Y%d" % i, [128, 8, TT_], BF16, kind=kd).ap() for i in range(3)]


INPUT_NAMES = ["x", "c", "ctx", "c_ctx", "w_ada", "b_ada", "ln_g", "ln_b", "ffn_w_up", "ffn_w_down", "w_in",
               "ssd_conv_w", "ssd_conv_b", "ssd_dt_bias", "ssd_a_log", "ssd_d", "ssd_norm_g", "lru_conv_w",
               "lru_conv_b", "lru_w_a", "lru_b_a", "lru_w_x", "lru_b_x", "lru_lam", "gla_w_gate", "gla_b_gate",
               "gla_norm_g", "w_branch", "w_out"]

CF = {}
_o = 0
for _n, _sz in [("ln_g", NL * 3 * 8), ("ln_b", NL * 3 * 8), ("bada", NL * 9 * 8), ("scw", NL * 4 * 16),
                ("scb", NL * 16), ("sng", NL * 8), ("lcw", NL * 4 * 8), ("lcb", NL * 8), ("lba", NL * 16),
                ("lbx", NL * 16), ("llam", NL * 16), ("c", 16), ("cctx", 8), ("lsp8", NL * 16), ("lsp16", NL * 16),
                ("lsp24", NL * 16)]:
    CF[_n] = _o
    _o += _sz
NCF = _o


def mod_ap(g, l, j, kc, cls):
    i = (((l * 9 + j) * 8) + kc) * 4 + cls
    return g.modT[:, i:i + 1]


def cf(g, name, idx):
    o = CF[name] + idx
    return g.constf[:, o:o + 1]


def phase_const(nc, g):
    with contextlib.ExitStack() as st:
        sb = lambda n, s, d: st.enter_context(nc.sbuf_tensor(uname(n), s, d))
        rowbuf = [sb("rowbuf%d" % i, [128, 128], F32) for i in range(2)]
        wbuf = [sb("wadab%d" % i, [128, 8, 1024], BF16) for i in range(2)]
        sT = sb("sT", [128, 8, 4], BF16)
        tmpm = sb("tmpm", [128, 128], F32)
        pst = [st.enter_context(nc.psum_tensor(uname("pst%d" % i), [128, 512], F32)) for i in range(4)]
        S = Sched(nc)
        Bm = Buf("masks")
        Brow = [Buf(), Buf()]
        Bw = [Buf(), Buf()]
        Bps = [Buf() for _ in range(4)]
        Bcf, BsT, Bmod, Btm = Buf("cf"), Buf(), Buf("mod"), Buf()
        g.Bconst = Buf("constall")
        M = g.masks

        def amask(dst, cm, step, base, op):
            S.pool(lambda e: e.memset(dst, 1.0), (), [Bm])
            S.pool(lambda e: e.affine_select(out=dst, in_=dst, compare_op=op, fill=0.0, base=base,
                                             pattern=[[step, 128]], channel_multiplier=cm), [Bm], [Bm])

        S.pool(lambda e: e.memset(M["ONES"][:], 1.0), (), [Bm])
        S.pool(lambda e: e.memset(g.onesb[:], 1.0), (), [Bm])
        amask(M["LE"][:], -1, 1, 0, ALU.is_ge)
        amask(M["GT"][:], 1, -1, 0, ALU.is_gt)
        amask(M["LT"][:], -1, 1, 0, ALU.is_gt)
        amask(M["GE"][:], 1, -1, 0, ALU.is_ge)
        amask(M["ID"][:], 1, -1, 0, ALU.is_equal)
        S.pool(lambda e: e.memset(M["BD"][:], 0.0), (), [Bm])
        S.pool(lambda e: e.memset(M["BD"][0:64, 0:64], 1.0), [Bm], [Bm])
        S.pool(lambda e: e.memset(M["BD"][64:128, 64:128], 1.0), [Bm], [Bm])
        for nm in ("LE", "GT", "LT", "GE"):
            TT(S, POOL, M[nm + "64"][:], M[nm][:], M["BD"][:], ALU.mult, [Bm], [Bm])

        items = [
            ("ln_g", g.ln_g.rearrange("l i (k p) -> (l i k) p", p=128)),
            ("ln_b", g.ln_b.rearrange("l i (k p) -> (l i k) p", p=128)),
            ("bada", g.b_ada.rearrange("l (j p) -> (l j) p", p=128)),
            ("scw", g.ssd_conv_w.rearrange("l k (c p) -> (l k c) p", p=128)),
            ("scb", g.ssd_conv_b.rearrange("l (c p) -> (l c) p", p=128)),
            ("sng", g.ssd_norm_g.rearrange("l (c p) -> (l c) p", p=128)),
            ("lcw", g.lru_conv_w.rearrange("l k (c p) -> (l k c) p", p=128)),
            ("lcb", g.lru_conv_b.rearrange("l (c p) -> (l c) p", p=128)),
            ("lba", g.lru_b_a.rearrange("l d (c p) -> (l d c) p", p=128)),
            ("lbx", g.lru_b_x.rearrange("l d (c p) -> (l d c) p", p=128)),
            ("llam", g.lru_lam.rearrange("l d (c p) -> (l d c) p", p=128)),
            ("c", g.c.rearrange("b (c p) -> (b c) p", p=128)),
            ("cctx", g.c_ctx.rearrange("b (c p) -> (b c) p", p=128)),
        ]
        k = 0
        for nm, ap in items:
            R = ap.shape[0]
            r0 = 0
            while r0 < R:
                nr = min(128, R - r0)
                i = k % 2
                k += 1
                S.dma(SP, rowbuf[i][0:nr, :], ap[r0:r0 + nr, :], (), [Brow[i]])
                TR(S, pst[i][:, 0:nr], rowbuf[i][0:nr, :], M["ID"][0:nr, 0:nr], [Brow[i], Bm], [Bps[i]])
                o = CF[nm] + r0
                COPY(S, DVE, g.constf[:, o:o + nr], pst[i][:, 0:nr], [Bps[i]], [Bcf])
                r0 += nr
        n16 = NL * 16
        lam = g.constf[:, CF["llam"]:CF["llam"] + n16]
        ACTV(S, tmpm[:, 0:n16], lam, AF.Exp, [Bcf], [Btm], scale=-1.0)
        ACTV(S, tmpm[:, 0:n16], tmpm[:, 0:n16], AF.Ln, [Btm], [Btm], bias=1.0)
        for nm, sc in (("lsp8", -8.0), ("lsp16", -16.0), ("lsp24", -16.0 / 24.0)):
            TSC(S, DVE, g.constf[:, CF[nm]:CF[nm] + n16], tmpm[:, 0:n16], sc, None, ALU.mult, None, [Btm], [Bcf])
        S.dma(SP, g.rb_dtb[:], g.ssd_dt_bias.rearrange("l d h -> (l d h)").partition_broadcast(128), (), [Bcf])
        S.dma(SP, g.rb_A[:], g.ssd_a_log.rearrange("l d h -> (l d h)").partition_broadcast(128), (), [Bcf])
        S.dma(SP, g.rb_D[:], g.ssd_d.rearrange("l d h -> (l d h)").partition_broadcast(128), (), [Bcf])
        ACTV(S, g.rb_A[:], g.rb_A[:], AF.Exp, [Bcf], [Bcf])
        TSC(S, DVE, g.rb_A[:], g.rb_A[:], -1.0, None, ALU.mult, None, [Bcf], [Bcf])
        rbD = g.rb_D[:].rearrange("p (l d h) -> p l d h", l=NL, d=2)
        TT(S, DVE, g.rb_Ds[:].rearrange("p (l h) -> p l h", l=NL), rbD[:, :, 0, :], rbD[:, :, 1, :], ALU.add, [Bcf], [Bcf])
        S.pool(lambda e: e.memset(sT[:], 0.0), (), [BsT])
        for b in range(NB):
            o = CF["c"] + b * 8
            ACTV(S, sT[:, :, b], g.constf[:, o:o + 8], AF.Silu, [Bcf, BsT], [BsT])
        o = CF["cctx"]
        ACTV(S, sT[:, :, 2], g.constf[:, o:o + 8], AF.Silu, [Bcf, BsT], [BsT])
        k = 0
        for l in range(NL):
            wv = g.w_ada[l].rearrange("(k p) n -> p k n", p=128)
            for j in range(9):
                i = k % 2
                pi = 2 + (k % 2)
                k += 1
                S.dma(POOL, wbuf[i][:], wv[:, :, j * 1024:(j + 1) * 1024], (), [Bw[i]])
                for oc in range(8):
                    for kc in range(8):
                        MM(S, pst[pi][:, oc * 4:oc * 4 + 4], wbuf[i][:, kc, oc * 128:(oc + 1) * 128], sT[:, kc, :],
                           kc == 0, kc == 7, [Bw[i], BsT], [Bps[pi]])
                mo = ((l * 9 + j) * 8) * 4
                bo = CF["bada"] + (l * 9 + j) * 8
                TT(S, DVE, g.modT[:, mo:mo + 32].rearrange("p (k c) -> p k c", c=4),
                   pst[pi][:, 0:32].rearrange("p (k c) -> p k c", c=4),
                   bc(g.constf[:, bo:bo + 8].unsqueeze(2), [128, 8, 4]), ALU.add, [Bps[pi], Bcf], [Bmod])
        mv = g.modT[:].rearrange("p (l j r) -> p l j r", l=NL, j=9)
        for j in (1, 4, 7):
            TSC(S, DVE, mv[:, :, j, :], mv[:, :, j, :], 1.0, None, ALU.add, None, [Bmod], [Bmod])
        for j, sc in ((2, 0.5 / ALPHA), (8, 0.5 / ALPHA), (5, 1.0 / ALPHA)):
            TSC(S, DVE, mv[:, :, j, :], mv[:, :, j, :], sc, None, ALU.mult, None, [Bmod], [Bmod])
        S.emit()
    nc.all_engine_barrier()


def phase_p0(nc, g):
    with contextlib.ExitStack() as st:
        sb = lambda n, s, d: st.enter_context(nc.sbuf_tensor(uname(n), s, d))
        tin = [sb("p0in%d" % i, [128, 1024], F32) for i in range(3)]
        stg = [sb("p0st%d" % i, [128, 8, 512], F32) for i in range(2)]
        ps = [st.enter_context(nc.psum_tensor(uname("p0ps%d" % i), [128, 512], F32)) for i in range(4)]
        S = Sched(nc)
        Bin = [Buf() for _ in range(3)]
        Bst = [Buf(), Buf()]
        Bps = [Buf() for _ in range(4)]
        Bc = g.Bconst
        k = 0
        gi = 0
        for b in range(NB):
            groups = [(g.ctx[b], 0, 256)] + [(g.x[b, i * 512:(i + 1) * 512, :], CTX + i * 512, 512) for i in range(4)]
            for src, s0, n in groups:
                sg = gi % 2
                gi += 1
                for t in range(n // 128):
                    i = k % 3
                    S.dma(SP, tin[i][:], src[t * 128:(t + 1) * 128, :], (), [Bin[i]])
                    for half in range(2):
                        pi = (2 * k + half) % 4
                        for q in range(4):
                            kc = half * 4 + q
                            TR(S, ps[pi][:, q * 128:(q + 1) * 128], tin[i][:, kc * 128:(kc + 1) * 128], g.masks["ID"][:],
                               [Bin[i], Bc], [Bps[pi]])
                        COPY(S, ACT if half == 0 else DVE, stg[sg][:, half * 4:half * 4 + 4, t * 128:(t + 1) * 128],
                             ps[pi][:].rearrange("p (q t) -> p q t", q=4), [Bps[pi]], [Bst[sg]])
                    k += 1
                col = b * TB + s0
                S.dma(SP, g.HT[:, :, col:col + n], stg[sg][:, :, 0:n], [Bst[sg]], ())
        S.emit()
    nc.all_engine_barrier()


def phase_final(nc, g):
    with contextlib.ExitStack() as st:
        sb = lambda n, s, d: st.enter_context(nc.sbuf_tensor(uname(n), s, d))
        hin = [sb("pfin%d" % i, [128, 8, 512], F32) for i in range(2)]
        to = [sb("pfo%d" % i, [128, 1024], F32) for i in range(3)]
        ps = [st.enter_context(nc.psum_tensor(uname("pfps%d" % i), [128, 512], F32)) for i in range(4)]
        S = Sched(nc)
        Bin = [Buf(), Buf()]
        Bo = [Buf() for _ in range(3)]
        Bps = [Buf() for _ in range(4)]
        Bc = g.Bconst
        k = 0
        gi = 0
        for b in range(NB):
            for i4 in range(4):
                sg = gi % 2
                gi += 1
                col = b * TB + CTX + i4 * 512
                S.dma(SP, hin[sg][:], g.HT[:, :, col:col + 512], (), [Bin[sg]])
                for t in range(4):
                    oi = k % 3
                    for half in range(2):
                        pi = (2 * k + half) % 4
                        for q in range(4):
                            kc = half * 4 + q
                            TR(S, ps[pi][:, q * 128:(q + 1) * 128], hin[sg][:, kc, t * 128:(t + 1) * 128], g.masks["ID"][:],
                               [Bin[sg], Bc], [Bps[pi]])
                        COPY(S, ACT if half == 0 else DVE, to[oi][:, half * 512:(half + 1) * 512], ps[pi][:], [Bps[pi]], [Bo[oi]])
                    r0 = i4 * 512 + t * 128
                    S.dma(SP, g.out[b, r0:r0 + 128, :], to[oi][:], [Bo[oi]], ())
                    k += 1
        S.emit()
    nc.all_engine_barrier()


def load_weight_cast(S, dst3, src2, nk, ncols, Bw, piece=1024):
    sv = src2.rearrange("(k p) n -> p k n", p=128)
    for kc in range(nk):
        c0 = 0
        while c0 < ncols:
            cn = min(piece, ncols - c0)
            S.dma(POOL, dst3[:, kc, c0:c0 + cn], sv[:, kc, c0:c0 + cn], (), [Bw[kc]])
            c0 += cn


def ln_part1(S, g, zt, Bz, n, W):
    zbf, sq, Bzs = W["zbf"], W["sq"], W["Bzs"]
    COPY(S, ACT, zbf[:, :, 0:n], zt[:, :, 0:n], [Bz], [Bzs])
    ACTV(S, sq[:, :, 0:n], zt[:, :, 0:n], AF.Square, [Bz], [Bzs])


def ln_part2(S, g, l, i, zt, Bz, n, W):
    zbf, sq, Bzs = W["zbf"], W["sq"], W["Bzs"]
    psm, psq, Bpm, Bpq = W["psm"], W["psq"], W["Bpm"], W["Bpq"]
    mean, msq, var, Bsm = W["mean"], W["msq"], W["var"], W["Bsm"]
    Bc = g.Bconst
    for kc in range(8):
        MM(S, psm[:, 0:n], g.onesb[:], zbf[:, kc, 0:n], kc == 0, kc == 7, [Bzs, Bc], [Bpm])
    for kc in range(8):
        MM(S, psq[:, 0:n], g.onesb[:], sq[:, kc, 0:n], kc == 0, kc == 7, [Bzs, Bc], [Bpq])
    ACTV(S, mean[:, 0:n], psm[:, 0:n], AF.Identity, [Bpm], [Bsm], scale=1.0 / 1024)
    ACTV(S, msq[:, 0:n], psm[:, 0:n], AF.Square, [Bpm], [Bsm], scale=1.0 / 1024)
    STT(S, var[:, 0:n], psq[:, 0:n], 1.0 / 1024, msq[:, 0:n], ALU.mult, ALU.subtract, [Bpq, Bsm], [Bsm])
    TSC(S, DVE, var[:, 0:n], var[:, 0:n], EPSP, None, ALU.add, None, [Bsm], [Bsm])
    ACTV(S, var[:, 0:n], var[:, 0:n], AF.Sqrt, [Bsm], [Bsm])
    S.dve(lambda e: e.reciprocal(out=var[:, 0:n], in_=var[:, 0:n]), [Bsm], [Bsm])
    TT(S, DVE, zt[:, :, 0:n], zt[:, :, 0:n], bc(mean[:, 0:n].unsqueeze(1), [128, 8, n]), ALU.subtract, [Bz, Bsm], [Bz])
    TT(S, DVE, zt[:, :, 0:n], zt[:, :, 0:n], bc(var[:, 0:n].unsqueeze(1), [128, 8, n]), ALU.mult, [Bz, Bsm], [Bz])
    for kc in range(8):
        ACTV(S, zt[:, kc, 0:n], zt[:, kc, 0:n], AF.Identity, [Bz, Bc], [Bz],
             bias=cf(g, "ln_b", (l * 3 + i) * 8 + kc), scale=cf(g, "ln_g", (l * 3 + i) * 8 + kc))


def phase_ffn(nc, g, l, j, skip_ctx):
    n = TS_
    with contextlib.ExitStack() as st:
        sb = lambda nm, s, d: st.enter_context(nc.sbuf_tensor(uname(nm), s, d))
        wup = sb("wup", [128, 8, 2 * DFF], BF16)
        wdn = sb("wdn", [128, 22, D], BF16)
        hb = [sb("ffh%d" % i, [128, 8, n], F32) for i in range(3)]
        ub = [sb("ffu%d" % i, [128, 8, n], BF16) for i in range(2)]
        hid = sb("ffhid", [128, 22, n], BF16)
        sil = [sb("ffsil%d" % i, [128, n], F32) for i in range(2)]
        zbf = sb("ffzbf", [128, 8, n], BF16)
        sq = sb("ffsq", [128, 8, n], BF16)
        mean = sb("ffmean", [128, n], F32)
        msq = sb("ffmsq", [128, n], F32)
        var = sb("ffvar", [128, n], F32)
        ps = [st.enter_context(nc.psum_tensor(uname("ffps%d" % i), [128, 512], F32)) for i in range(8)]
        S = Sched(nc)
        Bc = g.Bconst
        Bwu = [Buf() for _ in range(8)]
        Bwd = [Buf() for _ in range(22)]
        Bh = [Buf(), Buf(), Buf()]
        Bu = [Buf(), Buf()]
        Bhid, Bzs, Bsm = Buf(), Buf(), Buf()
        Bsil = [Buf(), Buf()]
        Bps = [Buf() for _ in range(8)]
        load_weight_cast(S, wup, g.ffn_w_up[l, j], 8, 2 * DFF, Bwu, piece=1408)
        load_weight_cast(S, wdn, g.ffn_w_down[l, j], 22, D, Bwd)
        W = dict(zbf=zbf, sq=sq, Bzs=Bzs, psm=ps[6], psq=ps[7], Bpm=Bps[6], Bpq=Bps[7], mean=mean, msq=msq, var=var, Bsm=Bsm)
        tiles = [t for t in TILES if not (skip_ctx and t[3] == 2)]
        NT = len(tiles)
        pkc = [0]

        def stA(i):
            b, s0, nn, cls = tiles[i]
            col = b * TB + s0
            S.dma(SP, hb[i % 3][:], g.HT[:, :, col:col + n], (), [Bh[i % 3]])
            for kc in range(8):
                ACTV(S, ub[i % 2][:, kc, :], hb[i % 3][:, kc, :], AF.Identity, [Bh[i % 3], Bc], [Bu[i % 2]],
                     bias=mod_ap(g, l, 3 * (2 * j) + 0, kc, cls), scale=mod_ap(g, l, 3 * (2 * j) + 1, kc, cls))

        def stB(i):
            u_, Bu_ = ub[i % 2], Bu[i % 2]
            for fc in range(22):
                pk = pkc[0]
                pa, pv = ps[(pk % 2) * 2], ps[(pk % 2) * 2 + 1]
                Bpa, Bpv = Bps[(pk % 2) * 2], Bps[(pk % 2) * 2 + 1]
                si = pk % 2
                pkc[0] += 1
                for kc in range(8):
                    MM(S, pa[:, 0:n], wup[:, kc, fc * 128:(fc + 1) * 128], u_[:, kc, :], kc == 0, kc == 7, [Bwu[kc], Bu_], [Bpa])
                for kc in range(8):
                    MM(S, pv[:, 0:n], wup[:, kc, DFF + fc * 128:DFF + (fc + 1) * 128], u_[:, kc, :], kc == 0, kc == 7, [Bwu[kc], Bu_], [Bpv])
                ACTV(S, sil[si][:], pa[:, 0:n], AF.Silu, [Bpa], [Bsil[si]])
                TT(S, DVE, hid[:, fc, :], sil[si][:], pv[:, 0:n], ALU.mult, [Bsil[si], Bpv], [Bhid])

        def stC(i):
            b, s0, nn, cls = tiles[i]
            h_, Bh_ = hb[i % 3], Bh[i % 3]
            for oc in range(8):
                py, Bpy = ps[4 + oc % 2], Bps[4 + oc % 2]
                for fc in range(22):
                    MM(S, py[:, 0:n], wdn[:, fc, oc * 128:(oc + 1) * 128], hid[:, fc, :], fc == 0, fc == 21, [Bwd[fc], Bhid], [Bpy])
                STT(S, h_[:, oc, :], py[:, 0:n], mod_ap(g, l, 3 * (2 * j) + 2, oc, cls), h_[:, oc, :], ALU.mult, ALU.add,
                    [Bpy, Bh_, Bc], [Bh_])
            ln_part1(S, g, h_, Bh_, n, W)

        def stD(i):
            b, s0, nn, cls = tiles[i]
            col = b * TB + s0
            ln_part2(S, g, l, 2 * j, hb[i % 3], Bh[i % 3], n, W)
            S.dma(SP, g.HT[:, :, col:col + n], hb[i % 3][:], [Bh[i % 3]], ())

        stA(0)
        stB(0)
        for i in range(NT):
            if i + 1 < NT:
                stA(i + 1)
            stC(i)
            if i + 1 < NT:
                stB(i + 1)
            stD(i)
        S.emit()
    nc.all_engine_barrier()


def phase_mixpro(nc, g, l, b, U, BU):
    n = TS_
    odd = (l % 2 == 1)
    with contextlib.ExitStack() as st:
        sb = lambda nm, s, d: st.enter_context(nc.sbuf_tensor(uname(nm), s, d))
        hb = [sb("mph%d" % i, [128, 8, n], F32) for i in range(3)]
        S = Sched(nc)
        Bh = [Buf() for _ in range(3)]
        Bc = g.Bconst
        ti = 0
        for (bb, s0, nn, cls) in TILES:
            if bb != b:
                continue
            hi = ti % 3
            ti += 1
            col = b * TB + s0
            S.dma(SP, hb[hi][:], g.HT[:, :, col:col + n], (), [Bh[hi]])
            for kc in range(8):
                if cls == 2 or not odd:
                    dst = U[:, kc, s0:s0 + n]
                    src = hb[hi][:, kc, :]
                else:
                    r0 = (s0 - CTX) // 64
                    nr = n // 64
                    dst = U[:, kc, CTX:TB].rearrange("p (c r) -> p r c", r=32)[:, r0:r0 + nr, :]
                    src = hb[hi][:, kc, :].rearrange("p (r c) -> p r c", c=64)
                ACTV(S, dst, src, AF.Identity, [Bh[hi], Bc], [BU],
                     bias=mod_ap(g, l, 3, kc, cls), scale=mod_ap(g, l, 4, kc, cls))
        S.emit()
    nc.all_engine_barrier()


def phase_merge(nc, g, l, skip_ctx):
    n = TS_
    with contextlib.ExitStack() as st:
        sb = lambda nm, s, d: st.enter_context(nc.sbuf_tensor(uname(nm), s, d))
        wgt = sb("mgwg", [128, 8, 3072], BF16)
        wbr = sb("mgwb", [128, 24, 1024], BF16)
        wo = sb("mgwo", [128, 8, 1024], BF16)
        hb = [sb("mgh%d" % i, [128, 8, n], F32) for i in range(3)]
        ub = [sb("mgu0", [128, 8, n], BF16)] * 2
        sq0 = sb("mgsq0", [128, 8, n], BF16)
        yb = [[sb("mgy%d_%d" % (i, k), [128, 8, n], BF16) for k in range(3)] for i in range(2)]
        mb = sb("mgm", [128, 8, n], BF16)
        zbf = sb("mgzbf", [128, 8, n], BF16)
        sq = sb("mgsq", [128, 8, n], BF16)
        mean = sb("mgmean", [128, n], F32)
        msq = sb("mgmsq", [128, n], F32)
        var = sb("mgvar", [128, n], F32)
        rstd0 = [sb("mgrstd0%d" % i, [128, n], F32) for i in range(2)]
        sig = [sb("mgsig%d" % i, [128, n], F32) for i in range(2)]
        acc = sb("mgacc", [128, n], F32)
        tmp = sb("mgtmp", [128, n], F32)
        ps = [st.enter_context(nc.psum_tensor(uname("mgps%d" % i), [128, 512], F32)) for i in range(8)]
        S = Sched(nc)
        Bc = g.Bconst
        Bwg = [Buf() for _ in range(8)]
        Bwb = [Buf() for _ in range(24)]
        Bwo = [Buf() for _ in range(8)]
        Bh = [Buf(), Buf(), Buf()]
        By = [[Buf() for _ in range(3)] for _ in range(2)]
        Bm, Bzs, Bsm, Bacc, Btmp, Bsq0 = Buf(), Buf(), Buf(), Buf(), Buf(), Buf()
        Bu = [Buf()] * 2
        Br0 = [Buf(), Buf()]
        Bsig = [Buf(), Buf()]
        Bps = [Buf() for _ in range(8)]
        load_weight_cast(S, wgt, g.w_in[l][:, OFF_GATE:OFF_GATE + 3072], 8, 3072, Bwg)
        load_weight_cast(S, wbr, g.w_branch[l].rearrange("n k m -> (n k) m"), 24, 1024, Bwb)
        load_weight_cast(S, wo, g.w_out[l], 8, 1024, Bwo)
        import os as _os
        PL = DVE
        for kc in range(0 if _os.environ.get('MG_NOFOLD') else 8):
            TSC(S, DVE, wbr[:, kc, :], wbr[:, kc, :], cf(g, "sng", l * 8 + kc), None, ALU.mult, None, [Bwb[kc], Bc], [Bwb[kc]])
        W = dict(zbf=zbf, sq=sq, Bzs=Bzs, psm=ps[6], psq=ps[7], Bpm=Bps[6], Bpq=Bps[7], mean=mean, msq=msq, var=var, Bsm=Bsm)
        tiles = [t for t in TILES if not (skip_ctx and t[3] == 2)]
        NT = len(tiles)
        pkc = [0]

        def stA(i):
            b, s0, nn, cls = tiles[i]
            col = b * TB + s0
            hi = i % 2
            S.dma(SP, hb[i % 3][:], g.HT[:, :, col:col + n], (), [Bh[i % 3]])
            for k in range(3):
                S.dma(SP, yb[hi][k][:], g.Y[k][:, :, col:col + n], (), [By[hi][k]])
            for kc in range(8):
                ACTV(S, ub[hi][:, kc, :], hb[i % 3][:, kc, :], AF.Identity, [Bh[i % 3], Bc], [Bu[hi]],
                     bias=mod_ap(g, l, 3, kc, cls), scale=mod_ap(g, l, 4, kc, cls))
            ACTV(S, sq0[:], yb[hi][0][:], AF.Square, [By[hi][0]], [Bsq0])
            for kc in range(8):
                MM(S, ps[7][:, 0:n], g.onesb[:], sq0[:, kc, :], kc == 0, kc == 7, [Bsq0, Bc], [Bps[7]])
            TSC(S, DVE, rstd0[hi][:], ps[7][:, 0:n], 1.0 / 1024, EPS, ALU.mult, ALU.add, [Bps[7]], [Br0[hi]])
            ACTV(S, rstd0[hi][:], rstd0[hi][:], AF.Sqrt, [Br0[hi]], [Br0[hi]])
            S.dve(lambda e: e.reciprocal(out=rstd0[hi][:], in_=rstd0[hi][:]), [Br0[hi]], [Br0[hi]])

        def stB(i):
            hi = i % 2
            for oc in range(8):
                for k in range(3):
                    pk = pkc[0]
                    pg, pb = ps[(pk % 2) * 2], ps[(pk % 2) * 2 + 1]
                    Bpg, Bpb = Bps[(pk % 2) * 2], Bps[(pk % 2) * 2 + 1]
                    si = pk % 2
                    pkc[0] += 1
                    for kc in range(8):
                        MM(S, pg[:, 0:n], wgt[:, kc, k * 1024 + oc * 128:k * 1024 + (oc + 1) * 128], ub[hi][:, kc, :], kc == 0, kc == 7,
                           [Bwg[kc], Bu[hi]], [Bpg])
                    for kc in range(8):
                        MM(S, pb[:, 0:n], wbr[:, k * 8 + kc, oc * 128:(oc + 1) * 128], yb[hi][k][:, kc, :], kc == 0, kc == 7,
                           [Bwb[k * 8 + kc], By[hi][k]], [Bpb])
                    ACTV(S, sig[si][:], pg[:, 0:n], AF.Sigmoid, [Bpg], [Bsig[si]])
                    if k == 0:
                        TT(S, PL, sig[si][:], sig[si][:], rstd0[hi][:], ALU.mult, [Bsig[si], Br0[hi]], [Bsig[si]])
                        TT(S, DVE, acc[:], sig[si][:], pb[:, 0:n], ALU.mult, [Bsig[si], Bpb], [Bacc])
                    elif k == 1:
                        TT(S, DVE, tmp[:], sig[si][:], pb[:, 0:n], ALU.mult, [Bsig[si], Bpb], [Btmp])
                        TT(S, PL, acc[:], acc[:], tmp[:], ALU.add, [Bacc, Btmp], [Bacc])
                    else:
                        TT(S, DVE, tmp[:], sig[si][:], pb[:, 0:n], ALU.mult, [Bsig[si], Bpb], [Btmp])
                        TT(S, PL, mb[:, oc, :], acc[:], tmp[:], ALU.add, [Bacc, Btmp], [Bm])

        def stC(i):
            b, s0, nn, cls = tiles[i]
            h_, Bh_ = hb[i % 3], Bh[i % 3]
            for oc in range(8):
                py, Bpy = ps[4 + oc % 2], Bps[4 + oc % 2]
                for kc in range(8):
                    MM(S, py[:, 0:n], wo[:, kc, oc * 128:(oc + 1) * 128], mb[:, kc, :], kc == 0, kc == 7, [Bwo[kc], Bm], [Bpy])
                STT(S, h_[:, oc, :], py[:, 0:n], mod_ap(g, l, 5, oc, cls), h_[:, oc, :], ALU.mult, ALU.add,
                    [Bpy, Bh_, Bc], [Bh_])
            ln_part1(S, g, h_, Bh_, n, W)

        def stD(i):
            b, s0, nn, cls = tiles[i]
            col = b * TB + s0
            ln_part2(S, g, l, 1, hb[i % 3], Bh[i % 3], n, W)
            S.dma(SP, g.HT[:, :, col:col + n], hb[i % 3][:], [Bh[i % 3]], ())

        stA(0)
        stB(0)
        for i in range(NT):
            if i + 1 < NT:
                stA(i + 1)
            stC(i)
            if i + 1 < NT:
                stB(i + 1)
            stD(i)
        S.emit()
    nc.all_engine_barrier()


SEGS = [(0, 256)] + [(CTX + i * 512, 512) for i in range(4)]


def phase_lru(nc, g, l, b, U, BU):
    odd = (l % 2 == 1)
    Lh = 32 if odd else 64
    with contextlib.ExitStack() as st:
        sb = lambda nm, s, d: st.enter_context(nc.sbuf_tensor(uname(nm), s, d))
        wl = [sb("lrw%d" % i, [128, 8, 256], BF16) for i in range(2)]
        wblk = sb("lrblk", [128, 8, 4, 128], BF16)
        xr = sb("lrxr", [128, TB], F32)
        xc = sb("lrxc", [128, TB], F32)
        xcb = sb("lrxcb", [128, TB], BF16)
        gg = sb("lrgg", [128, TB], F32)
        T = [sb("lrT%d" % i, [128, TB], F32) for i in range(5)]
        hf = sb("lrhf", [128, TB], F32)
        hbk = sb("lrhb", [128, TB], F32)
        yst = [sb("lryst%d" % i, [128, TB], BF16) for i in range(2)]
        ps = [st.enter_context(nc.psum_tensor(uname("lrps%d" % i), [128, 512], F32)) for i in range(8)]
        S = Sched(nc)
        Bc = g.Bconst
        Bwl = [Buf(), Buf()]
        Bblk, Bxr, Bxc, Bxcb, Bgg, Bhf, Bhb = Buf(), Buf(), Buf(), Buf(), Buf(), Buf(), Buf()
        BT = [Buf() for _ in range(5)]
        Byst = [Buf(), Buf()]
        Bps = [Buf() for _ in range(8)]
        S.pool(lambda e: e.memset(wblk[:], 0.0), (), [Bblk])
        for d in range(2):
            for t, wsrc in enumerate((g.lru_w_a, g.lru_w_x)):
                for h in range(2):
                    src = wsrc[l, d].rearrange("(j h) k c -> h k j c", h=2)[h]
                    S.dma(POOL, wblk[h * 64:(h + 1) * 64, :, d * 2 + t, h * 64:(h + 1) * 64], src, (), [Bblk])
        wv = g.w_in[l].rearrange("(k p) n -> p k n", p=128)
        pk = 0
        for j in range(8):
            wi = j % 2
            S.dma(POOL, wl[wi][:, :, 0:128], wv[:, :, OFF_LX + j * 128:OFF_LX + (j + 1) * 128], (), [Bwl[wi]])
            S.dma(POOL, wl[wi][:, :, 128:256], wv[:, :, OFF_LG + j * 128:OFF_LG + (j + 1) * 128], (), [Bwl[wi]])
            for (s0, n) in SEGS:
                px, Bpx = ps[pk % 4], Bps[pk % 4]
                pg, Bpg = ps[(pk + 1) % 4], Bps[(pk + 1) % 4]
                pk += 2
                for kc in range(8):
                    MM(S, px[:, 0:n], wl[wi][:, kc, 0:128], U[:, kc, s0:s0 + n], kc == 0, kc == 7, [Bwl[wi], BU], [Bpx])
                for kc in range(8):
                    MM(S, pg[:, 0:n], wl[wi][:, kc, 128:256], U[:, kc, s0:s0 + n], kc == 0, kc == 7, [Bwl[wi], BU], [Bpg])
                COPY(S, DVE, xr[:, s0:s0 + n], px[:, 0:n], [Bpx], [Bxr])
                ACTV(S, gg[:, s0:s0 + n], pg[:, 0:n], AF.Gelu_apprx_tanh, [Bpg], [Bgg])
            cw = lambda k: cf(g, "lcw", (l * 4 + k) * 8 + j)
            ACTV(S, xc[:], xr[:], AF.Identity, [Bxr, Bc], [Bxc], bias=cf(g, "lcb", l * 8 + j), scale=cw(2))
            for (o0, ln_, nl) in ((0, 256, 1), (CTX, Lh, SEQ // Lh)):
                xv = xr[:, o0:o0 + ln_ * nl].rearrange("p (a b) -> p a b", b=ln_)
                ov = xc[:, o0:o0 + ln_ * nl].rearrange("p (a b) -> p a b", b=ln_)
                STT(S, ov[:, :, 2:ln_], xv[:, :, 0:ln_ - 2], cw(0), ov[:, :, 2:ln_], ALU.mult, ALU.add, [Bxr, Bxc, Bc], [Bxc])
                STT(S, ov[:, :, 1:ln_], xv[:, :, 0:ln_ - 1], cw(1), ov[:, :, 1:ln_], ALU.mult, ALU.add, [Bxr, Bxc, Bc], [Bxc])
                STT(S, ov[:, :, 0:ln_ - 1], xv[:, :, 1:ln_], cw(3), ov[:, :, 0:ln_ - 1], ALU.mult, ALU.add, [Bxr, Bxc, Bc], [Bxc])
            COPY(S, ACT, xcb[:], xc[:], [Bxc], [Bxcb])
            for d in range(2):
                ci = (l * 2 + d) * 8 + j
                for (s0, n) in SEGS:
                    pr, Bpr = ps[4 + pk % 4], Bps[4 + pk % 4]
                    pi_, Bpi = ps[4 + (pk + 1) % 4], Bps[4 + (pk + 1) % 4]
                    pk += 2
                    MM(S, pr[:, 0:n], wblk[:, j, d * 2 + 0, :], xcb[:, s0:s0 + n], True, True, [Bblk, Bxcb], [Bpr])
                    MM(S, pi_[:, 0:n], wblk[:, j, d * 2 + 1, :], xcb[:, s0:s0 + n], True, True, [Bblk, Bxcb], [Bpi])
                    ACTV(S, T[0][:, s0:s0 + n], pr[:, 0:n], AF.Sigmoid, [Bpr, Bc], [BT[0]], bias=cf(g, "lba", ci))
                    ACTV(S, T[1][:, s0:s0 + n], pi_[:, 0:n], AF.Sigmoid, [Bpi, Bc], [BT[1]], bias=cf(g, "lbx", ci))
                ACTV(S, T[2][:], T[0][:], AF.Exp, [BT[0], Bc], [BT[2]], scale=cf(g, "lsp8", ci))
                TSC(S, DVE, T[3][:], T[0][:], cf(g, "lsp16", ci), None, ALU.mult, None, [BT[0], Bc], [BT[3]])
                TSC(S, DVE, T[4][:], T[3][:], 1.0 / 120, 1.0 / 24, ALU.mult, ALU.add, [BT[3]], [BT[4]])
                TT(S, DVE, T[4][:], T[4][:], T[3][:], ALU.mult, [BT[4], BT[3]], [BT[4]])
                for cst in (1.0 / 6, 0.5, 1.0):
                    STT(S, T[4][:], T[4][:], cst, T[3][:], ALU.add, ALU.mult, [BT[4], BT[3]], [BT[4]])
                ACTV(S, T[4][:], T[4][:], AF.Sqrt, [BT[4]], [BT[4]], scale=-1.0)
                TT(S, POOL, T[1][:], T[1][:], T[4][:], ALU.mult, [BT[1], BT[4]], [BT[1]])
                TT(S, POOL, T[1][:], T[1][:], xc[:], ALU.mult, [BT[1], Bxc], [BT[1]])
                if d == 0:
                    S.dve(lambda e: e.tensor_tensor_scan(out=hf[:], data0=T[2][:], data1=T[1][:], initial=0.0,
                                                         op0=ALU.mult, op1=ALU.add), [BT[2], BT[1]], [Bhf])
                else:
                    S.dve(lambda e: e.tensor_tensor_scan(out=hbk[:, 0:CTX][:, ::-1], data0=T[2][:, 0:CTX][:, ::-1],
                                                         data1=T[1][:, 0:CTX][:, ::-1], initial=0.0,
                                                         op0=ALU.mult, op1=ALU.add), [BT[2], BT[1]], [Bhb])
                    S.dve(lambda e: e.tensor_tensor_scan(out=hbk[:, CTX:TB][:, ::-1], data0=T[2][:, CTX:TB][:, ::-1],
                                                         data1=T[1][:, CTX:TB][:, ::-1], initial=hbk[:, 0:1],
                                                         op0=ALU.mult, op1=ALU.add), [BT[2], BT[1], Bhb], [Bhb])
            yi = j % 2
            TT(S, POOL, hf[:], hf[:], hbk[:], ALU.add, [Bhf, Bhb], [Bhf])
            TT(S, DVE, yst[yi][:, 0:CTX], hf[:, 0:CTX], gg[:, 0:CTX], ALU.mult, [Bhf, Bgg], [Byst[yi]])
            if odd:
                ov = yst[yi][:, CTX:TB].rearrange("p (r c) -> p c r", c=64)
                i0 = hf[:, CTX:TB].rearrange("p (c r) -> p c r", r=32)
                i1 = gg[:, CTX:TB].rearrange("p (c r) -> p c r", r=32)
                TT(S, DVE, ov, i0, i1, ALU.mult, [Bhf, Bgg], [Byst[yi]])
            else:
                TT(S, DVE, yst[yi][:, CTX:TB], hf[:, CTX:TB], gg[:, CTX:TB], ALU.mult, [Bhf, Bgg], [Byst[yi]])
            S.dma(SP, g.Y[1][:, j, b * TB:(b + 1) * TB], yst[yi][:], [Byst[yi]], ())
        S.emit()
    nc.all_engine_barrier()


def phase_mix(nc, g, l, which):
    with contextlib.ExitStack() as st:
        U = st.enter_context(nc.sbuf_tensor(uname("Umix"), [128, 8, TB], BF16))
        for b in range(NB):
            BU = Buf("U")
            phase_mixpro(nc, g, l, b, U, BU)
            BU = Buf("U")
            if "ssd" in which:
                phase_ssd(nc, g, l, b, U, BU)
            if "lru" in which:
                phase_lru(nc, g, l, b, U, BU)
            if "gla" in which:
                phase_gla(nc, g, l, b, U, BU)


FWD_CHUNKS = list(range(18))
REV_CHUNKS = [1, 0] + list(range(17, 1, -1))


def conv_block(S, g, raw, Braw, out, Bout, wfn, bias_ap, Lh):
    Bc = g.Bconst
    ACTV(S, out[:], raw[:], AF.Identity, [Braw, Bc], [Bout], bias=bias_ap, scale=wfn(2))
    for (o0, ln_, nl) in ((0, 256, 1), (CTX, Lh, SEQ // Lh)):
        xv = raw[:, o0:o0 + ln_ * nl].rearrange("p (a b) -> p a b", b=ln_)
        ov = out[:, o0:o0 + ln_ * nl].rearrange("p (a b) -> p a b", b=ln_)
        STT(S, ov[:, :, 2:ln_], xv[:, :, 0:ln_ - 2], wfn(0), ov[:, :, 2:ln_], ALU.mult, ALU.add, [Braw, Bout, Bc], [Bout])
        STT(S, ov[:, :, 1:ln_], xv[:, :, 0:ln_ - 1], wfn(1), ov[:, :, 1:ln_], ALU.mult, ALU.add, [Braw, Bout, Bc], [Bout])
        STT(S, ov[:, :, 0:ln_ - 1], xv[:, :, 1:ln_], wfn(3), ov[:, :, 0:ln_ - 1], ALU.mult, ALU.add, [Braw, Bout, Bc], [Bout])


def phase_ssd(nc, g, l, b, U, BU):
    odd = (l % 2 == 1)
    Lh = 32 if odd else 64
    M = g.masks
    with contextlib.ExitStack() as st:
        sb = lambda nm, s, d: st.enter_context(nc.sbuf_tensor(uname(nm), s, d))
        wg = [sb("sdw%d" % i, [128, 8, 784], BF16) for i in range(2)]
        szT = sb("sdsz", [128, 2, TB], BF16)
        craw = sb("sdcraw", [128, TB], F32)
        ctmp = sb("sdctmp", [128, TB], F32)
        xsT = sb("sdxsT", [128, 2, TB], F32)
        BTf = sb("sdBTf", [128, TB], F32)
        BT = sb("sdBT", [128, TB], BF16)
        CT = sb("sdCT", [128, TB], BF16)
        dtv = sb("sddt", [128, 144], F32)
        av = sb("sda", [128, 144], F32)
        acs = sb("sdacs", [128, 144], F32)
        tot = sb("sdtot", [128, 144], F32)
        tm = sb("sdtm", [128, 144], F32)
        fs = sb("sdfs", [128, 144], F32)
        te = sb("sdte", [128, 144], F32)
        cd = sb("sdcd", [128, 144], F32)
        yacc = sb("sdyacc", [128, 18, 256], F32)
        Sst = sb("sdS", [128, 2, 256], F32)
        Sbf = sb("sdSb", [128, 2, 256], BF16)
        yst = [sb("sdyst%d" % i, [128, 2, TB], BF16) for i in range(2)]
        xs_tok = sb("sdxstok", [128, 256], F32)
        B_tok = sb("sdBtok", [128, 128], BF16)
        xsd = sb("sdxsd", [128, 256], BF16)
        xw = sb("sdxw", [128, 256], BF16)
        scm = sb("sdscm", [128, 128], BF16)
        rhsA = sb("sdrhsA", [128, 512], F32)
        Eb = sb("sdE", [128, 512], BF16)
        MT = sb("sdMT", [128, 512], BF16)
        t1 = sb("sdt1", [128, 256], F32)
        t2 = sb("sdt2", [128, 256], F32)
        ps = [st.enter_context(nc.psum_tensor(uname("sdps%d" % i), [128, 512], F32)) for i in range(8)]
        S = Sched(nc)
        Bc = g.Bconst
        Bwg = [Buf(), Buf()]
        (Bsz, Bcraw, Bctmp, BxsT, BBTf, BBT, BCT, Bdt, Ba, Bacs, Btot, Btm, Bfs, Bte, Bcd, Byacc, BS, BSb,
         Bxt, BBtok, Bxsd, Bxw, Bscm, BrhsA, BE, BMT, Bt1, Bt2) = [Buf() for _ in range(28)]
        Byst = [Buf(), Buf()]
        Bps = [Buf() for _ in range(8)]
        wv = g.w_in[l].rearrange("(k p) n -> p k n", p=128)
        v4 = lambda t: t[:].rearrange("p (c d h) -> p c d h", d=2, h=4)
        pk = 0
        for gq in range(4):
            wi = gq % 2
            for (d0, c0, cn) in ((0, OFF_Z + 256 * gq, 256), (256, OFF_XBC + 256 * gq, 256),
                                 (512, OFF_XBC + 1024 + 128 * gq, 128), (640, OFF_XBC + 1536 + 128 * gq, 128),
                                 (768, OFF_DT + 4 * gq, 4), (772, OFF_DT + 16 + 4 * gq, 4)):
                S.dma(POOL, wg[wi][:, :, d0:d0 + cn], wv[:, :, c0:c0 + cn], (), [Bwg[wi]])

            def inproj(c0, evac):
                nonlocal pk
                for (s0, n) in SEGS:
                    p_, Bp = ps[pk % 2], Bps[pk % 2]
                    pk += 1
                    for kc in range(8):
                        MM(S, p_[:, 0:n], wg[wi][:, kc, c0:c0 + 128], U[:, kc, s0:s0 + n], kc == 0, kc == 7, [Bwg[wi], BU], [Bp])
                    evac(p_, Bp, s0, n)

            for i in range(2):
                inproj(i * 128, lambda p_, Bp, s0, n, i=i: ACTV(S, szT[:, i, s0:s0 + n], p_[:, 0:n], AF.Silu, [Bp], [Bsz]))
            for ci in range(4):
                inproj(256 + ci * 128, lambda p_, Bp, s0, n: COPY(S, DVE, craw[:, s0:s0 + n], p_[:, 0:n], [Bp], [Bcraw]))
                ch16 = (2 * gq + ci) if ci < 2 else (8 + gq if ci == 2 else 12 + gq)
                conv_block(S, g, craw, Bcraw, ctmp, Bctmp, lambda k, ch16=ch16: cf(g, "scw", (l * 4 + k) * 16 + ch16),
                           cf(g, "scb", l * 16 + ch16), Lh)
                if ci < 2:
                    ACTV(S, xsT[:, ci, :], ctmp[:], AF.Silu, [Bctmp], [BxsT])
                elif ci == 2:
                    ACTV(S, BTf[:], ctmp[:], AF.Silu, [Bctmp], [BBTf])
                    COPY(S, DVE, BT[:], BTf[:], [BBTf], [BBT])
                else:
                    ACTV(S, CT[:], ctmp[:], AF.Silu, [Bctmp], [BCT])
            pdt, Bpdt = ps[2], Bps[2]
            for c in range(18):
                for kc in range(8):
                    MM(S, pdt[:, c * 8:(c + 1) * 8], U[:, kc, c * 128:(c + 1) * 128], wg[wi][:, kc, 768:776], kc == 0, kc == 7,
                       [Bwg[wi], BU], [Bpdt])
            rb = lambda t: bc(t[:, l * 32:(l + 1) * 32].rearrange("p (d h) -> p d h", d=2)[:, :, 4 * gq:4 * gq + 4].unsqueeze(1), [128, 18, 2, 4])
            TT(S, DVE, v4(dtv), pdt[:, 0:144].rearrange("p (c d h) -> p c d h", d=2, h=4), rb(g.rb_dtb), ALU.add, [Bpdt, Bc], [Bdt])
            ACTV(S, dtv[:], dtv[:], AF.Exp, [Bdt], [Bdt])
            ACTV(S, dtv[:], dtv[:], AF.Ln, [Bdt], [Bdt], bias=1.0)
            TT(S, DVE, v4(av), v4(dtv), rb(g.rb_A), ALU.mult, [Bdt, Bc], [Ba])
            MM(S, ps[3][:, 0:144], M["LE"][:], av[:], True, True, [Ba, Bc], [Bps[3]])
            MM(S, ps[4][:, 0:144], M["ONES"][:], av[:], True, True, [Ba, Bc], [Bps[4]])
            COPY(S, DVE, acs[:], ps[3][:, 0:144], [Bps[3]], [Bacs])
            COPY(S, DVE, tot[:], ps[4][:, 0:144], [Bps[4]], [Btot])
            ACTV(S, cd[:], tot[:], AF.Exp, [Btot], [Bcd])
            ACTV(S, v4(fs)[:, :, 0, :], v4(acs)[:, :, 0, :], AF.Exp, [Bacs], [Bfs])
            TT(S, DVE, v4(tm)[:, :, 0, :], v4(tot)[:, :, 0, :], v4(acs)[:, :, 0, :], ALU.subtract, [Btot, Bacs], [Btm])
            TT(S, DVE, v4(tm)[:, :, 1, :], v4(acs)[:, :, 1, :], v4(av)[:, :, 1, :], ALU.subtract, [Bacs, Ba], [Btm])
            ACTV(S, te[:], tm[:], AF.Exp, [Btm], [Bte])
            TT(S, DVE, v4(tm)[:, :, 1, :], v4(tot)[:, :, 1, :], v4(tm)[:, :, 1, :], ALU.subtract, [Btot, Btm], [Btm])
            ACTV(S, v4(fs)[:, :, 1, :], v4(tm)[:, :, 1, :], AF.Exp, [Btm], [Bfs])
            S.pool(lambda e: e.memset(Sst[:], 0.0), (), [BS])
            S.pool(lambda e: e.memset(Sbf[:], 0.0), (), [BSb])
            yi = gq % 2
            for d in range(2):
                M1 = M["GT"] if d == 0 else M["LT"]
                M2 = M["LE"] if d == 0 else M["GE"]
                for c in (FWD_CHUNKS if d == 0 else REV_CHUNKS):
                    cs = slice(c * 128, (c + 1) * 128)
                    ptr, Bptr = ps[2], Bps[2]
                    for i in range(2):
                        TR(S, ptr[:, i * 128:(i + 1) * 128], xsT[:, i, cs], M["ID"][:], [BxsT, Bc], [Bptr])
                    TR(S, ptr[:, 256:384], BTf[:, cs], M["ID"][:], [BBTf, Bc], [Bptr])
                    COPY(S, ACT, xs_tok[:], ptr[:, 0:256], [Bptr], [Bxt])
                    COPY(S, ACT, B_tok[:], ptr[:, 256:384], [Bptr], [BBtok])
                    dt_c = bc(v4(dtv)[:, c, d, :].unsqueeze(2), [128, 4, 64])
                    te_c = bc(v4(te)[:, c, d, :].unsqueeze(2), [128, 4, 64])
                    fs_c = bc(v4(fs)[:, c, d, :].unsqueeze(2), [128, 4, 64])
                    cd_c = bc(v4(cd)[:, c, d, :].unsqueeze(2), [128, 4, 64])
                    h3 = lambda t: t.rearrange("p (h q) -> p h q", h=4)
                    TT(S, DVE, h3(xsd[:]), h3(xs_tok[:]), dt_c, ALU.mult, [Bxt, Bdt], [Bxsd])
                    TT(S, DVE, h3(xw[:]), h3(xsd[:]), te_c, ALU.mult, [Bxsd, Bte], [Bxw])
                    MM(S, ps[3][:, 0:128], BT[:, cs], CT[:, cs], True, True, [BBT, BCT], [Bps[3]])
                    TT(S, DVE, scm[:], ps[3][:, 0:128], (M["LE"] if d == 0 else M["GE"])[:], ALU.mult, [Bps[3], Bc], [Bscm])
                    TT(S, POOL, h3(rhsA[:]), bc(M2[:].unsqueeze(1), [128, 4, 128]), bc(v4(av)[:, c, d, :].unsqueeze(2), [128, 4, 128]),
                       ALU.mult, [Ba, Bc], [BrhsA])
                    MM(S, ps[4][:], M1[:], rhsA[:], True, True, [BrhsA, Bc], [Bps[4]])
                    ACTV(S, Eb[:], ps[4][:], AF.Exp, [Bps[4]], [BE])
                    TT(S, DVE, h3(MT[:]), h3(Eb[:]), bc(scm[:].unsqueeze(1), [128, 4, 128]), ALU.mult, [BE, Bscm], [BMT])
                    for hh in range(4):
                        MM(S, ps[5][:, hh * 64:(hh + 1) * 64], MT[:, hh * 128:(hh + 1) * 128], xsd[:, hh * 64:(hh + 1) * 64], True, True,
                           [BMT, Bxsd], [Bps[5]])
                    MM(S, ps[5][:, 256:512], CT[:, cs], Sbf[:, d, :], True, True, [BCT, BSb], [Bps[5]])
                    TT(S, DVE, h3(t1[:]), h3(ps[5][:, 256:512]), fs_c, ALU.mult, [Bps[5], Bfs], [Bt1])
                    if d == 0:
                        TT(S, DVE, yacc[:, c, :], ps[5][:, 0:256], t1[:], ALU.add, [Bps[5], Bt1], [Byacc])
                    else:
                        TT(S, DVE, t1[:], ps[5][:, 0:256], t1[:], ALU.add, [Bps[5], Bt1], [Bt1])
                        TT(S, POOL, h3(t2[:]), h3(xs_tok[:]),
                           bc(g.rb_Ds[:, l * 16 + 4 * gq:l * 16 + 4 * gq + 4].unsqueeze(2), [128, 4, 64]), ALU.mult, [Bxt, Bc], [Bt2])
                        TT(S, POOL, t1[:], t1[:], t2[:], ALU.add, [Bt1, Bt2], [Bt1])
                        TT(S, DVE, yacc[:, c, :], yacc[:, c, :], t1[:], ALU.add, [Byacc, Bt1], [Byacc])
                        pf, Bpf = ps[7], Bps[7]
                        for i in range(2):
                            TR(S, pf[:, i * 128:(i + 1) * 128], yacc[:, c, i * 128:(i + 1) * 128], M["ID"][:], [Byacc, Bc], [Bpf])
                        for i in range(2):
                            if odd and c >= 2:
                                cc0 = (c - 2) * 4
                                ov = yst[yi][:, i, CTX:TB].rearrange("p (r c) -> p c r", c=64)[:, cc0:cc0 + 4, :]
                                i0 = pf[:, i * 128:(i + 1) * 128].rearrange("p (c r) -> p c r", r=32)
                                i1 = szT[:, i, cs].rearrange("p (c r) -> p c r", r=32)
                            else:
                                ov, i0, i1 = yst[yi][:, i, cs], pf[:, i * 128:(i + 1) * 128], szT[:, i, cs]
                            TT(S, DVE, ov, i0, i1, ALU.mult, [Bpf, Bsz], [Byst[yi]])
                    MM(S, ps[6][:, 0:256], B_tok[:], xw[:], True, True, [BBtok, Bxw], [Bps[6]])
                    TT(S, POOL, h3(Sst[:, d, :]), h3(Sst[:, d, :]), cd_c, ALU.mult, [BS, Bcd], [BS])
                    TT(S, DVE, Sst[:, d, :], Sst[:, d, :], ps[6][:, 0:256], ALU.add, [BS, Bps[6]], [BS])
                    COPY(S, ACT, Sbf[:, d, :], Sst[:, d, :], [BS], [BSb])
            S.dma(SP, g.Y[0][:, 2 * gq:2 * gq + 2, b * TB:(b + 1) * TB], yst[yi][:], [Byst[yi]], ())
        S.emit()
    nc.all_engine_barrier()


def phase_gla(nc, g, l, b, U, BU):
    odd = (l % 2 == 1)
    M = g.masks
    QS = 128.0 ** -0.5
    with contextlib.ExitStack() as st:
        sb = lambda nm, s, d: st.enter_context(nc.sbuf_tensor(uname(nm), s, d))
        wq = sb("glw", [128, 8, 896], BF16)
        WG = sb("glWG", [128, 256], BF16)
        bgb = sb("glbg", [128, 256], F32)
        gng = sb("glgng", [128, 256], F32)
        qT = sb("glqT", [128, TB], F32)
        kT = sb("glkT", [128, TB], F32)
        sgT = sb("glsg", [128, 2, TB], BF16)
        alrT = sb("glalr", [128, TB], BF16)
        v_tok = sb("glv", [128, 18, 256], BF16)
        k_tok = sb("glk", [128, 18, 128], F32)
        lsp = sb("gllsp", [128, 18, 256], F32)
        oacc = sb("gloacc", [128, 18, 256], F32)
        yst = [sb("glyst%d" % i, [128, 2, TB], BF16) for i in range(2)]
        eq = sb("gleq", [128, 128], F32)
        ek = sb("glek", [128, 128], F32)
        qin = sb("glqin", [128, 128], BF16)
        kin = sb("glkin", [128, 128], BF16)
        er = sb("gler", [128, 128], F32)
        kst = [sb("glkst%d" % i, [128, 128], BF16) for i in range(2)]
        qinh = [sb("glqinh%d" % i, [128, 128], BF16) for i in range(2)]
        attT = sb("glatt", [128, 128], BF16)
        Sst = sb("glS", [128, 256], F32)
        Sb0 = sb("glSb0", [128, 256], BF16)
        Sb1 = sb("glSb1", [128, 256], BF16)
        osq = sb("glosq", [128, 256], F32)
        ssq = sb("glssq", [128, 1], F32)
        on = sb("glon", [128, 256], F32)
        ps = [st.enter_context(nc.psum_tensor(uname("glps%d" % i), [128, 512], F32)) for i in range(8)]
        S = Sched(nc)
        Bc = g.Bconst
        (Bwq, BWG, Bbg, BqT, BkT, Bsg, Balr, Bv, Bk, Blsp, Boacc, Beq, Bek, Bqin, Bkin, Ber, Bkst, Batt, BS, BSb0, BSb1,
         Bosq, Bssq, Bon) = [Buf() for _ in range(24)]
        Byst = [Buf(), Buf()]
        Bqinh = Buf()
        Bps = [Buf() for _ in range(8)]
        wv = g.w_in[l].rearrange("(k p) n -> p k n", p=128)
        pk = 0
        Bgng = Buf()
        S.dma(SP, gng[:], g.gla_norm_g[l].partition_broadcast(128), (), [Bgng])
        for i_ in range(2):
            S.pool(lambda e, i_=i_: e.memset(qinh[i_][:], 0.0), (), [Bqinh])
        for hd in range(4):
            S.pool(lambda e: e.memset(wq[:, :, 768:896], 0.0), (), [Bwq])
            for (d0, c0, cn) in ((0, OFF_Q + 128 * hd, 128), (128, OFF_K + 128 * hd, 128), (256, OFF_V + 256 * hd, 256),
                                 (512, OFF_G + 256 * hd, 256), (768, OFF_ALR, 16), (800, OFF_ALR + 16, 16)):
                S.dma(POOL, wq[:, :, d0:d0 + cn], wv[:, :, c0:c0 + cn], (), [Bwq])
            import os as _os
            S.pool(lambda e: e.memset(WG[:], 0.0), (), [BWG])
            for d in range(0 if _os.environ.get('GLA_NOWG') else 2):
                S.dma(POOL, WG[32 * d:32 * d + 16, d * 128:(d + 1) * 128], g.gla_w_gate[l, d, :, hd * 128:(hd + 1) * 128], (), [BWG])
                S.dma(SP, bgb[:, d * 128:(d + 1) * 128], g.gla_b_gate[l, d, hd * 128:(hd + 1) * 128].partition_broadcast(128), (), [Bbg])

            _ninp = [0]

            def inproj(c0, m, evac):
                nonlocal pk
                _ninp[0] += 1
                if _ninp[0] > int(_os.environ.get('GLA_INP', '9')):
                    return
                for (s0, n) in SEGS:
                    p_, Bp = ps[pk % 2], Bps[pk % 2]
                    pk += 1
                    for kc in range(8):
                        MM(S, p_[0:m, 0:n], wq[:, kc, c0:c0 + m], U[:, kc, s0:s0 + n], kc == 0, kc == 7, [Bwq, BU], [Bp])
                    evac(p_, Bp, s0, n)

            if float(_os.environ.get('GLA_DBG', '9')) == 0:
                continue
            inproj(0, 128, lambda p_, Bp, s0, n: COPY(S, DVE, qT[:, s0:s0 + n], p_[:, 0:n], [Bp], [BqT]))
            inproj(128, 128, lambda p_, Bp, s0, n: COPY(S, DVE, kT[:, s0:s0 + n], p_[:, 0:n], [Bp], [BkT]))
            for i in range(2):
                inproj(512 + i * 128, 128, lambda p_, Bp, s0, n, i=i: ACTV(S, sgT[:, i, s0:s0 + n], p_[:, 0:n], AF.Silu, [Bp], [Bsg]))
            inproj(768, 128, lambda p_, Bp, s0, n: COPY(S, DVE, alrT[:, s0:s0 + n], p_[:, 0:n], [Bp], [Balr]))
            _l2 = float(_os.environ.get('GLA_DBG', '9'))
            for c in range(18 if _l2 > 0.5 else 0):
                cs = slice(c * 128, (c + 1) * 128)
                p_, Bp = ps[pk % 2], Bps[pk % 2]
                pk += 1
                for kc in range(8):
                    MM(S, p_[:, 0:256], U[:, kc, cs], wq[:, kc, 256:512], kc == 0, kc == 7, [Bwq, BU], [Bp])
                for kc in range(8):
                    MM(S, ps[7][:, 0:128], U[:, kc, cs], wq[:, kc, 128:256], kc == 0, kc == 7, [Bwq, BU], [Bps[7]])
                COPY(S, ACT, v_tok[:, c, :], p_[:, 0:256], [Bp], [Bv])
                COPY(S, DVE, k_tok[:, c, :], ps[7][:, 0:128], [Bps[7]], [Bk])
                if _l2 > 0.7:
                    MM(S, ps[2][:, 0:256], alrT[:, cs], WG[:, :], True, True, [Balr, BWG], [Bps[2]])
                    TT(S, DVE, lsp[:, c, :], ps[2][:, 0:256], bgb[:], ALU.add, [Bps[2], Bbg], [Blsp])
            if _l2 > 0.8:
                ACTV(S, lsp[:], lsp[:], AF.Exp, [Blsp], [Blsp], scale=-1.0)
                ACTV(S, lsp[:], lsp[:], AF.Ln, [Blsp], [Blsp], bias=1.0)
            yi = hd % 2
            import os as _os
            _lvl = float(_os.environ.get('GLA_DBG', '9'))
            for d in range(2 if _lvl >= 3 else (1 if _lvl == 2 else 0)):
                CM = M["LE64"] if d == 0 else M["GE64"]
                RM = M["GT64"] if d == 0 else M["LT64"]
                blocks = (0, 1) if d == 0 else (1, 0)
                S.pool(lambda e: e.memset(Sst[:], 0.0), (), [BS])
                S.pool(lambda e: e.memset(Sb0[:], 0.0), (), [BSb0])
                for c in (FWD_CHUNKS if d == 0 else REV_CHUNKS):
                    cs = slice(c * 128, (c + 1) * 128)
                    ld = lsp[:, c, d * 128:(d + 1) * 128]
                    MM(S, ps[2][:, 0:128], ld, CM[:], True, True, [Blsp, Bc], [Bps[2]])
                    MM(S, ps[2][:, 128:256], RM[:], ld, True, True, [Blsp, Bc], [Bps[2]])
                    ACTV(S, eq[:], ps[2][:, 0:128], AF.Exp, [Bps[2]], [Beq], scale=-1.0 / 16)
                    ACTV(S, ek[:], ps[2][:, 0:128], AF.Exp, [Bps[2]], [Bek], scale=1.0 / 16)
                    ACTV(S, er[:], ps[2][:, 128:256], AF.Exp, [Bps[2]], [Ber], scale=-1.0 / 16)
                    STT(S, qin[:], qT[:, cs], QS, eq[:], ALU.mult, ALU.mult, [BqT, Beq], [Bqin])
                    TT(S, DVE, kin[:], kT[:, cs], ek[:], ALU.mult, [BkT, Bek], [Bkin])
                    for bi_ in range(2):
                        STT(S, kst[bi_][:], k_tok[:, c, :], M["BD"][:, 64 * bi_:64 * bi_ + 1], er[:], ALU.mult, ALU.mult, [Bk, Ber, Bc], [Bkst])
                        hs_ = slice(64 * bi_, 64 * bi_ + 64)
                        COPY(S, POOL, qinh[bi_][:, hs_], qin[:, hs_], [Bqin], [Bqinh])
                    MM(S, ps[3][:, 0:128], kin[:], qin[:], True, True, [Bkin, Bqin], [Bps[3]])
                    TT(S, DVE, attT[:], ps[3][:, 0:128], CM[:], ALU.mult, [Bps[3], Bc], [Batt])
                    MM(S, ps[4][:, 0:256], attT[:], v_tok[:, c, :], True, False, [Batt, Bv], [Bps[4]])
                    for bi, blk in enumerate(blocks):
                        r = slice(blk * 64, (blk + 1) * 64)
                        Sb, BSb = (Sb0, BSb0) if bi == 0 else (Sb1, BSb1)
                        MM(S, ps[4][:, 0:256], qinh[blk][:], Sb[:], False, bi == 1, [Bqinh, BSb], [Bps[4]])
                        MM(S, ps[5][:, 0:256], kst[blk][:], v_tok[:, c, :], True, True, [Bkst, Bv], [Bps[5]])
                        ecol = (blk * 64 + 63) if d == 0 else (blk * 64)
                        STT(S, Sst[:], Sst[:], eq[:, ecol:ecol + 1], ps[5][:, 0:256], ALU.mult, ALU.add, [BS, Beq, Bps[5]], [BS])
                        if bi == 0:
                            COPY(S, ACT, Sb1[:], Sst[:], [BS], [BSb1])
                        else:
                            COPY(S, ACT, Sb0[:], Sst[:], [BS], [BSb0])
                    if d == 0:
                        COPY(S, ACT, oacc[:, c, :], ps[4][:, 0:256], [Bps[4]], [Boacc])
                    else:
                        TT(S, DVE, oacc[:, c, :], oacc[:, c, :], ps[4][:, 0:256], ALU.add, [Boacc, Bps[4]], [Boacc])
                        ACTV(S, osq[:], oacc[:, c, :], AF.Square, [Boacc], [Bosq])
                        S.dve(lambda e: e.reduce_sum(out=ssq[:], in_=osq[:], axis=AX.X), [Bosq], [Bssq])
                        TSC(S, DVE, ssq[:], ssq[:], 1.0 / 256, EPS, ALU.mult, ALU.add, [Bssq], [Bssq])
                        ACTV(S, ssq[:], ssq[:], AF.Sqrt, [Bssq], [Bssq])
                        S.dve(lambda e: e.reciprocal(out=ssq[:], in_=ssq[:]), [Bssq], [Bssq])
                        STT(S, on[:], oacc[:, c, :], ssq[:, 0:1], gng[:], ALU.mult, ALU.mult,
                            [Boacc, Bssq, Bgng], [Bon])
                        pf, Bpf = ps[6], Bps[6]
                        for i in range(2):
                            TR(S, pf[:, i * 128:(i + 1) * 128], on[:, i * 128:(i + 1) * 128], M["ID"][:], [Bon, Bc], [Bpf])
                        for i in range(2):
                            if odd and c >= 2:
                                cc0 = (c - 2) * 4
                                ov = yst[yi][:, i, CTX:TB].rearrange("p (r c) -> p c r", c=64)[:, cc0:cc0 + 4, :]
                                i0 = pf[:, i * 128:(i + 1) * 128].rearrange("p (c r) -> p c r", r=32)
                                i1 = sgT[:, i, cs].rearrange("p (c r) -> p c r", r=32)
                            else:
                                ov, i0, i1 = yst[yi][:, i, cs], pf[:, i * 128:(i + 1) * 128], sgT[:, i, cs]
                            TT(S, DVE, ov, i0, i1, ALU.mult, [Bpf, Bsg], [Byst[yi]])
            if _lvl >= 3:
                S.dma(SP, g.Y[2][:, 2 * hd:2 * hd + 2, b * TB:(b + 1) * TB], yst[yi][:], [Byst[yi]], ())
        S.emit()
    nc.all_engine_barrier()


def build_program(stages=None):
    nc = bass.Bass("TRN2", target_bir_lowering=False)
    g = G()
    declare_io(nc, g)
    with contextlib.ExitStack() as st:
        init_gsync(nc, st)
        sb = lambda nm, s, d: st.enter_context(nc.sbuf_tensor(uname(nm), s, d))
        g.constf = sb("constf", [128, NCF], F32)
        g.modT = sb("modT", [128, NL * 9 * 8 * 4], F32)
        g.masks = {nm: sb("mask_" + nm, [128, 128], F32) for nm in
                   ("ONES", "LE", "GT", "LT", "GE", "ID", "BD", "LE64", "GT64", "LT64", "GE64")}
        g.onesb = sb("onesb", [128, 128], BF16)
        g.rb_dtb = sb("rb_dtb", [128, NL * 32], F32)
        g.rb_A = sb("rb_A", [128, NL * 32], F32)
        g.rb_D = sb("rb_D", [128, NL * 32], F32)
        g.rb_Ds = sb("rb_Ds", [128, NL * 16], F32)
        if stages is None:
            stages = ["const", "p0"]
            for l in range(NL):
                stages += [("ffn", l, 0), ("mix", l), ("merge", l), ("ffn", l, 1)]
            stages += ["final"]
        for sg in stages:
            if sg == "const":
                phase_const(nc, g)
            elif sg == "p0":
                phase_p0(nc, g)
            elif sg == "final":
                phase_final(nc, g)
            elif sg[0] == "ffn":
                phase_ffn(nc, g, sg[1], sg[2], skip_ctx=(sg[1] == NL - 1 and sg[2] == 1))
            elif sg[0] == "mix":
                phase_mix(nc, g, sg[1], sg[2] if len(sg) > 2 else ("ssd", "lru", "gla"))
            elif sg[0] == "merge":
                phase_merge(nc, g, sg[1], skip_ctx=(sg[1] == NL - 1))
    return nc


_NC_CACHE = {}


def kernel(**inputs):
    from concourse.bass_utils import run_bass_kernel_spmd
    if "nc" not in _NC_CACHE:
        _NC_CACHE["nc"] = build_program()
    nc = _NC_CACHE["nc"]
    ncores = 8
    in_maps = []
    for i in range(ncores):
        m = {}
        for nm in INPUT_NAMES:
            a = np.asarray(inputs[nm], dtype=np.float32)
            if nm in ("x", "c", "ctx"):
                a = a[i * NB:(i + 1) * NB]
            elif nm == "c_ctx":
                a = a.reshape(1, D)
            m[nm] = np.ascontiguousarray(a)
        in_maps.append(m)
    res = run_bass_kernel_spmd(nc, in_maps, core_ids=list(range(ncores)))
    return np.concatenate([np.asarray(r["out"]) for r in res.results], axis=0).astype(np.float32)
```

```python
import numpy as np
import concourse.bass as bass
import concourse.mybir as mybir
from concourse.ap import AP

F32 = mybir.dt.float32
BF16 = mybir.dt.bfloat16
AF = mybir.ActivationFunctionType
ALU = mybir.AluOpType
AX = mybir.AxisListType

PE, ACT, DVE, POOL, SP = "pe", "act", "dve", "pool", "sp"
COMPUTE = (PE, ACT, DVE, POOL)
NDMASEM = 6


class Buf:
    __slots__ = ("name", "last_w", "readers")

    def __init__(self, name=""):
        self.name = name
        self.last_w = None
        self.readers = {}


class Op:
    __slots__ = ("eng", "fn", "deps", "idx", "dma", "signal", "sigval", "sem", "semval", "tag")


class Sched:
    def __init__(self, nc):
        self.nc = nc
        self.ops = {e: [] for e in (PE, ACT, DVE, POOL, SP)}
        self.n_dma = {SP: 0, POOL: 0, ACT: 0}

    def add(self, eng, fn, reads=(), writes=(), dma=False, tag=None):
        op = Op()
        op.eng, op.fn, op.dma, op.signal, op.tag = eng, fn, dma, False, tag
        op.idx = len(self.ops[eng])
        deps = {}

        def dep(d, kind):
            if d is None or d is op:
                return
            if d.eng == eng and not d.dma:
                if eng == PE or eng == SP:
                    return
                if kind == "WAR":
                    return
            key = id(d) if d.dma else d.eng
            cur = deps.get(key)
            if cur is None or (not d.dma and d.idx > cur.idx):
                deps[key] = d

        for b in reads:
            dep(b.last_w, "RAW")
        for b in writes:
            dep(b.last_w, "WAW")
            for r in b.readers.values():
                if isinstance(r, list):
                    for rr in r:
                        dep(rr, "WAR")
                else:
                    dep(r, "WAR")
        for b in reads:
            if dma:
                b.readers.setdefault("dma", []).append(op)
            else:
                b.readers[eng] = op
        for b in writes:
            b.last_w = op
            b.readers = {}
        op.deps = list(deps.values())
        for d in op.deps:
            d.signal = True
        self.ops[eng].append(op)
        return op

    def pe(self, fn, reads=(), writes=()):
        return self.add(PE, fn, reads, writes)

    def act(self, fn, reads=(), writes=()):
        return self.add(ACT, fn, reads, writes)

    def dve(self, fn, reads=(), writes=()):
        return self.add(DVE, fn, reads, writes)

    def pool(self, fn, reads=(), writes=()):
        return self.add(POOL, fn, reads, writes)

    def dma(self, q, out, in_, reads=(), writes=(), **kw):
        return self.add(q, lambda e: e.dma_start(out=out, in_=in_, **kw), reads, writes, dma=True)

    def emit(self, final_wait_all_dma=True):
        nc = self.nc
        gs = GSYNC[0]
        esem, dsem = gs["esem"], gs["dsem"]
        for e in COMPUTE:
            c = gs["ebase"][e]
            for op in self.ops[e]:
                if op.dma:
                    continue
                if op.signal:
                    c += 1
                    op.sigval = c
            gs["ebase"][e] = c
        for q in (SP, POOL, ACT):
            k = gs["dk"][q]
            vals = gs["dvals"][q]
            for op in self.ops[q]:
                if op.dma:
                    s = k % NDMASEM
                    k += 1
                    vals[s] += 16
                    op.sem = dsem[q][s]
                    op.semval = vals[s]
            gs["dk"][q] = k
        engobj = {PE: "tensor", ACT: "scalar", DVE: "vector", POOL: "gpsimd", SP: "sync"}
        with nc.Block() as block:

            def run(e, eng):
                waited = {}

                def wait(sem, val):
                    k = id(sem)
                    if waited.get(k, 0) >= val:
                        return
                    waited[k] = val
                    eng.wait_ge(sem, val)

                for op in self.ops[e]:
                    for d in op.deps:
                        if d.dma:
                            wait(d.sem, d.semval)
                        else:
                            wait(esem[d.eng], d.sigval)
                    if op.dma:
                        if op.semval > 16:
                            wait(op.sem, op.semval - 16)
                        ins = op.fn(eng)
                        ins.then_inc(op.sem, 16)
                    else:
                        ins = op.fn(eng)
                        if op.signal:
                            ins.then_inc(esem[e], 1)
                if final_wait_all_dma:
                    last = {}
                    for op in self.ops[e]:
                        if op.dma:
                            last[id(op.sem)] = (op.sem, op.semval)
                    for sem, val in last.values():
                        wait(sem, val)

            for e in (PE, ACT, DVE, POOL, SP):
                if not self.ops[e]:
                    continue
                getattr(block, engobj[e])(lambda eng, e=e: run(e, eng))


GSYNC = [None]


def init_gsync(nc, st):
    gs = {"esem": {e: st.enter_context(nc.semaphore("s_" + e)) for e in COMPUTE}, "dsem": {},
          "ebase": {e: 0 for e in COMPUTE}, "dk": {}, "dvals": {}}
    for q in (SP, POOL, ACT):
        gs["dsem"][q] = [st.enter_context(nc.semaphore("d_%s%d" % (q, i))) for i in range(NDMASEM)]
        gs["dk"][q] = 0
        gs["dvals"][q] = [0] * NDMASEM
    GSYNC[0] = gs

import contextlib

NL, D = 4, 1024
NB = 2
CTX, SEQ = 256, 2048
TB = CTX + SEQ
TT_ = NB * TB
DFF = 2816
ALPHA = 8.0 ** 0.25
EPS = 1e-5
EPSP = EPS / (ALPHA * ALPHA)
IN_TOTAL = 11328
OFF_Z, OFF_XBC, OFF_DT, OFF_LX, OFF_LG = 0, 1024, 3072, 3104, 4128
OFF_Q, OFF_K, OFF_V, OFF_G, OFF_ALR, OFF_GATE = 5152, 5664, 6176, 7200, 8224, 8256
TS_ = 256
TILES = []
for _b in range(NB):
    TILES.append((_b, 0, 256, 2))
    for _i in range(SEQ // TS_):
        TILES.append((_b, CTX + _i * TS_, TS_, _b))


def MM(S, out, lhsT, rhs, start, stop, R, W):
    return S.pe(lambda e: e.matmul(out, lhsT, rhs, start=start, stop=stop), R, W)


def TR(S, out, in_, ident, R, W):
    return S.pe(lambda e: e.transpose(out, in_, ident), R, W)


def ACTV(S, out, in_, func, R, W, bias=None, scale=None):
    kw = {}
    if bias is not None:
        kw["bias"] = bias
    if scale is not None:
        kw["scale"] = scale
    return S.act(lambda e: e.activation(out=out, in_=in_, func=func, **kw), R, W)


def TT(S, eng, out, in0, in1, op, R, W):
    return S.add(eng, lambda e: e.tensor_tensor(out=out, in0=in0, in1=in1, op=op), R, W)


def TSC(S, eng, out, in0, s1, s2, op0, op1, R, W):
    if s2 is None:
        return S.add(eng, lambda e: e.tensor_scalar(out=out, in0=in0, scalar1=s1, scalar2=None, op0=op0), R, W)
    return S.add(eng, lambda e: e.tensor_scalar(out=out, in0=in0, scalar1=s1, scalar2=s2, op0=op0, op1=op1), R, W)


def STT(S, out, in0, scalar, in1, op0, op1, R, W):
    return S.dve(lambda e: e.scalar_tensor_tensor(out=out, in0=in0, scalar=scalar, in1=in1, op0=op0, op1=op1), R, W)


def COPY(S, eng, out, in_, R, W):
    if eng == ACT:
        return S.act(lambda e: e.activation(out=out, in_=in_, func=AF.Identity), R, W)
    return S.add(eng, lambda e: e.tensor_copy(out=out, in_=in_), R, W)


def bc(ap, shape):
    return ap.to_broadcast(list(shape))


DEBUG_OUT = [False]
_UID = [0]


def uname(n):
    _UID[0] += 1
    return '%s_u%d' % (n, _UID[0])


class G:
    pass


def declare_io(nc, g):
    def din(name, shape):
        return nc.dram_tensor(name, list(shape), F32, kind="ExternalInput").ap()
    g.x = din("x", [NB, SEQ, D])
    g.c = din("c", [NB, D])
    g.ctx = din("ctx", [NB, CTX, D])
    g.c_ctx = din("c_ctx", [1, D])
    g.w_ada = din("w_ada", [NL, D, 9 * D])
    g.b_ada = din("b_ada", [NL, 9 * D])
    g.ln_g = din("ln_g", [NL, 3, D])
    g.ln_b = din("ln_b", [NL, 3, D])
    g.ffn_w_up = din("ffn_w_up", [NL, 2, D, 2 * DFF])
    g.ffn_w_down = din("ffn_w_down", [NL, 2, DFF, D])
    g.w_in = din("w_in", [NL, D, IN_TOTAL])
    g.ssd_conv_w = din("ssd_conv_w", [NL, 4, 2048])
    g.ssd_conv_b = din("ssd_conv_b", [NL, 2048])
    g.ssd_dt_bias = din("ssd_dt_bias", [NL, 2, 16])
    g.ssd_a_log = din("ssd_a_log", [NL, 2, 16])
    g.ssd_d = din("ssd_d", [NL, 2, 16])
    g.ssd_norm_g = din("ssd_norm_g", [NL, 1024])
    g.lru_conv_w = din("lru_conv_w", [NL, 4, 1024])
    g.lru_conv_b = din("lru_conv_b", [NL, 1024])
    g.lru_w_a = din("lru_w_a", [NL, 2, 16, 64, 64])
    g.lru_b_a = din("lru_b_a", [NL, 2, 1024])
    g.lru_w_x = din("lru_w_x", [NL, 2, 16, 64, 64])
    g.lru_b_x = din("lru_b_x", [NL, 2, 1024])
    g.lru_lam = din("lru_lam", [NL, 2, 1024])
    g.gla_w_gate = din("gla_w_gate", [NL, 2, 16, 512])
    g.gla_b_gate = din("gla_b_gate", [NL, 2, 512])
    g.gla_norm_g = din("gla_norm_g", [NL, 256])
    g.w_branch = din("w_branch", [NL, 3, 1024, 1024])
    g.w_out = din("w_out", [NL, 1024, 1024])
    g.out = nc.dram_tensor("out", [NB, SEQ, D], F32, kind="ExternalOutput").ap()
    kd = "ExternalOutput" if DEBUG_OUT[0] else "Internal"
    g.HT = nc.dram_tensor("HT", [128, 8, TT_], F32, kind=kd).ap()
    g.Y = [nc.dram_tensor("Y%d" % i, [128, 8, TT_], BF16, kind=kd).ap() for i in range(3)]


INPUT_NAMES = ["x", "c", "ctx", "c_ctx", "w_ada", "b_ada", "ln_g", "ln_b", "ffn_w_up", "ffn_w_down", "w_in",
               "ssd_conv_w", "ssd_conv_b", "ssd_dt_bias", "ssd_a_log", "ssd_d", "ssd_norm_g", "lru_conv_w",
               "lru_conv_b", "lru_w_a", "lru_b_a", "lru_w_x", "lru_b_x", "lru_lam", "gla_w_gate", "gla_b_gate",
               "gla_norm_g", "w_branch", "w_out"]

CF = {}
_o = 0
for _n, _sz in [("ln_g", NL * 3 * 8), ("ln_b", NL * 3 * 8), ("bada", NL * 9 * 8), ("scw", NL * 4 * 16),
                ("scb", NL * 16), ("sng", NL * 8), ("lcw", NL * 4 * 8), ("lcb", NL * 8), ("lba", NL * 16),
                ("lbx", NL * 16), ("llam", NL * 16), ("c", 16), ("cctx", 8), ("lsp8", NL * 16), ("lsp16", NL * 16),
                ("lsp24", NL * 16)]:
    CF[_n] = _o
    _o += _sz
NCF = _o


def mod_ap(g, l, j, kc, cls):
    i = (((l * 9 + j) * 8) + kc) * 4 + cls
    return g.modT[:, i:i + 1]


def cf(g, name, idx):
    o = CF[name] + idx
    return g.constf[:, o:o + 1]


def phase_const(nc, g):
    with contextlib.ExitStack() as st:
        sb = lambda n, s, d: st.enter_context(nc.sbuf_tensor(uname(n), s, d))
        rowbuf = [sb("rowbuf%d" % i, [128, 128], F32) for i in range(2)]
        wbuf = [sb("wadab%d" % i, [128, 8, 1024], BF16) for i in range(2)]
        sT = sb("sT", [128, 8, 4], BF16)
        tmpm = sb("tmpm", [128, 128], F32)
        pst = [st.enter_context(nc.psum_tensor(uname("pst%d" % i), [128, 512], F32)) for i in range(4)]
        S = Sched(nc)
        Bm = Buf("masks")
        Brow = [Buf(), Buf()]
        Bw = [Buf(), Buf()]
        Bps = [Buf() for _ in range(4)]
        Bcf, BsT, Bmod, Btm = Buf("cf"), Buf(), Buf("mod"), Buf()
        g.Bconst = Buf("constall")
        M = g.masks

        def amask(dst, cm, step, base, op):
            S.pool(lambda e: e.memset(dst, 1.0), (), [Bm])
            S.pool(lambda e: e.affine_select(out=dst, in_=dst, compare_op=op, fill=0.0, base=base,
                                             pattern=[[step, 128]], channel_multiplier=cm), [Bm], [Bm])

        S.pool(lambda e: e.memset(M["ONES"][:], 1.0), (), [Bm])
        S.pool(lambda e: e.memset(g.onesb[:], 1.0), (), [Bm])
        amask(M["LE"][:], -1, 1, 0, ALU.is_ge)
        amask(M["GT"][:], 1, -1, 0, ALU.is_gt)
        amask(M["LT"][:], -1, 1, 0, ALU.is_gt)
        amask(M["GE"][:], 1, -1, 0, ALU.is_ge)
        amask(M["ID"][:], 1, -1, 0, ALU.is_equal)
        S.pool(lambda e: e.memset(M["BD"][:], 0.0), (), [Bm])
        S.pool(lambda e: e.memset(M["BD"][0:64, 0:64], 1.0), [Bm], [Bm])
        S.pool(lambda e: e.memset(M["BD"][64:128, 64:128], 1.0), [Bm], [Bm])
        for nm in ("LE", "GT", "LT", "GE"):
            TT(S, POOL, M[nm + "64"][:], M[nm][:], M["BD"][:], ALU.mult, [Bm], [Bm])

        items = [
            ("ln_g", g.ln_g.rearrange("l i (k p) -> (l i k) p", p=128)),
            ("ln_b", g.ln_b.rearrange("l i (k p) -> (l i k) p", p=128)),
            ("bada", g.b_ada.rearrange("l (j p) -> (l j) p", p=128)),
            ("scw", g.ssd_conv_w.rearrange("l k (c p) -> (l k c) p", p=128)),
            ("scb", g.ssd_conv_b.rearrange("l (c p) -> (l c) p", p=128)),
            ("sng", g.ssd_norm_g.rearrange("l (c p) -> (l c) p", p=128)),
            ("lcw", g.lru_conv_w.rearrange("l k (c p) -> (l k c) p", p=128)),
            ("lcb", g.lru_conv_b.rearrange("l (c p) -> (l c) p", p=128)),
            ("lba", g.lru_b_a.rearrange("l d (c p) -> (l d c) p", p=128)),
            ("lbx", g.lru_b_x.rearrange("l d (c p) -> (l d c) p", p=128)),
            ("llam", g.lru_lam.rearrange("l d (c p) -> (l d c) p", p=128)),
            ("c", g.c.rearrange("b (c p) -> (b c) p", p=128)),
            ("cctx", g.c_ctx.rearrange("b (c p) -> (b c) p", p=128)),
        ]
        k = 0
        for nm, ap in items:
            R = ap.shape[0]
            r0 = 0
            while r0 < R:
                nr = min(128, R - r0)
                i = k % 2
                k += 1
                S.dma(SP, rowbuf[i][0:nr, :], ap[r0:r0 + nr, :], (), [Brow[i]])
                TR(S, pst[i][:, 0:nr], rowbuf[i][0:nr, :], M["ID"][0:nr, 0:nr], [Brow[i], Bm], [Bps[i]])
                o = CF[nm] + r0
                COPY(S, DVE, g.constf[:, o:o + nr], pst[i][:, 0:nr], [Bps[i]], [Bcf])
                r0 += nr
        n16 = NL * 16
        lam = g.constf[:, CF["llam"]:CF["llam"] + n16]
        ACTV(S, tmpm[:, 0:n16], lam, AF.Exp, [Bcf], [Btm], scale=-1.0)
        ACTV(S, tmpm[:, 0:n16], tmpm[:, 0:n16], AF.Ln, [Btm], [Btm], bias=1.0)
        for nm, sc in (("lsp8", -8.0), ("lsp16", -16.0), ("lsp24", -16.0 / 24.0)):
            TSC(S, DVE, g.constf[:, CF[nm]:CF[nm] + n16], tmpm[:, 0:n16], sc, None, ALU.mult, None, [Btm], [Bcf])
        S.dma(SP, g.rb_dtb[:], g.ssd_dt_bias.rearrange("l d h -> (l d h)").partition_broadcast(128), (), [Bcf])
        S.dma(SP, g.rb_A[:], g.ssd_a_log.rearrange("l d h -> (l d h)").partition_broadcast(128), (), [Bcf])
        S.dma(SP, g.rb_D[:], g.ssd_d.rearrange("l d h -> (l d h)").partition_broadcast(128), (), [Bcf])
        ACTV(S, g.rb_A[:], g.rb_A[:], AF.Exp, [Bcf], [Bcf])
        TSC(S, DVE, g.rb_A[:], g.rb_A[:], -1.0, None, ALU.mult, None, [Bcf], [Bcf])
        rbD = g.rb_D[:].rearrange("p (l d h) -> p l d h", l=NL, d=2)
        TT(S, DVE, g.rb_Ds[:].rearrange("p (l h) -> p l h", l=NL), rbD[:, :, 0, :], rbD[:, :, 1, :], ALU.add, [Bcf], [Bcf])
        S.pool(lambda e: e.memset(sT[:], 0.0), (), [BsT])
        for b in range(NB):
            o = CF["c"] + b * 8
            ACTV(S, sT[:, :, b], g.constf[:, o:o + 8], AF.Silu, [Bcf, BsT], [BsT])
        o = CF["cctx"]
        ACTV(S, sT[:, :, 2], g.constf[:, o:o + 8], AF.Silu, [Bcf, BsT], [BsT])
        k = 0
        for l in range(NL):
            wv = g.w_ada[l].rearrange("(k p) n -> p k n", p=128)
            for j in range(9):
                i = k % 2
                pi = 2 + (k % 2)
                k += 1
                S.dma(POOL, wbuf[i][:], wv[:, :, j * 1024:(j + 1) * 1024], (), [Bw[i]])
                for oc in range(8):
                    for kc in range(8):
                        MM(S, pst[pi][:, oc * 4:oc * 4 + 4], wbuf[i][:, kc, oc * 128:(oc + 1) * 128], sT[:, kc, :],
                           kc == 0, kc == 7, [Bw[i], BsT], [Bps[pi]])
                mo = ((l * 9 + j) * 8) * 4
                bo = CF["bada"] + (l * 9 + j) * 8
                TT(S, DVE, g.modT[:, mo:mo + 32].rearrange("p (k c) -> p k c", c=4),
                   pst[pi][:, 0:32].rearrange("p (k c) -> p k c", c=4),
                   bc(g.constf[:, bo:bo + 8].unsqueeze(2), [128, 8, 4]), ALU.add, [Bps[pi], Bcf], [Bmod])
        mv = g.modT[:].rearrange("p (l j r) -> p l j r", l=NL, j=9)
        for j in (1, 4, 7):
            TSC(S, DVE, mv[:, :, j, :], mv[:, :, j, :], 1.0, None, ALU.add, None, [Bmod], [Bmod])
        for j, sc in ((2, 0.5 / ALPHA), (8, 0.5 / ALPHA), (5, 1.0 / ALPHA)):
            TSC(S, DVE, mv[:, :, j, :], mv[:, :, j, :], sc, None, ALU.mult, None, [Bmod], [Bmod])
        S.emit()
    nc.all_engine_barrier()


def phase_p0(nc, g):
    with contextlib.ExitStack() as st:
        sb = lambda n, s, d: st.enter_context(nc.sbuf_tensor(uname(n), s, d))
        tin = [sb("p0in%d" % i, [128, 1024], F32) for i in range(3)]
        stg = [sb("p0st%d" % i, [128, 8, 512], F32) for i in range(2)]
        ps = [st.enter_context(nc.psum_tensor(uname("p0ps%d" % i), [128, 512], F32)) for i in range(4)]
        S = Sched(nc)
        Bin = [Buf() for _ in range(3)]
        Bst = [Buf(), Buf()]
        Bps = [Buf() for _ in range(4)]
        Bc = g.Bconst
        k = 0
        gi = 0
        for b in range(NB):
            groups = [(g.ctx[b], 0, 256)] + [(g.x[b, i * 512:(i + 1) * 512, :], CTX + i * 512, 512) for i in range(4)]
            for src, s0, n in groups:
                sg = gi % 2
                gi += 1
                for t in range(n // 128):
                    i = k % 3
                    S.dma(SP, tin[i][:], src[t * 128:(t + 1) * 128, :], (), [Bin[i]])
                    for half in range(2):
                        pi = (2 * k + half) % 4
                        for q in range(4):
                            kc = half * 4 + q
                            TR(S, ps[pi][:, q * 128:(q + 1) * 128], tin[i][:, kc * 128:(kc + 1) * 128], g.masks["ID"][:],
                               [Bin[i], Bc], [Bps[pi]])
                        COPY(S, ACT if half == 0 else DVE, stg[sg][:, half * 4:half * 4 + 4, t * 128:(t + 1) * 128],
                             ps[pi][:].rearrange("p (q t) -> p q t", q=4), [Bps[pi]], [Bst[sg]])
                    k += 1
                col = b * TB + s0
                S.dma(SP, g.HT[:, :, col:col + n], stg[sg][:, :, 0:n], [Bst[sg]], ())
        S.emit()
    nc.all_engine_barrier()


def phase_final(nc, g):
    with contextlib.ExitStack() as st:
        sb = lambda n, s, d: st.enter_context(nc.sbuf_tensor(uname(n), s, d))
        hin = [sb("pfin%d" % i, [128, 8, 512], F32) for i in range(2)]
        to = [sb("pfo%d" % i, [128, 1024], F32) for i in range(3)]
        ps = [st.enter_context(nc.psum_tensor(uname("pfps%d" % i), [128, 512], F32)) for i in range(4)]
        S = Sched(nc)
        Bin = [Buf(), Buf()]
        Bo = [Buf() for _ in range(3)]
        Bps = [Buf() for _ in range(4)]
        Bc = g.Bconst
        k = 0
        gi = 0
        for b in range(NB):
            for i4 in range(4):
                sg = gi % 2
                gi += 1
                col = b * TB + CTX + i4 * 512
                S.dma(SP, hin[sg][:], g.HT[:, :, col:col + 512], (), [Bin[sg]])
                for t in range(4):
                    oi = k % 3
                    for half in range(2):
                        pi = (2 * k + half) % 4
                        for q in range(4):
                            kc = half * 4 + q
                            TR(S, ps[pi][:, q * 128:(q + 1) * 128], hin[sg][:, kc, t * 128:(t + 1) * 128], g.masks["ID"][:],
                               [Bin[sg], Bc], [Bps[pi]])
                        COPY(S, ACT if half == 0 else DVE, to[oi][:, half * 512:(half + 1) * 512], ps[pi][:], [Bps[pi]], [Bo[oi]])
                    r0 = i4 * 512 + t * 128
                    S.dma(SP, g.out[b, r0:r0 + 128, :], to[oi][:], [Bo[oi]], ())
                    k += 1
        S.emit()
    nc.all_engine_barrier()


def load_weight_cast(S, dst3, src2, nk, ncols, Bw, piece=1024):
    sv = src2.rearrange("(k p) n -> p k n", p=128)
    for kc in range(nk):
        c0 = 0
        while c0 < ncols:
            cn = min(piece, ncols - c0)
            S.dma(POOL, dst3[:, kc, c0:c0 + cn], sv[:, kc, c0:c0 + cn], (), [Bw[kc]])
            c0 += cn


def ln_part1(S, g, zt, Bz, n, W):
    zbf, sq, Bzs = W["zbf"], W["sq"], W["Bzs"]
    COPY(S, ACT, zbf[:, :, 0:n], zt[:, :, 0:n], [Bz], [Bzs])
    ACTV(S, sq[:, :, 0:n], zt[:, :, 0:n], AF.Square, [Bz], [Bzs])


def ln_part2(S, g, l, i, zt, Bz, n, W):
    zbf, sq, Bzs = W["zbf"], W["sq"], W["Bzs"]
    psm, psq, Bpm, Bpq = W["psm"], W["psq"], W["Bpm"], W["Bpq"]
    mean, msq, var, Bsm = W["mean"], W["msq"], W["var"], W["Bsm"]
    Bc = g.Bconst
    for kc in range(8):
        MM(S, psm[:, 0:n], g.onesb[:], zbf[:, kc, 0:n], kc == 0, kc == 7, [Bzs, Bc], [Bpm])
    for kc in range(8):
        MM(S, psq[:, 0:n], g.onesb[:], sq[:, kc, 0:n], kc == 0, kc == 7, [Bzs, Bc], [Bpq])
    ACTV(S, mean[:, 0:n], psm[:, 0:n], AF.Identity, [Bpm], [Bsm], scale=1.0 / 1024)
    ACTV(S, msq[:, 0:n], psm[:, 0:n], AF.Square, [Bpm], [Bsm], scale=1.0 / 1024)
    STT(S, var[:, 0:n], psq[:, 0:n], 1.0 / 1024, msq[:, 0:n], ALU.mult, ALU.subtract, [Bpq, Bsm], [Bsm])
    TSC(S, DVE, var[:, 0:n], var[:, 0:n], EPSP, None, ALU.add, None, [Bsm], [Bsm])
    ACTV(S, var[:, 0:n], var[:, 0:n], AF.Sqrt, [Bsm], [Bsm])
    S.dve(lambda e: e.reciprocal(out=var[:, 0:n], in_=var[:, 0:n]), [Bsm], [Bsm])
    TT(S, DVE, zt[:, :, 0:n], zt[:, :, 0:n], bc(mean[:, 0:n].unsqueeze(1), [128, 8, n]), ALU.subtract, [Bz, Bsm], [Bz])
    TT(S, DVE, zt[:, :, 0:n], zt[:, :, 0:n], bc(var[:, 0:n].unsqueeze(1), [128, 8, n]), ALU.mult, [Bz, Bsm], [Bz])
    for kc in range(8):
        ACTV(S, zt[:, kc, 0:n], zt[:, kc, 0:n], AF.Identity, [Bz, Bc], [Bz],
             bias=cf(g, "ln_b", (l * 3 + i) * 8 + kc), scale=cf(g, "ln_g", (l * 3 + i) * 8 + kc))


def phase_ffn(nc, g, l, j, skip_ctx):
    n = TS_
    with contextlib.ExitStack() as st:
        sb = lambda nm, s, d: st.enter_context(nc.sbuf_tensor(uname(nm), s, d))
        wup = sb("wup", [128, 8, 2 * DFF], BF16)
        wdn = sb("wdn", [128, 22, D], BF16)
        hb = [sb("ffh%d" % i, [128, 8, n], F32) for i in range(3)]
        ub = [sb("ffu%d" % i, [128, 8, n], BF16) for i in range(2)]
        hid = sb("ffhid", [128, 22, n], BF16)
        sil = [sb("ffsil%d" % i, [128, n], F32) for i in range(2)]
        zbf = sb("ffzbf", [128, 8, n], BF16)
        sq = sb("ffsq", [128, 8, n], BF16)
        mean = sb("ffmean", [128, n], F32)
        msq = sb("ffmsq", [128, n], F32)
        var = sb("ffvar", [128, n], F32)
        ps = [st.enter_context(nc.psum_tensor(uname("ffps%d" % i), [128, 512], F32)) for i in range(8)]
        S = Sched(nc)
        Bc = g.Bconst
        Bwu = [Buf() for _ in range(8)]
        Bwd = [Buf() for _ in range(22)]
        Bh = [Buf(), Buf(), Buf()]
        Bu = [Buf(), Buf()]
        Bhid, Bzs, Bsm = Buf(), Buf(), Buf()
        Bsil = [Buf(), Buf()]
        Bps = [Buf() for _ in range(8)]
        load_weight_cast(S, wup, g.ffn_w_up[l, j], 8, 2 * DFF, Bwu, piece=1408)
        load_weight_cast(S, wdn, g.ffn_w_down[l, j], 22, D, Bwd)
        W = dict(zbf=zbf, sq=sq, Bzs=Bzs, psm=ps[6], psq=ps[7], Bpm=Bps[6], Bpq=Bps[7], mean=mean, msq=msq, var=var, Bsm=Bsm)
        tiles = [t for t in TILES if not (skip_ctx and t[3] == 2)]
        NT = len(tiles)
        pkc = [0]

        def stA(i):
            b, s0, nn, cls = tiles[i]
            col = b * TB + s0
            S.dma(SP, hb[i % 3][:], g.HT[:, :, col:col + n], (), [Bh[i % 3]])
            for kc in range(8):
                ACTV(S, ub[i % 2][:, kc, :], hb[i % 3][:, kc, :], AF.Identity, [Bh[i % 3], Bc], [Bu[i % 2]],
                     bias=mod_ap(g, l, 3 * (2 * j) + 0, kc, cls), scale=mod_ap(g, l, 3 * (2 * j) + 1, kc, cls))

        def stB(i):
            u_, Bu_ = ub[i % 2], Bu[i % 2]
            for fc in range(22):
                pk = pkc[0]
                pa, pv = ps[(pk % 2) * 2], ps[(pk % 2) * 2 + 1]
                Bpa, Bpv = Bps[(pk % 2) * 2], Bps[(pk % 2) * 2 + 1]
                si = pk % 2
                pkc[0] += 1
                for kc in range(8):
                    MM(S, pa[:, 0:n], wup[:, kc, fc * 128:(fc + 1) * 128], u_[:, kc, :], kc == 0, kc == 7, [Bwu[kc], Bu_], [Bpa])
                for kc in range(8):
                    MM(S, pv[:, 0:n], wup[:, kc, DFF + fc * 128:DFF + (fc + 1) * 128], u_[:, kc, :], kc == 0, kc == 7, [Bwu[kc], Bu_], [Bpv])
                ACTV(S, sil[si][:], pa[:, 0:n], AF.Silu, [Bpa], [Bsil[si]])
                TT(S, DVE, hid[:, fc, :], sil[si][:], pv[:, 0:n], ALU.mult, [Bsil[si], Bpv], [Bhid])

        def stC(i):
            b, s0, nn, cls = tiles[i]
            h_, Bh_ = hb[i % 3], Bh[i % 3]
            for oc in range(8):
                py, Bpy = ps[4 + oc % 2], Bps[4 + oc % 2]
                for fc in range(22):
                    MM(S, py[:, 0:n], wdn[:, fc, oc * 128:(oc + 1) * 128], hid[:, fc, :], fc == 0, fc == 21, [Bwd[fc], Bhid], [Bpy])
                STT(S, h_[:, oc, :], py[:, 0:n], mod_ap(g, l, 3 * (2 * j) + 2, oc, cls), h_[:, oc, :], ALU.mult, ALU.add,
                    [Bpy, Bh_, Bc], [Bh_])
            ln_part1(S, g, h_, Bh_, n, W)

        def stD(i):
            b, s0, nn, cls = tiles[i]
            col = b * TB + s0
            ln_part2(S, g, l, 2 * j, hb[i % 3], Bh[i % 3], n, W)
            S.dma(SP, g.HT[:, :, col:col + n], hb[i % 3][:], [Bh[i % 3]], ())

        stA(0)
        stB(0)
        for i in range(NT):
            if i + 1 < NT:
                stA(i + 1)
            stC(i)
            if i + 1 < NT:
                stB(i + 1)
            stD(i)
        S.emit()
    nc.all_engine_barrier()


def phase_mixpro(nc, g, l, b, U, BU):
    n = TS_
    odd = (l % 2 == 1)
    with contextlib.ExitStack() as st:
        sb = lambda nm, s, d: st.enter_context(nc.sbuf_tensor(uname(nm), s, d))
        hb = [sb("mph%d" % i, [128, 8, n], F32) for i in range(3)]
        S = Sched(nc)
        Bh = [Buf() for _ in range(3)]
        Bc = g.Bconst
        ti = 0
        for (bb, s0, nn, cls) in TILES:
            if bb != b:
                continue
            hi = ti % 3
            ti += 1
            col = b * TB + s0
            S.dma(SP, hb[hi][:], g.HT[:, :, col:col + n], (), [Bh[hi]])
            for kc in range(8):
                if cls == 2 or not odd:
                    dst = U[:, kc, s0:s0 + n]
                    src = hb[hi][:, kc, :]
                else:
                    r0 = (s0 - CTX) // 64
                    nr = n // 64
                    dst = U[:, kc, CTX:TB].rearrange("p (c r) -> p r c", r=32)[:, r0:r0 + nr, :]
                    src = hb[hi][:, kc, :].rearrange("p (r c) -> p r c", c=64)
                ACTV(S, dst, src, AF.Identity, [Bh[hi], Bc], [BU],
                     bias=mod_ap(g, l, 3, kc, cls), scale=mod_ap(g, l, 4, kc, cls))
        S.emit()
    nc.all_engine_barrier()


def phase_merge(nc, g, l, skip_ctx):
    n = TS_
    with contextlib.ExitStack() as st:
        sb = lambda nm, s, d: st.enter_context(nc.sbuf_tensor(uname(nm), s, d))
        wgt = sb("mgwg", [128, 8, 3072], BF16)
        wbr = sb("mgwb", [128, 24, 1024], BF16)
        wo = sb("mgwo", [128, 8, 1024], BF16)
        hb = [sb("mgh%d" % i, [128, 8, n], F32) for i in range(3)]
        ub = [sb("mgu0", [128, 8, n], BF16)] * 2
        sq0 = sb("mgsq0", [128, 8, n], BF16)
        yb = [[sb("mgy%d_%d" % (i, k), [128, 8, n], BF16) for k in range(3)] for i in range(2)]
        mb = sb("mgm", [128, 8, n], BF16)
        zbf = sb("mgzbf", [128, 8, n], BF16)
        sq = sb("mgsq", [128, 8, n], BF16)
        mean = sb("mgmean", [128, n], F32)
        msq = sb("mgmsq", [128, n], F32)
        var = sb("mgvar", [128, n], F32)
        rstd0 = [sb("mgrstd0%d" % i, [128, n], F32) for i in range(2)]
        sig = [sb("mgsig%d" % i, [128, n], F32) for i in range(2)]
        acc = sb("mgacc", [128, n], F32)
        tmp = sb("mgtmp", [128, n], F32)
        ps = [st.enter_context(nc.psum_tensor(uname("mgps%d" % i), [128, 512], F32)) for i in range(8)]
        S = Sched(nc)
        Bc = g.Bconst
        Bwg = [Buf() for _ in range(8)]
        Bwb = [Buf() for _ in range(24)]
        Bwo = [Buf() for _ in range(8)]
        Bh = [Buf(), Buf(), Buf()]
        By = [[Buf() for _ in range(3)] for _ in range(2)]
        Bm, Bzs, Bsm, Bacc, Btmp, Bsq0 = Buf(), Buf(), Buf(), Buf(), Buf(), Buf()
        Bu = [Buf()] * 2
        Br0 = [Buf(), Buf()]
        Bsig = [Buf(), Buf()]
        Bps = [Buf() for _ in range(8)]
        load_weight_cast(S, wgt, g.w_in[l][:, OFF_GATE:OFF_GATE + 3072], 8, 3072, Bwg)
        load_weight_cast(S, wbr, g.w_branch[l].rearrange("n k m -> (n k) m"), 24, 1024, Bwb)
        load_weight_cast(S, wo, g.w_out[l], 8, 1024, Bwo)
        import os as _os
        PL = DVE
        for kc in range(0 if _os.environ.get('MG_NOFOLD') else 8):
            TSC(S, DVE, wbr[:, kc, :], wbr[:, kc, :], cf(g, "sng", l * 8 + kc), None, ALU.mult, None, [Bwb[kc], Bc], [Bwb[kc]])
        W = dict(zbf=zbf, sq=sq, Bzs=Bzs, psm=ps[6], psq=ps[7], Bpm=Bps[6], Bpq=Bps[7], mean=mean, msq=msq, var=var, Bsm=Bsm)
        tiles = [t for t in TILES if not (skip_ctx and t[3] == 2)]
        NT = len(tiles)
        pkc = [0]

        def stA(i):
            b, s0, nn, cls = tiles[i]
            col = b * TB + s0
            hi = i % 2
            S.dma(SP, hb[i % 3][:], g.HT[:, :, col:col + n], (), [Bh[i % 3]])
            for k in range(3):
                S.dma(SP, yb[hi][k][:], g.Y[k][:, :, col:col + n], (), [By[hi][k]])
            for kc in range(8):
                ACTV(S, ub[hi][:, kc, :], hb[i % 3][:, kc, :], AF.Identity, [Bh[i % 3], Bc], [Bu[hi]],
                     bias=mod_ap(g, l, 3, kc, cls), scale=mod_ap(g, l, 4, kc, cls))
            ACTV(S, sq0[:], yb[hi][0][:], AF.Square, [By[hi][0]], [Bsq0])
            for kc in range(8):
                MM(S, ps[7][:, 0:n], g.onesb[:], sq0[:, kc, :], kc == 0, kc == 7, [Bsq0, Bc], [Bps[7]])
            TSC(S, DVE, rstd0[hi][:], ps[7][:, 0:n], 1.0 / 1024, EPS, ALU.mult, ALU.add, [Bps[7]], [Br0[hi]])
            ACTV(S, rstd0[hi][:], rstd0[hi][:], AF.Sqrt, [Br0[hi]], [Br0[hi]])
            S.dve(lambda e: e.reciprocal(out=rstd0[hi][:], in_=rstd0[hi][:]), [Br0[hi]], [Br0[hi]])

        def stB(i):
            hi = i % 2
            for oc in range(8):
                for k in range(3):
                    pk = pkc[0]
                    pg, pb = ps[(pk % 2) * 2], ps[(pk % 2) * 2 + 1]
                    Bpg, Bpb = Bps[(pk % 2) * 2], Bps[(pk % 2) * 2 + 1]
                    si = pk % 2
                    pkc[0] += 1
                    for kc in range(8):
                        MM(S, pg[:, 0:n], wgt[:, kc, k * 1024 + oc * 128:k * 1024 + (oc + 1) * 128], ub[hi][:, kc, :], kc == 0, kc == 7,
                           [Bwg[kc], Bu[hi]], [Bpg])
                    for kc in range(8):
                        MM(S, pb[:, 0:n], wbr[:, k * 8 + kc, oc * 128:(oc + 1) * 128], yb[hi][k][:, kc, :], kc == 0, kc == 7,
                           [Bwb[k * 8 + kc], By[hi][k]], [Bpb])
                    ACTV(S, sig[si][:], pg[:, 0:n], AF.Sigmoid, [Bpg], [Bsig[si]])
                    if k == 0:
                        TT(S, PL, sig[si][:], sig[si][:], rstd0[hi][:], ALU.mult, [Bsig[si], Br0[hi]], [Bsig[si]])
                        TT(S, DVE, acc[:], sig[si][:], pb[:, 0:n], ALU.mult, [Bsig[si], Bpb], [Bacc])
                    elif k == 1:
                        TT(S, DVE, tmp[:], sig[si][:], pb[:, 0:n], ALU.mult, [Bsig[si], Bpb], [Btmp])
                        TT(S, PL, acc[:], acc[:], tmp[:], ALU.add, [Bacc, Btmp], [Bacc])
                    else:
                        TT(S, DVE, tmp[:], sig[si][:], pb[:, 0:n], ALU.mult, [Bsig[si], Bpb], [Btmp])
                        TT(S, PL, mb[:, oc, :], acc[:], tmp[:], ALU.add, [Bacc, Btmp], [Bm])

        def stC(i):
            b, s0, nn, cls = tiles[i]
            h_, Bh_ = hb[i % 3], Bh[i % 3]
            for oc in range(8):
                py, Bpy = ps[4 + oc % 2], Bps[4 + oc % 2]
                for kc in range(8):
                    MM(S, py[:, 0:n], wo[:, kc, oc * 128:(oc + 1) * 128], mb[:, kc, :], kc == 0, kc == 7, [Bwo[kc], Bm], [Bpy])
                STT(S, h_[:, oc, :], py[:, 0:n], mod_ap(g, l, 5, oc, cls), h_[:, oc, :], ALU.mult, ALU.add,
                    [Bpy, Bh_, Bc], [Bh_])
            ln_part1(S, g, h_, Bh_, n, W)

        def stD(i):
            b, s0, nn, cls = tiles[i]
            col = b * TB + s0
            ln_part2(S, g, l, 1, hb[i % 3], Bh[i % 3], n, W)
            S.dma(SP, g.HT[:, :, col:col + n], hb[i % 3][:], [Bh[i % 3]], ())

        stA(0)
        stB(0)
        for i in range(NT):
            if i + 1 < NT:
                stA(i + 1)
            stC(i)
            if i + 1 < NT:
                stB(i + 1)
            stD(i)
        S.emit()
    nc.all_engine_barrier()


SEGS = [(0, 256)] + [(CTX + i * 512, 512) for i in range(4)]


def phase_lru(nc, g, l, b, U, BU):
    odd = (l % 2 == 1)
    Lh = 32 if odd else 64
    with contextlib.ExitStack() as st:
        sb = lambda nm, s, d: st.enter_context(nc.sbuf_tensor(uname(nm), s, d))
        wl = [sb("lrw%d" % i, [128, 8, 256], BF16) for i in range(2)]
        wblk = sb("lrblk", [128, 8, 4, 128], BF16)
        xr = sb("lrxr", [128, TB], F32)
        xc = sb("lrxc", [128, TB], F32)
        xcb = sb("lrxcb", [128, TB], BF16)
        gg = sb("lrgg", [128, TB], F32)
        T = [sb("lrT%d" % i, [128, TB], F32) for i in range(5)]
        hf = sb("lrhf", [128, TB], F32)
        hbk = sb("lrhb", [128, TB], F32)
        yst = [sb("lryst%d" % i, [128, TB], BF16) for i in range(2)]
        ps = [st.enter_context(nc.psum_tensor(uname("lrps%d" % i), [128, 512], F32)) for i in range(8)]
        S = Sched(nc)
        Bc = g.Bconst
        Bwl = [Buf(), Buf()]
        Bblk, Bxr, Bxc, Bxcb, Bgg, Bhf, Bhb = Buf(), Buf(), Buf(), Buf(), Buf(), Buf(), Buf()
        BT = [Buf() for _ in range(5)]
        Byst = [Buf(), Buf()]
        Bps = [Buf() for _ in range(8)]
        S.pool(lambda e: e.memset(wblk[:], 0.0), (), [Bblk])
        for d in range(2):
            for t, wsrc in enumerate((g.lru_w_a, g.lru_w_x)):
                for h in range(2):
                    src = wsrc[l, d].rearrange("(j h) k c -> h k j c", h=2)[h]
                    S.dma(POOL, wblk[h * 64:(h + 1) * 64, :, d * 2 + t, h * 64:(h + 1) * 64], src, (), [Bblk])
        wv = g.w_in[l].rearrange("(k p) n -> p k n", p=128)
        pk = 0
        for j in range(8):
            wi = j % 2
            S.dma(POOL, wl[wi][:, :, 0:128], wv[:, :, OFF_LX + j * 128:OFF_LX + (j + 1) * 128], (), [Bwl[wi]])
            S.dma(POOL, wl[wi][:, :, 128:256], wv[:, :, OFF_LG + j * 128:OFF_LG + (j + 1) * 128], (), [Bwl[wi]])
            for (s0, n) in SEGS:
                px, Bpx = ps[pk % 4], Bps[pk % 4]
                pg, Bpg = ps[(pk + 1) % 4], Bps[(pk + 1) % 4]
                pk += 2
                for kc in range(8):
                    MM(S, px[:, 0:n], wl[wi][:, kc, 0:128], U[:, kc, s0:s0 + n], kc == 0, kc == 7, [Bwl[wi], BU], [Bpx])
                for kc in range(8):
                    MM(S, pg[:, 0:n], wl[wi][:, kc, 128:256], U[:, kc, s0:s0 + n], kc == 0, kc == 7, [Bwl[wi], BU], [Bpg])
                COPY(S, DVE, xr[:, s0:s0 + n], px[:, 0:n], [Bpx], [Bxr])
                ACTV(S, gg[:, s0:s0 + n], pg[:, 0:n], AF.Gelu_apprx_tanh, [Bpg], [Bgg])
            cw = lambda k: cf(g, "lcw", (l * 4 + k) * 8 + j)
            ACTV(S, xc[:], xr[:], AF.Identity, [Bxr, Bc], [Bxc], bias=cf(g, "lcb", l * 8 + j), scale=cw(2))
            for (o0, ln_, nl) in ((0, 256, 1), (CTX, Lh, SEQ // Lh)):
                xv = xr[:, o0:o0 + ln_ * nl].rearrange("p (a b) -> p a b", b=ln_)
                ov = xc[:, o0:o0 + ln_ * nl].rearrange("p (a b) -> p a b", b=ln_)
                STT(S, ov[:, :, 2:ln_], xv[:, :, 0:ln_ - 2], cw(0), ov[:, :, 2:ln_], ALU.mult, ALU.add, [Bxr, Bxc, Bc], [Bxc])
                STT(S, ov[:, :, 1:ln_], xv[:, :, 0:ln_ - 1], cw(1), ov[:, :, 1:ln_], ALU.mult, ALU.add, [Bxr, Bxc, Bc], [Bxc])
                STT(S, ov[:, :, 0:ln_ - 1], xv[:, :, 1:ln_], cw(3), ov[:, :, 0:ln_ - 1], ALU.mult, ALU.add, [Bxr, Bxc, Bc], [Bxc])
            COPY(S, ACT, xcb[:], xc[:], [Bxc], [Bxcb])
            for d in range(2):
                ci = (l * 2 + d) * 8 + j
                for (s0, n) in SEGS:
                    pr, Bpr = ps[4 + pk % 4], Bps[4 + pk % 4]
                    pi_, Bpi = ps[4 + (pk + 1) % 4], Bps[4 + (pk + 1) % 4]
                    pk += 2
                    MM(S, pr[:, 0:n], wblk[:, j, d * 2 + 0, :], xcb[:, s0:s0 + n], True, True, [Bblk, Bxcb], [Bpr])
                    MM(S, pi_[:, 0:n], wblk[:, j, d * 2 + 1, :], xcb[:, s0:s0 + n], True, True, [Bblk, Bxcb], [Bpi])
                    ACTV(S, T[0][:, s0:s0 + n], pr[:, 0:n], AF.Sigmoid, [Bpr, Bc], [BT[0]], bias=cf(g, "lba", ci))
                    ACTV(S, T[1][:, s0:s0 + n], pi_[:, 0:n], AF.Sigmoid, [Bpi, Bc], [BT[1]], bias=cf(g, "lbx", ci))
                ACTV(S, T[2][:], T[0][:], AF.Exp, [BT[0], Bc], [BT[2]], scale=cf(g, "lsp8", ci))
                TSC(S, DVE, T[3][:], T[0][:], cf(g, "lsp16", ci), None, ALU.mult, None, [BT[0], Bc], [BT[3]])
                TSC(S, DVE, T[4][:], T[3][:], 1.0 / 120, 1.0 / 24, ALU.mult, ALU.add, [BT[3]], [BT[4]])
                TT(S, DVE, T[4][:], T[4][:], T[3][:], ALU.mult, [BT[4], BT[3]], [BT[4]])
                for cst in (1.0 / 6, 0.5, 1.0):
                    STT(S, T[4][:], T[4][:], cst, T[3][:], ALU.add, ALU.mult, [BT[4], BT[3]], [BT[4]])
                ACTV(S, T[4][:], T[4][:], AF.Sqrt, [BT[4]], [BT[4]], scale=-1.0)
                TT(S, POOL, T[1][:], T[1][:], T[4][:], ALU.mult, [BT[1], BT[4]], [BT[1]])
                TT(S, POOL, T[1][:], T[1][:], xc[:], ALU.mult, [BT[1], Bxc], [BT[1]])
                if d == 0:
                    S.dve(lambda e: e.tensor_tensor_scan(out=hf[:], data0=T[2][:], data1=T[1][:], initial=0.0,
                                                         op0=ALU.mult, op1=ALU.add), [BT[2], BT[1]], [Bhf])
                else:
                    S.dve(lambda e: e.tensor_tensor_scan(out=hbk[:, 0:CTX][:, ::-1], data0=T[2][:, 0:CTX][:, ::-1],
                                                         data1=T[1][:, 0:CTX][:, ::-1], initial=0.0,
                                                         op0=ALU.mult, op1=ALU.add), [BT[2], BT[1]], [Bhb])
                    S.dve(lambda e: e.tensor_tensor_scan(out=hbk[:, CTX:TB][:, ::-1], data0=T[2][:, CTX:TB][:, ::-1],
                                                         data1=T[1][:, CTX:TB][:, ::-1], initial=hbk[:, 0:1],
                                                         op0=ALU.mult, op1=ALU.add), [BT[2], BT[1], Bhb], [Bhb])
            yi = j % 2
            TT(S, POOL, hf[:], hf[:], hbk[:], ALU.add, [Bhf, Bhb], [Bhf])
            TT(S, DVE, yst[yi][:, 0:CTX], hf[:, 0:CTX], gg[:, 0:CTX], ALU.mult, [Bhf, Bgg], [Byst[yi]])
            if odd:
                ov = yst[yi][:, CTX:TB].rearrange("p (r c) -> p c r", c=64)
                i0 = hf[:, CTX:TB].rearrange("p (c r) -> p c r", r=32)
                i1 = gg[:, CTX:TB].rearrange("p (c r) -> p c r", r=32)
                TT(S, DVE, ov, i0, i1, ALU.mult, [Bhf, Bgg], [Byst[yi]])
            else:
                TT(S, DVE, yst[yi][:, CTX:TB], hf[:, CTX:TB], gg[:, CTX:TB], ALU.mult, [Bhf, Bgg], [Byst[yi]])
            S.dma(SP, g.Y[1][:, j, b * TB:(b + 1) * TB], yst[yi][:], [Byst[yi]], ())
        S.emit()
    nc.all_engine_barrier()


def phase_mix(nc, g, l, which):
    with contextlib.ExitStack() as st:
        U = st.enter_context(nc.sbuf_tensor(uname("Umix"), [128, 8, TB], BF16))
        for b in range(NB):
            BU = Buf("U")
            phase_mixpro(nc, g, l, b, U, BU)
            BU = Buf("U")
            if "ssd" in which:
                phase_ssd(nc, g, l, b, U, BU)
            if "lru" in which:
                phase_lru(nc, g, l, b, U, BU)
            if "gla" in which:
                phase_gla(nc, g, l, b, U, BU)


FWD_CHUNKS = list(range(18))
REV_CHUNKS = [1, 0] + list(range(17, 1, -1))


def conv_block(S, g, raw, Braw, out, Bout, wfn, bias_ap, Lh):
    Bc = g.Bconst
    ACTV(S, out[:], raw[:], AF.Identity, [Braw, Bc], [Bout], bias=bias_ap, scale=wfn(2))
    for (o0, ln_, nl) in ((0, 256, 1), (CTX, Lh, SEQ // Lh)):
        xv = raw[:, o0:o0 + ln_ * nl].rearrange("p (a b) -> p a b", b=ln_)
        ov = out[:, o0:o0 + ln_ * nl].rearrange("p (a b) -> p a b", b=ln_)
        STT(S, ov[:, :, 2:ln_], xv[:, :, 0:ln_ - 2], wfn(0), ov[:, :, 2:ln_], ALU.mult, ALU.add, [Braw, Bout, Bc], [Bout])
        STT(S, ov[:, :, 1:ln_], xv[:, :, 0:ln_ - 1], wfn(1), ov[:, :, 1:ln_], ALU.mult, ALU.add, [Braw, Bout, Bc], [Bout])
        STT(S, ov[:, :, 0:ln_ - 1], xv[:, :, 1:ln_], wfn(3), ov[:, :, 0:ln_ - 1], ALU.mult, ALU.add, [Braw, Bout, Bc], [Bout])


def phase_ssd(nc, g, l, b, U, BU):
    odd = (l % 2 == 1)
    Lh = 32 if odd else 64
    M = g.masks
    with contextlib.ExitStack() as st:
        sb = lambda nm, s, d: st.enter_context(nc.sbuf_tensor(uname(nm), s, d))
        wg = [sb("sdw0", [128, 8, 784], BF16)] * 2
        szT = sb("sdsz", [128, 2, TB], BF16)
        craw = sb("sdcraw", [128, TB], F32)
        ctmp = sb("sdctmp", [128, TB], F32)
        xsT = sb("sdxsT", [128, 2, TB], F32)
        BTf = sb("sdBTf", [128, TB], F32)
        BT = sb("sdBT", [128, TB], BF16)
        CT = sb("sdCT", [128, TB], BF16)
        dtv = sb("sddt", [128, 144], F32)
        av = sb("sda", [128, 144], F32)
        acs = sb("sdacs", [128, 144], F32)
        tot = sb("sdtot", [128, 144], F32)
        tm = sb("sdtm", [128, 144], F32)
        fs = sb("sdfs", [128, 144], F32)
        te = sb("sdte", [128, 144], F32)
        cd = sb("sdcd", [128, 144], F32)
        yacc = sb("sdyacc", [128, 18, 256], F32)
        Sst = sb("sdS", [128, 2, 256], F32)
        Sbf = sb("sdSb", [128, 2, 256], BF16)
        yst = [sb("sdyst0", [128, 2, TB], BF16)] * 2
        xs_all = sb("sdxsall", [128, 18, 256], F32)
        B_all = sb("sdBall", [128, 18, 128], BF16)
        xsd_ = [sb("sdxsd%d" % i, [128, 256], BF16) for i in range(2)]
        xw_ = [sb("sdxw%d" % i, [128, 256], BF16) for i in range(2)]
        scm_ = [sb("sdscm%d" % i, [128, 128], BF16) for i in range(2)]
        rhsA_ = [sb("sdrhsA%d" % i, [128, 512], F32) for i in range(2)]
        Eb_ = [sb("sdE%d" % i, [128, 512], BF16) for i in range(2)]
        MT_ = [sb("sdMT%d" % i, [128, 512], BF16) for i in range(2)]
        t1_ = [sb("sdt1%d" % i, [128, 256], F32) for i in range(2)]
        t2 = sb("sdt2", [128, 256], F32)
        ps = [st.enter_context(nc.psum_tensor(uname("sdps%d" % i), [128, 512], F32)) for i in range(8)]
        S = Sched(nc)
        Bc = g.Bconst
        Bwg = [Buf()] * 2
        (Bsz, Bcraw, Bctmp, BxsT, BBTf, BBT, BCT, Bdt, Ba, Bacs, Btot, Btm, Bfs, Bte, Bcd, Byacc, BS, BSb,
         Bxt, BBtok, Bxsd, Bxw, Bscm, BrhsA, BE, BMT, Bt1, Bt2) = [Buf() for _ in range(28)]
        Byst = [Buf()] * 2
        Bxall, BBall = Buf(), Buf()
        Bxsd_, Bxw_, Bscm_, BrhsA_, BE_, BMT_, Bt1_, BS_, BSb_ = [[Buf(), Buf()] for _ in range(9)]
        Bps = [Buf() for _ in range(8)]
        wv = g.w_in[l].rearrange("(k p) n -> p k n", p=128)
        v4 = lambda t: t[:].rearrange("p (c d h) -> p c d h", d=2, h=4)
        pk = 0
        for gq in range(4):
            wi = gq % 2
            for (d0, c0, cn) in ((0, OFF_Z + 256 * gq, 256), (256, OFF_XBC + 256 * gq, 256),
                                 (512, OFF_XBC + 1024 + 128 * gq, 128), (640, OFF_XBC + 1536 + 128 * gq, 128),
                                 (768, OFF_DT + 4 * gq, 4), (772, OFF_DT + 16 + 4 * gq, 4)):
                S.dma(POOL, wg[wi][:, :, d0:d0 + cn], wv[:, :, c0:c0 + cn], (), [Bwg[wi]])

            def inproj(c0, evac):
                nonlocal pk
                for (s0, n) in SEGS:
                    p_, Bp = ps[pk % 2], Bps[pk % 2]
                    pk += 1
                    for kc in range(8):
                        MM(S, p_[:, 0:n], wg[wi][:, kc, c0:c0 + 128], U[:, kc, s0:s0 + n], kc == 0, kc == 7, [Bwg[wi], BU], [Bp])
                    evac(p_, Bp, s0, n)

            for i in range(2):
                inproj(i * 128, lambda p_, Bp, s0, n, i=i: ACTV(S, szT[:, i, s0:s0 + n], p_[:, 0:n], AF.Silu, [Bp], [Bsz]))
            for ci in range(4):
                inproj(256 + ci * 128, lambda p_, Bp, s0, n: COPY(S, DVE, craw[:, s0:s0 + n], p_[:, 0:n], [Bp], [Bcraw]))
                ch16 = (2 * gq + ci) if ci < 2 else (8 + gq if ci == 2 else 12 + gq)
                conv_block(S, g, craw, Bcraw, ctmp, Bctmp, lambda k, ch16=ch16: cf(g, "scw", (l * 4 + k) * 16 + ch16),
                           cf(g, "scb", l * 16 + ch16), Lh)
                if ci < 2:
                    ACTV(S, xsT[:, ci, :], ctmp[:], AF.Silu, [Bctmp], [BxsT])
                elif ci == 2:
                    ACTV(S, BTf[:], ctmp[:], AF.Silu, [Bctmp], [BBTf])
                    COPY(S, DVE, BT[:], BTf[:], [BBTf], [BBT])
                else:
                    ACTV(S, CT[:], ctmp[:], AF.Silu, [Bctmp], [BCT])
            pdt, Bpdt = ps[2], Bps[2]
            for c in range(18):
                for kc in range(8):
                    MM(S, pdt[:, c * 8:(c + 1) * 8], U[:, kc, c * 128:(c + 1) * 128], wg[wi][:, kc, 768:776], kc == 0, kc == 7,
                       [Bwg[wi], BU], [Bpdt])
            rb = lambda t: bc(t[:, l * 32:(l + 1) * 32].rearrange("p (d h) -> p d h", d=2)[:, :, 4 * gq:4 * gq + 4].unsqueeze(1), [128, 18, 2, 4])
            TT(S, DVE, v4(dtv), pdt[:, 0:144].rearrange("p (c d h) -> p c d h", d=2, h=4), rb(g.rb_dtb), ALU.add, [Bpdt, Bc], [Bdt])
            ACTV(S, dtv[:], dtv[:], AF.Exp, [Bdt], [Bdt])
            ACTV(S, dtv[:], dtv[:], AF.Ln, [Bdt], [Bdt], bias=1.0)
            TT(S, DVE, v4(av), v4(dtv), rb(g.rb_A), ALU.mult, [Bdt, Bc], [Ba])
            MM(S, ps[3][:, 0:144], M["LE"][:], av[:], True, True, [Ba, Bc], [Bps[3]])
            MM(S, ps[4][:, 0:144], M["ONES"][:], av[:], True, True, [Ba, Bc], [Bps[4]])
            COPY(S, DVE, acs[:], ps[3][:, 0:144], [Bps[3]], [Bacs])
            COPY(S, DVE, tot[:], ps[4][:, 0:144], [Bps[4]], [Btot])
            ACTV(S, cd[:], tot[:], AF.Exp, [Btot], [Bcd])
            ACTV(S, v4(fs)[:, :, 0, :], v4(acs)[:, :, 0, :], AF.Exp, [Bacs], [Bfs])
            TT(S, DVE, v4(tm)[:, :, 0, :], v4(tot)[:, :, 0, :], v4(acs)[:, :, 0, :], ALU.subtract, [Btot, Bacs], [Btm])
            TT(S, DVE, v4(tm)[:, :, 1, :], v4(acs)[:, :, 1, :], v4(av)[:, :, 1, :], ALU.subtract, [Bacs, Ba], [Btm])
            ACTV(S, te[:], tm[:], AF.Exp, [Btm], [Bte])
            TT(S, DVE, v4(tm)[:, :, 1, :], v4(tot)[:, :, 1, :], v4(tm)[:, :, 1, :], ALU.subtract, [Btot, Btm], [Btm])
            ACTV(S, v4(fs)[:, :, 1, :], v4(tm)[:, :, 1, :], AF.Exp, [Btm], [Bfs])
            yi = 0
            h3 = lambda t: t.rearrange("p (h q) -> p h q", h=4)
            for c in range(18):
                cs = slice(c * 128, (c + 1) * 128)
                ptr, Bptr = ps[2 + c % 2], Bps[2 + c % 2]
                for i in range(2):
                    TR(S, ptr[:, i * 128:(i + 1) * 128], xsT[:, i, cs], M["ID"][:], [BxsT, Bc], [Bptr])
                TR(S, ptr[:, 256:384], BTf[:, cs], M["ID"][:], [BBTf, Bc], [Bptr])
                COPY(S, ACT, xs_all[:, c, :], ptr[:, 0:256], [Bptr], [Bxall])
                COPY(S, ACT, B_all[:, c, :], ptr[:, 256:384], [Bptr], [BBall])
            seen = set()

            def ssd_iter(d, c):
                M1 = M["GT"] if d == 0 else M["LT"]
                M2 = M["LE"] if d == 0 else M["GE"]
                cs = slice(c * 128, (c + 1) * 128)
                xsd, xw, scm, rhsA, Eb, MT, t1 = xsd_[d], xw_[d], scm_[d], rhsA_[d], Eb_[d], MT_[d], t1_[d]
                Bxsd, Bxw, Bscm, BrhsA, BE, BMT, Bt1 = Bxsd_[d], Bxw_[d], Bscm_[d], BrhsA_[d], BE_[d], BMT_[d], Bt1_[d]
                pA, BpA = ps[2 + 3 * d], Bps[2 + 3 * d]
                pS, BpS = ps[3 + 3 * d], Bps[3 + 3 * d]
                pY, BpY = ps[4 + 3 * d], Bps[4 + 3 * d]
                xs_tok = xs_all[:, c, :]
                dt_c = bc(v4(dtv)[:, c, d, :].unsqueeze(2), [128, 4, 64])
                te_c = bc(v4(te)[:, c, d, :].unsqueeze(2), [128, 4, 64])
                fs_c = bc(v4(fs)[:, c, d, :].unsqueeze(2), [128, 4, 64])
                cd_c = bc(v4(cd)[:, c, d, :].unsqueeze(2), [128, 4, 64])
                TT(S, DVE, h3(xsd[:]), h3(xs_tok), dt_c, ALU.mult, [Bxall, Bdt], [Bxsd])
                TT(S, DVE, h3(xw[:]), h3(xsd[:]), te_c, ALU.mult, [Bxsd, Bte], [Bxw])
                MM(S, pA[:, 0:128], BT[:, cs], CT[:, cs], True, True, [BBT, BCT], [BpA])
                TT(S, DVE, scm[:], pA[:, 0:128], (M["LE"] if d == 0 else M["GE"])[:], ALU.mult, [BpA, Bc], [Bscm])
                TT(S, POOL, h3(rhsA[:]), bc(M2[:].unsqueeze(1), [128, 4, 128]), bc(v4(av)[:, c, d, :].unsqueeze(2), [128, 4, 128]),
                   ALU.mult, [Ba, Bc], [BrhsA])
                MM(S, pS[:], M1[:], rhsA[:], True, True, [BrhsA, Bc], [BpS])
                ACTV(S, Eb[:], pS[:], AF.Exp, [BpS], [BE])
                TT(S, DVE, h3(MT[:]), h3(Eb[:]), bc(scm[:].unsqueeze(1), [128, 4, 128]), ALU.mult, [BE, Bscm], [BMT])
                for hh in range(4):
                    MM(S, pY[:, hh * 64:(hh + 1) * 64], MT[:, hh * 128:(hh + 1) * 128], xsd[:, hh * 64:(hh + 1) * 64], True, True,
                       [BMT, Bxsd], [BpY])
                MM(S, pY[:, 256:512], CT[:, cs], Sbf[:, d, :], True, True, [BCT, BSb_[d]], [BpY])
                TT(S, DVE, h3(t1[:]), h3(pY[:, 256:512]), fs_c, ALU.mult, [BpY, Bfs], [Bt1])
                if c not in seen:
                    seen.add(c)
                    TT(S, DVE, yacc[:, c, :], pY[:, 0:256], t1[:], ALU.add, [BpY, Bt1], [Byacc])
                else:
                    TT(S, DVE, t1[:], pY[:, 0:256], t1[:], ALU.add, [BpY, Bt1], [Bt1])
                    TT(S, POOL, h3(t2[:]), h3(xs_tok),
                       bc(g.rb_Ds[:, l * 16 + 4 * gq:l * 16 + 4 * gq + 4].unsqueeze(2), [128, 4, 64]), ALU.mult, [Bxall, Bc], [Bt2])
                    TT(S, POOL, t1[:], t1[:], t2[:], ALU.add, [Bt1, Bt2], [Bt1])
                    TT(S, DVE, yacc[:, c, :], yacc[:, c, :], t1[:], ALU.add, [Byacc, Bt1], [Byacc])
                    pf, Bpf = ps[1], Bps[1]
                    for i in range(2):
                        TR(S, pf[:, i * 128:(i + 1) * 128], yacc[:, c, i * 128:(i + 1) * 128], M["ID"][:], [Byacc, Bc], [Bpf])
                    for i in range(2):
                        if odd and c >= 2:
                            cc0 = (c - 2) * 4
                            ov = yst[yi][:, i, CTX:TB].rearrange("p (r c) -> p c r", c=64)[:, cc0:cc0 + 4, :]
                            i0 = pf[:, i * 128:(i + 1) * 128].rearrange("p (c r) -> p c r", r=32)
                            i1 = szT[:, i, cs].rearrange("p (c r) -> p c r", r=32)
                        else:
                            ov, i0, i1 = yst[yi][:, i, cs], pf[:, i * 128:(i + 1) * 128], szT[:, i, cs]
                        TT(S, DVE, ov, i0, i1, ALU.mult, [Bpf, Bsz], [Byst[yi]])
                MM(S, pA[:, 128:384], B_all[:, c, :], xw[:], True, True, [BBall, Bxw], [BpA])
                TT(S, POOL, h3(Sst[:, d, :]), h3(Sst[:, d, :]), cd_c, ALU.mult, [BS_[d], Bcd], [BS_[d]])
                TT(S, DVE, Sst[:, d, :], Sst[:, d, :], pA[:, 128:384], ALU.add, [BS_[d], BpA], [BS_[d]])
                COPY(S, ACT, Sbf[:, d, :], Sst[:, d, :], [BS_[d]], [BSb_[d]])

            for d in range(2):
                S.pool(lambda e, d=d: e.memset(Sst[:, d, :], 0.0), (), [BS_[d]])
                S.pool(lambda e, d=d: e.memset(Sbf[:, d, :], 0.0), (), [BSb_[d]])
            for it in range(18):
                ssd_iter(0, FWD_CHUNKS[it])
                ssd_iter(1, REV_CHUNKS[it])
            S.dma(SP, g.Y[0][:, 2 * gq:2 * gq + 2, b * TB:(b + 1) * TB], yst[yi][:], [Byst[yi]], ())
        S.emit()
    nc.all_engine_barrier()


def phase_gla(nc, g, l, b, U, BU):
    odd = (l % 2 == 1)
    M = g.masks
    QS = 128.0 ** -0.5
    with contextlib.ExitStack() as st:
        sb = lambda nm, s, d: st.enter_context(nc.sbuf_tensor(uname(nm), s, d))
        wq = sb("glw", [128, 8, 896], BF16)
        WG = sb("glWG", [128, 256], BF16)
        bgb = sb("glbg", [128, 256], F32)
        gng = sb("glgng", [128, 256], F32)
        qT = sb("glqT", [128, TB], F32)
        kT = sb("glkT", [128, TB], F32)
        sgT = sb("glsg", [128, 2, TB], BF16)
        alrT = sb("glalr", [128, TB], BF16)
        v_tok = sb("glv", [128, 18, 256], BF16)
        k_tok = sb("glk", [128, 18, 128], F32)
        lsp = sb("gllsp", [128, 18, 256], F32)
        oacc = sb("gloacc", [128, 18, 256], F32)
        yst = [sb("glyst0", [128, 2, TB], BF16)] * 2
        eq_ = [sb("gleq%d" % i, [128, 128], F32) for i in range(2)]
        ek_ = [sb("glek%d" % i, [128, 128], F32) for i in range(2)]
        qin_ = [sb("glqin%d" % i, [128, 128], BF16) for i in range(2)]
        kin_ = [sb("glkin%d" % i, [128, 128], BF16) for i in range(2)]
        er_ = [sb("gler%d" % i, [128, 128], F32) for i in range(2)]
        kst_ = [[sb("glkst%d_%d" % (d_, i), [128, 128], BF16) for i in range(2)] for d_ in range(2)]
        qinh_ = [[sb("glqinh%d_%d" % (d_, i), [128, 128], BF16) for i in range(2)] for d_ in range(2)]
        attT_ = [sb("glatt%d" % i, [128, 128], BF16) for i in range(2)]
        Sst_ = [sb("glS%d" % i, [128, 256], F32) for i in range(2)]
        Sb0_ = [sb("glSb0%d" % i, [128, 256], BF16) for i in range(2)]
        Sb1_ = [sb("glSb1%d" % i, [128, 256], BF16) for i in range(2)]
        osq = sb("glosq", [128, 256], F32)
        ssq = sb("glssq", [128, 1], F32)
        on = sb("glon", [128, 256], F32)
        ps = [st.enter_context(nc.psum_tensor(uname("glps%d" % i), [128, 512], F32)) for i in range(8)]
        S = Sched(nc)
        Bc = g.Bconst
        (Bwq, BWG, Bbg, BqT, BkT, Bsg, Balr, Bv, Bk, Blsp, Boacc, Beq, Bek, Bqin, Bkin, Ber, Bkst, Batt, BS, BSb0, BSb1,
         Bosq, Bssq, Bon) = [Buf() for _ in range(24)]
        Byst = [Buf()] * 2
        Beq_, Bek_, Bqin_, Bkin_, Ber_, Bkst_, Bqinh_, Batt_, BS_, BSb0_, BSb1_ = [[Buf(), Buf()] for _ in range(11)]
        Bps = [Buf() for _ in range(8)]
        wv = g.w_in[l].rearrange("(k p) n -> p k n", p=128)
        pk = 0
        Bgng = Buf()
        S.dma(SP, gng[:], g.gla_norm_g[l].partition_broadcast(128), (), [Bgng])
        for d_ in range(2):
            for i_ in range(2):
                S.pool(lambda e, i_=i_, d_=d_: e.memset(qinh_[d_][i_][:], 0.0), (), [Bqinh_[d_]])
        for hd in range(4):
            S.pool(lambda e: e.memset(wq[:, :, 768:896], 0.0), (), [Bwq])
            for (d0, c0, cn) in ((0, OFF_Q + 128 * hd, 128), (128, OFF_K + 128 * hd, 128), (256, OFF_V + 256 * hd, 256),
                                 (512, OFF_G + 256 * hd, 256), (768, OFF_ALR, 16), (800, OFF_ALR + 16, 16)):
                S.dma(POOL, wq[:, :, d0:d0 + cn], wv[:, :, c0:c0 + cn], (), [Bwq])
            import os as _os
            S.pool(lambda e: e.memset(WG[:], 0.0), (), [BWG])
            for d in range(0 if _os.environ.get('GLA_NOWG') else 2):
                S.dma(POOL, WG[32 * d:32 * d + 16, d * 128:(d + 1) * 128], g.gla_w_gate[l, d, :, hd * 128:(hd + 1) * 128], (), [BWG])
                S.dma(SP, bgb[:, d * 128:(d + 1) * 128], g.gla_b_gate[l, d, hd * 128:(hd + 1) * 128].partition_broadcast(128), (), [Bbg])

            _ninp = [0]

            def inproj(c0, m, evac):
                nonlocal pk
                _ninp[0] += 1
                if _ninp[0] > int(_os.environ.get('GLA_INP', '9')):
                    return
                for (s0, n) in SEGS:
                    p_, Bp = ps[pk % 2], Bps[pk % 2]
                    pk += 1
                    for kc in range(8):
                        MM(S, p_[0:m, 0:n], wq[:, kc, c0:c0 + m], U[:, kc, s0:s0 + n], kc == 0, kc == 7, [Bwq, BU], [Bp])
                    evac(p_, Bp, s0, n)

            if float(_os.environ.get('GLA_DBG', '9')) == 0:
                continue
            inproj(0, 128, lambda p_, Bp, s0, n: COPY(S, DVE, qT[:, s0:s0 + n], p_[:, 0:n], [Bp], [BqT]))
            inproj(128, 128, lambda p_, Bp, s0, n: COPY(S, DVE, kT[:, s0:s0 + n], p_[:, 0:n], [Bp], [BkT]))
            for i in range(2):
                inproj(512 + i * 128, 128, lambda p_, Bp, s0, n, i=i: ACTV(S, sgT[:, i, s0:s0 + n], p_[:, 0:n], AF.Silu, [Bp], [Bsg]))
            inproj(768, 128, lambda p_, Bp, s0, n: COPY(S, DVE, alrT[:, s0:s0 + n], p_[:, 0:n], [Bp], [Balr]))
            _l2 = float(_os.environ.get('GLA_DBG', '9'))
            for c in range(18 if _l2 > 0.5 else 0):
                cs = slice(c * 128, (c + 1) * 128)
                p_, Bp = ps[pk % 2], Bps[pk % 2]
                pk += 1
                for kc in range(8):
                    MM(S, p_[:, 0:256], U[:, kc, cs], wq[:, kc, 256:512], kc == 0, kc == 7, [Bwq, BU], [Bp])
                for kc in range(8):
                    MM(S, ps[7][:, 0:128], U[:, kc, cs], wq[:, kc, 128:256], kc == 0, kc == 7, [Bwq, BU], [Bps[7]])
                COPY(S, ACT, v_tok[:, c, :], p_[:, 0:256], [Bp], [Bv])
                COPY(S, DVE, k_tok[:, c, :], ps[7][:, 0:128], [Bps[7]], [Bk])
                if _l2 > 0.7:
                    MM(S, ps[2][:, 0:256], alrT[:, cs], WG[:, :], True, True, [Balr, BWG], [Bps[2]])
                    TT(S, DVE, lsp[:, c, :], ps[2][:, 0:256], bgb[:], ALU.add, [Bps[2], Bbg], [Blsp])
            if _l2 > 0.8:
                ACTV(S, lsp[:], lsp[:], AF.Exp, [Blsp], [Blsp], scale=-1.0)
                ACTV(S, lsp[:], lsp[:], AF.Ln, [Blsp], [Blsp], bias=1.0)
            yi = 0
            _lvl = 9
            seen = set()

            def gla_iter(d, c):
                CM = M["LE64"] if d == 0 else M["GE64"]
                RM = M["GT64"] if d == 0 else M["LT64"]
                blocks = (0, 1) if d == 0 else (1, 0)
                eq, ek, er, qin, kin, kst, qinh, attT = eq_[d], ek_[d], er_[d], qin_[d], kin_[d], kst_[d], qinh_[d], attT_[d]
                Beq, Bek, Ber, Bqin, Bkin, Bkst, Bqinh, Batt = Beq_[d], Bek_[d], Ber_[d], Bqin_[d], Bkin_[d], Bkst_[d], Bqinh_[d], Batt_[d]
                Sst, Sb0, Sb1, BS, BSb0, BSb1 = Sst_[d], Sb0_[d], Sb1_[d], BS_[d], BSb0_[d], BSb1_[d]
                pP, BpP = ps[2 + 3 * d], Bps[2 + 3 * d]
                pA, BpA = ps[3 + 3 * d], Bps[3 + 3 * d]
                pO, BpO = ps[4 + 3 * d], Bps[4 + 3 * d]
                cs = slice(c * 128, (c + 1) * 128)
                ld = lsp[:, c, d * 128:(d + 1) * 128]
                MM(S, pP[:, 0:128], ld, CM[:], True, True, [Blsp, Bc], [BpP])
                MM(S, pP[:, 128:256], RM[:], ld, True, True, [Blsp, Bc], [BpP])
                ACTV(S, eq[:], pP[:, 0:128], AF.Exp, [BpP], [Beq], scale=-1.0 / 16)
                ACTV(S, ek[:], pP[:, 0:128], AF.Exp, [BpP], [Bek], scale=1.0 / 16)
                ACTV(S, er[:], pP[:, 128:256], AF.Exp, [BpP], [Ber], scale=-1.0 / 16)
                STT(S, qin[:], qT[:, cs], QS, eq[:], ALU.mult, ALU.mult, [BqT, Beq], [Bqin])
                TT(S, DVE, kin[:], kT[:, cs], ek[:], ALU.mult, [BkT, Bek], [Bkin])
                for bi_ in range(2):
                    STT(S, kst[bi_][:], k_tok[:, c, :], M["BD"][:, 64 * bi_:64 * bi_ + 1], er[:], ALU.mult, ALU.mult, [Bk, Ber, Bc], [Bkst])
                    hs_ = slice(64 * bi_, 64 * bi_ + 64)
                    COPY(S, POOL, qinh[bi_][:, hs_], qin[:, hs_], [Bqin], [Bqinh])
                MM(S, pA[:, 0:128], kin[:], qin[:], True, True, [Bkin, Bqin], [BpA])
                TT(S, DVE, attT[:], pA[:, 0:128], CM[:], ALU.mult, [BpA, Bc], [Batt])
                MM(S, pO[:, 0:256], attT[:], v_tok[:, c, :], True, False, [Batt, Bv], [BpO])
                for bi, blk in enumerate(blocks):
                    Sb, BSb = (Sb0, BSb0) if bi == 0 else (Sb1, BSb1)
                    MM(S, pO[:, 0:256], qinh[blk][:], Sb[:], False, bi == 1, [Bqinh, BSb], [BpO])
                    MM(S, pA[:, 128:384], kst[blk][:], v_tok[:, c, :], True, True, [Bkst, Bv], [BpA])
                    ecol = (blk * 64 + 63) if d == 0 else (blk * 64)
                    STT(S, Sst[:], Sst[:], eq[:, ecol:ecol + 1], pA[:, 128:384], ALU.mult, ALU.add, [BS, Beq, BpA], [BS])
                    if bi == 0:
                        COPY(S, ACT, Sb1[:], Sst[:], [BS], [BSb1])
                    else:
                        COPY(S, ACT, Sb0[:], Sst[:], [BS], [BSb0])
                if c not in seen:
                    seen.add(c)
                    COPY(S, ACT if d == 0 else DVE, oacc[:, c, :], pO[:, 0:256], [BpO], [Boacc])
                else:
                    TT(S, DVE, oacc[:, c, :], oacc[:, c, :], pO[:, 0:256], ALU.add, [Boacc, BpO], [Boacc])
                    ACTV(S, osq[:], oacc[:, c, :], AF.Square, [Boacc], [Bosq])
                    S.dve(lambda e: e.reduce_sum(out=ssq[:], in_=osq[:], axis=AX.X), [Bosq], [Bssq])
                    TSC(S, DVE, ssq[:], ssq[:], 1.0 / 256, EPS, ALU.mult, ALU.add, [Bssq], [Bssq])
                    ACTV(S, ssq[:], ssq[:], AF.Sqrt, [Bssq], [Bssq])
                    S.dve(lambda e: e.reciprocal(out=ssq[:], in_=ssq[:]), [Bssq], [Bssq])
                    STT(S, on[:], oacc[:, c, :], ssq[:, 0:1], gng[:], ALU.mult, ALU.mult, [Boacc, Bssq, Bgng], [Bon])
                    pf, Bpf = ps[1], Bps[1]
                    for i in range(2):
                        TR(S, pf[:, i * 128:(i + 1) * 128], on[:, i * 128:(i + 1) * 128], M["ID"][:], [Bon, Bc], [Bpf])
                    for i in range(2):
                        if odd and c >= 2:
                            cc0 = (c - 2) * 4
                            ov = yst[yi][:, i, CTX:TB].rearrange("p (r c) -> p c r", c=64)[:, cc0:cc0 + 4, :]
                            i0 = pf[:, i * 128:(i + 1) * 128].rearrange("p (c r) -> p c r", r=32)
                            i1 = sgT[:, i, cs].rearrange("p (c r) -> p c r", r=32)
                        else:
                            ov, i0, i1 = yst[yi][:, i, cs], pf[:, i * 128:(i + 1) * 128], sgT[:, i, cs]
                        TT(S, DVE, ov, i0, i1, ALU.mult, [Bpf, Bsg], [Byst[yi]])

            for d in range(2):
                S.pool(lambda e, d=d: e.memset(Sst_[d][:], 0.0), (), [BS_[d]])
                S.pool(lambda e, d=d: e.memset(Sb0_[d][:], 0.0), (), [BSb0_[d]])
            for it in range(18):
                gla_iter(0, FWD_CHUNKS[it])
                gla_iter(1, REV_CHUNKS[it])
            if _lvl >= 3:
                S.dma(SP, g.Y[2][:, 2 * hd:2 * hd + 2, b * TB:(b + 1) * TB], yst[yi][:], [Byst[yi]], ())
        S.emit()
    nc.all_engine_barrier()


def build_program(stages=None):
    nc = bass.Bass("TRN2", target_bir_lowering=False)
    g = G()
    declare_io(nc, g)
    with contextlib.ExitStack() as st:
        init_gsync(nc, st)
        sb = lambda nm, s, d: st.enter_context(nc.sbuf_tensor(uname(nm), s, d))
        g.constf = sb("constf", [128, NCF], F32)
        g.modT = sb("modT", [128, NL * 9 * 8 * 4], F32)
        g.masks = {nm: sb("mask_" + nm, [128, 128], F32) for nm in
                   ("ONES", "LE", "GT", "LT", "GE", "ID", "BD", "LE64", "GT64", "LT64", "GE64")}
        g.onesb = sb("onesb", [128, 128], BF16)
        g.rb_dtb = sb("rb_dtb", [128, NL * 32], F32)
        g.rb_A = sb("rb_A", [128, NL * 32], F32)
        g.rb_D = sb("rb_D", [128, NL * 32], F32)
        g.rb_Ds = sb("rb_Ds", [128, NL * 16], F32)
        if stages is None:
            stages = ["const", "p0"]
            for l in range(NL):
                stages += [("ffn", l, 0), ("mix", l), ("merge", l), ("ffn", l, 1)]
            stages += ["final"]
        for sg in stages:
            if sg == "const":
                phase_const(nc, g)
            elif sg == "p0":
                phase_p0(nc, g)
            elif sg == "final":
                phase_final(nc, g)
            elif sg[0] == "ffn":
                phase_ffn(nc, g, sg[1], sg[2], skip_ctx=(sg[1] == NL - 1 and sg[2] == 1))
            elif sg[0] == "mix":
                phase_mix(nc, g, sg[1], sg[2] if len(sg) > 2 else ("ssd", "lru", "gla"))
            elif sg[0] == "merge":
                phase_merge(nc, g, sg[1], skip_ctx=(sg[1] == NL - 1))
    return nc


_NC_CACHE = {}


def kernel(**inputs):
    from concourse.bass_utils import run_bass_kernel_spmd
    if "nc" not in _NC_CACHE:
        _NC_CACHE["nc"] = build_program()
    nc = _NC_CACHE["nc"]
    ncores = 8
    in_maps = []
    for i in range(ncores):
        m = {}
        for nm in INPUT_NAMES:
            a = np.asarray(inputs[nm], dtype=np.float32)
            if nm in ("x", "c", "ctx"):
                a = a[i * NB:(i + 1) * NB]
            elif nm == "c_ctx":
                a = a.reshape(1, D)
            m[nm] = np.ascontiguousarray(a)
        in_maps.append(m)
    res = run_bass_kernel_spmd(nc, in_maps, core_ids=list(range(ncores)))
    return np.concatenate([np.asarray(r["out"]) for r in res.results], axis=0).astype(np.float32)
```

```python
import numpy as np
import concourse.bass as bass
import concourse.mybir as mybir
from concourse.ap import AP

F32 = mybir.dt.float32
BF16 = mybir.dt.bfloat16
AF = mybir.ActivationFunctionType
ALU = mybir.AluOpType
AX = mybir.AxisListType

PE, ACT, DVE, POOL, SP = "pe", "act", "dve", "pool", "sp"
COMPUTE = (PE, ACT, DVE, POOL)
NDMASEM = 6


class Buf:
    __slots__ = ("name", "last_w", "readers")

    def __init__(self, name=""):
        self.name = name
        self.last_w = None
        self.readers = {}


class Op:
    __slots__ = ("eng", "fn", "deps", "idx", "dma", "signal", "sigval", "sem", "semval", "tag")


class Sched:
    def __init__(self, nc):
        self.nc = nc
        self.ops = {e: [] for e in (PE, ACT, DVE, POOL, SP)}
        self.n_dma = {SP: 0, POOL: 0, ACT: 0}

    def add(self, eng, fn, reads=(), writes=(), dma=False, tag=None):
        op = Op()
        op.eng, op.fn, op.dma, op.signal, op.tag = eng, fn, dma, False, tag
        op.idx = len(self.ops[eng])
        deps = {}

        def dep(d, kind):
            if d is None or d is op:
                return
            if d.eng == eng and not d.dma:
                if eng == PE or eng == SP:
                    return
                if kind == "WAR":
                    return
            key = id(d) if d.dma else d.eng
            cur = deps.get(key)
            if cur is None or (not d.dma and d.idx > cur.idx):
                deps[key] = d

        for b in reads:
            dep(b.last_w, "RAW")
        for b in writes:
            dep(b.last_w, "WAW")
            for r in b.readers.values():
                if isinstance(r, list):
                    for rr in r:
                        dep(rr, "WAR")
                else:
                    dep(r, "WAR")
        for b in reads:
            if dma:
                b.readers.setdefault("dma", []).append(op)
            else:
                b.readers[eng] = op
        for b in writes:
            b.last_w = op
            b.readers = {}
        op.deps = list(deps.values())
        for d in op.deps:
            d.signal = True
        self.ops[eng].append(op)
        return op

    def pe(self, fn, reads=(), writes=()):
        return self.add(PE, fn, reads, writes)

    def act(self, fn, reads=(), writes=()):
        return self.add(ACT, fn, reads, writes)

    def dve(self, fn, reads=(), writes=()):
        return self.add(DVE, fn, reads, writes)

    def pool(self, fn, reads=(), writes=()):
        return self.add(POOL, fn, reads, writes)

    def dma(self, q, out, in_, reads=(), writes=(), **kw):
        return self.add(q, lambda e: e.dma_start(out=out, in_=in_, **kw), reads, writes, dma=True)

    def emit(self, final_wait_all_dma=True):
        nc = self.nc
        gs = GSYNC[0]
        esem, dsem = gs["esem"], gs["dsem"]
        for e in COMPUTE:
            c = gs["ebase"][e]
            for op in self.ops[e]:
                if op.dma:
                    continue
                if op.signal:
                    c += 1
                    op.sigval = c
            gs["ebase"][e] = c
        for q in (SP, POOL, ACT):
            k = gs["dk"][q]
            vals = gs["dvals"][q]
            for op in self.ops[q]:
                if op.dma:
                    s = k % NDMASEM
                    k += 1
                    vals[s] += 16
                    op.sem = dsem[q][s]
                    op.semval = vals[s]
            gs["dk"][q] = k
        engobj = {PE: "tensor", ACT: "scalar", DVE: "vector", POOL: "gpsimd", SP: "sync"}
        with nc.Block() as block:

            def run(e, eng):
                waited = {}

                def wait(sem, val):
                    k = id(sem)
                    if waited.get(k, 0) >= val:
                        return
                    waited[k] = val
                    eng.wait_ge(sem, val)

                for op in self.ops[e]:
                    for d in op.deps:
                        if d.dma:
                            wait(d.sem, d.semval)
                        else:
                            wait(esem[d.eng], d.sigval)
                    if op.dma:
                        if op.semval > 16:
                            wait(op.sem, op.semval - 16)
                        ins = op.fn(eng)
                        ins.then_inc(op.sem, 16)
                    else:
                        ins = op.fn(eng)
                        if op.signal:
                            ins.then_inc(esem[e], 1)
                if final_wait_all_dma:
                    last = {}
                    for op in self.ops[e]:
                        if op.dma:
                            last[id(op.sem)] = (op.sem, op.semval)
                    for sem, val in last.values():
                        wait(sem, val)

            for e in (PE, ACT, DVE, POOL, SP):
                if not self.ops[e]:
                    continue
                getattr(block, engobj[e])(lambda eng, e=e: run(e, eng))


GSYNC = [None]


def init_gsync(nc, st):
    gs = {"esem": {e: st.enter_context(nc.semaphore("s_" + e)) for e in COMPUTE}, "dsem": {},
          "ebase": {e: 0 for e in COMPUTE}, "dk": {}, "dvals": {}}
    for q in (SP, POOL, ACT):
        gs["dsem"][q] = [st.enter_context(nc.semaphore("d_%s%d" % (q, i))) for i in range(NDMASEM)]
        gs["dk"][q] = 0
        gs["dvals"][q] = [0] * NDMASEM
    GSYNC[0] = gs

import contextlib

NL, D = 4, 1024
NB = 2
CTX, SEQ = 256, 2048
TB = CTX + SEQ
TT_ = NB * TB
DFF = 2816
ALPHA = 8.0 ** 0.25
EPS = 1e-5
EPSP = EPS / (ALPHA * ALPHA)
IN_TOTAL = 11328
OFF_Z, OFF_XBC, OFF_DT, OFF_LX, OFF_LG = 0, 1024, 3072, 3104, 4128
OFF_Q, OFF_K, OFF_V, OFF_G, OFF_ALR, OFF_GATE = 5152, 5664, 6176, 7200, 8224, 8256
TS_ = 256
TILES = []
for _b in range(NB):
    TILES.append((_b, 0, 256, 2))
    for _i in range(SEQ // TS_):
        TILES.append((_b, CTX + _i * TS_, TS_, _b))


def MM(S, out, lhsT, rhs, start, stop, R, W):
    return S.pe(lambda e: e.matmul(out, lhsT, rhs, start=start, stop=stop), R, W)


def TR(S, out, in_, ident, R, W):
    return S.pe(lambda e: e.transpose(out, in_, ident), R, W)


def ACTV(S, out, in_, func, R, W, bias=None, scale=None):
    kw = {}
    if bias is not None:
        kw["bias"] = bias
    if scale is not None:
        kw["scale"] = scale
    return S.act(lambda e: e.activation(out=out, in_=in_, func=func, **kw), R, W)


def TT(S, eng, out, in0, in1, op, R, W):
    return S.add(eng, lambda e: e.tensor_tensor(out=out, in0=in0, in1=in1, op=op), R, W)


def TSC(S, eng, out, in0, s1, s2, op0, op1, R, W):
    if s2 is None:
        return S.add(eng, lambda e: e.tensor_scalar(out=out, in0=in0, scalar1=s1, scalar2=None, op0=op0), R, W)
    return S.add(eng, lambda e: e.tensor_scalar(out=out, in0=in0, scalar1=s1, scalar2=s2, op0=op0, op1=op1), R, W)


def STT(S, out, in0, scalar, in1, op0, op1, R, W):
    return S.dve(lambda e: e.scalar_tensor_tensor(out=out, in0=in0, scalar=scalar, in1=in1, op0=op0, op1=op1), R, W)


def COPY(S, eng, out, in_, R, W):
    if eng == ACT:
        return S.act(lambda e: e.activation(out=out, in_=in_, func=AF.Identity), R, W)
    return S.add(eng, lambda e: e.tensor_copy(out=out, in_=in_), R, W)


def bc(ap, shape):
    return ap.to_broadcast(list(shape))


DEBUG_OUT = [False]
_UID = [0]


def uname(n):
    _UID[0] += 1
    return '%s_u%d' % (n, _UID[0])


class G:
    pass


def declare_io(nc, g):
    def din(name, shape):
        return nc.dram_tensor(name, list(shape), F32, kind="ExternalInput").ap()
    g.x = din("x", [NB, SEQ, D])
    g.c = din("c", [NB, D])
    g.ctx = din("ctx", [NB, CTX, D])
    g.c_ctx = din("c_ctx", [1, D])
    g.w_ada = din("w_ada", [NL, D, 9 * D])
    g.b_ada = din("b_ada", [NL, 9 * D])
    g.ln_g = din("ln_g", [NL, 3, D])
    g.ln_b = din("ln_b", [NL, 3, D])
    g.ffn_w_up = din("ffn_w_up", [NL, 2, D, 2 * DFF])
    g.ffn_w_down = din("ffn_w_down", [NL, 2, DFF, D])
    g.w_in = din("w_in", [NL, D, IN_TOTAL])
    g.ssd_conv_w = din("ssd_conv_w", [NL, 4, 2048])
    g.ssd_conv_b = din("ssd_conv_b", [NL, 2048])
    g.ssd_dt_bias = din("ssd_dt_bias", [NL, 2, 16])
    g.ssd_a_log = din("ssd_a_log", [NL, 2, 16])
    g.ssd_d = din("ssd_d", [NL, 2, 16])
    g.ssd_norm_g = din("ssd_norm_g", [NL, 1024])
    g.lru_conv_w = din("lru_conv_w", [NL, 4, 1024])
    g.lru_conv_b = din("lru_conv_b", [NL, 1024])
    g.lru_w_a = din("lru_w_a", [NL, 2, 16, 64, 64])
    g.lru_b_a = din("lru_b_a", [NL, 2, 1024])
    g.lru_w_x = din("lru_w_x", [NL, 2, 16, 64, 64])
    g.lru_b_x = din("lru_b_x", [NL, 2, 1024])
    g.lru_lam = din("lru_lam", [NL, 2, 1024])
    g.gla_w_gate = din("gla_w_gate", [NL, 2, 16, 512])
    g.gla_b_gate = din("gla_b_gate", [NL, 2, 512])
    g.gla_norm_g = din("gla_norm_g", [NL, 256])
    g.w_branch = din("w_branch", [NL, 3, 1024, 1024])
    g.w_out = din("w_out", [NL, 1024, 1024])
    g.out = nc.dram_tensor("out", [NB, SEQ, D], F32, kind="ExternalOutput").ap()
    kd = "ExternalOutput" if DEBUG_OUT[0] else "Internal"
    g.HT = nc.dram_tensor("HT", [128, 8, TT_], F32, kind=kd).ap()
    g.Y = [nc.dram_tensor("Y%d" % i, [128, 8, TT_], BF16, kind=kd).ap() for i in range(3)]


INPUT_NAMES = ["x", "c", "ctx", "c_ctx", "w_ada", "b_ada", "ln_g", "ln_b", "ffn_w_up", "ffn_w_down", "w_in",
               "ssd_conv_w", "ssd_conv_b", "ssd_dt_bias", "ssd_a_log", "ssd_d", "ssd_norm_g", "lru_conv_w",
               "lru_conv_b", "lru_w_a", "lru_b_a", "lru_w_x", "lru_b_x", "lru_lam", "gla_w_gate", "gla_b_gate",
               "gla_norm_g", "w_branch", "w_out"]

CF = {}
_o = 0
for _n, _sz in [("ln_g", NL * 3 * 8), ("ln_b", NL * 3 * 8), ("bada", NL * 9 * 8), ("scw", NL * 4 * 16),
                ("scb", NL * 16), ("sng", NL * 8), ("lcw", NL * 4 * 8), ("lcb", NL * 8), ("lba", NL * 16),
                ("lbx", NL * 16), ("llam", NL * 16), ("c", 16), ("cctx", 8), ("lsp8", NL * 16), ("lsp16", NL * 16),
                ("lsp24", NL * 16)]:
    CF[_n] = _o
    _o += _sz
NCF = _o


def mod_ap(g, l, j, kc, cls):
    i = (((l * 9 + j) * 8) + kc) * 4 + cls
    return g.modT[:, i:i + 1]


def cf(g, name, idx):
    o = CF[name] + idx
    return g.constf[:, o:o + 1]


def phase_const(nc, g):
    with contextlib.ExitStack() as st:
        sb = lambda n, s, d: st.enter_context(nc.sbuf_tensor(uname(n), s, d))
        rowbuf = [sb("rowbuf%d" % i, [128, 128], F32) for i in range(2)]
        wbuf = [sb("wadab%d" % i, [128, 8, 1024], BF16) for i in range(2)]
        sT = sb("sT", [128, 8, 4], BF16)
        tmpm = sb("tmpm", [128, 128], F32)
        pst = [st.enter_context(nc.psum_tensor(uname("pst%d" % i), [128, 512], F32)) for i in range(4)]
        S = Sched(nc)
        Bm = Buf("masks")
        Brow = [Buf(), Buf()]
        Bw = [Buf(), Buf()]
        Bps = [Buf() for _ in range(4)]
        Bcf, BsT, Bmod, Btm = Buf("cf"), Buf(), Buf("mod"), Buf()
        g.Bconst = Buf("constall")
        M = g.masks

        def amask(dst, cm, step, base, op):
            S.pool(lambda e: e.memset(dst, 1.0), (), [Bm])
            S.pool(lambda e: e.affine_select(out=dst, in_=dst, compare_op=op, fill=0.0, base=base,
                                             pattern=[[step, 128]], channel_multiplier=cm), [Bm], [Bm])

        S.pool(lambda e: e.memset(M["ONES"][:], 1.0), (), [Bm])
        S.pool(lambda e: e.memset(g.onesb[:], 1.0), (), [Bm])
        amask(M["LE"][:], -1, 1, 0, ALU.is_ge)
        amask(M["GT"][:], 1, -1, 0, ALU.is_gt)
        amask(M["LT"][:], -1, 1, 0, ALU.is_gt)
        amask(M["GE"][:], 1, -1, 0, ALU.is_ge)
        amask(M["ID"][:], 1, -1, 0, ALU.is_equal)
        S.pool(lambda e: e.memset(M["BD"][:], 0.0), (), [Bm])
        S.pool(lambda e: e.memset(M["BD"][0:64, 0:64], 1.0), [Bm], [Bm])
        S.pool(lambda e: e.memset(M["BD"][64:128, 64:128], 1.0), [Bm], [Bm])
        for nm in ("LE", "GT", "LT", "GE"):
            TT(S, POOL, M[nm + "64"][:], M[nm][:], M["BD"][:], ALU.mult, [Bm], [Bm])

        items = [
            ("ln_g", g.ln_g.rearrange("l i (k p) -> (l i k) p", p=128)),
            ("ln_b", g.ln_b.rearrange("l i (k p) -> (l i k) p", p=128)),
            ("bada", g.b_ada.rearrange("l (j p) -> (l j) p", p=128)),
            ("scw", g.ssd_conv_w.rearrange("l k (c p) -> (l k c) p", p=128)),
            ("scb", g.ssd_conv_b.rearrange("l (c p) -> (l c) p", p=128)),
            ("sng", g.ssd_norm_g.rearrange("l (c p) -> (l c) p", p=128)),
            ("lcw", g.lru_conv_w.rearrange("l k (c p) -> (l k c) p", p=128)),
            ("lcb", g.lru_conv_b.rearrange("l (c p) -> (l c) p", p=128)),
            ("lba", g.lru_b_a.rearrange("l d (c p) -> (l d c) p", p=128)),
            ("lbx", g.lru_b_x.rearrange("l d (c p) -> (l d c) p", p=128)),
            ("llam", g.lru_lam.rearrange("l d (c p) -> (l d c) p", p=128)),
            ("c", g.c.rearrange("b (c p) -> (b c) p", p=128)),
            ("cctx", g.c_ctx.rearrange("b (c p) -> (b c) p", p=128)),
        ]
        k = 0
        for nm, ap in items:
            R = ap.shape[0]
            r0 = 0
            while r0 < R:
                nr = min(128, R - r0)
                i = k % 2
                k += 1
                S.dma(SP, rowbuf[i][0:nr, :], ap[r0:r0 + nr, :], (), [Brow[i]])
                TR(S, pst[i][:, 0:nr], rowbuf[i][0:nr, :], M["ID"][0:nr, 0:nr], [Brow[i], Bm], [Bps[i]])
                o = CF[nm] + r0
                COPY(S, DVE, g.constf[:, o:o + nr], pst[i][:, 0:nr], [Bps[i]], [Bcf])
                r0 += nr
        n16 = NL * 16
        lam = g.constf[:, CF["llam"]:CF["llam"] + n16]
        ACTV(S, tmpm[:, 0:n16], lam, AF.Exp, [Bcf], [Btm], scale=-1.0)
        ACTV(S, tmpm[:, 0:n16], tmpm[:, 0:n16], AF.Ln, [Btm], [Btm], bias=1.0)
        for nm, sc in (("lsp8", -8.0), ("lsp16", -16.0), ("lsp24", -16.0 / 24.0)):
            TSC(S, DVE, g.constf[:, CF[nm]:CF[nm] + n16], tmpm[:, 0:n16], sc, None, ALU.mult, None, [Btm], [Bcf])
        S.dma(SP, g.rb_dtb[:], g.ssd_dt_bias.rearrange("l d h -> (l d h)").partition_broadcast(128), (), [Bcf])
        S.dma(SP, g.rb_A[:], g.ssd_a_log.rearrange("l d h -> (l d h)").partition_broadcast(128), (), [Bcf])
        S.dma(SP, g.rb_D[:], g.ssd_d.rearrange("l d h -> (l d h)").partition_broadcast(128), (), [Bcf])
        ACTV(S, g.rb_A[:], g.rb_A[:], AF.Exp, [Bcf], [Bcf])
        TSC(S, DVE, g.rb_A[:], g.rb_A[:], -1.0, None, ALU.mult, None, [Bcf], [Bcf])
        rbD = g.rb_D[:].rearrange("p (l d h) -> p l d h", l=NL, d=2)
        TT(S, DVE, g.rb_Ds[:].rearrange("p (l h) -> p l h", l=NL), rbD[:, :, 0, :], rbD[:, :, 1, :], ALU.add, [Bcf], [Bcf])
        S.pool(lambda e: e.memset(sT[:], 0.0), (), [BsT])
        for b in range(NB):
            o = CF["c"] + b * 8
            ACTV(S, sT[:, :, b], g.constf[:, o:o + 8], AF.Silu, [Bcf, BsT], [BsT])
        o = CF["cctx"]
        ACTV(S, sT[:, :, 2], g.constf[:, o:o + 8], AF.Silu, [Bcf, BsT], [BsT])
        k = 0
        for l in range(NL):
            wv = g.w_ada[l].rearrange("(k p) n -> p k n", p=128)
            for j in range(9):
                i = k % 2
                pi = 2 + (k % 2)
                k += 1
                S.dma(POOL, wbuf[i][:], wv[:, :, j * 1024:(j + 1) * 1024], (), [Bw[i]])
                for oc in range(8):
                    for kc in range(8):
                        MM(S, pst[pi][:, oc * 4:oc * 4 + 4], wbuf[i][:, kc, oc * 128:(oc + 1) * 128], sT[:, kc, :],
                           kc == 0, kc == 7, [Bw[i], BsT], [Bps[pi]])
                mo = ((l * 9 + j) * 8) * 4
                bo = CF["bada"] + (l * 9 + j) * 8
                TT(S, DVE, g.modT[:, mo:mo + 32].rearrange("p (k c) -> p k c", c=4),
                   pst[pi][:, 0:32].rearrange("p (k c) -> p k c", c=4),
                   bc(g.constf[:, bo:bo + 8].unsqueeze(2), [128, 8, 4]), ALU.add, [Bps[pi], Bcf], [Bmod])
        mv = g.modT[:].rearrange("p (l j r) -> p l j r", l=NL, j=9)
        for j in (1, 4, 7):
            TSC(S, DVE, mv[:, :, j, :], mv[:, :, j, :], 1.0, None, ALU.add, None, [Bmod], [Bmod])
        for j, sc in ((2, 0.5 / ALPHA), (8, 0.5 / ALPHA), (5, 1.0 / ALPHA)):
            TSC(S, DVE, mv[:, :, j, :], mv[:, :, j, :], sc, None, ALU.mult, None, [Bmod], [Bmod])
        S.emit()
    nc.all_engine_barrier()


def phase_p0(nc, g):
    with contextlib.ExitStack() as st:
        sb = lambda n, s, d: st.enter_context(nc.sbuf_tensor(uname(n), s, d))
        tin = [sb("p0in%d" % i, [128, 1024], F32) for i in range(3)]
        stg = [sb("p0st%d" % i, [128, 8, 512], F32) for i in range(2)]
        ps = [st.enter_context(nc.psum_tensor(uname("p0ps%d" % i), [128, 512], F32)) for i in range(4)]
        S = Sched(nc)
        Bin = [Buf() for _ in range(3)]
        Bst = [Buf(), Buf()]
        Bps = [Buf() for _ in range(4)]
        Bc = g.Bconst
        k = 0
        gi = 0
        for b in range(NB):
            groups = [(g.ctx[b], 0, 256)] + [(g.x[b, i * 512:(i + 1) * 512, :], CTX + i * 512, 512) for i in range(4)]
            for src, s0, n in groups:
                sg = gi % 2
                gi += 1
                for t in range(n // 128):
                    i = k % 3
                    S.dma(SP, tin[i][:], src[t * 128:(t + 1) * 128, :], (), [Bin[i]])
                    for half in range(2):
                        pi = (2 * k + half) % 4
                        for q in range(4):
                            kc = half * 4 + q
                            TR(S, ps[pi][:, q * 128:(q + 1) * 128], tin[i][:, kc * 128:(kc + 1) * 128], g.masks["ID"][:],
                               [Bin[i], Bc], [Bps[pi]])
                        COPY(S, ACT if half == 0 else DVE, stg[sg][:, half * 4:half * 4 + 4, t * 128:(t + 1) * 128],
                             ps[pi][:].rearrange("p (q t) -> p q t", q=4), [Bps[pi]], [Bst[sg]])
                    k += 1
                col = b * TB + s0
                S.dma(SP, g.HT[:, :, col:col + n], stg[sg][:, :, 0:n], [Bst[sg]], ())
        S.emit()
    nc.all_engine_barrier()


def phase_final(nc, g):
    with contextlib.ExitStack() as st:
        sb = lambda n, s, d: st.enter_context(nc.sbuf_tensor(uname(n), s, d))
        hin = [sb("pfin%d" % i, [128, 8, 512], F32) for i in range(2)]
        to = [sb("pfo%d" % i, [128, 1024], F32) for i in range(3)]
        ps = [st.enter_context(nc.psum_tensor(uname("pfps%d" % i), [128, 512], F32)) for i in range(4)]
        S = Sched(nc)
        Bin = [Buf(), Buf()]
        Bo = [Buf() for _ in range(3)]
        Bps = [Buf() for _ in range(4)]
        Bc = g.Bconst
        k = 0
        gi = 0
        for b in range(NB):
            for i4 in range(4):
                sg = gi % 2
                gi += 1
                col = b * TB + CTX + i4 * 512
                S.dma(SP, hin[sg][:], g.HT[:, :, col:col + 512], (), [Bin[sg]])
                for t in range(4):
                    oi = k % 3
                    for half in range(2):
                        pi = (2 * k + half) % 4
                        for q in range(4):
                            kc = half * 4 + q
                            TR(S, ps[pi][:, q * 128:(q + 1) * 128], hin[sg][:, kc, t * 128:(t + 1) * 128], g.masks["ID"][:],
                               [Bin[sg], Bc], [Bps[pi]])
                        COPY(S, ACT if half == 0 else DVE, to[oi][:, half * 512:(half + 1) * 512], ps[pi][:], [Bps[pi]], [Bo[oi]])
                    r0 = i4 * 512 + t * 128
                    S.dma(SP, g.out[b, r0:r0 + 128, :], to[oi][:], [Bo[oi]], ())
                    k += 1
        S.emit()
    nc.all_engine_barrier()


def load_weight_cast(S, dst3, src2, nk, ncols, Bw, piece=1024):
    sv = src2.rearrange("(k p) n -> p k n", p=128)
    for kc in range(nk):
        c0 = 0
        while c0 < ncols:
            cn = min(piece, ncols - c0)
            S.dma(POOL, dst3[:, kc, c0:c0 + cn], sv[:, kc, c0:c0 + cn], (), [Bw[kc]])
            c0 += cn


def ln_part1(S, g, zt, Bz, n, W):
    zbf, sq, Bzs = W["zbf"], W["sq"], W["Bzs"]
    COPY(S, ACT, zbf[:, :, 0:n], zt[:, :, 0:n], [Bz], [Bzs])
    ACTV(S, sq[:, :, 0:n], zt[:, :, 0:n], AF.Square, [Bz], [Bzs])


def ln_part2(S, g, l, i, zt, Bz, n, W):
    zbf, sq, Bzs = W["zbf"], W["sq"], W["Bzs"]
    psm, psq, Bpm, Bpq = W["psm"], W["psq"], W["Bpm"], W["Bpq"]
    mean, msq, var, Bsm = W["mean"], W["msq"], W["var"], W["Bsm"]
    Bc = g.Bconst
    for kc in range(8):
        MM(S, psm[:, 0:n], g.onesb[:], zbf[:, kc, 0:n], kc == 0, kc == 7, [Bzs, Bc], [Bpm])
    for kc in range(8):
        MM(S, psq[:, 0:n], g.onesb[:], sq[:, kc, 0:n], kc == 0, kc == 7, [Bzs, Bc], [Bpq])
    ACTV(S, mean[:, 0:n], psm[:, 0:n], AF.Identity, [Bpm], [Bsm], scale=1.0 / 1024)
    ACTV(S, msq[:, 0:n], psm[:, 0:n], AF.Square, [Bpm], [Bsm], scale=1.0 / 1024)
    STT(S, var[:, 0:n], psq[:, 0:n], 1.0 / 1024, msq[:, 0:n], ALU.mult, ALU.subtract, [Bpq, Bsm], [Bsm])
    TSC(S, DVE, var[:, 0:n], var[:, 0:n], EPSP, None, ALU.add, None, [Bsm], [Bsm])
    ACTV(S, var[:, 0:n], var[:, 0:n], AF.Sqrt, [Bsm], [Bsm])
    S.dve(lambda e: e.reciprocal(out=var[:, 0:n], in_=var[:, 0:n]), [Bsm], [Bsm])
    TT(S, DVE, zt[:, :, 0:n], zt[:, :, 0:n], bc(mean[:, 0:n].unsqueeze(1), [128, 8, n]), ALU.subtract, [Bz, Bsm], [Bz])
    TT(S, DVE, zt[:, :, 0:n], zt[:, :, 0:n], bc(var[:, 0:n].unsqueeze(1), [128, 8, n]), ALU.mult, [Bz, Bsm], [Bz])
    for kc in range(8):
        ACTV(S, zt[:, kc, 0:n], zt[:, kc, 0:n], AF.Identity, [Bz, Bc], [Bz],
             bias=cf(g, "ln_b", (l * 3 + i) * 8 + kc), scale=cf(g, "ln_g", (l * 3 + i) * 8 + kc))


def phase_ffn(nc, g, l, j, skip_ctx):
    n = TS_
    with contextlib.ExitStack() as st:
        sb = lambda nm, s, d: st.enter_context(nc.sbuf_tensor(uname(nm), s, d))
        wup = sb("wup", [128, 8, 2 * DFF], BF16)
        wdn = sb("wdn", [128, 22, D], BF16)
        hb = [sb("ffh%d" % i, [128, 8, n], F32) for i in range(3)]
        ub = [sb("ffu%d" % i, [128, 8, n], BF16) for i in range(2)]
        hid = sb("ffhid", [128, 22, n], BF16)
        sil = [sb("ffsil%d" % i, [128, n], F32) for i in range(2)]
        zbf = sb("ffzbf", [128, 8, n], BF16)
        sq = sb("ffsq", [128, 8, n], BF16)
        mean = sb("ffmean", [128, n], F32)
        msq = sb("ffmsq", [128, n], F32)
        var = sb("ffvar", [128, n], F32)
        ps = [st.enter_context(nc.psum_tensor(uname("ffps%d" % i), [128, 512], F32)) for i in range(8)]
        S = Sched(nc)
        Bc = g.Bconst
        Bwu = [Buf() for _ in range(8)]
        Bwd = [Buf() for _ in range(22)]
        Bh = [Buf(), Buf(), Buf()]
        Bu = [Buf(), Buf()]
        Bhid, Bzs, Bsm = Buf(), Buf(), Buf()
        Bsil = [Buf(), Buf()]
        Bps = [Buf() for _ in range(8)]
        load_weight_cast(S, wup, g.ffn_w_up[l, j], 8, 2 * DFF, Bwu, piece=1408)
        load_weight_cast(S, wdn, g.ffn_w_down[l, j], 22, D, Bwd)
        W = dict(zbf=zbf, sq=sq, Bzs=Bzs, psm=ps[6], psq=ps[7], Bpm=Bps[6], Bpq=Bps[7], mean=mean, msq=msq, var=var, Bsm=Bsm)
        tiles = [t for t in TILES if not (skip_ctx and t[3] == 2)]
        NT = len(tiles)
        pkc = [0]

        def stA(i):
            b, s0, nn, cls = tiles[i]
            col = b * TB + s0
            S.dma(SP, hb[i % 3][:], g.HT[:, :, col:col + n], (), [Bh[i % 3]])
            for kc in range(8):
                ACTV(S, ub[i % 2][:, kc, :], hb[i % 3][:, kc, :], AF.Identity, [Bh[i % 3], Bc], [Bu[i % 2]],
                     bias=mod_ap(g, l, 3 * (2 * j) + 0, kc, cls), scale=mod_ap(g, l, 3 * (2 * j) + 1, kc, cls))

        def stB(i):
            u_, Bu_ = ub[i % 2], Bu[i % 2]
            for fc in range(22):
                pk = pkc[0]
                pa, pv = ps[(pk % 2) * 2], ps[(pk % 2) * 2 + 1]
                Bpa, Bpv = Bps[(pk % 2) * 2], Bps[(pk % 2) * 2 + 1]
                si = pk % 2
                pkc[0] += 1
                for kc in range(8):
                    MM(S, pa[:, 0:n], wup[:, kc, fc * 128:(fc + 1) * 128], u_[:, kc, :], kc == 0, kc == 7, [Bwu[kc], Bu_], [Bpa])
                for kc in range(8):
                    MM(S, pv[:, 0:n], wup[:, kc, DFF + fc * 128:DFF + (fc + 1) * 128], u_[:, kc, :], kc == 0, kc == 7, [Bwu[kc], Bu_], [Bpv])
                ACTV(S, sil[si][:], pa[:, 0:n], AF.Silu, [Bpa], [Bsil[si]])
                TT(S, DVE, hid[:, fc, :], sil[si][:], pv[:, 0:n], ALU.mult, [Bsil[si], Bpv], [Bhid])

        def stC(i):
            b, s0, nn, cls = tiles[i]
            h_, Bh_ = hb[i % 3], Bh[i % 3]
            for oc in range(8):
                py, Bpy = ps[4 + oc % 2], Bps[4 + oc % 2]
                for fc in range(22):
                    MM(S, py[:, 0:n], wdn[:, fc, oc * 128:(oc + 1) * 128], hid[:, fc, :], fc == 0, fc == 21, [Bwd[fc], Bhid], [Bpy])
                STT(S, h_[:, oc, :], py[:, 0:n], mod_ap(g, l, 3 * (2 * j) + 2, oc, cls), h_[:, oc, :], ALU.mult, ALU.add,
                    [Bpy, Bh_, Bc], [Bh_])
            ln_part1(S, g, h_, Bh_, n, W)

        def stD(i):
            b, s0, nn, cls = tiles[i]
            col = b * TB + s0
            ln_part2(S, g, l, 2 * j, hb[i % 3], Bh[i % 3], n, W)
            S.dma(SP, g.HT[:, :, col:col + n], hb[i % 3][:], [Bh[i % 3]], ())

        stA(0)
        stB(0)
        for i in range(NT):
            if i + 1 < NT:
                stA(i + 1)
            stC(i)
            if i + 1 < NT:
                stB(i + 1)
            stD(i)
        S.emit()
    nc.all_engine_barrier()


def phase_mixpro(nc, g, l, b, U, BU):
    n = TS_
    odd = (l % 2 == 1)
    with contextlib.ExitStack() as st:
        sb = lambda nm, s, d: st.enter_context(nc.sbuf_tensor(uname(nm), s, d))
        hb = [sb("mph%d" % i, [128, 8, n], F32) for i in range(3)]
        S = Sched(nc)
        Bh = [Buf() for _ in range(3)]
        Bc = g.Bconst
        ti = 0
        for (bb, s0, nn, cls) in TILES:
            if bb != b:
                continue
            hi = ti % 3
            ti += 1
            col = b * TB + s0
            S.dma(SP, hb[hi][:], g.HT[:, :, col:col + n], (), [Bh[hi]])
            for kc in range(8):
                if cls == 2 or not odd:
                    dst = U[:, kc, s0:s0 + n]
                    src = hb[hi][:, kc, :]
                else:
                    r0 = (s0 - CTX) // 64
                    nr = n // 64
                    dst = U[:, kc, CTX:TB].rearrange("p (c r) -> p r c", r=32)[:, r0:r0 + nr, :]
                    src = hb[hi][:, kc, :].rearrange("p (r c) -> p r c", c=64)
                ACTV(S, dst, src, AF.Identity, [Bh[hi], Bc], [BU],
                     bias=mod_ap(g, l, 3, kc, cls), scale=mod_ap(g, l, 4, kc, cls))
        S.emit()
    nc.all_engine_barrier()


def phase_merge(nc, g, l, skip_ctx):
    n = TS_
    with contextlib.ExitStack() as st:
        sb = lambda nm, s, d: st.enter_context(nc.sbuf_tensor(uname(nm), s, d))
        wgt = sb("mgwg", [128, 8, 3072], BF16)
        wbr = sb("mgwb", [128, 24, 1024], BF16)
        wo = sb("mgwo", [128, 8, 1024], BF16)
        hb = [sb("mgh%d" % i, [128, 8, n], F32) for i in range(3)]
        ub = [sb("mgu0", [128, 8, n], BF16)] * 2
        sq0 = sb("mgsq0", [128, 8, n], BF16)
        yb = [[sb("mgy%d_%d" % (i, k), [128, 8, n], BF16) for k in range(3)] for i in range(2)]
        mb = sb("mgm", [128, 8, n], BF16)
        zbf = sb("mgzbf", [128, 8, n], BF16)
        sq = sb("mgsq", [128, 8, n], BF16)
        mean = sb("mgmean", [128, n], F32)
        msq = sb("mgmsq", [128, n], F32)
        var = sb("mgvar", [128, n], F32)
        rstd0 = [sb("mgrstd0%d" % i, [128, n], F32) for i in range(2)]
        sig = [sb("mgsig%d" % i, [128, n], F32) for i in range(2)]
        acc = sb("mgacc", [128, n], F32)
        tmp = sb("mgtmp", [128, n], F32)
        ps = [st.enter_context(nc.psum_tensor(uname("mgps%d" % i), [128, 512], F32)) for i in range(8)]
        S = Sched(nc)
        Bc = g.Bconst
        Bwg = [Buf() for _ in range(8)]
        Bwb = [Buf() for _ in range(24)]
        Bwo = [Buf() for _ in range(8)]
        Bh = [Buf(), Buf(), Buf()]
        By = [[Buf() for _ in range(3)] for _ in range(2)]
        Bm, Bzs, Bsm, Bacc, Btmp, Bsq0 = Buf(), Buf(), Buf(), Buf(), Buf(), Buf()
        Bu = [Buf()] * 2
        Br0 = [Buf(), Buf()]
        Bsig = [Buf(), Buf()]
        Bps = [Buf() for _ in range(8)]
        load_weight_cast(S, wgt, g.w_in[l][:, OFF_GATE:OFF_GATE + 3072], 8, 3072, Bwg)
        load_weight_cast(S, wbr, g.w_branch[l].rearrange("n k m -> (n k) m"), 24, 1024, Bwb)
        load_weight_cast(S, wo, g.w_out[l], 8, 1024, Bwo)
        import os as _os
        PL = DVE
        for kc in range(0 if _os.environ.get('MG_NOFOLD') else 8):
            TSC(S, DVE, wbr[:, kc, :], wbr[:, kc, :], cf(g, "sng", l * 8 + kc), None, ALU.mult, None, [Bwb[kc], Bc], [Bwb[kc]])
        W = dict(zbf=zbf, sq=sq, Bzs=Bzs, psm=ps[6], psq=ps[7], Bpm=Bps[6], Bpq=Bps[7], mean=mean, msq=msq, var=var, Bsm=Bsm)
        tiles = [t for t in TILES if not (skip_ctx and t[3] == 2)]
        NT = len(tiles)
        pkc = [0]

        def stA(i):
            b, s0, nn, cls = tiles[i]
            col = b * TB + s0
            hi = i % 2
            S.dma(SP, hb[i % 3][:], g.HT[:, :, col:col + n], (), [Bh[i % 3]])
            for k in range(3):
                S.dma(SP, yb[hi][k][:], g.Y[k][:, :, col:col + n], (), [By[hi][k]])
            for kc in range(8):
                ACTV(S, ub[hi][:, kc, :], hb[i % 3][:, kc, :], AF.Identity, [Bh[i % 3], Bc], [Bu[hi]],
                     bias=mod_ap(g, l, 3, kc, cls), scale=mod_ap(g, l, 4, kc, cls))
            ACTV(S, sq0[:], yb[hi][0][:], AF.Square, [By[hi][0]], [Bsq0])
            for kc in range(8):
                MM(S, ps[7][:, 0:n], g.onesb[:], sq0[:, kc, :], kc == 0, kc == 7, [Bsq0, Bc], [Bps[7]])
            TSC(S, DVE, rstd0[hi][:], ps[7][:, 0:n], 1.0 / 1024, EPS, ALU.mult, ALU.add, [Bps[7]], [Br0[hi]])
            ACTV(S, rstd0[hi][:], rstd0[hi][:], AF.Sqrt, [Br0[hi]], [Br0[hi]])
            S.dve(lambda e: e.reciprocal(out=rstd0[hi][:], in_=rstd0[hi][:]), [Br0[hi]], [Br0[hi]])

        def stB(i):
            hi = i % 2
            for oc in range(8):
                for k in range(3):
                    pk = pkc[0]
                    pg, pb = ps[(pk % 2) * 2], ps[(pk % 2) * 2 + 1]
                    Bpg, Bpb = Bps[(pk % 2) * 2], Bps[(pk % 2) * 2 + 1]
                    si = pk % 2
                    pkc[0] += 1
                    for kc in range(8):
                        MM(S, pg[:, 0:n], wgt[:, kc, k * 1024 + oc * 128:k * 1024 + (oc + 1) * 128], ub[hi][:, kc, :], kc == 0, kc == 7,
                           [Bwg[kc], Bu[hi]], [Bpg])
                    for kc in range(8):
                        MM(S, pb[:, 0:n], wbr[:, k * 8 + kc, oc * 128:(oc + 1) * 128], yb[hi][k][:, kc, :], kc == 0, kc == 7,
                           [Bwb[k * 8 + kc], By[hi][k]], [Bpb])
                    ACTV(S, sig[si][:], pg[:, 0:n], AF.Sigmoid, [Bpg], [Bsig[si]])
                    if k == 0:
                        TT(S, PL, sig[si][:], sig[si][:], rstd0[hi][:], ALU.mult, [Bsig[si], Br0[hi]], [Bsig[si]])
                        TT(S, DVE, acc[:], sig[si][:], pb[:, 0:n], ALU.mult, [Bsig[si], Bpb], [Bacc])
                    elif k == 1:
                        TT(S, DVE, tmp[:], sig[si][:], pb[:, 0:n], ALU.mult, [Bsig[si], Bpb], [Btmp])
                        TT(S, PL, acc[:], acc[:], tmp[:], ALU.add, [Bacc, Btmp], [Bacc])
                    else:
                        TT(S, DVE, tmp[:], sig[si][:], pb[:, 0:n], ALU.mult, [Bsig[si], Bpb], [Btmp])
                        TT(S, PL, mb[:, oc, :], acc[:], tmp[:], ALU.add, [Bacc, Btmp], [Bm])

        def stC(i):
            b, s0, nn, cls = tiles[i]
            h_, Bh_ = hb[i % 3], Bh[i % 3]
            for oc in range(8):
                py, Bpy = ps[4 + oc % 2], Bps[4 + oc % 2]
                for kc in range(8):
                    MM(S, py[:, 0:n], wo[:, kc, oc * 128:(oc + 1) * 128], mb[:, kc, :], kc == 0, kc == 7, [Bwo[kc], Bm], [Bpy])
                STT(S, h_[:, oc, :], py[:, 0:n], mod_ap(g, l, 5, oc, cls), h_[:, oc, :], ALU.mult, ALU.add,
                    [Bpy, Bh_, Bc], [Bh_])
            ln_part1(S, g, h_, Bh_, n, W)

        def stD(i):
            b, s0, nn, cls = tiles[i]
            col = b * TB + s0
            ln_part2(S, g, l, 1, hb[i % 3], Bh[i % 3], n, W)
            S.dma(SP, g.HT[:, :, col:col + n], hb[i % 3][:], [Bh[i % 3]], ())

        stA(0)
        stB(0)
        for i in range(NT):
            if i + 1 < NT:
                stA(i + 1)
            stC(i)
            if i + 1 < NT:
                stB(i + 1)
            stD(i)
        S.emit()
    nc.all_engine_barrier()


SEGS = [(0, 256)] + [(CTX + i * 512, 512) for i in range(4)]


def phase_lru(nc, g, l, b, U, BU):
    odd = (l % 2 == 1)
    Lh = 32 if odd else 64
    with contextlib.ExitStack() as st:
        sb = lambda nm, s, d: st.enter_context(nc.sbuf_tensor(uname(nm), s, d))
        wl = [sb("lrw%d" % i, [128, 8, 256], BF16) for i in range(2)]
        wblk = sb("lrblk", [128, 8, 4, 128], BF16)
        xr = sb("lrxr", [128, TB], F32)
        xc = sb("lrxc", [128, TB], F32)
        xcb = sb("lrxcb", [128, TB], BF16)
        gg = sb("lrgg", [128, TB], F32)
        T_ = [[sb("lrT%d_%d" % (d_, i), [128, TB], F32) for i in range(4)] for d_ in range(2)]
        hf = sb("lrhf", [128, TB], F32)
        hbk = sb("lrhb", [128, TB], F32)
        yst = [sb("lryst%d" % i, [128, TB], BF16) for i in range(2)]
        ps = [st.enter_context(nc.psum_tensor(uname("lrps%d" % i), [128, 512], F32)) for i in range(8)]
        S = Sched(nc)
        Bc = g.Bconst
        Bwl = [Buf(), Buf()]
        Bblk, Bxr, Bxc, Bxcb, Bgg, Bhf, Bhb = Buf(), Buf(), Buf(), Buf(), Buf(), Buf(), Buf()
        BT_ = [[Buf() for _ in range(4)] for _ in range(2)]
        Byst = [Buf(), Buf()]
        Bps = [Buf() for _ in range(8)]
        S.pool(lambda e: e.memset(wblk[:], 0.0), (), [Bblk])
        for d in range(2):
            for t, wsrc in enumerate((g.lru_w_a, g.lru_w_x)):
                for h in range(2):
                    src = wsrc[l, d].rearrange("(j h) k c -> h k j c", h=2)[h]
                    S.dma(POOL, wblk[h * 64:(h + 1) * 64, :, d * 2 + t, h * 64:(h + 1) * 64], src, (), [Bblk])
        wv = g.w_in[l].rearrange("(k p) n -> p k n", p=128)
        pk = 0
        for j in range(8):
            wi = j % 2
            S.dma(POOL, wl[wi][:, :, 0:128], wv[:, :, OFF_LX + j * 128:OFF_LX + (j + 1) * 128], (), [Bwl[wi]])
            S.dma(POOL, wl[wi][:, :, 128:256], wv[:, :, OFF_LG + j * 128:OFF_LG + (j + 1) * 128], (), [Bwl[wi]])
            for (s0, n) in SEGS:
                px, Bpx = ps[pk % 4], Bps[pk % 4]
                pg, Bpg = ps[(pk + 1) % 4], Bps[(pk + 1) % 4]
                pk += 2
                for kc in range(8):
                    MM(S, px[:, 0:n], wl[wi][:, kc, 0:128], U[:, kc, s0:s0 + n], kc == 0, kc == 7, [Bwl[wi], BU], [Bpx])
                for kc in range(8):
                    MM(S, pg[:, 0:n], wl[wi][:, kc, 128:256], U[:, kc, s0:s0 + n], kc == 0, kc == 7, [Bwl[wi], BU], [Bpg])
                COPY(S, DVE, xr[:, s0:s0 + n], px[:, 0:n], [Bpx], [Bxr])
                ACTV(S, gg[:, s0:s0 + n], pg[:, 0:n], AF.Gelu_apprx_tanh, [Bpg], [Bgg])
            cw = lambda k: cf(g, "lcw", (l * 4 + k) * 8 + j)
            ACTV(S, xc[:], xr[:], AF.Identity, [Bxr, Bc], [Bxc], bias=cf(g, "lcb", l * 8 + j), scale=cw(2))
            for (o0, ln_, nl) in ((0, 256, 1), (CTX, Lh, SEQ // Lh)):
                xv = xr[:, o0:o0 + ln_ * nl].rearrange("p (a b) -> p a b", b=ln_)
                ov = xc[:, o0:o0 + ln_ * nl].rearrange("p (a b) -> p a b", b=ln_)
                STT(S, ov[:, :, 2:ln_], xv[:, :, 0:ln_ - 2], cw(0), ov[:, :, 2:ln_], ALU.mult, ALU.add, [Bxr, Bxc, Bc], [Bxc])
                STT(S, ov[:, :, 1:ln_], xv[:, :, 0:ln_ - 1], cw(1), ov[:, :, 1:ln_], ALU.mult, ALU.add, [Bxr, Bxc, Bc], [Bxc])
                STT(S, ov[:, :, 0:ln_ - 1], xv[:, :, 1:ln_], cw(3), ov[:, :, 0:ln_ - 1], ALU.mult, ALU.add, [Bxr, Bxc, Bc], [Bxc])
            COPY(S, ACT, xcb[:], xc[:], [Bxc], [Bxcb])
            def lru_dir(d):
                T = T_[d]
                BT = BT_[d]
                ci = (l * 2 + d) * 8 + j
                pk_ = 0
                for (s0, n) in SEGS:
                    pr, Bpr = ps[4 + 2 * d], Bps[4 + 2 * d]
                    pi_, Bpi = ps[5 + 2 * d], Bps[5 + 2 * d]
                    MM(S, pr[:, 0:n], wblk[:, j, d * 2 + 0, :], xcb[:, s0:s0 + n], True, True, [Bblk, Bxcb], [Bpr])
                    yield
                    MM(S, pi_[:, 0:n], wblk[:, j, d * 2 + 1, :], xcb[:, s0:s0 + n], True, True, [Bblk, Bxcb], [Bpi])
                    yield
                    ACTV(S, T[0][:, s0:s0 + n], pr[:, 0:n], AF.Sigmoid, [Bpr, Bc], [BT[0]], bias=cf(g, "lba", ci))
                    yield
                    ACTV(S, T[1][:, s0:s0 + n], pi_[:, 0:n], AF.Sigmoid, [Bpi, Bc], [BT[1]], bias=cf(g, "lbx", ci))
                    yield
                ACTV(S, T[2][:], T[0][:], AF.Exp, [BT[0], Bc], [BT[2]], scale=cf(g, "lsp8", ci))
                yield
                TSC(S, DVE, T[3][:], T[0][:], cf(g, "lsp16", ci), None, ALU.mult, None, [BT[0], Bc], [BT[3]])
                yield
                TSC(S, DVE, T[0][:], T[3][:], 1.0 / 120, 1.0 / 24, ALU.mult, ALU.add, [BT[3]], [BT[0]])
                yield
                TT(S, DVE, T[0][:], T[0][:], T[3][:], ALU.mult, [BT[0], BT[3]], [BT[0]])
                yield
                for cst in (1.0 / 6, 0.5, 1.0):
                    STT(S, T[0][:], T[0][:], cst, T[3][:], ALU.add, ALU.mult, [BT[0], BT[3]], [BT[0]])
                    yield
                ACTV(S, T[0][:], T[0][:], AF.Sqrt, [BT[0]], [BT[0]], scale=-1.0)
                yield
                TT(S, POOL, T[1][:], T[1][:], T[0][:], ALU.mult, [BT[1], BT[0]], [BT[1]])
                yield
                TT(S, POOL, T[1][:], T[1][:], xc[:], ALU.mult, [BT[1], Bxc], [BT[1]])
                yield
                if d == 0:
                    S.dve(lambda e: e.tensor_tensor_scan(out=hf[:], data0=T[2][:], data1=T[1][:], initial=0.0,
                                                         op0=ALU.mult, op1=ALU.add), [BT[2], BT[1]], [Bhf])
                    yield
                else:
                    S.dve(lambda e: e.tensor_tensor_scan(out=hbk[:, 0:CTX][:, ::-1], data0=T[2][:, 0:CTX][:, ::-1],
                                                         data1=T[1][:, 0:CTX][:, ::-1], initial=0.0,
                                                         op0=ALU.mult, op1=ALU.add), [BT[2], BT[1]], [Bhb])
                    yield
                    S.dve(lambda e: e.tensor_tensor_scan(out=hbk[:, CTX:TB][:, ::-1], data0=T[2][:, CTX:TB][:, ::-1],
                                                         data1=T[1][:, CTX:TB][:, ::-1], initial=hbk[:, 0:1],
                                                         op0=ALU.mult, op1=ALU.add), [BT[2], BT[1], Bhb], [Bhb])
                    yield

            run_interleaved([lru_dir(0), lru_dir(1)])
            yi = j % 2
            TT(S, POOL, hf[:], hf[:], hbk[:], ALU.add, [Bhf, Bhb], [Bhf])
            TT(S, DVE, yst[yi][:, 0:CTX], hf[:, 0:CTX], gg[:, 0:CTX], ALU.mult, [Bhf, Bgg], [Byst[yi]])
            if odd:
                ov = yst[yi][:, CTX:TB].rearrange("p (r c) -> p c r", c=64)
                i0 = hf[:, CTX:TB].rearrange("p (c r) -> p c r", r=32)
                i1 = gg[:, CTX:TB].rearrange("p (c r) -> p c r", r=32)
                TT(S, DVE, ov, i0, i1, ALU.mult, [Bhf, Bgg], [Byst[yi]])
            else:
                TT(S, DVE, yst[yi][:, CTX:TB], hf[:, CTX:TB], gg[:, CTX:TB], ALU.mult, [Bhf, Bgg], [Byst[yi]])
            S.dma(SP, g.Y[1][:, j, b * TB:(b + 1) * TB], yst[yi][:], [Byst[yi]], ())
        S.emit()
    nc.all_engine_barrier()


def phase_mix(nc, g, l, which):
    with contextlib.ExitStack() as st:
        U = st.enter_context(nc.sbuf_tensor(uname("Umix"), [128, 8, TB], BF16))
        for b in range(NB):
            BU = Buf("U")
            phase_mixpro(nc, g, l, b, U, BU)
            BU = Buf("U")
            if "ssd" in which:
                phase_ssd(nc, g, l, b, U, BU)
            if "lru" in which:
                phase_lru(nc, g, l, b, U, BU)
            if "gla" in which:
                phase_gla(nc, g, l, b, U, BU)


FWD_CHUNKS = list(range(18))
REV_CHUNKS = [1, 0] + list(range(17, 1, -1))


def run_interleaved(gens):
    gens = list(gens)
    while gens:
        for g_ in list(gens):
            try:
                next(g_)
            except StopIteration:
                gens.remove(g_)


def conv_block(S, g, raw, Braw, out, Bout, wfn, bias_ap, Lh):
    Bc = g.Bconst
    ACTV(S, out[:], raw[:], AF.Identity, [Braw, Bc], [Bout], bias=bias_ap, scale=wfn(2))
    for (o0, ln_, nl) in ((0, 256, 1), (CTX, Lh, SEQ // Lh)):
        xv = raw[:, o0:o0 + ln_ * nl].rearrange("p (a b) -> p a b", b=ln_)
        ov = out[:, o0:o0 + ln_ * nl].rearrange("p (a b) -> p a b", b=ln_)
        STT(S, ov[:, :, 2:ln_], xv[:, :, 0:ln_ - 2], wfn(0), ov[:, :, 2:ln_], ALU.mult, ALU.add, [Braw, Bout, Bc], [Bout])
        STT(S, ov[:, :, 1:ln_], xv[:, :, 0:ln_ - 1], wfn(1), ov[:, :, 1:ln_], ALU.mult, ALU.add, [Braw, Bout, Bc], [Bout])
        STT(S, ov[:, :, 0:ln_ - 1], xv[:, :, 1:ln_], wfn(3), ov[:, :, 0:ln_ - 1], ALU.mult, ALU.add, [Braw, Bout, Bc], [Bout])


def phase_ssd(nc, g, l, b, U, BU):
    odd = (l % 2 == 1)
    Lh = 32 if odd else 64
    M = g.masks
    with contextlib.ExitStack() as st:
        sb = lambda nm, s, d: st.enter_context(nc.sbuf_tensor(uname(nm), s, d))
        wg = [sb("sdw0", [128, 8, 784], BF16)] * 2
        szT = sb("sdsz", [128, 2, TB], BF16)
        craw = sb("sdcraw", [128, TB], F32)
        ctmp = sb("sdctmp", [128, TB], F32)
        xsT = sb("sdxsT", [128, 2, TB], F32)
        BTf = sb("sdBTf", [128, TB], F32)
        BT = sb("sdBT", [128, TB], BF16)
        CT = sb("sdCT", [128, TB], BF16)
        dtv = sb("sddt", [128, 144], F32)
        av = sb("sda", [128, 144], F32)
        acs = sb("sdacs", [128, 144], F32)
        tot = sb("sdtot", [128, 144], F32)
        tm = sb("sdtm", [128, 144], F32)
        fs = sb("sdfs", [128, 144], F32)
        te = sb("sdte", [128, 144], F32)
        cd = sb("sdcd", [128, 144], F32)
        yacc = sb("sdyacc", [128, 18, 256], F32)
        Sst = sb("sdS", [128, 2, 256], F32)
        Sbf = sb("sdSb", [128, 2, 256], BF16)
        yst = [sb("sdyst0", [128, 2, TB], BF16)] * 2
        xs_all = sb("sdxsall", [128, 18, 256], F32)
        B_all = sb("sdBall", [128, 18, 128], BF16)
        xsd_ = [sb("sdxsd%d" % i, [128, 256], BF16) for i in range(2)]
        xw_ = [sb("sdxw%d" % i, [128, 256], BF16) for i in range(2)]
        scm_ = [sb("sdscm%d" % i, [128, 128], BF16) for i in range(2)]
        rhsA_ = [sb("sdrhsA%d" % i, [128, 512], F32) for i in range(2)]
        Eb_ = [sb("sdE%d" % i, [128, 512], BF16) for i in range(2)]
        MT_ = [sb("sdMT%d" % i, [128, 512], BF16) for i in range(2)]
        t1_ = [sb("sdt1%d" % i, [128, 256], F32) for i in range(2)]
        t2 = sb("sdt2", [128, 256], F32)
        ps = [st.enter_context(nc.psum_tensor(uname("sdps%d" % i), [128, 512], F32)) for i in range(8)]
        S = Sched(nc)
        Bc = g.Bconst
        Bwg = [Buf()] * 2
        (Bsz, Bcraw, Bctmp, BxsT, BBTf, BBT, BCT, Bdt, Ba, Bacs, Btot, Btm, Bfs, Bte, Bcd, Byacc, BS, BSb,
         Bxt, BBtok, Bxsd, Bxw, Bscm, BrhsA, BE, BMT, Bt1, Bt2) = [Buf() for _ in range(28)]
        Byst = [Buf()] * 2
        Bxall, BBall = Buf(), Buf()
        Byacc = [Buf() for _ in range(18)]
        Bxsd_, Bxw_, Bscm_, BrhsA_, BE_, BMT_, Bt1_, BS_, BSb_ = [[Buf(), Buf()] for _ in range(9)]
        Bps = [Buf() for _ in range(8)]
        wv = g.w_in[l].rearrange("(k p) n -> p k n", p=128)
        v4 = lambda t: t[:].rearrange("p (c d h) -> p c d h", d=2, h=4)
        pk = 0
        for gq in range(4):
            wi = gq % 2
            for (d0, c0, cn) in ((0, OFF_Z + 256 * gq, 256), (256, OFF_XBC + 256 * gq, 256),
                                 (512, OFF_XBC + 1024 + 128 * gq, 128), (640, OFF_XBC + 1536 + 128 * gq, 128),
                                 (768, OFF_DT + 4 * gq, 4), (772, OFF_DT + 16 + 4 * gq, 4)):
                S.dma(POOL, wg[wi][:, :, d0:d0 + cn], wv[:, :, c0:c0 + cn], (), [Bwg[wi]])

            def inproj(c0, evac):
                nonlocal pk
                for (s0, n) in SEGS:
                    p_, Bp = ps[pk % 2], Bps[pk % 2]
                    pk += 1
                    for kc in range(8):
                        MM(S, p_[:, 0:n], wg[wi][:, kc, c0:c0 + 128], U[:, kc, s0:s0 + n], kc == 0, kc == 7, [Bwg[wi], BU], [Bp])
                    evac(p_, Bp, s0, n)

            for i in range(2):
                inproj(i * 128, lambda p_, Bp, s0, n, i=i: ACTV(S, szT[:, i, s0:s0 + n], p_[:, 0:n], AF.Silu, [Bp], [Bsz]))
            for ci in range(4):
                inproj(256 + ci * 128, lambda p_, Bp, s0, n: COPY(S, DVE, craw[:, s0:s0 + n], p_[:, 0:n], [Bp], [Bcraw]))
                ch16 = (2 * gq + ci) if ci < 2 else (8 + gq if ci == 2 else 12 + gq)
                conv_block(S, g, craw, Bcraw, ctmp, Bctmp, lambda k, ch16=ch16: cf(g, "scw", (l * 4 + k) * 16 + ch16),
                           cf(g, "scb", l * 16 + ch16), Lh)
                if ci < 2:
                    ACTV(S, xsT[:, ci, :], ctmp[:], AF.Silu, [Bctmp], [BxsT])
                elif ci == 2:
                    ACTV(S, BTf[:], ctmp[:], AF.Silu, [Bctmp], [BBTf])
                    COPY(S, DVE, BT[:], BTf[:], [BBTf], [BBT])
                else:
                    ACTV(S, CT[:], ctmp[:], AF.Silu, [Bctmp], [BCT])
            pdt, Bpdt = ps[2], Bps[2]
            for c in range(18):
                for kc in range(8):
                    MM(S, pdt[:, c * 8:(c + 1) * 8], U[:, kc, c * 128:(c + 1) * 128], wg[wi][:, kc, 768:776], kc == 0, kc == 7,
                       [Bwg[wi], BU], [Bpdt])
            rb = lambda t: bc(t[:, l * 32:(l + 1) * 32].rearrange("p (d h) -> p d h", d=2)[:, :, 4 * gq:4 * gq + 4].unsqueeze(1), [128, 18, 2, 4])
            TT(S, DVE, v4(dtv), pdt[:, 0:144].rearrange("p (c d h) -> p c d h", d=2, h=4), rb(g.rb_dtb), ALU.add, [Bpdt, Bc], [Bdt])
            ACTV(S, dtv[:], dtv[:], AF.Exp, [Bdt], [Bdt])
            ACTV(S, dtv[:], dtv[:], AF.Ln, [Bdt], [Bdt], bias=1.0)
            TT(S, DVE, v4(av), v4(dtv), rb(g.rb_A), ALU.mult, [Bdt, Bc], [Ba])
            MM(S, ps[3][:, 0:144], M["LE"][:], av[:], True, True, [Ba, Bc], [Bps[3]])
            MM(S, ps[4][:, 0:144], M["ONES"][:], av[:], True, True, [Ba, Bc], [Bps[4]])
            COPY(S, DVE, acs[:], ps[3][:, 0:144], [Bps[3]], [Bacs])
            COPY(S, DVE, tot[:], ps[4][:, 0:144], [Bps[4]], [Btot])
            ACTV(S, cd[:], tot[:], AF.Exp, [Btot], [Bcd])
            ACTV(S, v4(fs)[:, :, 0, :], v4(acs)[:, :, 0, :], AF.Exp, [Bacs], [Bfs])
            TT(S, DVE, v4(tm)[:, :, 0, :], v4(tot)[:, :, 0, :], v4(acs)[:, :, 0, :], ALU.subtract, [Btot, Bacs], [Btm])
            TT(S, DVE, v4(tm)[:, :, 1, :], v4(acs)[:, :, 1, :], v4(av)[:, :, 1, :], ALU.subtract, [Bacs, Ba], [Btm])
            ACTV(S, te[:], tm[:], AF.Exp, [Btm], [Bte])
            TT(S, DVE, v4(tm)[:, :, 1, :], v4(tot)[:, :, 1, :], v4(tm)[:, :, 1, :], ALU.subtract, [Btot, Btm], [Btm])
            ACTV(S, v4(fs)[:, :, 1, :], v4(tm)[:, :, 1, :], AF.Exp, [Btm], [Bfs])
            yi = 0
            h3 = lambda t: t.rearrange("p (h q) -> p h q", h=4)
            for c in range(18):
                cs = slice(c * 128, (c + 1) * 128)
                ptr, Bptr = ps[2 + c % 2], Bps[2 + c % 2]
                for i in range(2):
                    TR(S, ptr[:, i * 128:(i + 1) * 128], xsT[:, i, cs], M["ID"][:], [BxsT, Bc], [Bptr])
                TR(S, ptr[:, 256:384], BTf[:, cs], M["ID"][:], [BBTf, Bc], [Bptr])
                COPY(S, ACT, xs_all[:, c, :], ptr[:, 0:256], [Bptr], [Bxall])
                COPY(S, ACT, B_all[:, c, :], ptr[:, 256:384], [Bptr], [BBall])
            seen = set()

            def ssd_iter(d, c):
                M1 = M["GT"] if d == 0 else M["LT"]
                M2 = M["LE"] if d == 0 else M["GE"]
                cs = slice(c * 128, (c + 1) * 128)
                xsd, xw, scm, rhsA, Eb, MT, t1 = xsd_[d], xw_[d], scm_[d], rhsA_[d], Eb_[d], MT_[d], t1_[d]
                Bxsd, Bxw, Bscm, BrhsA, BE, BMT, Bt1 = Bxsd_[d], Bxw_[d], Bscm_[d], BrhsA_[d], BE_[d], BMT_[d], Bt1_[d]
                pA, BpA = ps[2 + 3 * d], Bps[2 + 3 * d]
                pS, BpS = ps[3 + 3 * d], Bps[3 + 3 * d]
                pY, BpY = ps[4 + 3 * d], Bps[4 + 3 * d]
                xs_tok = xs_all[:, c, :]
                dt_c = bc(v4(dtv)[:, c, d, :].unsqueeze(2), [128, 4, 64])
                te_c = bc(v4(te)[:, c, d, :].unsqueeze(2), [128, 4, 64])
                fs_c = bc(v4(fs)[:, c, d, :].unsqueeze(2), [128, 4, 64])
                cd_c = bc(v4(cd)[:, c, d, :].unsqueeze(2), [128, 4, 64])
                TT(S, DVE, h3(xsd[:]), h3(xs_tok), dt_c, ALU.mult, [Bxall, Bdt], [Bxsd])
                yield
                TT(S, DVE, h3(xw[:]), h3(xsd[:]), te_c, ALU.mult, [Bxsd, Bte], [Bxw])
                yield
                MM(S, pA[:, 0:128], BT[:, cs], CT[:, cs], True, True, [BBT, BCT], [BpA])
                yield
                TT(S, DVE, scm[:], pA[:, 0:128], (M["LE"] if d == 0 else M["GE"])[:], ALU.mult, [BpA, Bc], [Bscm])
                yield
                TT(S, POOL, h3(rhsA[:]), bc(M2[:].unsqueeze(1), [128, 4, 128]), bc(v4(av)[:, c, d, :].unsqueeze(2), [128, 4, 128]),
                   ALU.mult, [Ba, Bc], [BrhsA])
                yield
                MM(S, pS[:], M1[:], rhsA[:], True, True, [BrhsA, Bc], [BpS])
                yield
                ACTV(S, Eb[:], pS[:], AF.Exp, [BpS], [BE])
                yield
                TT(S, DVE, h3(MT[:]), h3(Eb[:]), bc(scm[:].unsqueeze(1), [128, 4, 128]), ALU.mult, [BE, Bscm], [BMT])
                yield
                for hh in range(4):
                    MM(S, pY[:, hh * 64:(hh + 1) * 64], MT[:, hh * 128:(hh + 1) * 128], xsd[:, hh * 64:(hh + 1) * 64], True, True,
                       [BMT, Bxsd], [BpY])
                MM(S, pY[:, 256:512], CT[:, cs], Sbf[:, d, :], True, True, [BCT, BSb_[d]], [BpY])
                yield
                TT(S, DVE, h3(t1[:]), h3(pY[:, 256:512]), fs_c, ALU.mult, [BpY, Bfs], [Bt1])
                yield
                if c not in seen:
                    seen.add(c)
                    TT(S, DVE, yacc[:, c, :], pY[:, 0:256], t1[:], ALU.add, [BpY, Bt1], [Byacc[c]])
                else:
                    TT(S, DVE, t1[:], pY[:, 0:256], t1[:], ALU.add, [BpY, Bt1], [Bt1])
                    TT(S, POOL, h3(t2[:]), h3(xs_tok),
                       bc(g.rb_Ds[:, l * 16 + 4 * gq:l * 16 + 4 * gq + 4].unsqueeze(2), [128, 4, 64]), ALU.mult, [Bxall, Bc], [Bt2])
                    TT(S, POOL, t1[:], t1[:], t2[:], ALU.add, [Bt1, Bt2], [Bt1])
                    TT(S, DVE, yacc[:, c, :], yacc[:, c, :], t1[:], ALU.add, [Byacc[c], Bt1], [Byacc[c]])
                    pf, Bpf = ps[1], Bps[1]
                    for i in range(2):
                        TR(S, pf[:, i * 128:(i + 1) * 128], yacc[:, c, i * 128:(i + 1) * 128], M["ID"][:], [Byacc[c], Bc], [Bpf])
                    for i in range(2):
                        if odd and c >= 2:
                            cc0 = (c - 2) * 4
                            ov = yst[yi][:, i, CTX:TB].rearrange("p (r c) -> p c r", c=64)[:, cc0:cc0 + 4, :]
                            i0 = pf[:, i * 128:(i + 1) * 128].rearrange("p (c r) -> p c r", r=32)
                            i1 = szT[:, i, cs].rearrange("p (c r) -> p c r", r=32)
                        else:
                            ov, i0, i1 = yst[yi][:, i, cs], pf[:, i * 128:(i + 1) * 128], szT[:, i, cs]
                        TT(S, DVE, ov, i0, i1, ALU.mult, [Bpf, Bsz], [Byst[yi]])
                MM(S, pA[:, 128:384], B_all[:, c, :], xw[:], True, True, [BBall, Bxw], [BpA])
                yield
                TT(S, POOL, h3(Sst[:, d, :]), h3(Sst[:, d, :]), cd_c, ALU.mult, [BS_[d], Bcd], [BS_[d]])
                yield
                TT(S, DVE, Sst[:, d, :], Sst[:, d, :], pA[:, 128:384], ALU.add, [BS_[d], BpA], [BS_[d]])
                yield
                COPY(S, ACT, Sbf[:, d, :], Sst[:, d, :], [BS_[d]], [BSb_[d]])
                yield

            for d in range(2):
                S.pool(lambda e, d=d: e.memset(Sst[:, d, :], 0.0), (), [BS_[d]])
                S.pool(lambda e, d=d: e.memset(Sbf[:, d, :], 0.0), (), [BSb_[d]])
            def ssd_dir(d):
                for c in (FWD_CHUNKS if d == 0 else REV_CHUNKS):
                    yield from ssd_iter(d, c)

            run_interleaved([ssd_dir(0), ssd_dir(1)])
            S.dma(SP, g.Y[0][:, 2 * gq:2 * gq + 2, b * TB:(b + 1) * TB], yst[yi][:], [Byst[yi]], ())
        S.emit()
    nc.all_engine_barrier()


def phase_gla(nc, g, l, b, U, BU):
    odd = (l % 2 == 1)
    M = g.masks
    QS = 128.0 ** -0.5
    with contextlib.ExitStack() as st:
        sb = lambda nm, s, d: st.enter_context(nc.sbuf_tensor(uname(nm), s, d))
        wq = sb("glw", [128, 8, 896], BF16)
        WG = sb("glWG", [128, 256], BF16)
        bgb = sb("glbg", [128, 256], F32)
        gng = sb("glgng", [128, 256], F32)
        qT = sb("glqT", [128, TB], F32)
        kT = sb("glkT", [128, TB], F32)
        sgT = sb("glsg", [128, 2, TB], BF16)
        alrT = sb("glalr", [128, TB], BF16)
        v_tok = sb("glv", [128, 18, 256], BF16)
        k_tok = sb("glk", [128, 18, 128], F32)
        lsp = sb("gllsp", [128, 18, 256], F32)
        oacc = sb("gloacc", [128, 18, 256], F32)
        yst = [sb("glyst0", [128, 2, TB], BF16)] * 2
        eq_ = [sb("gleq%d" % i, [128, 128], F32) for i in range(2)]
        ek_ = [sb("glek%d" % i, [128, 128], F32) for i in range(2)]
        qin_ = [sb("glqin%d" % i, [128, 128], BF16) for i in range(2)]
        kin_ = [sb("glkin%d" % i, [128, 128], BF16) for i in range(2)]
        er_ = [sb("gler%d" % i, [128, 128], F32) for i in range(2)]
        kst_ = [[sb("glkst%d_%d" % (d_, i), [128, 128], BF16) for i in range(2)] for d_ in range(2)]
        qinh_ = [[sb("glqinh%d_%d" % (d_, i), [128, 128], BF16) for i in range(2)] for d_ in range(2)]
        attT_ = [sb("glatt%d" % i, [128, 128], BF16) for i in range(2)]
        Sst_ = [sb("glS%d" % i, [128, 256], F32) for i in range(2)]
        Sb0_ = [sb("glSb0%d" % i, [128, 256], BF16) for i in range(2)]
        Sb1_ = [sb("glSb1%d" % i, [128, 256], BF16) for i in range(2)]
        osq = sb("glosq", [128, 256], F32)
        ssq = sb("glssq", [128, 1], F32)
        on = sb("glon", [128, 256], F32)
        ps = [st.enter_context(nc.psum_tensor(uname("glps%d" % i), [128, 512], F32)) for i in range(8)]
        S = Sched(nc)
        Bc = g.Bconst
        (Bwq, BWG, Bbg, BqT, BkT, Bsg, Balr, Bv, Bk, Blsp, Boacc, Beq, Bek, Bqin, Bkin, Ber, Bkst, Batt, BS, BSb0, BSb1,
         Bosq, Bssq, Bon) = [Buf() for _ in range(24)]
        Byst = [Buf()] * 2
        Boacc = [Buf() for _ in range(18)]
        Beq_, Bek_, Bqin_, Bkin_, Ber_, Bkst_, Bqinh_, Batt_, BS_, BSb0_, BSb1_ = [[Buf(), Buf()] for _ in range(11)]
        Bps = [Buf() for _ in range(8)]
        wv = g.w_in[l].rearrange("(k p) n -> p k n", p=128)
        pk = 0
        Bgng = Buf()
        S.dma(SP, gng[:], g.gla_norm_g[l].partition_broadcast(128), (), [Bgng])
        for d_ in range(2):
            for i_ in range(2):
                S.pool(lambda e, i_=i_, d_=d_: e.memset(qinh_[d_][i_][:], 0.0), (), [Bqinh_[d_]])
        for hd in range(4):
            S.pool(lambda e: e.memset(wq[:, :, 768:896], 0.0), (), [Bwq])
            for (d0, c0, cn) in ((0, OFF_Q + 128 * hd, 128), (128, OFF_K + 128 * hd, 128), (256, OFF_V + 256 * hd, 256),
                                 (512, OFF_G + 256 * hd, 256), (768, OFF_ALR, 16), (800, OFF_ALR + 16, 16)):
                S.dma(POOL, wq[:, :, d0:d0 + cn], wv[:, :, c0:c0 + cn], (), [Bwq])
            import os as _os
            S.pool(lambda e: e.memset(WG[:], 0.0), (), [BWG])
            for d in range(0 if _os.environ.get('GLA_NOWG') else 2):
                S.dma(POOL, WG[32 * d:32 * d + 16, d * 128:(d + 1) * 128], g.gla_w_gate[l, d, :, hd * 128:(hd + 1) * 128], (), [BWG])
                S.dma(SP, bgb[:, d * 128:(d + 1) * 128], g.gla_b_gate[l, d, hd * 128:(hd + 1) * 128].partition_broadcast(128), (), [Bbg])

            _ninp = [0]

            def inproj(c0, m, evac):
                nonlocal pk
                _ninp[0] += 1
                if _ninp[0] > int(_os.environ.get('GLA_INP', '9')):
                    return
                for (s0, n) in SEGS:
                    p_, Bp = ps[pk % 2], Bps[pk % 2]
                    pk += 1
                    for kc in range(8):
                        MM(S, p_[0:m, 0:n], wq[:, kc, c0:c0 + m], U[:, kc, s0:s0 + n], kc == 0, kc == 7, [Bwq, BU], [Bp])
                    evac(p_, Bp, s0, n)

            if float(_os.environ.get('GLA_DBG', '9')) == 0:
                continue
            inproj(0, 128, lambda p_, Bp, s0, n: COPY(S, DVE, qT[:, s0:s0 + n], p_[:, 0:n], [Bp], [BqT]))
            inproj(128, 128, lambda p_, Bp, s0, n: COPY(S, DVE, kT[:, s0:s0 + n], p_[:, 0:n], [Bp], [BkT]))
            for i in range(2):
                inproj(512 + i * 128, 128, lambda p_, Bp, s0, n, i=i: ACTV(S, sgT[:, i, s0:s0 + n], p_[:, 0:n], AF.Silu, [Bp], [Bsg]))
            inproj(768, 128, lambda p_, Bp, s0, n: COPY(S, DVE, alrT[:, s0:s0 + n], p_[:, 0:n], [Bp], [Balr]))
            _l2 = float(_os.environ.get('GLA_DBG', '9'))
            for c in range(18 if _l2 > 0.5 else 0):
                cs = slice(c * 128, (c + 1) * 128)
                p_, Bp = ps[pk % 2], Bps[pk % 2]
                pk += 1
                for kc in range(8):
                    MM(S, p_[:, 0:256], U[:, kc, cs], wq[:, kc, 256:512], kc == 0, kc == 7, [Bwq, BU], [Bp])
                for kc in range(8):
                    MM(S, ps[7][:, 0:128], U[:, kc, cs], wq[:, kc, 128:256], kc == 0, kc == 7, [Bwq, BU], [Bps[7]])
                COPY(S, ACT, v_tok[:, c, :], p_[:, 0:256], [Bp], [Bv])
                COPY(S, DVE, k_tok[:, c, :], ps[7][:, 0:128], [Bps[7]], [Bk])
                if _l2 > 0.7:
                    MM(S, ps[2][:, 0:256], alrT[:, cs], WG[:, :], True, True, [Balr, BWG], [Bps[2]])
                    TT(S, DVE, lsp[:, c, :], ps[2][:, 0:256], bgb[:], ALU.add, [Bps[2], Bbg], [Blsp])
            if _l2 > 0.8:
                ACTV(S, lsp[:], lsp[:], AF.Exp, [Blsp], [Blsp], scale=-1.0)
                ACTV(S, lsp[:], lsp[:], AF.Ln, [Blsp], [Blsp], bias=1.0)
            yi = 0
            _lvl = 9
            seen = set()

            def gla_iter(d, c):
                CM = M["LE64"] if d == 0 else M["GE64"]
                RM = M["GT64"] if d == 0 else M["LT64"]
                blocks = (0, 1) if d == 0 else (1, 0)
                eq, ek, er, qin, kin, kst, qinh, attT = eq_[d], ek_[d], er_[d], qin_[d], kin_[d], kst_[d], qinh_[d], attT_[d]
                Beq, Bek, Ber, Bqin, Bkin, Bkst, Bqinh, Batt = Beq_[d], Bek_[d], Ber_[d], Bqin_[d], Bkin_[d], Bkst_[d], Bqinh_[d], Batt_[d]
                Sst, Sb0, Sb1, BS, BSb0, BSb1 = Sst_[d], Sb0_[d], Sb1_[d], BS_[d], BSb0_[d], BSb1_[d]
                pP, BpP = ps[2 + 3 * d], Bps[2 + 3 * d]
                pA, BpA = ps[3 + 3 * d], Bps[3 + 3 * d]
                pO, BpO = ps[4 + 3 * d], Bps[4 + 3 * d]
                cs = slice(c * 128, (c + 1) * 128)
                ld = lsp[:, c, d * 128:(d + 1) * 128]
                MM(S, pP[:, 0:128], ld, CM[:], True, True, [Blsp, Bc], [BpP])
                yield
                MM(S, pP[:, 128:256], RM[:], ld, True, True, [Blsp, Bc], [BpP])
                yield
                ACTV(S, eq[:], pP[:, 0:128], AF.Exp, [BpP], [Beq], scale=-1.0 / 16)
                yield
                ACTV(S, ek[:], pP[:, 0:128], AF.Exp, [BpP], [Bek], scale=1.0 / 16)
                yield
                ACTV(S, er[:], pP[:, 128:256], AF.Exp, [BpP], [Ber], scale=-1.0 / 16)
                yield
                STT(S, qin[:], qT[:, cs], QS, eq[:], ALU.mult, ALU.mult, [BqT, Beq], [Bqin])
                yield
                TT(S, DVE, kin[:], kT[:, cs], ek[:], ALU.mult, [BkT, Bek], [Bkin])
                yield
                for bi_ in range(2):
                    STT(S, kst[bi_][:], k_tok[:, c, :], M["BD"][:, 64 * bi_:64 * bi_ + 1], er[:], ALU.mult, ALU.mult, [Bk, Ber, Bc], [Bkst])
                    hs_ = slice(64 * bi_, 64 * bi_ + 64)
                    COPY(S, POOL, qinh[bi_][:, hs_], qin[:, hs_], [Bqin], [Bqinh])
                MM(S, pA[:, 0:128], kin[:], qin[:], True, True, [Bkin, Bqin], [BpA])
                yield
                TT(S, DVE, attT[:], pA[:, 0:128], CM[:], ALU.mult, [BpA, Bc], [Batt])
                yield
                MM(S, pO[:, 0:256], attT[:], v_tok[:, c, :], True, False, [Batt, Bv], [BpO])
                yield
                for bi, blk in enumerate(blocks):
                    Sb, BSb = (Sb0, BSb0) if bi == 0 else (Sb1, BSb1)
                    MM(S, pO[:, 0:256], qinh[blk][:], Sb[:], False, bi == 1, [Bqinh, BSb], [BpO])
                    MM(S, pA[:, 128:384], kst[blk][:], v_tok[:, c, :], True, True, [Bkst, Bv], [BpA])
                    ecol = (blk * 64 + 63) if d == 0 else (blk * 64)
                    STT(S, Sst[:], Sst[:], eq[:, ecol:ecol + 1], pA[:, 128:384], ALU.mult, ALU.add, [BS, Beq, BpA], [BS])
                    if bi == 0:
                        COPY(S, ACT, Sb1[:], Sst[:], [BS], [BSb1])
                    else:
                        COPY(S, ACT, Sb0[:], Sst[:], [BS], [BSb0])
                if c not in seen:
                    seen.add(c)
                    COPY(S, ACT if d == 0 else DVE, oacc[:, c, :], pO[:, 0:256], [BpO], [Boacc[c]])
                else:
                    TT(S, DVE, oacc[:, c, :], oacc[:, c, :], pO[:, 0:256], ALU.add, [Boacc[c], BpO], [Boacc[c]])
                    ACTV(S, osq[:], oacc[:, c, :], AF.Square, [Boacc[c]], [Bosq])
                    S.dve(lambda e: e.reduce_sum(out=ssq[:], in_=osq[:], axis=AX.X), [Bosq], [Bssq])
                    TSC(S, DVE, ssq[:], ssq[:], 1.0 / 256, EPS, ALU.mult, ALU.add, [Bssq], [Bssq])
                    ACTV(S, ssq[:], ssq[:], AF.Sqrt, [Bssq], [Bssq])
                    S.dve(lambda e: e.reciprocal(out=ssq[:], in_=ssq[:]), [Bssq], [Bssq])
                    STT(S, on[:], oacc[:, c, :], ssq[:, 0:1], gng[:], ALU.mult, ALU.mult, [Boacc[c], Bssq, Bgng], [Bon])
                    pf, Bpf = ps[1], Bps[1]
                    for i in range(2):
                        TR(S, pf[:, i * 128:(i + 1) * 128], on[:, i * 128:(i + 1) * 128], M["ID"][:], [Bon, Bc], [Bpf])
                    for i in range(2):
                        if odd and c >= 2:
                            cc0 = (c - 2) * 4
                            ov = yst[yi][:, i, CTX:TB].rearrange("p (r c) -> p c r", c=64)[:, cc0:cc0 + 4, :]
                            i0 = pf[:, i * 128:(i + 1) * 128].rearrange("p (c r) -> p c r", r=32)
                            i1 = sgT[:, i, cs].rearrange("p (c r) -> p c r", r=32)
                        else:
                            ov, i0, i1 = yst[yi][:, i, cs], pf[:, i * 128:(i + 1) * 128], sgT[:, i, cs]
                        TT(S, DVE, ov, i0, i1, ALU.mult, [Bpf, Bsg], [Byst[yi]])

            for d in range(2):
                S.pool(lambda e, d=d: e.memset(Sst_[d][:], 0.0), (), [BS_[d]])
                S.pool(lambda e, d=d: e.memset(Sb0_[d][:], 0.0), (), [BSb0_[d]])
            def gla_dir(d):
                for c in (FWD_CHUNKS if d == 0 else REV_CHUNKS):
                    yield from gla_iter(d, c)

            run_interleaved([gla_dir(0), gla_dir(1)])
            if _lvl >= 3:
                S.dma(SP, g.Y[2][:, 2 * hd:2 * hd + 2, b * TB:(b + 1) * TB], yst[yi][:], [Byst[yi]], ())
        S.emit()
    nc.all_engine_barrier()


def build_program(stages=None):
    nc = bass.Bass("TRN2", target_bir_lowering=False)
    g = G()
    declare_io(nc, g)
    with contextlib.ExitStack() as st:
        init_gsync(nc, st)
        sb = lambda nm, s, d: st.enter_context(nc.sbuf_tensor(uname(nm), s, d))
        g.constf = sb("constf", [128, NCF], F32)
        g.modT = sb("modT", [128, NL * 9 * 8 * 4], F32)
        g.masks = {nm: sb("mask_" + nm, [128, 128], F32) for nm in
                   ("ONES", "LE", "GT", "LT", "GE", "ID", "BD", "LE64", "GT64", "LT64", "GE64")}
        g.onesb = sb("onesb", [128, 128], BF16)
        g.rb_dtb = sb("rb_dtb", [128, NL * 32], F32)
        g.rb_A = sb("rb_A", [128, NL * 32], F32)
        g.rb_D = sb("rb_D", [128, NL * 32], F32)
        g.rb_Ds = sb("rb_Ds", [128, NL * 16], F32)
        if stages is None:
            stages = ["const", "p0"]
            for l in range(NL):
                stages += [("ffn", l, 0), ("mix", l), ("merge", l), ("ffn", l, 1)]
            stages += ["final"]
        for sg in stages:
            if sg == "const":
                phase_const(nc, g)
            elif sg == "p0":
                phase_p0(nc, g)
            elif sg == "final":
                phase_final(nc, g)
            elif sg[0] == "ffn":
                phase_ffn(nc, g, sg[1], sg[2], skip_ctx=(sg[1] == NL - 1 and sg[2] == 1))
            elif sg[0] == "mix":
                phase_mix(nc, g, sg[1], sg[2] if len(sg) > 2 else ("ssd", "lru", "gla"))
            elif sg[0] == "merge":
                phase_merge(nc, g, sg[1], skip_ctx=(sg[1] == NL - 1))
    return nc


_NC_CACHE = {}


def kernel(**inputs):
    from concourse.bass_utils import run_bass_kernel_spmd
    if "nc" not in _NC_CACHE:
        _NC_CACHE["nc"] = build_program()
    nc = _NC_CACHE["nc"]
    ncores = 8
    in_maps = []
    for i in range(ncores):
        m = {}
        for nm in INPUT_NAMES:
            a = np.asarray(inputs[nm], dtype=np.float32)
            if nm in ("x", "c", "ctx"):
                a = a[i * NB:(i + 1) * NB]
            elif nm == "c_ctx":
                a = a.reshape(1, D)
            m[nm] = np.ascontiguousarray(a)
        in_maps.append(m)
    res = run_bass_kernel_spmd(nc, in_maps, core_ids=list(range(ncores)))
    return np.concatenate([np.asarray(r["out"]) for r in res.results], axis=0).astype(np.float32)
```

```python
import numpy as np
import concourse.bass as bass
import concourse.mybir as mybir
from concourse.ap import AP

F32 = mybir.dt.float32
BF16 = mybir.dt.bfloat16
AF = mybir.ActivationFunctionType
ALU = mybir.AluOpType
AX = mybir.AxisListType

PE, ACT, DVE, POOL, SP = "pe", "act", "dve", "pool", "sp"
COMPUTE = (PE, ACT, DVE, POOL)
NDMASEM = 6


class Buf:
    __slots__ = ("name", "last_w", "readers")

    def __init__(self, name=""):
        self.name = name
        self.last_w = None
        self.readers = {}


class Op:
    __slots__ = ("eng", "fn", "deps", "idx", "dma", "signal", "sigval", "sem", "semval", "tag")


class Sched:
    def __init__(self, nc):
        self.nc = nc
        self.ops = {e: [] for e in (PE, ACT, DVE, POOL, SP)}
        self.n_dma = {SP: 0, POOL: 0, ACT: 0}

    def add(self, eng, fn, reads=(), writes=(), dma=False, tag=None):
        op = Op()
        op.eng, op.fn, op.dma, op.signal, op.tag = eng, fn, dma, False, tag
        op.idx = len(self.ops[eng])
        deps = {}

        def dep(d, kind):
            if d is None or d is op:
                return
            if d.eng == eng and not d.dma:
                if eng == PE or eng == SP:
                    return
                if kind == "WAR":
                    return
            key = id(d) if d.dma else d.eng
            cur = deps.get(key)
            if cur is None or (not d.dma and d.idx > cur.idx):
                deps[key] = d

        for b in reads:
            dep(b.last_w, "RAW")
        for b in writes:
            dep(b.last_w, "WAW")
            for r in b.readers.values():
                if isinstance(r, list):
                    for rr in r:
                        dep(rr, "WAR")
                else:
                    dep(r, "WAR")
        for b in reads:
            if dma:
                b.readers.setdefault("dma", []).append(op)
            else:
                b.readers[eng] = op
        for b in writes:
            b.last_w = op
            b.readers = {}
        op.deps = list(deps.values())
        for d in op.deps:
            d.signal = True
        self.ops[eng].append(op)
        return op

    def pe(self, fn, reads=(), writes=()):
        return self.add(PE, fn, reads, writes)

    def act(self, fn, reads=(), writes=()):
        return self.add(ACT, fn, reads, writes)

    def dve(self, fn, reads=(), writes=()):
        return self.add(DVE, fn, reads, writes)

    def pool(self, fn, reads=(), writes=()):
        return self.add(POOL, fn, reads, writes)

    def dma(self, q, out, in_, reads=(), writes=(), **kw):
        return self.add(q, lambda e: e.dma_start(out=out, in_=in_, **kw), reads, writes, dma=True)

    def emit(self, final_wait_all_dma=True):
        nc = self.nc
        gs = GSYNC[0]
        esem, dsem = gs["esem"], gs["dsem"]
        for e in COMPUTE:
            c = gs["ebase"][e]
            for op in self.ops[e]:
                if op.dma:
                    continue
                if op.signal:
                    c += 1
                    op.sigval = c
            gs["ebase"][e] = c
        for q in (SP, POOL, ACT):
            k = gs["dk"][q]
            vals = gs["dvals"][q]
            for op in self.ops[q]:
                if op.dma:
                    s = k % NDMASEM
                    k += 1
                    vals[s] += 16
                    op.sem = dsem[q][s]
                    op.semval = vals[s]
            gs["dk"][q] = k
        engobj = {PE: "tensor", ACT: "scalar", DVE: "vector", POOL: "gpsimd", SP: "sync"}
        with nc.Block() as block:

            def run(e, eng):
                waited = {}

                def wait(sem, val):
                    k = id(sem)
                    if waited.get(k, 0) >= val:
                        return
                    waited[k] = val
                    eng.wait_ge(sem, val)

                for op in self.ops[e]:
                    for d in op.deps:
                        if d.dma:
                            wait(d.sem, d.semval)
                        else:
                            wait(esem[d.eng], d.sigval)
                    if op.dma:
                        if op.semval > 16:
                            wait(op.sem, op.semval - 16)
                        ins = op.fn(eng)
                        ins.then_inc(op.sem, 16)
                    else:
                        ins = op.fn(eng)
                        if op.signal:
                            ins.then_inc(esem[e], 1)
                if final_wait_all_dma:
                    last = {}
                    for op in self.ops[e]:
                        if op.dma:
                            last[id(op.sem)] = (op.sem, op.semval)
                    for sem, val in last.values():
                        wait(sem, val)

            for e in (PE, ACT, DVE, POOL, SP):
                if not self.ops[e]:
                    continue
                getattr(block, engobj[e])(lambda eng, e=e: run(e, eng))


GSYNC = [None]


def init_gsync(nc, st):
    gs = {"esem": {e: st.enter_context(nc.semaphore("s_" + e)) for e in COMPUTE}, "dsem": {},
          "ebase": {e: 0 for e in COMPUTE}, "dk": {}, "dvals": {}}
    for q in (SP, POOL, ACT):
        gs["dsem"][q] = [st.enter_context(nc.semaphore("d_%s%d" % (q, i))) for i in range(NDMASEM)]
        gs["dk"][q] = 0
        gs["dvals"][q] = [0] * NDMASEM
    GSYNC[0] = gs

import contextlib

NL, D = 4, 1024
NB = 2
CTX, SEQ = 256, 2048
TB = CTX + SEQ
TT_ = NB * TB
DFF = 2816
ALPHA = 8.0 ** 0.25
EPS = 1e-5
EPSP = EPS / (ALPHA * ALPHA)
IN_TOTAL = 11328
OFF_Z, OFF_XBC, OFF_DT, OFF_LX, OFF_LG = 0, 1024, 3072, 3104, 4128
OFF_Q, OFF_K, OFF_V, OFF_G, OFF_ALR, OFF_GATE = 5152, 5664, 6176, 7200, 8224, 8256
TS_ = 256
TILES = []
for _b in range(NB):
    TILES.append((_b, 0, 256, 2))
    for _i in range(SEQ // TS_):
        TILES.append((_b, CTX + _i * TS_, TS_, _b))


def MM(S, out, lhsT, rhs, start, stop, R, W):
    return S.pe(lambda e: e.matmul(out, lhsT, rhs, start=start, stop=stop), R, W)


def TR(S, out, in_, ident, R, W):
    return S.pe(lambda e: e.transpose(out, in_, ident), R, W)


def ACTV(S, out, in_, func, R, W, bias=None, scale=None):
    kw = {}
    if bias is not None:
        kw["bias"] = bias
    if scale is not None:
        kw["scale"] = scale
    return S.act(lambda e: e.activation(out=out, in_=in_, func=func, **kw), R, W)


def TT(S, eng, out, in0, in1, op, R, W):
    return S.add(eng, lambda e: e.tensor_tensor(out=out, in0=in0, in1=in1, op=op), R, W)


def TSC(S, eng, out, in0, s1, s2, op0, op1, R, W):
    if s2 is None:
        return S.add(eng, lambda e: e.tensor_scalar(out=out, in0=in0, scalar1=s1, scalar2=None, op0=op0), R, W)
    return S.add(eng, lambda e: e.tensor_scalar(out=out, in0=in0, scalar1=s1, scalar2=s2, op0=op0, op1=op1), R, W)


def STT(S, out, in0, scalar, in1, op0, op1, R, W):
    return S.dve(lambda e: e.scalar_tensor_tensor(out=out, in0=in0, scalar=scalar, in1=in1, op0=op0, op1=op1), R, W)


def COPY(S, eng, out, in_, R, W):
    if eng == ACT:
        return S.act(lambda e: e.activation(out=out, in_=in_, func=AF.Identity), R, W)
    return S.add(eng, lambda e: e.tensor_copy(out=out, in_=in_), R, W)


def bc(ap, shape):
    return ap.to_broadcast(list(shape))


DEBUG_OUT = [False]
_UID = [0]


def uname(n):
    _UID[0] += 1
    return '%s_u%d' % (n, _UID[0])


class G:
    pass


def declare_io(nc, g):
    def din(name, shape):
        return nc.dram_tensor(name, list(shape), F32, kind="ExternalInput").ap()
    g.x = din("x", [NB, SEQ, D])
    g.c = din("c", [NB, D])
    g.ctx = din("ctx", [NB, CTX, D])
    g.c_ctx = din("c_ctx", [1, D])
    g.w_ada = din("w_ada", [NL, D, 9 * D])
    g.b_ada = din("b_ada", [NL, 9 * D])
    g.ln_g = din("ln_g", [NL, 3, D])
    g.ln_b = din("ln_b", [NL, 3, D])
    g.ffn_w_up = din("ffn_w_up", [NL, 2, D, 2 * DFF])
    g.ffn_w_down = din("ffn_w_down", [NL, 2, DFF, D])
    g.w_in = din("w_in", [NL, D, IN_TOTAL])
    g.ssd_conv_w = din("ssd_conv_w", [NL, 4, 2048])
    g.ssd_conv_b = din("ssd_conv_b", [NL, 2048])
    g.ssd_dt_bias = din("ssd_dt_bias", [NL, 2, 16])
    g.ssd_a_log = din("ssd_a_log", [NL, 2, 16])
    g.ssd_d = din("ssd_d", [NL, 2, 16])
    g.ssd_norm_g = din("ssd_norm_g", [NL, 1024])
    g.lru_conv_w = din("lru_conv_w", [NL, 4, 1024])
    g.lru_conv_b = din("lru_conv_b", [NL, 1024])
    g.lru_w_a = din("lru_w_a", [NL, 2, 16, 64, 64])
    g.lru_b_a = din("lru_b_a", [NL, 2, 1024])
    g.lru_w_x = din("lru_w_x", [NL, 2, 16, 64, 64])
    g.lru_b_x = din("lru_b_x", [NL, 2, 1024])
    g.lru_lam = din("lru_lam", [NL, 2, 1024])
    g.gla_w_gate = din("gla_w_gate", [NL, 2, 16, 512])
    g.gla_b_gate = din("gla_b_gate", [NL, 2, 512])
    g.gla_norm_g = din("gla_norm_g", [NL, 256])
    g.w_branch = din("w_branch", [NL, 3, 1024, 1024])
    g.w_out = din("w_out", [NL, 1024, 1024])
    g.out = nc.dram_tensor("out", [NB, SEQ, D], F32, kind="ExternalOutput").ap()
    kd = "ExternalOutput" if DEBUG_OUT[0] else "Internal"
    g.HT = nc.dram_tensor("HT", [128, 8, TT_], F32, kind=kd).ap()
    g.Y = [nc.dram_tensor("Y%d" % i, [128, 8, TT_], BF16, kind=kd).ap() for i in range(3)]


INPUT_NAMES = ["x", "c", "ctx", "c_ctx", "w_ada", "b_ada", "ln_g", "ln_b", "ffn_w_up", "ffn_w_down", "w_in",
               "ssd_conv_w", "ssd_conv_b", "ssd_dt_bias", "ssd_a_log", "ssd_d", "ssd_norm_g", "lru_conv_w",
               "lru_conv_b", "lru_w_a", "lru_b_a", "lru_w_x", "lru_b_x", "lru_lam", "gla_w_gate", "gla_b_gate",
               "gla_norm_g", "w_branch", "w_out"]

CF = {}
_o = 0
for _n, _sz in [("ln_g", NL * 3 * 8), ("ln_b", NL * 3 * 8), ("bada", NL * 9 * 8), ("scw", NL * 4 * 16),
                ("scb", NL * 16), ("sng", NL * 8), ("lcw", NL * 4 * 8), ("lcb", NL * 8), ("lba", NL * 16),
                ("lbx", NL * 16), ("llam", NL * 16), ("c", 16), ("cctx", 8), ("lsp8", NL * 16), ("lsp16", NL * 16),
                ("lsp24", NL * 16)]:
    CF[_n] = _o
    _o += _sz
NCF = _o


def mod_ap(g, l, j, kc, cls):
    i = (((l * 9 + j) * 8) + kc) * 4 + cls
    return g.modT[:, i:i + 1]


def cf(g, name, idx):
    o = CF[name] + idx
    return g.constf[:, o:o + 1]


def phase_const(nc, g):
    with contextlib.ExitStack() as st:
        sb = lambda n, s, d: st.enter_context(nc.sbuf_tensor(uname(n), s, d))
        rowbuf = [sb("rowbuf%d" % i, [128, 128], F32) for i in range(2)]
        wbuf = [sb("wadab%d" % i, [128, 8, 1024], BF16) for i in range(2)]
        sT = sb("sT", [128, 8, 4], BF16)
        tmpm = sb("tmpm", [128, 128], F32)
        pst = [st.enter_context(nc.psum_tensor(uname("pst%d" % i), [128, 512], F32)) for i in range(4)]
        S = Sched(nc)
        Bm = Buf("masks")
        Brow = [Buf(), Buf()]
        Bw = [Buf(), Buf()]
        Bps = [Buf() for _ in range(4)]
        Bcf, BsT, Bmod, Btm = Buf("cf"), Buf(), Buf("mod"), Buf()
        g.Bconst = Buf("constall")
        M = g.masks

        def amask(dst, cm, step, base, op):
            S.pool(lambda e: e.memset(dst, 1.0), (), [Bm])
            S.pool(lambda e: e.affine_select(out=dst, in_=dst, compare_op=op, fill=0.0, base=base,
                                             pattern=[[step, 128]], channel_multiplier=cm), [Bm], [Bm])

        S.pool(lambda e: e.memset(M["ONES"][:], 1.0), (), [Bm])
        S.pool(lambda e: e.memset(g.onesb[:], 1.0), (), [Bm])
        amask(M["LE"][:], -1, 1, 0, ALU.is_ge)
        amask(M["GT"][:], 1, -1, 0, ALU.is_gt)
        amask(M["LT"][:], -1, 1, 0, ALU.is_gt)
        amask(M["GE"][:], 1, -1, 0, ALU.is_ge)
        amask(M["ID"][:], 1, -1, 0, ALU.is_equal)
        S.pool(lambda e: e.memset(M["BD"][:], 0.0), (), [Bm])
        S.pool(lambda e: e.memset(M["BD"][0:64, 0:64], 1.0), [Bm], [Bm])
        S.pool(lambda e: e.memset(M["BD"][64:128, 64:128], 1.0), [Bm], [Bm])
        for nm in ("LE", "GT", "LT", "GE"):
            TT(S, POOL, M[nm + "64"][:], M[nm][:], M["BD"][:], ALU.mult, [Bm], [Bm])

        items = [
            ("ln_g", g.ln_g.rearrange("l i (k p) -> (l i k) p", p=128)),
            ("ln_b", g.ln_b.rearrange("l i (k p) -> (l i k) p", p=128)),
            ("bada", g.b_ada.rearrange("l (j p) -> (l j) p", p=128)),
            ("scw", g.ssd_conv_w.rearrange("l k (c p) -> (l k c) p", p=128)),
            ("scb", g.ssd_conv_b.rearrange("l (c p) -> (l c) p", p=128)),
            ("sng", g.ssd_norm_g.rearrange("l (c p) -> (l c) p", p=128)),
            ("lcw", g.lru_conv_w.rearrange("l k (c p) -> (l k c) p", p=128)),
            ("lcb", g.lru_conv_b.rearrange("l (c p) -> (l c) p", p=128)),
            ("lba", g.lru_b_a.rearrange("l d (c p) -> (l d c) p", p=128)),
            ("lbx", g.lru_b_x.rearrange("l d (c p) -> (l d c) p", p=128)),
            ("llam", g.lru_lam.rearrange("l d (c p) -> (l d c) p", p=128)),
            ("c", g.c.rearrange("b (c p) -> (b c) p", p=128)),
            ("cctx", g.c_ctx.rearrange("b (c p) -> (b c) p", p=128)),
        ]
        k = 0
        for nm, ap in items:
            R = ap.shape[0]
            r0 = 0
            while r0 < R:
                nr = min(128, R - r0)
                i = k % 2
                k += 1
                S.dma(SP, rowbuf[i][0:nr, :], ap[r0:r0 + nr, :], (), [Brow[i]])
                TR(S, pst[i][:, 0:nr], rowbuf[i][0:nr, :], M["ID"][0:nr, 0:nr], [Brow[i], Bm], [Bps[i]])
                o = CF[nm] + r0
                COPY(S, DVE, g.constf[:, o:o + nr], pst[i][:, 0:nr], [Bps[i]], [Bcf])
                r0 += nr
        n16 = NL * 16
        lam = g.constf[:, CF["llam"]:CF["llam"] + n16]
        ACTV(S, tmpm[:, 0:n16], lam, AF.Exp, [Bcf], [Btm], scale=-1.0)
        ACTV(S, tmpm[:, 0:n16], tmpm[:, 0:n16], AF.Ln, [Btm], [Btm], bias=1.0)
        for nm, sc in (("lsp8", -8.0), ("lsp16", -16.0), ("lsp24", -16.0 / 24.0)):
            TSC(S, DVE, g.constf[:, CF[nm]:CF[nm] + n16], tmpm[:, 0:n16], sc, None, ALU.mult, None, [Btm], [Bcf])
        S.dma(SP, g.rb_dtb[:], g.ssd_dt_bias.rearrange("l d h -> (l d h)").partition_broadcast(128), (), [Bcf])
        S.dma(SP, g.rb_A[:], g.ssd_a_log.rearrange("l d h -> (l d h)").partition_broadcast(128), (), [Bcf])
        S.dma(SP, g.rb_D[:], g.ssd_d.rearrange("l d h -> (l d h)").partition_broadcast(128), (), [Bcf])
        ACTV(S, g.rb_A[:], g.rb_A[:], AF.Exp, [Bcf], [Bcf])
        TSC(S, DVE, g.rb_A[:], g.rb_A[:], -1.0, None, ALU.mult, None, [Bcf], [Bcf])
        rbD = g.rb_D[:].rearrange("p (l d h) -> p l d h", l=NL, d=2)
        TT(S, DVE, g.rb_Ds[:].rearrange("p (l h) -> p l h", l=NL), rbD[:, :, 0, :], rbD[:, :, 1, :], ALU.add, [Bcf], [Bcf])
        S.pool(lambda e: e.memset(sT[:], 0.0), (), [BsT])
        for b in range(NB):
            o = CF["c"] + b * 8
            ACTV(S, sT[:, :, b], g.constf[:, o:o + 8], AF.Silu, [Bcf, BsT], [BsT])
        o = CF["cctx"]
        ACTV(S, sT[:, :, 2], g.constf[:, o:o + 8], AF.Silu, [Bcf, BsT], [BsT])
        k = 0
        for l in range(NL):
            wv = g.w_ada[l].rearrange("(k p) n -> p k n", p=128)
            for j in range(9):
                i = k % 2
                pi = 2 + (k % 2)
                k += 1
                S.dma(POOL, wbuf[i][:], wv[:, :, j * 1024:(j + 1) * 1024], (), [Bw[i]])
                for oc in range(8):
                    for kc in range(8):
                        MM(S, pst[pi][:, oc * 4:oc * 4 + 4], wbuf[i][:, kc, oc * 128:(oc + 1) * 128], sT[:, kc, :],
                           kc == 0, kc == 7, [Bw[i], BsT], [Bps[pi]])
                mo = ((l * 9 + j) * 8) * 4
                bo = CF["bada"] + (l * 9 + j) * 8
                TT(S, DVE, g.modT[:, mo:mo + 32].rearrange("p (k c) -> p k c", c=4),
                   pst[pi][:, 0:32].rearrange("p (k c) -> p k c", c=4),
                   bc(g.constf[:, bo:bo + 8].unsqueeze(2), [128, 8, 4]), ALU.add, [Bps[pi], Bcf], [Bmod])
        mv = g.modT[:].rearrange("p (l j r) -> p l j r", l=NL, j=9)
        for j in (1, 4, 7):
            TSC(S, DVE, mv[:, :, j, :], mv[:, :, j, :], 1.0, None, ALU.add, None, [Bmod], [Bmod])
        for j, sc in ((2, 0.5 / ALPHA), (8, 0.5 / ALPHA), (5, 1.0 / ALPHA)):
            TSC(S, DVE, mv[:, :, j, :], mv[:, :, j, :], sc, None, ALU.mult, None, [Bmod], [Bmod])
        S.emit()
    nc.all_engine_barrier()


def phase_p0(nc, g):
    with contextlib.ExitStack() as st:
        sb = lambda n, s, d: st.enter_context(nc.sbuf_tensor(uname(n), s, d))
        tin = [sb("p0in%d" % i, [128, 1024], F32) for i in range(3)]
        stg = [sb("p0st%d" % i, [128, 8, 512], F32) for i in range(2)]
        ps = [st.enter_context(nc.psum_tensor(uname("p0ps%d" % i), [128, 512], F32)) for i in range(4)]
        S = Sched(nc)
        Bin = [Buf() for _ in range(3)]
        Bst = [Buf(), Buf()]
        Bps = [Buf() for _ in range(4)]
        Bc = g.Bconst
        k = 0
        gi = 0
        for b in range(NB):
            groups = [(g.ctx[b], 0, 256)] + [(g.x[b, i * 512:(i + 1) * 512, :], CTX + i * 512, 512) for i in range(4)]
            for src, s0, n in groups:
                sg = gi % 2
                gi += 1
                for t in range(n // 128):
                    i = k % 3
                    S.dma(SP, tin[i][:], src[t * 128:(t + 1) * 128, :], (), [Bin[i]])
                    for half in range(2):
                        pi = (2 * k + half) % 4
                        for q in range(4):
                            kc = half * 4 + q
                            TR(S, ps[pi][:, q * 128:(q + 1) * 128], tin[i][:, kc * 128:(kc + 1) * 128], g.masks["ID"][:],
                               [Bin[i], Bc], [Bps[pi]])
                        COPY(S, ACT if half == 0 else DVE, stg[sg][:, half * 4:half * 4 + 4, t * 128:(t + 1) * 128],
                             ps[pi][:].rearrange("p (q t) -> p q t", q=4), [Bps[pi]], [Bst[sg]])
                    k += 1
                col = b * TB + s0
                S.dma(SP, g.HT[:, :, col:col + n], stg[sg][:, :, 0:n], [Bst[sg]], ())
        S.emit()
    nc.all_engine_barrier()


def phase_final(nc, g):
    with contextlib.ExitStack() as st:
        sb = lambda n, s, d: st.enter_context(nc.sbuf_tensor(uname(n), s, d))
        hin = [sb("pfin%d" % i, [128, 8, 512], F32) for i in range(2)]
        to = [sb("pfo%d" % i, [128, 1024], F32) for i in range(3)]
        ps = [st.enter_context(nc.psum_tensor(uname("pfps%d" % i), [128, 512], F32)) for i in range(4)]
        S = Sched(nc)
        Bin = [Buf(), Buf()]
        Bo = [Buf() for _ in range(3)]
        Bps = [Buf() for _ in range(4)]
        Bc = g.Bconst
        k = 0
        gi = 0
        for b in range(NB):
            for i4 in range(4):
                sg = gi % 2
                gi += 1
                col = b * TB + CTX + i4 * 512
                S.dma(SP, hin[sg][:], g.HT[:, :, col:col + 512], (), [Bin[sg]])
                for t in range(4):
                    oi = k % 3
                    for half in range(2):
                        pi = (2 * k + half) % 4
                        for q in range(4):
                            kc = half * 4 + q
                            TR(S, ps[pi][:, q * 128:(q + 1) * 128], hin[sg][:, kc, t * 128:(t + 1) * 128], g.masks["ID"][:],
                               [Bin[sg], Bc], [Bps[pi]])
                        COPY(S, ACT if half == 0 else DVE, to[oi][:, half * 512:(half + 1) * 512], ps[pi][:], [Bps[pi]], [Bo[oi]])
                    r0 = i4 * 512 + t * 128
                    S.dma(SP, g.out[b, r0:r0 + 128, :], to[oi][:], [Bo[oi]], ())
                    k += 1
        S.emit()
    nc.all_engine_barrier()


def load_weight_cast(S, dst3, src2, nk, ncols, Bw, piece=1024):
    sv = src2.rearrange("(k p) n -> p k n", p=128)
    for kc in range(nk):
        c0 = 0
        while c0 < ncols:
            cn = min(piece, ncols - c0)
            S.dma(POOL, dst3[:, kc, c0:c0 + cn], sv[:, kc, c0:c0 + cn], (), [Bw[kc]])
            c0 += cn


def ln_part1(S, g, zt, Bz, n, W):
    zbf, sq, Bzs = W["zbf"], W["sq"], W["Bzs"]
    COPY(S, ACT, zbf[:, :, 0:n], zt[:, :, 0:n], [Bz], [Bzs])
    ACTV(S, sq[:, :, 0:n], zt[:, :, 0:n], AF.Square, [Bz], [Bzs])


def ln_part2(S, g, l, i, zt, Bz, n, W):
    zbf, sq, Bzs = W["zbf"], W["sq"], W["Bzs"]
    psm, psq, Bpm, Bpq = W["psm"], W["psq"], W["Bpm"], W["Bpq"]
    mean, msq, var, Bsm = W["mean"], W["msq"], W["var"], W["Bsm"]
    Bc = g.Bconst
    for kc in range(8):
        MM(S, psm[:, 0:n], g.onesb[:], zbf[:, kc, 0:n], kc == 0, kc == 7, [Bzs, Bc], [Bpm])
    for kc in range(8):
        MM(S, psq[:, 0:n], g.onesb[:], sq[:, kc, 0:n], kc == 0, kc == 7, [Bzs, Bc], [Bpq])
    ACTV(S, mean[:, 0:n], psm[:, 0:n], AF.Identity, [Bpm], [Bsm], scale=1.0 / 1024)
    ACTV(S, msq[:, 0:n], psm[:, 0:n], AF.Square, [Bpm], [Bsm], scale=1.0 / 1024)
    STT(S, var[:, 0:n], psq[:, 0:n], 1.0 / 1024, msq[:, 0:n], ALU.mult, ALU.subtract, [Bpq, Bsm], [Bsm])
    TSC(S, DVE, var[:, 0:n], var[:, 0:n], EPSP, None, ALU.add, None, [Bsm], [Bsm])
    ACTV(S, var[:, 0:n], var[:, 0:n], AF.Sqrt, [Bsm], [Bsm])
    S.dve(lambda e: e.reciprocal(out=var[:, 0:n], in_=var[:, 0:n]), [Bsm], [Bsm])
    TT(S, DVE, zt[:, :, 0:n], zt[:, :, 0:n], bc(mean[:, 0:n].unsqueeze(1), [128, 8, n]), ALU.subtract, [Bz, Bsm], [Bz])
    TT(S, DVE, zt[:, :, 0:n], zt[:, :, 0:n], bc(var[:, 0:n].unsqueeze(1), [128, 8, n]), ALU.mult, [Bz, Bsm], [Bz])
    for kc in range(8):
        ACTV(S, zt[:, kc, 0:n], zt[:, kc, 0:n], AF.Identity, [Bz, Bc], [Bz],
             bias=cf(g, "ln_b", (l * 3 + i) * 8 + kc), scale=cf(g, "ln_g", (l * 3 + i) * 8 + kc))


def phase_ffn(nc, g, l, j, skip_ctx):
    n = TS_
    with contextlib.ExitStack() as st:
        sb = lambda nm, s, d: st.enter_context(nc.sbuf_tensor(uname(nm), s, d))
        wup = sb("wup", [128, 8, 2 * DFF], BF16)
        wdn = sb("wdn", [128, 22, D], BF16)
        hb = [sb("ffh%d" % i, [128, 8, n], F32) for i in range(3)]
        ub = [sb("ffu%d" % i, [128, 8, n], BF16) for i in range(2)]
        hid = sb("ffhid", [128, 22, n], BF16)
        sil = [sb("ffsil%d" % i, [128, n], F32) for i in range(2)]
        zbf = sb("ffzbf", [128, 8, n], BF16)
        sq = sb("ffsq", [128, 8, n], BF16)
        mean = sb("ffmean", [128, n], F32)
        msq = sb("ffmsq", [128, n], F32)
        var = sb("ffvar", [128, n], F32)
        ps = [st.enter_context(nc.psum_tensor(uname("ffps%d" % i), [128, 512], F32)) for i in range(8)]
        S = Sched(nc)
        Bc = g.Bconst
        Bwu = [Buf() for _ in range(8)]
        Bwd = [Buf() for _ in range(22)]
        Bh = [Buf(), Buf(), Buf()]
        Bu = [Buf(), Buf()]
        Bhid, Bzs, Bsm = Buf(), Buf(), Buf()
        Bsil = [Buf(), Buf()]
        Bps = [Buf() for _ in range(8)]
        load_weight_cast(S, wup, g.ffn_w_up[l, j], 8, 2 * DFF, Bwu, piece=1408)
        load_weight_cast(S, wdn, g.ffn_w_down[l, j], 22, D, Bwd)
        W = dict(zbf=zbf, sq=sq, Bzs=Bzs, psm=ps[6], psq=ps[7], Bpm=Bps[6], Bpq=Bps[7], mean=mean, msq=msq, var=var, Bsm=Bsm)
        tiles = [t for t in TILES if not (skip_ctx and t[3] == 2)]
        NT = len(tiles)
        pkc = [0]

        def stA(i):
            b, s0, nn, cls = tiles[i]
            col = b * TB + s0
            S.dma(SP, hb[i % 3][:], g.HT[:, :, col:col + n], (), [Bh[i % 3]])
            for kc in range(8):
                ACTV(S, ub[i % 2][:, kc, :], hb[i % 3][:, kc, :], AF.Identity, [Bh[i % 3], Bc], [Bu[i % 2]],
                     bias=mod_ap(g, l, 3 * (2 * j) + 0, kc, cls), scale=mod_ap(g, l, 3 * (2 * j) + 1, kc, cls))

        def stB(i):
            u_, Bu_ = ub[i % 2], Bu[i % 2]
            for fc in range(22):
                pk = pkc[0]
                pa, pv = ps[(pk % 2) * 2], ps[(pk % 2) * 2 + 1]
                Bpa, Bpv = Bps[(pk % 2) * 2], Bps[(pk % 2) * 2 + 1]
                si = pk % 2
                pkc[0] += 1
                for kc in range(8):
                    MM(S, pa[:, 0:n], wup[:, kc, fc * 128:(fc + 1) * 128], u_[:, kc, :], kc == 0, kc == 7, [Bwu[kc], Bu_], [Bpa])
                for kc in range(8):
                    MM(S, pv[:, 0:n], wup[:, kc, DFF + fc * 128:DFF + (fc + 1) * 128], u_[:, kc, :], kc == 0, kc == 7, [Bwu[kc], Bu_], [Bpv])
                ACTV(S, sil[si][:], pa[:, 0:n], AF.Silu, [Bpa], [Bsil[si]])
                TT(S, DVE, hid[:, fc, :], sil[si][:], pv[:, 0:n], ALU.mult, [Bsil[si], Bpv], [Bhid])

        def stC(i):
            b, s0, nn, cls = tiles[i]
            h_, Bh_ = hb[i % 3], Bh[i % 3]
            for oc in range(8):
                py, Bpy = ps[4 + oc % 2], Bps[4 + oc % 2]
                for fc in range(22):
                    MM(S, py[:, 0:n], wdn[:, fc, oc * 128:(oc + 1) * 128], hid[:, fc, :], fc == 0, fc == 21, [Bwd[fc], Bhid], [Bpy])
                STT(S, h_[:, oc, :], py[:, 0:n], mod_ap(g, l, 3 * (2 * j) + 2, oc, cls), h_[:, oc, :], ALU.mult, ALU.add,
                    [Bpy, Bh_, Bc], [Bh_])
            ln_part1(S, g, h_, Bh_, n, W)

        def stD(i):
            b, s0, nn, cls = tiles[i]
            col = b * TB + s0
            ln_part2(S, g, l, 2 * j, hb[i % 3], Bh[i % 3], n, W)
            S.dma(SP, g.HT[:, :, col:col + n], hb[i % 3][:], [Bh[i % 3]], ())

        stA(0)
        stB(0)
        for i in range(NT):
            if i + 1 < NT:
                stA(i + 1)
            stC(i)
            if i + 1 < NT:
                stB(i + 1)
            stD(i)
        S.emit()
    nc.all_engine_barrier()


def phase_mixpro(nc, g, l, b, U, BU):
    n = TS_
    odd = (l % 2 == 1)
    with contextlib.ExitStack() as st:
        sb = lambda nm, s, d: st.enter_context(nc.sbuf_tensor(uname(nm), s, d))
        hb = [sb("mph%d" % i, [128, 8, n], F32) for i in range(3)]
        S = Sched(nc)
        Bh = [Buf() for _ in range(3)]
        Bc = g.Bconst
        ti = 0
        for (bb, s0, nn, cls) in TILES:
            if bb != b:
                continue
            hi = ti % 3
            ti += 1
            col = b * TB + s0
            S.dma(SP, hb[hi][:], g.HT[:, :, col:col + n], (), [Bh[hi]])
            for kc in range(8):
                if cls == 2 or not odd:
                    dst = U[:, kc, s0:s0 + n]
                    src = hb[hi][:, kc, :]
                else:
                    r0 = (s0 - CTX) // 64
                    nr = n // 64
                    dst = U[:, kc, CTX:TB].rearrange("p (c r) -> p r c", r=32)[:, r0:r0 + nr, :]
                    src = hb[hi][:, kc, :].rearrange("p (r c) -> p r c", c=64)
                ACTV(S, dst, src, AF.Identity, [Bh[hi], Bc], [BU],
                     bias=mod_ap(g, l, 3, kc, cls), scale=mod_ap(g, l, 4, kc, cls))
        S.emit()
    nc.all_engine_barrier()


def phase_merge(nc, g, l, skip_ctx):
    n = TS_
    with contextlib.ExitStack() as st:
        sb = lambda nm, s, d: st.enter_context(nc.sbuf_tensor(uname(nm), s, d))
        wgt = sb("mgwg", [128, 8, 3072], BF16)
        wbr = sb("mgwb", [128, 24, 1024], BF16)
        wo = sb("mgwo", [128, 8, 1024], BF16)
        hb = [sb("mgh%d" % i, [128, 8, n], F32) for i in range(3)]
        ub = [sb("mgu0", [128, 8, n], BF16)] * 2
        sq0 = sb("mgsq0", [128, 8, n], BF16)
        yb = [[sb("mgy%d_%d" % (i, k), [128, 8, n], BF16) for k in range(3)] for i in range(2)]
        mb = sb("mgm", [128, 8, n], BF16)
        zbf = sb("mgzbf", [128, 8, n], BF16)
        sq = sb("mgsq", [128, 8, n], BF16)
        mean = sb("mgmean", [128, n], F32)
        msq = sb("mgmsq", [128, n], F32)
        var = sb("mgvar", [128, n], F32)
        rstd0 = [sb("mgrstd0%d" % i, [128, n], F32) for i in range(2)]
        sig = [sb("mgsig%d" % i, [128, n], F32) for i in range(3)]
        acc = sb("mgacc", [128, n], F32)
        tmp = sb("mgtmp", [128, n], F32)
        ps = [st.enter_context(nc.psum_tensor(uname("mgps%d" % i), [128, 512], F32)) for i in range(8)]
        S = Sched(nc)
        Bc = g.Bconst
        Bwg = [Buf() for _ in range(8)]
        Bwb = [Buf() for _ in range(24)]
        Bwo = [Buf() for _ in range(8)]
        Bh = [Buf(), Buf(), Buf()]
        By = [[Buf() for _ in range(3)] for _ in range(2)]
        Bm, Bzs, Bsm, Bacc, Btmp, Bsq0 = Buf(), Buf(), Buf(), Buf(), Buf(), Buf()
        Bu = [Buf()] * 2
        Br0 = [Buf(), Buf()]
        Bsig = [Buf(), Buf(), Buf()]
        Bps = [Buf() for _ in range(8)]
        load_weight_cast(S, wgt, g.w_in[l][:, OFF_GATE:OFF_GATE + 3072], 8, 3072, Bwg)
        load_weight_cast(S, wbr, g.w_branch[l].rearrange("n k m -> (n k) m"), 24, 1024, Bwb)
        load_weight_cast(S, wo, g.w_out[l], 8, 1024, Bwo)
        import os as _os
        PL = DVE
        for kc in range(0 if _os.environ.get('MG_NOFOLD') else 8):
            TSC(S, DVE, wbr[:, kc, :], wbr[:, kc, :], cf(g, "sng", l * 8 + kc), None, ALU.mult, None, [Bwb[kc], Bc], [Bwb[kc]])
        W = dict(zbf=zbf, sq=sq, Bzs=Bzs, psm=ps[6], psq=ps[7], Bpm=Bps[6], Bpq=Bps[7], mean=mean, msq=msq, var=var, Bsm=Bsm)
        tiles = [t for t in TILES if not (skip_ctx and t[3] == 2)]
        NT = len(tiles)
        pkc = [0]

        def stA(i):
            b, s0, nn, cls = tiles[i]
            col = b * TB + s0
            hi = i % 2
            S.dma(SP, hb[i % 3][:], g.HT[:, :, col:col + n], (), [Bh[i % 3]])
            for k in range(3):
                S.dma(SP, yb[hi][k][:], g.Y[k][:, :, col:col + n], (), [By[hi][k]])
            for kc in range(8):
                ACTV(S, ub[hi][:, kc, :], hb[i % 3][:, kc, :], AF.Identity, [Bh[i % 3], Bc], [Bu[hi]],
                     bias=mod_ap(g, l, 3, kc, cls), scale=mod_ap(g, l, 4, kc, cls))
            ACTV(S, sq0[:], yb[hi][0][:], AF.Square, [By[hi][0]], [Bsq0])
            for kc in range(8):
                MM(S, ps[7][:, 0:n], g.onesb[:], sq0[:, kc, :], kc == 0, kc == 7, [Bsq0, Bc], [Bps[7]])
            TSC(S, DVE, rstd0[hi][:], ps[7][:, 0:n], 1.0 / 1024, EPS, ALU.mult, ALU.add, [Bps[7]], [Br0[hi]])
            ACTV(S, rstd0[hi][:], rstd0[hi][:], AF.Sqrt, [Br0[hi]], [Br0[hi]])
            S.dve(lambda e: e.reciprocal(out=rstd0[hi][:], in_=rstd0[hi][:]), [Br0[hi]], [Br0[hi]])

        def stB(i):
            hi = i % 2
            for oc in range(8):
                for k in range(3):
                    pk = pkc[0]
                    pg, pb = ps[(pk % 3) * 2], ps[(pk % 3) * 2 + 1]
                    Bpg, Bpb = Bps[(pk % 3) * 2], Bps[(pk % 3) * 2 + 1]
                    si = pk % 3
                    pkc[0] += 1
                    for kc in range(8):
                        MM(S, pg[:, 0:n], wgt[:, kc, k * 1024 + oc * 128:k * 1024 + (oc + 1) * 128], ub[hi][:, kc, :], kc == 0, kc == 7,
                           [Bwg[kc], Bu[hi]], [Bpg])
                    for kc in range(8):
                        MM(S, pb[:, 0:n], wbr[:, k * 8 + kc, oc * 128:(oc + 1) * 128], yb[hi][k][:, kc, :], kc == 0, kc == 7,
                           [Bwb[k * 8 + kc], By[hi][k]], [Bpb])
                    ACTV(S, sig[si][:], pg[:, 0:n], AF.Sigmoid, [Bpg], [Bsig[si]])
                    if k == 0:
                        TT(S, DVE, acc[:], sig[si][:], pb[:, 0:n], ALU.mult, [Bsig[si], Bpb], [Bacc])
                        TT(S, PL, acc[:], acc[:], rstd0[hi][:], ALU.mult, [Bacc, Br0[hi]], [Bacc])
                    elif k == 1:
                        TT(S, DVE, tmp[:], sig[si][:], pb[:, 0:n], ALU.mult, [Bsig[si], Bpb], [Btmp])
                        TT(S, PL, acc[:], acc[:], tmp[:], ALU.add, [Bacc, Btmp], [Bacc])
                    else:
                        TT(S, DVE, tmp[:], sig[si][:], pb[:, 0:n], ALU.mult, [Bsig[si], Bpb], [Btmp])
                        TT(S, PL, mb[:, oc, :], acc[:], tmp[:], ALU.add, [Bacc, Btmp], [Bm])

        def stC(i):
            b, s0, nn, cls = tiles[i]
            h_, Bh_ = hb[i % 3], Bh[i % 3]
            for oc in range(8):
                py, Bpy = ps[6], Bps[6]
                for kc in range(8):
                    MM(S, py[:, 0:n], wo[:, kc, oc * 128:(oc + 1) * 128], mb[:, kc, :], kc == 0, kc == 7, [Bwo[kc], Bm], [Bpy])
                STT(S, h_[:, oc, :], py[:, 0:n], mod_ap(g, l, 5, oc, cls), h_[:, oc, :], ALU.mult, ALU.add,
                    [Bpy, Bh_, Bc], [Bh_])
            ln_part1(S, g, h_, Bh_, n, W)

        def stD(i):
            b, s0, nn, cls = tiles[i]
            col = b * TB + s0
            ln_part2(S, g, l, 1, hb[i % 3], Bh[i % 3], n, W)
            S.dma(SP, g.HT[:, :, col:col + n], hb[i % 3][:], [Bh[i % 3]], ())

        stA(0)
        stB(0)
        for i in range(NT):
            if i + 1 < NT:
                stA(i + 1)
            stC(i)
            if i + 1 < NT:
                stB(i + 1)
            stD(i)
        S.emit()
    nc.all_engine_barrier()


SEGS = [(0, 256)] + [(CTX + i * 512, 512) for i in range(4)]


def phase_lru(nc, g, l, b, U, BU):
    odd = (l % 2 == 1)
    Lh = 32 if odd else 64
    with contextlib.ExitStack() as st:
        sb = lambda nm, s, d: st.enter_context(nc.sbuf_tensor(uname(nm), s, d))
        wl = [sb("lrw%d" % i, [128, 8, 256], BF16) for i in range(2)]
        wblk = sb("lrblk", [128, 8, 4, 128], BF16)
        xr = sb("lrxr", [128, TB], F32)
        xc = sb("lrxc", [128, TB], F32)
        xcb = sb("lrxcb", [128, TB], BF16)
        gg = sb("lrgg", [128, TB], F32)
        T_ = [[sb("lrT%d_%d" % (d_, i), [128, TB], F32) for i in range(4)] for d_ in range(2)]
        hf = sb("lrhf", [128, TB], F32)
        hbk = sb("lrhb", [128, TB], F32)
        yst = [sb("lryst%d" % i, [128, TB], BF16) for i in range(2)]
        ps = [st.enter_context(nc.psum_tensor(uname("lrps%d" % i), [128, 512], F32)) for i in range(8)]
        S = Sched(nc)
        Bc = g.Bconst
        Bwl = [Buf(), Buf()]
        Bblk, Bxr, Bxc, Bxcb, Bgg, Bhf, Bhb = Buf(), Buf(), Buf(), Buf(), Buf(), Buf(), Buf()
        BT_ = [[Buf() for _ in range(4)] for _ in range(2)]
        Byst = [Buf(), Buf()]
        Bps = [Buf() for _ in range(8)]
        S.pool(lambda e: e.memset(wblk[:], 0.0), (), [Bblk])
        for d in range(2):
            for t, wsrc in enumerate((g.lru_w_a, g.lru_w_x)):
                for h in range(2):
                    src = wsrc[l, d].rearrange("(j h) k c -> h k j c", h=2)[h]
                    S.dma(POOL, wblk[h * 64:(h + 1) * 64, :, d * 2 + t, h * 64:(h + 1) * 64], src, (), [Bblk])
        wv = g.w_in[l].rearrange("(k p) n -> p k n", p=128)
        pk = 0
        for j in range(8):
            wi = j % 2
            S.dma(POOL, wl[wi][:, :, 0:128], wv[:, :, OFF_LX + j * 128:OFF_LX + (j + 1) * 128], (), [Bwl[wi]])
            S.dma(POOL, wl[wi][:, :, 128:256], wv[:, :, OFF_LG + j * 128:OFF_LG + (j + 1) * 128], (), [Bwl[wi]])
            for (s0, n) in SEGS:
                px, Bpx = ps[pk % 4], Bps[pk % 4]
                pg, Bpg = ps[(pk + 1) % 4], Bps[(pk + 1) % 4]
                pk += 2
                for kc in range(8):
                    MM(S, px[:, 0:n], wl[wi][:, kc, 0:128], U[:, kc, s0:s0 + n], kc == 0, kc == 7, [Bwl[wi], BU], [Bpx])
                for kc in range(8):
                    MM(S, pg[:, 0:n], wl[wi][:, kc, 128:256], U[:, kc, s0:s0 + n], kc == 0, kc == 7, [Bwl[wi], BU], [Bpg])
                COPY(S, DVE, xr[:, s0:s0 + n], px[:, 0:n], [Bpx], [Bxr])
                ACTV(S, gg[:, s0:s0 + n], pg[:, 0:n], AF.Gelu_apprx_tanh, [Bpg], [Bgg])
            cw = lambda k: cf(g, "lcw", (l * 4 + k) * 8 + j)
            ACTV(S, xc[:], xr[:], AF.Identity, [Bxr, Bc], [Bxc], bias=cf(g, "lcb", l * 8 + j), scale=cw(2))
            for (o0, ln_, nl) in ((0, 256, 1), (CTX, Lh, SEQ // Lh)):
                xv = xr[:, o0:o0 + ln_ * nl].rearrange("p (a b) -> p a b", b=ln_)
                ov = xc[:, o0:o0 + ln_ * nl].rearrange("p (a b) -> p a b", b=ln_)
                STT(S, ov[:, :, 2:ln_], xv[:, :, 0:ln_ - 2], cw(0), ov[:, :, 2:ln_], ALU.mult, ALU.add, [Bxr, Bxc, Bc], [Bxc])
                STT(S, ov[:, :, 1:ln_], xv[:, :, 0:ln_ - 1], cw(1), ov[:, :, 1:ln_], ALU.mult, ALU.add, [Bxr, Bxc, Bc], [Bxc])
                STT(S, ov[:, :, 0:ln_ - 1], xv[:, :, 1:ln_], cw(3), ov[:, :, 0:ln_ - 1], ALU.mult, ALU.add, [Bxr, Bxc, Bc], [Bxc])
            COPY(S, ACT, xcb[:], xc[:], [Bxc], [Bxcb])
            def lru_dir(d):
                T = T_[d]
                BT = BT_[d]
                ci = (l * 2 + d) * 8 + j
                pk_ = 0
                for (s0, n) in SEGS:
                    pr, Bpr = ps[4 + 2 * d], Bps[4 + 2 * d]
                    pi_, Bpi = ps[5 + 2 * d], Bps[5 + 2 * d]
                    MM(S, pr[:, 0:n], wblk[:, j, d * 2 + 0, :], xcb[:, s0:s0 + n], True, True, [Bblk, Bxcb], [Bpr])
                    yield
                    MM(S, pi_[:, 0:n], wblk[:, j, d * 2 + 1, :], xcb[:, s0:s0 + n], True, True, [Bblk, Bxcb], [Bpi])
                    yield
                    ACTV(S, T[0][:, s0:s0 + n], pr[:, 0:n], AF.Sigmoid, [Bpr, Bc], [BT[0]], bias=cf(g, "lba", ci))
                    yield
                    ACTV(S, T[1][:, s0:s0 + n], pi_[:, 0:n], AF.Sigmoid, [Bpi, Bc], [BT[1]], bias=cf(g, "lbx", ci))
                    yield
                TT(S, POOL, T[1][:], T[1][:], xc[:], ALU.mult, [BT[1], Bxc], [BT[1]])
                yield
                ACTV(S, T[2][:], T[0][:], AF.Exp, [BT[0], Bc], [BT[2]], scale=cf(g, "lsp8", ci))
                yield
                TSC(S, DVE, T[3][:], T[0][:], cf(g, "lsp16", ci), None, ALU.mult, None, [BT[0], Bc], [BT[3]])
                yield
                TSC(S, DVE, T[0][:], T[3][:], 1.0 / 24, 1.0 / 6, ALU.mult, ALU.add, [BT[3]], [BT[0]])
                yield
                TT(S, DVE, T[0][:], T[0][:], T[3][:], ALU.mult, [BT[0], BT[3]], [BT[0]])
                yield
                for cst in (0.5, 1.0):
                    STT(S, T[0][:], T[0][:], cst, T[3][:], ALU.add, ALU.mult, [BT[0], BT[3]], [BT[0]])
                    yield
                ACTV(S, T[0][:], T[0][:], AF.Sqrt, [BT[0]], [BT[0]], scale=-1.0)
                yield
                TT(S, DVE, T[1][:], T[1][:], T[0][:], ALU.mult, [BT[1], BT[0]], [BT[1]])
                yield
                if d == 0:
                    S.dve(lambda e: e.tensor_tensor_scan(out=hf[:], data0=T[2][:], data1=T[1][:], initial=0.0,
                                                         op0=ALU.mult, op1=ALU.add), [BT[2], BT[1]], [Bhf])
                    yield
                else:
                    S.dve(lambda e: e.tensor_tensor_scan(out=hbk[:, 0:CTX][:, ::-1], data0=T[2][:, 0:CTX][:, ::-1],
                                                         data1=T[1][:, 0:CTX][:, ::-1], initial=0.0,
                                                         op0=ALU.mult, op1=ALU.add), [BT[2], BT[1]], [Bhb])
                    yield
                    S.dve(lambda e: e.tensor_tensor_scan(out=hbk[:, CTX:TB][:, ::-1], data0=T[2][:, CTX:TB][:, ::-1],
                                                         data1=T[1][:, CTX:TB][:, ::-1], initial=hbk[:, 0:1],
                                                         op0=ALU.mult, op1=ALU.add), [BT[2], BT[1], Bhb], [Bhb])
                    yield

            run_interleaved([lru_dir(0), lru_dir(1)])
            yi = j % 2
            TT(S, DVE, hf[:], hf[:], hbk[:], ALU.add, [Bhf, Bhb], [Bhf])
            TT(S, DVE, yst[yi][:, 0:CTX], hf[:, 0:CTX], gg[:, 0:CTX], ALU.mult, [Bhf, Bgg], [Byst[yi]])
            if odd:
                ov = yst[yi][:, CTX:TB].rearrange("p (r c) -> p c r", c=64)
                i0 = hf[:, CTX:TB].rearrange("p (c r) -> p c r", r=32)
                i1 = gg[:, CTX:TB].rearrange("p (c r) -> p c r", r=32)
                TT(S, DVE, ov, i0, i1, ALU.mult, [Bhf, Bgg], [Byst[yi]])
            else:
                TT(S, DVE, yst[yi][:, CTX:TB], hf[:, CTX:TB], gg[:, CTX:TB], ALU.mult, [Bhf, Bgg], [Byst[yi]])
            S.dma(SP, g.Y[1][:, j, b * TB:(b + 1) * TB], yst[yi][:], [Byst[yi]], ())
        S.emit()
    nc.all_engine_barrier()


def phase_mix(nc, g, l, which):
    with contextlib.ExitStack() as st:
        U = st.enter_context(nc.sbuf_tensor(uname("Umix"), [128, 8, TB], BF16))
        for b in range(NB):
            BU = Buf("U")
            phase_mixpro(nc, g, l, b, U, BU)
            BU = Buf("U")
            if "ssd" in which:
                phase_ssd(nc, g, l, b, U, BU)
            if "lru" in which:
                phase_lru(nc, g, l, b, U, BU)
            if "gla" in which:
                phase_gla(nc, g, l, b, U, BU)


FWD_CHUNKS = list(range(18))
REV_CHUNKS = [1, 0] + list(range(17, 1, -1))


def run_interleaved(gens):
    gens = list(gens)
    while gens:
        for g_ in list(gens):
            try:
                next(g_)
            except StopIteration:
                gens.remove(g_)


def conv_block(S, g, raw, Braw, out, Bout, wfn, bias_ap, Lh):
    Bc = g.Bconst
    ACTV(S, out[:], raw[:], AF.Identity, [Braw, Bc], [Bout], bias=bias_ap, scale=wfn(2))
    for (o0, ln_, nl) in ((0, 256, 1), (CTX, Lh, SEQ // Lh)):
        xv = raw[:, o0:o0 + ln_ * nl].rearrange("p (a b) -> p a b", b=ln_)
        ov = out[:, o0:o0 + ln_ * nl].rearrange("p (a b) -> p a b", b=ln_)
        STT(S, ov[:, :, 2:ln_], xv[:, :, 0:ln_ - 2], wfn(0), ov[:, :, 2:ln_], ALU.mult, ALU.add, [Braw, Bout, Bc], [Bout])
        STT(S, ov[:, :, 1:ln_], xv[:, :, 0:ln_ - 1], wfn(1), ov[:, :, 1:ln_], ALU.mult, ALU.add, [Braw, Bout, Bc], [Bout])
        STT(S, ov[:, :, 0:ln_ - 1], xv[:, :, 1:ln_], wfn(3), ov[:, :, 0:ln_ - 1], ALU.mult, ALU.add, [Braw, Bout, Bc], [Bout])


def phase_ssd(nc, g, l, b, U, BU):
    odd = (l % 2 == 1)
    Lh = 32 if odd else 64
    M = g.masks
    with contextlib.ExitStack() as st:
        sb = lambda nm, s, d: st.enter_context(nc.sbuf_tensor(uname(nm), s, d))
        wg = [sb("sdw0", [128, 8, 784], BF16)] * 2
        szT = sb("sdsz", [128, 2, TB], BF16)
        craw = sb("sdcraw", [128, TB], F32)
        ctmp = sb("sdctmp", [128, TB], F32)
        xsT = sb("sdxsT", [128, 2, TB], F32)
        BTf = sb("sdBTf", [128, TB], F32)
        BT = sb("sdBT", [128, TB], BF16)
        CT = sb("sdCT", [128, TB], BF16)
        dtv = sb("sddt", [128, 144], F32)
        av = sb("sda", [128, 144], F32)
        acs = sb("sdacs", [128, 144], F32)
        tot = sb("sdtot", [128, 144], F32)
        tm = sb("sdtm", [128, 144], F32)
        fs = sb("sdfs", [128, 144], F32)
        te = sb("sdte", [128, 144], F32)
        cd = sb("sdcd", [128, 144], F32)
        yacc = sb("sdyacc", [128, 18, 256], F32)
        Sst = sb("sdS", [128, 2, 256], F32)
        Sbf = sb("sdSb", [128, 2, 256], BF16)
        yst = [sb("sdyst0", [128, 2, TB], BF16)] * 2
        xs_all = sb("sdxsall", [128, 18, 256], F32)
        B_all = sb("sdBall", [128, 18, 128], BF16)
        xsd_ = [sb("sdxsd%d" % i, [128, 256], BF16) for i in range(2)]
        xw_ = [sb("sdxw%d" % i, [128, 256], BF16) for i in range(2)]
        scm_ = [sb("sdscm%d" % i, [128, 128], BF16) for i in range(2)]
        rhsA_ = [sb("sdrhsA%d" % i, [128, 512], F32) for i in range(2)]
        Eb_ = [sb("sdE%d" % i, [128, 512], BF16) for i in range(2)]
        MT_ = [sb("sdMT%d" % i, [128, 512], BF16) for i in range(2)]
        t1_ = [sb("sdt1%d" % i, [128, 256], F32) for i in range(2)]
        t2 = sb("sdt2", [128, 256], F32)
        ps = [st.enter_context(nc.psum_tensor(uname("sdps%d" % i), [128, 512], F32)) for i in range(8)]
        S = Sched(nc)
        Bc = g.Bconst
        Bwg = [Buf()] * 2
        (Bsz, Bcraw, Bctmp, BxsT, BBTf, BBT, BCT, Bdt, Ba, Bacs, Btot, Btm, Bfs, Bte, Bcd, Byacc, BS, BSb,
         Bxt, BBtok, Bxsd, Bxw, Bscm, BrhsA, BE, BMT, Bt1, Bt2) = [Buf() for _ in range(28)]
        Byst = [Buf()] * 2
        Bxall, BBall = Buf(), Buf()
        Byacc = [Buf() for _ in range(18)]
        Bxsd_, Bxw_, Bscm_, BrhsA_, BE_, BMT_, Bt1_, BS_, BSb_ = [[Buf(), Buf()] for _ in range(9)]
        Bps = [Buf() for _ in range(8)]
        wv = g.w_in[l].rearrange("(k p) n -> p k n", p=128)
        v4 = lambda t: t[:].rearrange("p (c d h) -> p c d h", d=2, h=4)
        pk = 0
        for gq in range(4):
            wi = gq % 2
            for (d0, c0, cn) in ((0, OFF_Z + 256 * gq, 256), (256, OFF_XBC + 256 * gq, 256),
                                 (512, OFF_XBC + 1024 + 128 * gq, 128), (640, OFF_XBC + 1536 + 128 * gq, 128),
                                 (768, OFF_DT + 4 * gq, 4), (772, OFF_DT + 16 + 4 * gq, 4)):
                S.dma(POOL, wg[wi][:, :, d0:d0 + cn], wv[:, :, c0:c0 + cn], (), [Bwg[wi]])

            def inproj(c0, evac):
                nonlocal pk
                for (s0, n) in SEGS:
                    p_, Bp = ps[pk % 2], Bps[pk % 2]
                    pk += 1
                    for kc in range(8):
                        MM(S, p_[:, 0:n], wg[wi][:, kc, c0:c0 + 128], U[:, kc, s0:s0 + n], kc == 0, kc == 7, [Bwg[wi], BU], [Bp])
                    evac(p_, Bp, s0, n)

            for i in range(2):
                inproj(i * 128, lambda p_, Bp, s0, n, i=i: ACTV(S, szT[:, i, s0:s0 + n], p_[:, 0:n], AF.Silu, [Bp], [Bsz]))
            for ci in range(4):
                inproj(256 + ci * 128, lambda p_, Bp, s0, n: COPY(S, DVE, craw[:, s0:s0 + n], p_[:, 0:n], [Bp], [Bcraw]))
                ch16 = (2 * gq + ci) if ci < 2 else (8 + gq if ci == 2 else 12 + gq)
                conv_block(S, g, craw, Bcraw, ctmp, Bctmp, lambda k, ch16=ch16: cf(g, "scw", (l * 4 + k) * 16 + ch16),
                           cf(g, "scb", l * 16 + ch16), Lh)
                if ci < 2:
                    ACTV(S, xsT[:, ci, :], ctmp[:], AF.Silu, [Bctmp], [BxsT])
                elif ci == 2:
                    ACTV(S, BTf[:], ctmp[:], AF.Silu, [Bctmp], [BBTf])
                    COPY(S, DVE, BT[:], BTf[:], [BBTf], [BBT])
                else:
                    ACTV(S, CT[:], ctmp[:], AF.Silu, [Bctmp], [BCT])
            pdt, Bpdt = ps[2], Bps[2]
            for c in range(18):
                for kc in range(8):
                    MM(S, pdt[:, c * 8:(c + 1) * 8], U[:, kc, c * 128:(c + 1) * 128], wg[wi][:, kc, 768:776], kc == 0, kc == 7,
                       [Bwg[wi], BU], [Bpdt])
            rb = lambda t: bc(t[:, l * 32:(l + 1) * 32].rearrange("p (d h) -> p d h", d=2)[:, :, 4 * gq:4 * gq + 4].unsqueeze(1), [128, 18, 2, 4])
            TT(S, DVE, v4(dtv), pdt[:, 0:144].rearrange("p (c d h) -> p c d h", d=2, h=4), rb(g.rb_dtb), ALU.add, [Bpdt, Bc], [Bdt])
            ACTV(S, dtv[:], dtv[:], AF.Exp, [Bdt], [Bdt])
            ACTV(S, dtv[:], dtv[:], AF.Ln, [Bdt], [Bdt], bias=1.0)
            TT(S, DVE, v4(av), v4(dtv), rb(g.rb_A), ALU.mult, [Bdt, Bc], [Ba])
            MM(S, ps[3][:, 0:144], M["LE"][:], av[:], True, True, [Ba, Bc], [Bps[3]])
            MM(S, ps[4][:, 0:144], M["ONES"][:], av[:], True, True, [Ba, Bc], [Bps[4]])
            COPY(S, DVE, acs[:], ps[3][:, 0:144], [Bps[3]], [Bacs])
            COPY(S, DVE, tot[:], ps[4][:, 0:144], [Bps[4]], [Btot])
            ACTV(S, cd[:], tot[:], AF.Exp, [Btot], [Bcd])
            ACTV(S, v4(fs)[:, :, 0, :], v4(acs)[:, :, 0, :], AF.Exp, [Bacs], [Bfs])
            TT(S, DVE, v4(tm)[:, :, 0, :], v4(tot)[:, :, 0, :], v4(acs)[:, :, 0, :], ALU.subtract, [Btot, Bacs], [Btm])
            TT(S, DVE, v4(tm)[:, :, 1, :], v4(acs)[:, :, 1, :], v4(av)[:, :, 1, :], ALU.subtract, [Bacs, Ba], [Btm])
            ACTV(S, te[:], tm[:], AF.Exp, [Btm], [Bte])
            TT(S, DVE, v4(tm)[:, :, 1, :], v4(tot)[:, :, 1, :], v4(tm)[:, :, 1, :], ALU.subtract, [Btot, Btm], [Btm])
            ACTV(S, v4(fs)[:, :, 1, :], v4(tm)[:, :, 1, :], AF.Exp, [Btm], [Bfs])
            yi = 0
            h3 = lambda t: t.rearrange("p (h q) -> p h q", h=4)
            for c in range(18):
                cs = slice(c * 128, (c + 1) * 128)
                ptr, Bptr = ps[2 + c % 2], Bps[2 + c % 2]
                for i in range(2):
                    TR(S, ptr[:, i * 128:(i + 1) * 128], xsT[:, i, cs], M["ID"][:], [BxsT, Bc], [Bptr])
                TR(S, ptr[:, 256:384], BTf[:, cs], M["ID"][:], [BBTf, Bc], [Bptr])
                COPY(S, ACT, xs_all[:, c, :], ptr[:, 0:256], [Bptr], [Bxall])
                COPY(S, ACT, B_all[:, c, :], ptr[:, 256:384], [Bptr], [BBall])
            seen = set()

            def ssd_iter(d, c):
                M1 = M["GT"] if d == 0 else M["LT"]
                M2 = M["LE"] if d == 0 else M["GE"]
                cs = slice(c * 128, (c + 1) * 128)
                xsd, xw, scm, rhsA, Eb, MT, t1 = xsd_[d], xw_[d], scm_[d], rhsA_[d], Eb_[d], MT_[d], t1_[d]
                Bxsd, Bxw, Bscm, BrhsA, BE, BMT, Bt1 = Bxsd_[d], Bxw_[d], Bscm_[d], BrhsA_[d], BE_[d], BMT_[d], Bt1_[d]
                pA, BpA = ps[2 + 3 * d], Bps[2 + 3 * d]
                pS, BpS = ps[3 + 3 * d], Bps[3 + 3 * d]
                pY, BpY = ps[4 + 3 * d], Bps[4 + 3 * d]
                xs_tok = xs_all[:, c, :]
                dt_c = bc(v4(dtv)[:, c, d, :].unsqueeze(2), [128, 4, 64])
                te_c = bc(v4(te)[:, c, d, :].unsqueeze(2), [128, 4, 64])
                fs_c = bc(v4(fs)[:, c, d, :].unsqueeze(2), [128, 4, 64])
                cd_c = bc(v4(cd)[:, c, d, :].unsqueeze(2), [128, 4, 64])
                TT(S, DVE, h3(xsd[:]), h3(xs_tok), dt_c, ALU.mult, [Bxall, Bdt], [Bxsd])
                yield
                TT(S, DVE, h3(xw[:]), h3(xsd[:]), te_c, ALU.mult, [Bxsd, Bte], [Bxw])
                yield
                MM(S, pA[:, 0:128], BT[:, cs], CT[:, cs], True, True, [BBT, BCT], [BpA])
                yield
                TT(S, DVE, scm[:], pA[:, 0:128], (M["LE"] if d == 0 else M["GE"])[:], ALU.mult, [BpA, Bc], [Bscm])
                yield
                TT(S, POOL, h3(rhsA[:]), bc(M2[:].unsqueeze(1), [128, 4, 128]), bc(v4(av)[:, c, d, :].unsqueeze(2), [128, 4, 128]),
                   ALU.mult, [Ba, Bc], [BrhsA])
                yield
                MM(S, pS[:], M1[:], rhsA[:], True, True, [BrhsA, Bc], [BpS])
                yield
                ACTV(S, Eb[:], pS[:], AF.Exp, [BpS], [BE])
                yield
                TT(S, DVE, h3(MT[:]), h3(Eb[:]), bc(scm[:].unsqueeze(1), [128, 4, 128]), ALU.mult, [BE, Bscm], [BMT])
                yield
                for hh in range(4):
                    MM(S, pY[:, hh * 64:(hh + 1) * 64], MT[:, hh * 128:(hh + 1) * 128], xsd[:, hh * 64:(hh + 1) * 64], True, True,
                       [BMT, Bxsd], [BpY])
                MM(S, pY[:, 256:512], CT[:, cs], Sbf[:, d, :], True, True, [BCT, BSb_[d]], [BpY])
                yield
                TT(S, DVE, h3(t1[:]), h3(pY[:, 256:512]), fs_c, ALU.mult, [BpY, Bfs], [Bt1])
                yield
                if c not in seen:
                    seen.add(c)
                    TT(S, DVE, yacc[:, c, :], pY[:, 0:256], t1[:], ALU.add, [BpY, Bt1], [Byacc[c]])
                else:
                    TT(S, DVE, t1[:], pY[:, 0:256], t1[:], ALU.add, [BpY, Bt1], [Bt1])
                    TT(S, POOL, h3(t2[:]), h3(xs_tok),
                       bc(g.rb_Ds[:, l * 16 + 4 * gq:l * 16 + 4 * gq + 4].unsqueeze(2), [128, 4, 64]), ALU.mult, [Bxall, Bc], [Bt2])
                    TT(S, POOL, t1[:], t1[:], t2[:], ALU.add, [Bt1, Bt2], [Bt1])
                    TT(S, DVE, yacc[:, c, :], yacc[:, c, :], t1[:], ALU.add, [Byacc[c], Bt1], [Byacc[c]])
                    pf, Bpf = ps[1], Bps[1]
                    for i in range(2):
                        TR(S, pf[:, i * 128:(i + 1) * 128], yacc[:, c, i * 128:(i + 1) * 128], M["ID"][:], [Byacc[c], Bc], [Bpf])
                    for i in range(2):
                        if odd and c >= 2:
                            cc0 = (c - 2) * 4
                            ov = yst[yi][:, i, CTX:TB].rearrange("p (r c) -> p c r", c=64)[:, cc0:cc0 + 4, :]
                            i0 = pf[:, i * 128:(i + 1) * 128].rearrange("p (c r) -> p c r", r=32)
                            i1 = szT[:, i, cs].rearrange("p (c r) -> p c r", r=32)
                        else:
                            ov, i0, i1 = yst[yi][:, i, cs], pf[:, i * 128:(i + 1) * 128], szT[:, i, cs]
                        TT(S, DVE, ov, i0, i1, ALU.mult, [Bpf, Bsz], [Byst[yi]])
                MM(S, pA[:, 128:384], B_all[:, c, :], xw[:], True, True, [BBall, Bxw], [BpA])
                yield
                TT(S, POOL, h3(Sst[:, d, :]), h3(Sst[:, d, :]), cd_c, ALU.mult, [BS_[d], Bcd], [BS_[d]])
                yield
                TT(S, DVE, Sst[:, d, :], Sst[:, d, :], pA[:, 128:384], ALU.add, [BS_[d], BpA], [BS_[d]])
                yield
                COPY(S, ACT, Sbf[:, d, :], Sst[:, d, :], [BS_[d]], [BSb_[d]])
                yield

            for d in range(2):
                S.pool(lambda e, d=d: e.memset(Sst[:, d, :], 0.0), (), [BS_[d]])
                S.pool(lambda e, d=d: e.memset(Sbf[:, d, :], 0.0), (), [BSb_[d]])
            def ssd_dir(d):
                for c in (FWD_CHUNKS if d == 0 else REV_CHUNKS):
                    yield from ssd_iter(d, c)

            run_interleaved([ssd_dir(0), ssd_dir(1)])
            S.dma(SP, g.Y[0][:, 2 * gq:2 * gq + 2, b * TB:(b + 1) * TB], yst[yi][:], [Byst[yi]], ())
        S.emit()
    nc.all_engine_barrier()


def phase_gla(nc, g, l, b, U, BU):
    odd = (l % 2 == 1)
    M = g.masks
    QS = 128.0 ** -0.5
    with contextlib.ExitStack() as st:
        sb = lambda nm, s, d: st.enter_context(nc.sbuf_tensor(uname(nm), s, d))
        wq = sb("glw", [128, 8, 896], BF16)
        WG = sb("glWG", [128, 256], BF16)
        bgb = sb("glbg", [128, 256], F32)
        gng = sb("glgng", [128, 256], F32)
        qT = sb("glqT", [128, TB], F32)
        kT = sb("glkT", [128, TB], F32)
        sgT = sb("glsg", [128, 2, TB], BF16)
        alrT = sb("glalr", [128, TB], BF16)
        v_tok = sb("glv", [128, 18, 256], BF16)
        k_tok = sb("glk", [128, 18, 128], F32)
        lsp = sb("gllsp", [128, 18, 256], F32)
        oacc = sb("gloacc", [128, 18, 256], F32)
        yst = [sb("glyst0", [128, 2, TB], BF16)] * 2
        eq_ = [sb("gleq%d" % i, [128, 128], F32) for i in range(2)]
        ek_ = [sb("glek%d" % i, [128, 128], F32) for i in range(2)]
        qin_ = [sb("glqin%d" % i, [128, 128], BF16) for i in range(2)]
        kin_ = [sb("glkin%d" % i, [128, 128], BF16) for i in range(2)]
        er_ = [sb("gler%d" % i, [128, 128], F32) for i in range(2)]
        kst_ = [[sb("glkst%d_%d" % (d_, i), [128, 128], BF16) for i in range(2)] for d_ in range(2)]
        qinh_ = [[sb("glqinh%d_%d" % (d_, i), [128, 128], BF16) for i in range(2)] for d_ in range(2)]
        attT_ = [sb("glatt%d" % i, [128, 128], BF16) for i in range(2)]
        Sst_ = [sb("glS%d" % i, [128, 256], F32) for i in range(2)]
        Sb0_ = [sb("glSb0%d" % i, [128, 256], BF16) for i in range(2)]
        Sb1_ = [sb("glSb1%d" % i, [128, 256], BF16) for i in range(2)]
        osq = sb("glosq", [128, 256], F32)
        ssq = sb("glssq", [128, 1], F32)
        on = sb("glon", [128, 256], F32)
        ps = [st.enter_context(nc.psum_tensor(uname("glps%d" % i), [128, 512], F32)) for i in range(8)]
        S = Sched(nc)
        Bc = g.Bconst
        (Bwq, BWG, Bbg, BqT, BkT, Bsg, Balr, Bv, Bk, Blsp, Boacc, Beq, Bek, Bqin, Bkin, Ber, Bkst, Batt, BS, BSb0, BSb1,
         Bosq, Bssq, Bon) = [Buf() for _ in range(24)]
        Byst = [Buf()] * 2
        Boacc = [Buf() for _ in range(18)]
        Beq_, Bek_, Bqin_, Bkin_, Ber_, Bkst_, Bqinh_, Batt_, BS_, BSb0_, BSb1_ = [[Buf(), Buf()] for _ in range(11)]
        Bps = [Buf() for _ in range(8)]
        wv = g.w_in[l].rearrange("(k p) n -> p k n", p=128)
        pk = 0
        Bgng = Buf()
        S.dma(SP, gng[:], g.gla_norm_g[l].partition_broadcast(128), (), [Bgng])
        for d_ in range(2):
            for i_ in range(2):
                S.pool(lambda e, i_=i_, d_=d_: e.memset(qinh_[d_][i_][:], 0.0), (), [Bqinh_[d_]])
        for hd in range(4):
            S.pool(lambda e: e.memset(wq[:, :, 768:896], 0.0), (), [Bwq])
            for (d0, c0, cn) in ((0, OFF_Q + 128 * hd, 128), (128, OFF_K + 128 * hd, 128), (256, OFF_V + 256 * hd, 256),
                                 (512, OFF_G + 256 * hd, 256), (768, OFF_ALR, 16), (800, OFF_ALR + 16, 16)):
                S.dma(POOL, wq[:, :, d0:d0 + cn], wv[:, :, c0:c0 + cn], (), [Bwq])
            import os as _os
            S.pool(lambda e: e.memset(WG[:], 0.0), (), [BWG])
            for d in range(0 if _os.environ.get('GLA_NOWG') else 2):
                S.dma(POOL, WG[32 * d:32 * d + 16, d * 128:(d + 1) * 128], g.gla_w_gate[l, d, :, hd * 128:(hd + 1) * 128], (), [BWG])
                S.dma(SP, bgb[:, d * 128:(d + 1) * 128], g.gla_b_gate[l, d, hd * 128:(hd + 1) * 128].partition_broadcast(128), (), [Bbg])

            _ninp = [0]

            def inproj(c0, m, evac):
                nonlocal pk
                _ninp[0] += 1
                if _ninp[0] > int(_os.environ.get('GLA_INP', '9')):
                    return
                for (s0, n) in SEGS:
                    p_, Bp = ps[pk % 2], Bps[pk % 2]
                    pk += 1
                    for kc in range(8):
                        MM(S, p_[0:m, 0:n], wq[:, kc, c0:c0 + m], U[:, kc, s0:s0 + n], kc == 0, kc == 7, [Bwq, BU], [Bp])
                    evac(p_, Bp, s0, n)

            if float(_os.environ.get('GLA_DBG', '9')) == 0:
                continue
            inproj(0, 128, lambda p_, Bp, s0, n: COPY(S, DVE, qT[:, s0:s0 + n], p_[:, 0:n], [Bp], [BqT]))
            inproj(128, 128, lambda p_, Bp, s0, n: COPY(S, DVE, kT[:, s0:s0 + n], p_[:, 0:n], [Bp], [BkT]))
            for i in range(2):
                inproj(512 + i * 128, 128, lambda p_, Bp, s0, n, i=i: ACTV(S, sgT[:, i, s0:s0 + n], p_[:, 0:n], AF.Silu, [Bp], [Bsg]))
            inproj(768, 128, lambda p_, Bp, s0, n: COPY(S, DVE, alrT[:, s0:s0 + n], p_[:, 0:n], [Bp], [Balr]))
            _l2 = float(_os.environ.get('GLA_DBG', '9'))
            for c in range(18 if _l2 > 0.5 else 0):
                cs = slice(c * 128, (c + 1) * 128)
                p_, Bp = ps[pk % 2], Bps[pk % 2]
                pk += 1
                for kc in range(8):
                    MM(S, p_[:, 0:256], U[:, kc, cs], wq[:, kc, 256:512], kc == 0, kc == 7, [Bwq, BU], [Bp])
                for kc in range(8):
                    MM(S, ps[7][:, 0:128], U[:, kc, cs], wq[:, kc, 128:256], kc == 0, kc == 7, [Bwq, BU], [Bps[7]])
                COPY(S, ACT, v_tok[:, c, :], p_[:, 0:256], [Bp], [Bv])
                COPY(S, DVE, k_tok[:, c, :], ps[7][:, 0:128], [Bps[7]], [Bk])
                if _l2 > 0.7:
                    MM(S, ps[2][:, 0:256], alrT[:, cs], WG[:, :], True, True, [Balr, BWG], [Bps[2]])
                    TT(S, DVE, lsp[:, c, :], ps[2][:, 0:256], bgb[:], ALU.add, [Bps[2], Bbg], [Blsp])
            if _l2 > 0.8:
                ACTV(S, lsp[:], lsp[:], AF.Exp, [Blsp], [Blsp], scale=-1.0)
                ACTV(S, lsp[:], lsp[:], AF.Ln, [Blsp], [Blsp], bias=1.0)
            yi = 0
            _lvl = 9
            seen = set()

            def gla_iter(d, c):
                CM = M["LE64"] if d == 0 else M["GE64"]
                RM = M["GT64"] if d == 0 else M["LT64"]
                blocks = (0, 1) if d == 0 else (1, 0)
                eq, ek, er, qin, kin, kst, qinh, attT = eq_[d], ek_[d], er_[d], qin_[d], kin_[d], kst_[d], qinh_[d], attT_[d]
                Beq, Bek, Ber, Bqin, Bkin, Bkst, Bqinh, Batt = Beq_[d], Bek_[d], Ber_[d], Bqin_[d], Bkin_[d], Bkst_[d], Bqinh_[d], Batt_[d]
                Sst, Sb0, Sb1, BS, BSb0, BSb1 = Sst_[d], Sb0_[d], Sb1_[d], BS_[d], BSb0_[d], BSb1_[d]
                pP, BpP = ps[2 + 3 * d], Bps[2 + 3 * d]
                pA, BpA = ps[3 + 3 * d], Bps[3 + 3 * d]
                pO, BpO = ps[4 + 3 * d], Bps[4 + 3 * d]
                cs = slice(c * 128, (c + 1) * 128)
                ld = lsp[:, c, d * 128:(d + 1) * 128]
                MM(S, pP[:, 0:128], ld, CM[:], True, True, [Blsp, Bc], [BpP])
                yield
                MM(S, pP[:, 128:256], RM[:], ld, True, True, [Blsp, Bc], [BpP])
                yield
                ACTV(S, eq[:], pP[:, 0:128], AF.Exp, [BpP], [Beq], scale=-1.0 / 16)
                yield
                ACTV(S, ek[:], pP[:, 0:128], AF.Exp, [BpP], [Bek], scale=1.0 / 16)
                yield
                ACTV(S, er[:], pP[:, 128:256], AF.Exp, [BpP], [Ber], scale=-1.0 / 16)
                yield
                STT(S, qin[:], qT[:, cs], QS, eq[:], ALU.mult, ALU.mult, [BqT, Beq], [Bqin])
                yield
                TT(S, DVE, kin[:], kT[:, cs], ek[:], ALU.mult, [BkT, Bek], [Bkin])
                yield
                for bi_ in range(2):
                    STT(S, kst[bi_][:], k_tok[:, c, :], M["BD"][:, 64 * bi_:64 * bi_ + 1], er[:], ALU.mult, ALU.mult, [Bk, Ber, Bc], [Bkst])
                    hs_ = slice(64 * bi_, 64 * bi_ + 64)
                    COPY(S, POOL, qinh[bi_][:, hs_], qin[:, hs_], [Bqin], [Bqinh])
                MM(S, pA[:, 0:128], kin[:], qin[:], True, True, [Bkin, Bqin], [BpA])
                yield
                TT(S, DVE, attT[:], pA[:, 0:128], CM[:], ALU.mult, [BpA, Bc], [Batt])
                yield
                MM(S, pO[:, 0:256], attT[:], v_tok[:, c, :], True, False, [Batt, Bv], [BpO])
                yield
                for bi, blk in enumerate(blocks):
                    Sb, BSb = (Sb0, BSb0) if bi == 0 else (Sb1, BSb1)
                    MM(S, pO[:, 0:256], qinh[blk][:], Sb[:], False, bi == 1, [Bqinh, BSb], [BpO])
                    MM(S, pA[:, 128:384], kst[blk][:], v_tok[:, c, :], True, True, [Bkst, Bv], [BpA])
                    ecol = (blk * 64 + 63) if d == 0 else (blk * 64)
                    STT(S, Sst[:], Sst[:], eq[:, ecol:ecol + 1], pA[:, 128:384], ALU.mult, ALU.add, [BS, Beq, BpA], [BS])
                    if bi == 0:
                        COPY(S, ACT, Sb1[:], Sst[:], [BS], [BSb1])
                    else:
                        COPY(S, ACT, Sb0[:], Sst[:], [BS], [BSb0])
                if c not in seen:
                    seen.add(c)
                    COPY(S, ACT if d == 0 else DVE, oacc[:, c, :], pO[:, 0:256], [BpO], [Boacc[c]])
                else:
                    TT(S, DVE, oacc[:, c, :], oacc[:, c, :], pO[:, 0:256], ALU.add, [Boacc[c], BpO], [Boacc[c]])
                    ACTV(S, osq[:], oacc[:, c, :], AF.Square, [Boacc[c]], [Bosq])
                    S.dve(lambda e: e.reduce_sum(out=ssq[:], in_=osq[:], axis=AX.X), [Bosq], [Bssq])
                    TSC(S, DVE, ssq[:], ssq[:], 1.0 / 256, EPS, ALU.mult, ALU.add, [Bssq], [Bssq])
                    ACTV(S, ssq[:], ssq[:], AF.Sqrt, [Bssq], [Bssq])
                    S.dve(lambda e: e.reciprocal(out=ssq[:], in_=ssq[:]), [Bssq], [Bssq])
                    STT(S, on[:], oacc[:, c, :], ssq[:, 0:1], gng[:], ALU.mult, ALU.mult, [Boacc[c], Bssq, Bgng], [Bon])
                    pf, Bpf = ps[1], Bps[1]
                    for i in range(2):
                        TR(S, pf[:, i * 128:(i + 1) * 128], on[:, i * 128:(i + 1) * 128], M["ID"][:], [Bon, Bc], [Bpf])
                    for i in range(2):
                        if odd and c >= 2:
                            cc0 = (c - 2) * 4
                            ov = yst[yi][:, i, CTX:TB].rearrange("p (r c) -> p c r", c=64)[:, cc0:cc0 + 4, :]
                            i0 = pf[:, i * 128:(i + 1) * 128].rearrange("p (c r) -> p c r", r=32)
                            i1 = sgT[:, i, cs].rearrange("p (c r) -> p c r", r=32)
                        else:
                            ov, i0, i1 = yst[yi][:, i, cs], pf[:, i * 128:(i + 1) * 128], sgT[:, i, cs]
                        TT(S, DVE, ov, i0, i1, ALU.mult, [Bpf, Bsg], [Byst[yi]])

            for d in range(2):
                S.pool(lambda e, d=d: e.memset(Sst_[d][:], 0.0), (), [BS_[d]])
                S.pool(lambda e, d=d: e.memset(Sb0_[d][:], 0.0), (), [BSb0_[d]])
            def gla_dir(d):
                for c in (FWD_CHUNKS if d == 0 else REV_CHUNKS):
                    yield from gla_iter(d, c)

            run_interleaved([gla_dir(0), gla_dir(1)])
            if _lvl >= 3:
                S.dma(SP, g.Y[2][:, 2 * hd:2 * hd + 2, b * TB:(b + 1) * TB], yst[yi][:], [Byst[yi]], ())
        S.emit()
    nc.all_engine_barrier()


def build_program(stages=None):
    nc = bass.Bass("TRN2", target_bir_lowering=False)
    g = G()
    declare_io(nc, g)
    with contextlib.ExitStack() as st:
        init_gsync(nc, st)
        sb = lambda nm, s, d: st.enter_context(nc.sbuf_tensor(uname(nm), s, d))
        g.constf = sb("constf", [128, NCF], F32)
        g.modT = sb("modT", [128, NL * 9 * 8 * 4], F32)
        g.masks = {nm: sb("mask_" + nm, [128, 128], F32) for nm in
                   ("ONES", "LE", "GT", "LT", "GE", "ID", "BD", "LE64", "GT64", "LT64", "GE64")}
        g.onesb = sb("onesb", [128, 128], BF16)
        g.rb_dtb = sb("rb_dtb", [128, NL * 32], F32)
        g.rb_A = sb("rb_A", [128, NL * 32], F32)
        g.rb_D = sb("rb_D", [128, NL * 32], F32)
        g.rb_Ds = sb("rb_Ds", [128, NL * 16], F32)
        if stages is None:
            stages = ["const", "p0"]
            for l in range(NL):
                stages += [("ffn", l, 0), ("mix", l), ("merge", l), ("ffn", l, 1)]
            stages += ["final"]
        for sg in stages:
            if sg == "const":
                phase_const(nc, g)
            elif sg == "p0":
                phase_p0(nc, g)
            elif sg == "final":
                phase_final(nc, g)
            elif sg[0] == "ffn":
                phase_ffn(nc, g, sg[1], sg[2], skip_ctx=(sg[1] == NL - 1 and sg[2] == 1))
            elif sg[0] == "mix":
                phase_mix(nc, g, sg[1], sg[2] if len(sg) > 2 else ("ssd", "lru", "gla"))
            elif sg[0] == "merge":
                phase_merge(nc, g, sg[1], skip_ctx=(sg[1] == NL - 1))
    return nc


_NC_CACHE = {}


def kernel(**inputs):
    from concourse.bass_utils import run_bass_kernel_spmd
    if "nc" not in _NC_CACHE:
        _NC_CACHE["nc"] = build_program()
    nc = _NC_CACHE["nc"]
    ncores = 8
    in_maps = []
    for i in range(ncores):
        m = {}
        for nm in INPUT_NAMES:
            a = np.asarray(inputs[nm], dtype=np.float32)
            if nm in ("x", "c", "ctx"):
                a = a[i * NB:(i + 1) * NB]
            elif nm == "c_ctx":
                a = a.reshape(1, D)
            m[nm] = np.ascontiguousarray(a)
        in_maps.append(m)
    res = run_bass_kernel_spmd(nc, in_maps, core_ids=list(range(ncores)))
    return np.concatenate([np.asarray(r["out"]) for r in res.results], axis=0).astype(np.float32)
```

```python
import numpy as np
import concourse.bass as bass
import concourse.mybir as mybir
from concourse.ap import AP

F32 = mybir.dt.float32
BF16 = mybir.dt.bfloat16
AF = mybir.ActivationFunctionType
ALU = mybir.AluOpType
AX = mybir.AxisListType

PE, ACT, DVE, POOL, SP = "pe", "act", "dve", "pool", "sp"
COMPUTE = (PE, ACT, DVE, POOL)
NDMASEM = 6


class Buf:
    __slots__ = ("name", "last_w", "readers")

    def __init__(self, name=""):
        self.name = name
        self.last_w = None
        self.readers = {}


class Op:
    __slots__ = ("eng", "fn", "deps", "idx", "dma", "signal", "sigval", "sem", "semval", "tag")


class Sched:
    def __init__(self, nc):
        self.nc = nc
        self.ops = {e: [] for e in (PE, ACT, DVE, POOL, SP)}
        self.n_dma = {SP: 0, POOL: 0, ACT: 0}

    def add(self, eng, fn, reads=(), writes=(), dma=False, tag=None):
        op = Op()
        op.eng, op.fn, op.dma, op.signal, op.tag = eng, fn, dma, False, tag
        op.idx = len(self.ops[eng])
        deps = {}

        def dep(d, kind):
            if d is None or d is op:
                return
            if d.eng == eng and not d.dma:
                if eng == PE or eng == SP:
                    return
                if kind == "WAR":
                    return
            key = id(d) if d.dma else d.eng
            cur = deps.get(key)
            if cur is None or (not d.dma and d.idx > cur.idx):
                deps[key] = d

        for b in reads:
            dep(b.last_w, "RAW")
        for b in writes:
            dep(b.last_w, "WAW")
            for r in b.readers.values():
                if isinstance(r, list):
                    for rr in r:
                        dep(rr, "WAR")
                else:
                    dep(r, "WAR")
        for b in reads:
            if dma:
                b.readers.setdefault("dma", []).append(op)
            else:
                b.readers[eng] = op
        for b in writes:
            b.last_w = op
            b.readers = {}
        op.deps = list(deps.values())
        for d in op.deps:
            d.signal = True
        self.ops[eng].append(op)
        return op

    def pe(self, fn, reads=(), writes=()):
        return self.add(PE, fn, reads, writes)

    def act(self, fn, reads=(), writes=()):
        return self.add(ACT, fn, reads, writes)

    def dve(self, fn, reads=(), writes=()):
        return self.add(DVE, fn, reads, writes)

    def pool(self, fn, reads=(), writes=()):
        return self.add(POOL, fn, reads, writes)

    def dma(self, q, out, in_, reads=(), writes=(), **kw):
        return self.add(q, lambda e: e.dma_start(out=out, in_=in_, **kw), reads, writes, dma=True)

    def emit(self, final_wait_all_dma=True):
        nc = self.nc
        gs = GSYNC[0]
        esem, dsem = gs["esem"], gs["dsem"]
        for e in COMPUTE:
            c = gs["ebase"][e]
            for op in self.ops[e]:
                if op.dma:
                    continue
                if op.signal:
                    c += 1
                    op.sigval = c
            gs["ebase"][e] = c
        for q in (SP, POOL, ACT):
            k = gs["dk"][q]
            vals = gs["dvals"][q]
            for op in self.ops[q]:
                if op.dma:
                    s = k % NDMASEM
                    k += 1
                    vals[s] += 16
                    op.sem = dsem[q][s]
                    op.semval = vals[s]
            gs["dk"][q] = k
        engobj = {PE: "tensor", ACT: "scalar", DVE: "vector", POOL: "gpsimd", SP: "sync"}
        with nc.Block() as block:

            def run(e, eng):
                waited = {}

                def wait(sem, val):
                    k = id(sem)
                    if waited.get(k, 0) >= val:
                        return
                    waited[k] = val
                    eng.wait_ge(sem, val)

                for op in self.ops[e]:
                    for d in op.deps:
                        if d.dma:
                            wait(d.sem, d.semval)
                        else:
                            wait(esem[d.eng], d.sigval)
                    if op.dma:
                        if op.semval > 16:
                            wait(op.sem, op.semval - 16)
                        ins = op.fn(eng)
                        ins.then_inc(op.sem, 16)
                    else:
                        ins = op.fn(eng)
                        if op.signal:
                            ins.then_inc(esem[e], 1)
                if final_wait_all_dma:
                    last = {}
                    for op in self.ops[e]:
                        if op.dma:
                            last[id(op.sem)] = (op.sem, op.semval)
                    for sem, val in last.values():
                        wait(sem, val)

            for e in (PE, ACT, DVE, POOL, SP):
                if not self.ops[e]:
                    continue
                getattr(block, engobj[e])(lambda eng, e=e: run(e, eng))


GSYNC = [None]


def init_gsync(nc, st):
    gs = {"esem": {e: st.enter_context(nc.semaphore("s_" + e)) for e in COMPUTE}, "dsem": {},
          "ebase": {e: 0 for e in COMPUTE}, "dk": {}, "dvals": {}}
    for q in (SP, POOL, ACT):
        gs["dsem"][q] = [st.enter_context(nc.semaphore("d_%s%d" % (q, i))) for i in range(NDMASEM)]
        gs["dk"][q] = 0
        gs["dvals"][q] = [0] * NDMASEM
    GSYNC[0] = gs

import contextlib

NL, D = 4, 1024
NB = 2
CTX, SEQ = 256, 2048
TB = CTX + SEQ
TT_ = NB * TB
DFF = 2816
ALPHA = 8.0 ** 0.25
EPS = 1e-5
EPSP = EPS / (ALPHA * ALPHA)
IN_TOTAL = 11328
OFF_Z, OFF_XBC, OFF_DT, OFF_LX, OFF_LG = 0, 1024, 3072, 3104, 4128
OFF_Q, OFF_K, OFF_V, OFF_G, OFF_ALR, OFF_GATE = 5152, 5664, 6176, 7200, 8224, 8256
TS_ = 256
TILES = []
for _b in range(NB):
    TILES.append((_b, 0, 256, 2))
    for _i in range(SEQ // TS_):
        TILES.append((_b, CTX + _i * TS_, TS_, _b))


def MM(S, out, lhsT, rhs, start, stop, R, W):
    return S.pe(lambda e: e.matmul(out, lhsT, rhs, start=start, stop=stop), R, W)


def TR(S, out, in_, ident, R, W):
    return S.pe(lambda e: e.transpose(out, in_, ident), R, W)


def ACTV(S, out, in_, func, R, W, bias=None, scale=None):
    kw = {}
    if bias is not None:
        kw["bias"] = bias
    if scale is not None:
        kw["scale"] = scale
    return S.act(lambda e: e.activation(out=out, in_=in_, func=func, **kw), R, W)


def TT(S, eng, out, in0, in1, op, R, W):
    return S.add(eng, lambda e: e.tensor_tensor(out=out, in0=in0, in1=in1, op=op), R, W)


def TSC(S, eng, out, in0, s1, s2, op0, op1, R, W):
    if s2 is None:
        return S.add(eng, lambda e: e.tensor_scalar(out=out, in0=in0, scalar1=s1, scalar2=None, op0=op0), R, W)
    return S.add(eng, lambda e: e.tensor_scalar(out=out, in0=in0, scalar1=s1, scalar2=s2, op0=op0, op1=op1), R, W)


def STT(S, out, in0, scalar, in1, op0, op1, R, W):
    return S.dve(lambda e: e.scalar_tensor_tensor(out=out, in0=in0, scalar=scalar, in1=in1, op0=op0, op1=op1), R, W)


def COPY(S, eng, out, in_, R, W):
    if eng == ACT:
        return S.act(lambda e: e.activation(out=out, in_=in_, func=AF.Identity), R, W)
    return S.add(eng, lambda e: e.tensor_copy(out=out, in_=in_), R, W)


def bc(ap, shape):
    return ap.to_broadcast(list(shape))


DEBUG_OUT = [False]
_UID = [0]


def uname(n):
    _UID[0] += 1
    return '%s_u%d' % (n, _UID[0])


class G:
    pass


def declare_io(nc, g):
    def din(name, shape):
        return nc.dram_tensor(name, list(shape), F32, kind="ExternalInput").ap()
    g.x = din("x", [NB, SEQ, D])
    g.c = din("c", [NB, D])
    g.ctx = din("ctx", [NB, CTX, D])
    g.c_ctx = din("c_ctx", [1, D])
    g.w_ada = din("w_ada", [NL, D, 9 * D])
    g.b_ada = din("b_ada", [NL, 9 * D])
    g.ln_g = din("ln_g", [NL, 3, D])
    g.ln_b = din("ln_b", [NL, 3, D])
    g.ffn_w_up = din("ffn_w_up", [NL, 2, D, 2 * DFF])
    g.ffn_w_down = din("ffn_w_down", [NL, 2, DFF, D])
    g.w_in = din("w_in", [NL, D, IN_TOTAL])
    g.ssd_conv_w = din("ssd_conv_w", [NL, 4, 2048])
    g.ssd_conv_b = din("ssd_conv_b", [NL, 2048])
    g.ssd_dt_bias = din("ssd_dt_bias", [NL, 2, 16])
    g.ssd_a_log = din("ssd_a_log", [NL, 2, 16])
    g.ssd_d = din("ssd_d", [NL, 2, 16])
    g.ssd_norm_g = din("ssd_norm_g", [NL, 1024])
    g.lru_conv_w = din("lru_conv_w", [NL, 4, 1024])
    g.lru_conv_b = din("lru_conv_b", [NL, 1024])
    g.lru_w_a = din("lru_w_a", [NL, 2, 16, 64, 64])
    g.lru_b_a = din("lru_b_a", [NL, 2, 1024])
    g.lru_w_x = din("lru_w_x", [NL, 2, 16, 64, 64])
    g.lru_b_x = din("lru_b_x", [NL, 2, 1024])
    g.lru_lam = din("lru_lam", [NL, 2, 1024])
    g.gla_w_gate = din("gla_w_gate", [NL, 2, 16, 512])
    g.gla_b_gate = din("gla_b_gate", [NL, 2, 512])
    g.gla_norm_g = din("gla_norm_g", [NL, 256])
    g.w_branch = din("w_branch", [NL, 3, 1024, 1024])
    g.w_out = din("w_out", [NL, 1024, 1024])
    g.out = nc.dram_tensor("out", [NB, SEQ, D], F32, kind="ExternalOutput").ap()
    kd = "ExternalOutput" if DEBUG_OUT[0] else "Internal"
    g.HT = nc.dram_tensor("HT", [128, 8, TT_], F32, kind=kd).ap()
    g.Y = [nc.dram_tensor("Y%d" % i, [128, 8, TT_], BF16, kind=kd).ap() for i in range(3)]


INPUT_NAMES = ["x", "c", "ctx", "c_ctx", "w_ada", "b_ada", "ln_g", "ln_b", "ffn_w_up", "ffn_w_down", "w_in",
               "ssd_conv_w", "ssd_conv_b", "ssd_dt_bias", "ssd_a_log", "ssd_d", "ssd_norm_g", "lru_conv_w",
               "lru_conv_b", "lru_w_a", "lru_b_a", "lru_w_x", "lru_b_x", "lru_lam", "gla_w_gate", "gla_b_gate",
               "gla_norm_g", "w_branch", "w_out"]

CF = {}
_o = 0
for _n, _sz in [("ln_g", NL * 3 * 8), ("ln_b", NL * 3 * 8), ("bada", NL * 9 * 8), ("scw", NL * 4 * 16),
                ("scb", NL * 16), ("sng", NL * 8), ("lcw", NL * 4 * 8), ("lcb", NL * 8), ("lba", NL * 16),
                ("lbx", NL * 16), ("llam", NL * 16), ("c", 16), ("cctx", 8), ("lsp8", NL * 16), ("lsp16", NL * 16),
                ("lsp24", NL * 16)]:
    CF[_n] = _o
    _o += _sz
NCF = _o


def mod_ap(g, l, j, kc, cls):
    i = (((l * 9 + j) * 8) + kc) * 4 + cls
    return g.modT[:, i:i + 1]


def cf(g, name, idx):
    o = CF[name] + idx
    return g.constf[:, o:o + 1]


def phase_const(nc, g):
    with contextlib.ExitStack() as st:
        sb = lambda n, s, d: st.enter_context(nc.sbuf_tensor(uname(n), s, d))
        rowbuf = [sb("rowbuf%d" % i, [128, 128], F32) for i in range(2)]
        wbuf = [sb("wadab%d" % i, [128, 8, 1024], BF16) for i in range(2)]
        sT = sb("sT", [128, 8, 4], BF16)
        tmpm = sb("tmpm", [128, 128], F32)
        pst = [st.enter_context(nc.psum_tensor(uname("pst%d" % i), [128, 512], F32)) for i in range(4)]
        S = Sched(nc)
        Bm = Buf("masks")
        Brow = [Buf(), Buf()]
        Bw = [Buf(), Buf()]
        Bps = [Buf() for _ in range(4)]
        Bcf, BsT, Bmod, Btm = Buf("cf"), Buf(), Buf("mod"), Buf()
        g.Bconst = Buf("constall")
        M = g.masks

        def amask(dst, cm, step, base, op):
            S.pool(lambda e: e.memset(dst, 1.0), (), [Bm])
            S.pool(lambda e: e.affine_select(out=dst, in_=dst, compare_op=op, fill=0.0, base=base,
                                             pattern=[[step, 128]], channel_multiplier=cm), [Bm], [Bm])

        S.pool(lambda e: e.memset(M["ONES"][:], 1.0), (), [Bm])
        S.pool(lambda e: e.memset(g.onesb[:], 1.0), (), [Bm])
        amask(M["LE"][:], -1, 1, 0, ALU.is_ge)
        amask(M["GT"][:], 1, -1, 0, ALU.is_gt)
        amask(M["LT"][:], -1, 1, 0, ALU.is_gt)
        amask(M["GE"][:], 1, -1, 0, ALU.is_ge)
        amask(M["ID"][:], 1, -1, 0, ALU.is_equal)
        S.pool(lambda e: e.memset(M["BD"][:], 0.0), (), [Bm])
        S.pool(lambda e: e.memset(M["BD"][0:64, 0:64], 1.0), [Bm], [Bm])
        S.pool(lambda e: e.memset(M["BD"][64:128, 64:128], 1.0), [Bm], [Bm])
        for nm in ("LE", "GT", "LT", "GE"):
            TT(S, POOL, M[nm + "64"][:], M[nm][:], M["BD"][:], ALU.mult, [Bm], [Bm])

        items = [
            ("ln_g", g.ln_g.rearrange("l i (k p) -> (l i k) p", p=128)),
            ("ln_b", g.ln_b.rearrange("l i (k p) -> (l i k) p", p=128)),
            ("bada", g.b_ada.rearrange("l (j p) -> (l j) p", p=128)),
            ("scw", g.ssd_conv_w.rearrange("l k (c p) -> (l k c) p", p=128)),
            ("scb", g.ssd_conv_b.rearrange("l (c p) -> (l c) p", p=128)),
            ("sng", g.ssd_norm_g.rearrange("l (c p) -> (l c) p", p=128)),
            ("lcw", g.lru_conv_w.rearrange("l k (c p) -> (l k c) p", p=128)),
            ("lcb", g.lru_conv_b.rearrange("l (c p) -> (l c) p", p=128)),
            ("lba", g.lru_b_a.rearrange("l d (c p) -> (l d c) p", p=128)),
            ("lbx", g.lru_b_x.rearrange("l d (c p) -> (l d c) p", p=128)),
            ("llam", g.lru_lam.rearrange("l d (c p) -> (l d c) p", p=128)),
            ("c", g.c.rearrange("b (c p) -> (b c) p", p=128)),
            ("cctx", g.c_ctx.rearrange("b (c p) -> (b c) p", p=128)),
        ]
        k = 0
        for nm, ap in items:
            R = ap.shape[0]
            r0 = 0
            while r0 < R:
                nr = min(128, R - r0)
                i = k % 2
                k += 1
                S.dma(SP, rowbuf[i][0:nr, :], ap[r0:r0 + nr, :], (), [Brow[i]])
                TR(S, pst[i][:, 0:nr], rowbuf[i][0:nr, :], M["ID"][0:nr, 0:nr], [Brow[i], Bm], [Bps[i]])
                o = CF[nm] + r0
                COPY(S, DVE, g.constf[:, o:o + nr], pst[i][:, 0:nr], [Bps[i]], [Bcf])
                r0 += nr
        n16 = NL * 16
        lam = g.constf[:, CF["llam"]:CF["llam"] + n16]
        ACTV(S, tmpm[:, 0:n16], lam, AF.Exp, [Bcf], [Btm], scale=-1.0)
        ACTV(S, tmpm[:, 0:n16], tmpm[:, 0:n16], AF.Ln, [Btm], [Btm], bias=1.0)
        for nm, sc in (("lsp8", -8.0), ("lsp16", -16.0), ("lsp24", -16.0 / 24.0)):
            TSC(S, DVE, g.constf[:, CF[nm]:CF[nm] + n16], tmpm[:, 0:n16], sc, None, ALU.mult, None, [Btm], [Bcf])
        S.dma(SP, g.rb_dtb[:], g.ssd_dt_bias.rearrange("l d h -> (l d h)").partition_broadcast(128), (), [Bcf])
        S.dma(SP, g.rb_A[:], g.ssd_a_log.rearrange("l d h -> (l d h)").partition_broadcast(128), (), [Bcf])
        S.dma(SP, g.rb_D[:], g.ssd_d.rearrange("l d h -> (l d h)").partition_broadcast(128), (), [Bcf])
        ACTV(S, g.rb_A[:], g.rb_A[:], AF.Exp, [Bcf], [Bcf])
        TSC(S, DVE, g.rb_A[:], g.rb_A[:], -1.0, None, ALU.mult, None, [Bcf], [Bcf])
        rbD = g.rb_D[:].rearrange("p (l d h) -> p l d h", l=NL, d=2)
        TT(S, DVE, g.rb_Ds[:].rearrange("p (l h) -> p l h", l=NL), rbD[:, :, 0, :], rbD[:, :, 1, :], ALU.add, [Bcf], [Bcf])
        S.pool(lambda e: e.memset(sT[:], 0.0), (), [BsT])
        for b in range(NB):
            o = CF["c"] + b * 8
            ACTV(S, sT[:, :, b], g.constf[:, o:o + 8], AF.Silu, [Bcf, BsT], [BsT])
        o = CF["cctx"]
        ACTV(S, sT[:, :, 2], g.constf[:, o:o + 8], AF.Silu, [Bcf, BsT], [BsT])
        k = 0
        for l in range(NL):
            wv = g.w_ada[l].rearrange("(k p) n -> p k n", p=128)
            for j in range(9):
                i = k % 2
                pi = 2 + (k % 2)
                k += 1
                S.dma(POOL, wbuf[i][:], wv[:, :, j * 1024:(j + 1) * 1024], (), [Bw[i]])
                for oc in range(8):
                    for kc in range(8):
                        MM(S, pst[pi][:, oc * 4:oc * 4 + 4], wbuf[i][:, kc, oc * 128:(oc + 1) * 128], sT[:, kc, :],
                           kc == 0, kc == 7, [Bw[i], BsT], [Bps[pi]])
                mo = ((l * 9 + j) * 8) * 4
                bo = CF["bada"] + (l * 9 + j) * 8
                TT(S, DVE, g.modT[:, mo:mo + 32].rearrange("p (k c) -> p k c", c=4),
                   pst[pi][:, 0:32].rearrange("p (k c) -> p k c", c=4),
                   bc(g.constf[:, bo:bo + 8].unsqueeze(2), [128, 8, 4]), ALU.add, [Bps[pi], Bcf], [Bmod])
        mv = g.modT[:].rearrange("p (l j r) -> p l j r", l=NL, j=9)
        for j in (1, 4, 7):
            TSC(S, DVE, mv[:, :, j, :], mv[:, :, j, :], 1.0, None, ALU.add, None, [Bmod], [Bmod])
        for j, sc in ((2, 0.5 / ALPHA), (8, 0.5 / ALPHA), (5, 1.0 / ALPHA)):
            TSC(S, DVE, mv[:, :, j, :], mv[:, :, j, :], sc, None, ALU.mult, None, [Bmod], [Bmod])
        S.emit()
    nc.all_engine_barrier()


def phase_p0(nc, g):
    with contextlib.ExitStack() as st:
        sb = lambda n, s, d: st.enter_context(nc.sbuf_tensor(uname(n), s, d))
        tin = [sb("p0in%d" % i, [128, 1024], F32) for i in range(3)]
        stg = [sb("p0st%d" % i, [128, 8, 512], F32) for i in range(2)]
        ps = [st.enter_context(nc.psum_tensor(uname("p0ps%d" % i), [128, 512], F32)) for i in range(4)]
        S = Sched(nc)
        Bin = [Buf() for _ in range(3)]
        Bst = [Buf(), Buf()]
        Bps = [Buf() for _ in range(4)]
        Bc = g.Bconst
        k = 0
        gi = 0
        for b in range(NB):
            groups = [(g.ctx[b], 0, 256)] + [(g.x[b, i * 512:(i + 1) * 512, :], CTX + i * 512, 512) for i in range(4)]
            for src, s0, n in groups:
                sg = gi % 2
                gi += 1
                for t in range(n // 128):
                    i = k % 3
                    S.dma(SP, tin[i][:], src[t * 128:(t + 1) * 128, :], (), [Bin[i]])
                    for half in range(2):
                        pi = (2 * k + half) % 4
                        for q in range(4):
                            kc = half * 4 + q
                            TR(S, ps[pi][:, q * 128:(q + 1) * 128], tin[i][:, kc * 128:(kc + 1) * 128], g.masks["ID"][:],
                               [Bin[i], Bc], [Bps[pi]])
                        COPY(S, ACT if half == 0 else DVE, stg[sg][:, half * 4:half * 4 + 4, t * 128:(t + 1) * 128],
                             ps[pi][:].rearrange("p (q t) -> p q t", q=4), [Bps[pi]], [Bst[sg]])
                    k += 1
                col = b * TB + s0
                S.dma(SP, g.HT[:, :, col:col + n], stg[sg][:, :, 0:n], [Bst[sg]], ())
        S.emit()
    nc.all_engine_barrier()


def phase_final(nc, g):
    with contextlib.ExitStack() as st:
        sb = lambda n, s, d: st.enter_context(nc.sbuf_tensor(uname(n), s, d))
        hin = [sb("pfin%d" % i, [128, 8, 512], F32) for i in range(2)]
        to = [sb("pfo%d" % i, [128, 1024], F32) for i in range(3)]
        ps = [st.enter_context(nc.psum_tensor(uname("pfps%d" % i), [128, 512], F32)) for i in range(4)]
        S = Sched(nc)
        Bin = [Buf(), Buf()]
        Bo = [Buf() for _ in range(3)]
        Bps = [Buf() for _ in range(4)]
        Bc = g.Bconst
        k = 0
        gi = 0
        for b in range(NB):
            for i4 in range(4):
                sg = gi % 2
                gi += 1
                col = b * TB + CTX + i4 * 512
                S.dma(SP, hin[sg][:], g.HT[:, :, col:col + 512], (), [Bin[sg]])
                for t in range(4):
                    oi = k % 3
                    for half in range(2):
                        pi = (2 * k + half) % 4
                        for q in range(4):
                            kc = half * 4 + q
                            TR(S, ps[pi][:, q * 128:(q + 1) * 128], hin[sg][:, kc, t * 128:(t + 1) * 128], g.masks["ID"][:],
                               [Bin[sg], Bc], [Bps[pi]])
                        COPY(S, ACT if half == 0 else DVE, to[oi][:, half * 512:(half + 1) * 512], ps[pi][:], [Bps[pi]], [Bo[oi]])
                    r0 = i4 * 512 + t * 128
                    S.dma(SP, g.out[b, r0:r0 + 128, :], to[oi][:], [Bo[oi]], ())
                    k += 1
        S.emit()
    nc.all_engine_barrier()


def load_weight_cast(S, dst3, src2, nk, ncols, Bw, piece=1024):
    sv = src2.rearrange("(k p) n -> p k n", p=128)
    for kc in range(nk):
        c0 = 0
        while c0 < ncols:
            cn = min(piece, ncols - c0)
            S.dma(POOL, dst3[:, kc, c0:c0 + cn], sv[:, kc, c0:c0 + cn], (), [Bw[kc]])
            c0 += cn


def ln_part1(S, g, zt, Bz, n, W):
    zbf, sq, Bzs = W["zbf"], W["sq"], W["Bzs"]
    COPY(S, ACT, zbf[:, :, 0:n], zt[:, :, 0:n], [Bz], [Bzs])
    ACTV(S, sq[:, :, 0:n], zt[:, :, 0:n], AF.Square, [Bz], [Bzs])


def ln_part2(S, g, l, i, zt, Bz, n, W):
    zbf, sq, Bzs = W["zbf"], W["sq"], W["Bzs"]
    psm, psq, Bpm, Bpq = W["psm"], W["psq"], W["Bpm"], W["Bpq"]
    mean, msq, var, Bsm = W["mean"], W["msq"], W["var"], W["Bsm"]
    Bc = g.Bconst
    for kc in range(8):
        MM(S, psm[:, 0:n], g.onesb[:], zbf[:, kc, 0:n], kc == 0, kc == 7, [Bzs, Bc], [Bpm])
    for kc in range(8):
        MM(S, psq[:, 0:n], g.onesb[:], sq[:, kc, 0:n], kc == 0, kc == 7, [Bzs, Bc], [Bpq])
    ACTV(S, mean[:, 0:n], psm[:, 0:n], AF.Identity, [Bpm], [Bsm], scale=1.0 / 1024)
    ACTV(S, msq[:, 0:n], psm[:, 0:n], AF.Square, [Bpm], [Bsm], scale=1.0 / 1024)
    STT(S, var[:, 0:n], psq[:, 0:n], 1.0 / 1024, msq[:, 0:n], ALU.mult, ALU.subtract, [Bpq, Bsm], [Bsm])
    TSC(S, DVE, var[:, 0:n], var[:, 0:n], EPSP, None, ALU.add, None, [Bsm], [Bsm])
    ACTV(S, var[:, 0:n], var[:, 0:n], AF.Sqrt, [Bsm], [Bsm])
    S.dve(lambda e: e.reciprocal(out=var[:, 0:n], in_=var[:, 0:n]), [Bsm], [Bsm])
    TT(S, DVE, zt[:, :, 0:n], zt[:, :, 0:n], bc(mean[:, 0:n].unsqueeze(1), [128, 8, n]), ALU.subtract, [Bz, Bsm], [Bz])
    TT(S, DVE, zt[:, :, 0:n], zt[:, :, 0:n], bc(var[:, 0:n].unsqueeze(1), [128, 8, n]), ALU.mult, [Bz, Bsm], [Bz])
    for kc in range(8):
        ACTV(S, zt[:, kc, 0:n], zt[:, kc, 0:n], AF.Identity, [Bz, Bc], [Bz],
             bias=cf(g, "ln_b", (l * 3 + i) * 8 + kc), scale=cf(g, "ln_g", (l * 3 + i) * 8 + kc))


def phase_ffn(nc, g, l, j, skip_ctx):
    n = TS_
    with contextlib.ExitStack() as st:
        sb = lambda nm, s, d: st.enter_context(nc.sbuf_tensor(uname(nm), s, d))
        wup = sb("wup", [128, 8, 2 * DFF], BF16)
        wdn = sb("wdn", [128, 22, D], BF16)
        hb = [sb("ffh%d" % i, [128, 8, n], F32) for i in range(3)]
        ub = [sb("ffu%d" % i, [128, 8, n], BF16) for i in range(2)]
        hid = sb("ffhid", [128, 22, n], BF16)
        sil = [sb("ffsil%d" % i, [128, n], F32) for i in range(2)]
        zbf = sb("ffzbf", [128, 8, n], BF16)
        sq = sb("ffsq", [128, 8, n], BF16)
        mean = sb("ffmean", [128, n], F32)
        msq = sb("ffmsq", [128, n], F32)
        var = sb("ffvar", [128, n], F32)
        ps = [st.enter_context(nc.psum_tensor(uname("ffps%d" % i), [128, 512], F32)) for i in range(8)]
        S = Sched(nc)
        Bc = g.Bconst
        Bwu = [Buf() for _ in range(8)]
        Bwd = [Buf() for _ in range(22)]
        Bh = [Buf(), Buf(), Buf()]
        Bu = [Buf(), Buf()]
        Bhid, Bzs, Bsm = Buf(), Buf(), Buf()
        Bsil = [Buf(), Buf()]
        Bps = [Buf() for _ in range(8)]
        load_weight_cast(S, wup, g.ffn_w_up[l, j], 8, 2 * DFF, Bwu, piece=1408)
        load_weight_cast(S, wdn, g.ffn_w_down[l, j], 22, D, Bwd)
        W = dict(zbf=zbf, sq=sq, Bzs=Bzs, psm=ps[6], psq=ps[7], Bpm=Bps[6], Bpq=Bps[7], mean=mean, msq=msq, var=var, Bsm=Bsm)
        tiles = [t for t in TILES if not (skip_ctx and t[3] == 2)]
        NT = len(tiles)
        pkc = [0]

        def stA(i):
            b, s0, nn, cls = tiles[i]
            col = b * TB + s0
            S.dma(SP, hb[i % 3][:], g.HT[:, :, col:col + n], (), [Bh[i % 3]])
            for kc in range(8):
                ACTV(S, ub[i % 2][:, kc, :], hb[i % 3][:, kc, :], AF.Identity, [Bh[i % 3], Bc], [Bu[i % 2]],
                     bias=mod_ap(g, l, 3 * (2 * j) + 0, kc, cls), scale=mod_ap(g, l, 3 * (2 * j) + 1, kc, cls))

        def stB(i):
            u_, Bu_ = ub[i % 2], Bu[i % 2]
            for fc in range(22):
                pk = pkc[0]
                pa, pv = ps[(pk % 2) * 2], ps[(pk % 2) * 2 + 1]
                Bpa, Bpv = Bps[(pk % 2) * 2], Bps[(pk % 2) * 2 + 1]
                si = pk % 2
                pkc[0] += 1
                for kc in range(8):
                    MM(S, pa[:, 0:n], wup[:, kc, fc * 128:(fc + 1) * 128], u_[:, kc, :], kc == 0, kc == 7, [Bwu[kc], Bu_], [Bpa])
                for kc in range(8):
                    MM(S, pv[:, 0:n], wup[:, kc, DFF + fc * 128:DFF + (fc + 1) * 128], u_[:, kc, :], kc == 0, kc == 7, [Bwu[kc], Bu_], [Bpv])
                ACTV(S, sil[si][:], pa[:, 0:n], AF.Silu, [Bpa], [Bsil[si]])
                TT(S, DVE, hid[:, fc, :], sil[si][:], pv[:, 0:n], ALU.mult, [Bsil[si], Bpv], [Bhid])

        def stC(i):
            b, s0, nn, cls = tiles[i]
            h_, Bh_ = hb[i % 3], Bh[i % 3]
            for oc in range(8):
                py, Bpy = ps[4 + oc % 2], Bps[4 + oc % 2]
                for fc in range(22):
                    MM(S, py[:, 0:n], wdn[:, fc, oc * 128:(oc + 1) * 128], hid[:, fc, :], fc == 0, fc == 21, [Bwd[fc], Bhid], [Bpy])
                STT(S, h_[:, oc, :], py[:, 0:n], mod_ap(g, l, 3 * (2 * j) + 2, oc, cls), h_[:, oc, :], ALU.mult, ALU.add,
                    [Bpy, Bh_, Bc], [Bh_])
            ln_part1(S, g, h_, Bh_, n, W)

        def stD(i):
            b, s0, nn, cls = tiles[i]
            col = b * TB + s0
            ln_part2(S, g, l, 2 * j, hb[i % 3], Bh[i % 3], n, W)
            S.dma(ACT, g.HT[:, :, col:col + n], hb[i % 3][:], [Bh[i % 3]], ())

        stA(0)
        stB(0)
        for i in range(NT):
            if i + 1 < NT:
                stA(i + 1)
            stC(i)
            if i + 1 < NT:
                stB(i + 1)
            stD(i)
        S.emit()
    nc.all_engine_barrier()


def phase_mixpro(nc, g, l, b, U, BU):
    n = TS_
    odd = (l % 2 == 1)
    with contextlib.ExitStack() as st:
        sb = lambda nm, s, d: st.enter_context(nc.sbuf_tensor(uname(nm), s, d))
        hb = [sb("mph%d" % i, [128, 8, n], F32) for i in range(3)]
        S = Sched(nc)
        Bh = [Buf() for _ in range(3)]
        Bc = g.Bconst
        ti = 0
        for (bb, s0, nn, cls) in TILES:
            if bb != b:
                continue
            hi = ti % 3
            ti += 1
            col = b * TB + s0
            S.dma(SP, hb[hi][:], g.HT[:, :, col:col + n], (), [Bh[hi]])
            for kc in range(8):
                if cls == 2 or not odd:
                    dst = U[:, kc, s0:s0 + n]
                    src = hb[hi][:, kc, :]
                else:
                    r0 = (s0 - CTX) // 64
                    nr = n // 64
                    dst = U[:, kc, CTX:TB].rearrange("p (c r) -> p r c", r=32)[:, r0:r0 + nr, :]
                    src = hb[hi][:, kc, :].rearrange("p (r c) -> p r c", c=64)
                ACTV(S, dst, src, AF.Identity, [Bh[hi], Bc], [BU],
                     bias=mod_ap(g, l, 3, kc, cls), scale=mod_ap(g, l, 4, kc, cls))
        S.emit()
    nc.all_engine_barrier()


def phase_merge(nc, g, l, skip_ctx):
    n = TS_
    with contextlib.ExitStack() as st:
        sb = lambda nm, s, d: st.enter_context(nc.sbuf_tensor(uname(nm), s, d))
        wgt = sb("mgwg", [128, 8, 3072], BF16)
        wbr = sb("mgwb", [128, 24, 1024], BF16)
        wo = sb("mgwo", [128, 8, 1024], BF16)
        hb = [sb("mgh%d" % i, [128, 8, n], F32) for i in range(3)]
        ub = [sb("mgu0", [128, 8, n], BF16)] * 2
        sq0 = sb("mgsq0", [128, 8, n], BF16)
        yb = [[sb("mgy%d_%d" % (i, k), [128, 8, n], BF16) for k in range(3)] for i in range(2)]
        mb = sb("mgm", [128, 8, n], BF16)
        zbf = sb("mgzbf", [128, 8, n], BF16)
        sq = sb("mgsq", [128, 8, n], BF16)
        mean = sb("mgmean", [128, n], F32)
        msq = sb("mgmsq", [128, n], F32)
        var = sb("mgvar", [128, n], F32)
        rstd0 = [sb("mgrstd0%d" % i, [128, n], F32) for i in range(2)]
        sig = [sb("mgsig%d" % i, [128, n], F32) for i in range(3)]
        acc = sb("mgacc", [128, n], F32)
        tmp = sb("mgtmp", [128, n], F32)
        ps = [st.enter_context(nc.psum_tensor(uname("mgps%d" % i), [128, 512], F32)) for i in range(8)]
        S = Sched(nc)
        Bc = g.Bconst
        Bwg = [Buf() for _ in range(8)]
        Bwb = [Buf() for _ in range(24)]
        Bwo = [Buf() for _ in range(8)]
        Bh = [Buf(), Buf(), Buf()]
        By = [[Buf() for _ in range(3)] for _ in range(2)]
        Bm, Bzs, Bsm, Bacc, Btmp, Bsq0 = Buf(), Buf(), Buf(), Buf(), Buf(), Buf()
        Bu = [Buf()] * 2
        Br0 = [Buf(), Buf()]
        Bsig = [Buf(), Buf(), Buf()]
        Bps = [Buf() for _ in range(8)]
        load_weight_cast(S, wgt, g.w_in[l][:, OFF_GATE:OFF_GATE + 3072], 8, 3072, Bwg)
        load_weight_cast(S, wbr, g.w_branch[l].rearrange("n k m -> (n k) m"), 24, 1024, Bwb)
        load_weight_cast(S, wo, g.w_out[l], 8, 1024, Bwo)
        import os as _os
        PL = DVE
        for kc in range(0 if _os.environ.get('MG_NOFOLD') else 8):
            TSC(S, DVE, wbr[:, kc, :], wbr[:, kc, :], cf(g, "sng", l * 8 + kc), None, ALU.mult, None, [Bwb[kc], Bc], [Bwb[kc]])
        W = dict(zbf=zbf, sq=sq, Bzs=Bzs, psm=ps[6], psq=ps[7], Bpm=Bps[6], Bpq=Bps[7], mean=mean, msq=msq, var=var, Bsm=Bsm)
        tiles = [t for t in TILES if not (skip_ctx and t[3] == 2)]
        NT = len(tiles)
        pkc = [0]

        def stA(i):
            b, s0, nn, cls = tiles[i]
            col = b * TB + s0
            hi = i % 2
            S.dma(SP, hb[i % 3][:], g.HT[:, :, col:col + n], (), [Bh[i % 3]])
            for k in range(3):
                S.dma(SP, yb[hi][k][:], g.Y[k][:, :, col:col + n], (), [By[hi][k]])
            for kc in range(8):
                ACTV(S, ub[hi][:, kc, :], hb[i % 3][:, kc, :], AF.Identity, [Bh[i % 3], Bc], [Bu[hi]],
                     bias=mod_ap(g, l, 3, kc, cls), scale=mod_ap(g, l, 4, kc, cls))
            ACTV(S, sq0[:], yb[hi][0][:], AF.Square, [By[hi][0]], [Bsq0])
            for kc in range(8):
                MM(S, ps[7][:, 0:n], g.onesb[:], sq0[:, kc, :], kc == 0, kc == 7, [Bsq0, Bc], [Bps[7]])
            TSC(S, DVE, rstd0[hi][:], ps[7][:, 0:n], 1.0 / 1024, EPS, ALU.mult, ALU.add, [Bps[7]], [Br0[hi]])
            ACTV(S, rstd0[hi][:], rstd0[hi][:], AF.Sqrt, [Br0[hi]], [Br0[hi]])
            S.dve(lambda e: e.reciprocal(out=rstd0[hi][:], in_=rstd0[hi][:]), [Br0[hi]], [Br0[hi]])

        def stB(i):
            hi = i % 2
            for oc in range(8):
                for k in range(3):
                    pk = pkc[0]
                    pg, pb = ps[(pk % 3) * 2], ps[(pk % 3) * 2 + 1]
                    Bpg, Bpb = Bps[(pk % 3) * 2], Bps[(pk % 3) * 2 + 1]
                    si = pk % 3
                    pkc[0] += 1
                    for kc in range(8):
                        MM(S, pg[:, 0:n], wgt[:, kc, k * 1024 + oc * 128:k * 1024 + (oc + 1) * 128], ub[hi][:, kc, :], kc == 0, kc == 7,
                           [Bwg[kc], Bu[hi]], [Bpg])
                    for kc in range(8):
                        MM(S, pb[:, 0:n], wbr[:, k * 8 + kc, oc * 128:(oc + 1) * 128], yb[hi][k][:, kc, :], kc == 0, kc == 7,
                           [Bwb[k * 8 + kc], By[hi][k]], [Bpb])
                    ACTV(S, sig[si][:], pg[:, 0:n], AF.Sigmoid, [Bpg], [Bsig[si]])
                    if k == 0:
                        TT(S, DVE, acc[:], sig[si][:], pb[:, 0:n], ALU.mult, [Bsig[si], Bpb], [Bacc])
                        TT(S, PL, acc[:], acc[:], rstd0[hi][:], ALU.mult, [Bacc, Br0[hi]], [Bacc])
                    elif k == 1:
                        TT(S, DVE, tmp[:], sig[si][:], pb[:, 0:n], ALU.mult, [Bsig[si], Bpb], [Btmp])
                        TT(S, PL, acc[:], acc[:], tmp[:], ALU.add, [Bacc, Btmp], [Bacc])
                    else:
                        TT(S, DVE, tmp[:], sig[si][:], pb[:, 0:n], ALU.mult, [Bsig[si], Bpb], [Btmp])
                        TT(S, PL, mb[:, oc, :], acc[:], tmp[:], ALU.add, [Bacc, Btmp], [Bm])

        def stC(i):
            b, s0, nn, cls = tiles[i]
            h_, Bh_ = hb[i % 3], Bh[i % 3]
            for oc in range(8):
                py, Bpy = ps[6], Bps[6]
                for kc in range(8):
                    MM(S, py[:, 0:n], wo[:, kc, oc * 128:(oc + 1) * 128], mb[:, kc, :], kc == 0, kc == 7, [Bwo[kc], Bm], [Bpy])
                STT(S, h_[:, oc, :], py[:, 0:n], mod_ap(g, l, 5, oc, cls), h_[:, oc, :], ALU.mult, ALU.add,
                    [Bpy, Bh_, Bc], [Bh_])
            ln_part1(S, g, h_, Bh_, n, W)

        def stD(i):
            b, s0, nn, cls = tiles[i]
            col = b * TB + s0
            ln_part2(S, g, l, 1, hb[i % 3], Bh[i % 3], n, W)
            S.dma(ACT, g.HT[:, :, col:col + n], hb[i % 3][:], [Bh[i % 3]], ())

        stA(0)
        stB(0)
        for i in range(NT):
            if i + 1 < NT:
                stA(i + 1)
            stC(i)
            if i + 1 < NT:
                stB(i + 1)
            stD(i)
        S.emit()
    nc.all_engine_barrier()


SEGS = [(0, 256)] + [(CTX + i * 512, 512) for i in range(4)]


def phase_lru(nc, g, l, b, U, BU):
    odd = (l % 2 == 1)
    Lh = 32 if odd else 64
    with contextlib.ExitStack() as st:
        sb = lambda nm, s, d: st.enter_context(nc.sbuf_tensor(uname(nm), s, d))
        wl = [sb("lrw%d" % i, [128, 8, 256], BF16) for i in range(2)]
        wblk = sb("lrblk", [128, 8, 4, 128], BF16)
        xr = sb("lrxr", [128, TB], F32)
        xc = sb("lrxc", [128, TB], F32)
        xcb = sb("lrxcb", [128, TB], BF16)
        gg = sb("lrgg", [128, TB], F32)
        T_ = [[sb("lrT%d_%d" % (d_, i), [128, TB], F32) for i in range(4)] for d_ in range(2)]
        hf = sb("lrhf", [128, TB], F32)
        hbk = sb("lrhb", [128, TB], F32)
        yst = [sb("lryst%d" % i, [128, TB], BF16) for i in range(2)]
        ps = [st.enter_context(nc.psum_tensor(uname("lrps%d" % i), [128, 512], F32)) for i in range(8)]
        S = Sched(nc)
        Bc = g.Bconst
        Bwl = [Buf(), Buf()]
        Bblk, Bxr, Bxc, Bxcb, Bgg, Bhf, Bhb = Buf(), Buf(), Buf(), Buf(), Buf(), Buf(), Buf()
        BT_ = [[Buf() for _ in range(4)] for _ in range(2)]
        Byst = [Buf(), Buf()]
        Bps = [Buf() for _ in range(8)]
        S.pool(lambda e: e.memset(wblk[:], 0.0), (), [Bblk])
        for d in range(2):
            for t, wsrc in enumerate((g.lru_w_a, g.lru_w_x)):
                for h in range(2):
                    src = wsrc[l, d].rearrange("(j h) k c -> h k j c", h=2)[h]
                    S.dma(POOL, wblk[h * 64:(h + 1) * 64, :, d * 2 + t, h * 64:(h + 1) * 64], src, (), [Bblk])
        wv = g.w_in[l].rearrange("(k p) n -> p k n", p=128)
        pk = 0
        for j in range(8):
            wi = j % 2
            S.dma(POOL, wl[wi][:, :, 0:128], wv[:, :, OFF_LX + j * 128:OFF_LX + (j + 1) * 128], (), [Bwl[wi]])
            S.dma(POOL, wl[wi][:, :, 128:256], wv[:, :, OFF_LG + j * 128:OFF_LG + (j + 1) * 128], (), [Bwl[wi]])
            for (s0, n) in SEGS:
                px, Bpx = ps[pk % 4], Bps[pk % 4]
                pg, Bpg = ps[(pk + 1) % 4], Bps[(pk + 1) % 4]
                pk += 2
                for kc in range(8):
                    MM(S, px[:, 0:n], wl[wi][:, kc, 0:128], U[:, kc, s0:s0 + n], kc == 0, kc == 7, [Bwl[wi], BU], [Bpx])
                for kc in range(8):
                    MM(S, pg[:, 0:n], wl[wi][:, kc, 128:256], U[:, kc, s0:s0 + n], kc == 0, kc == 7, [Bwl[wi], BU], [Bpg])
                COPY(S, DVE, xr[:, s0:s0 + n], px[:, 0:n], [Bpx], [Bxr])
                ACTV(S, gg[:, s0:s0 + n], pg[:, 0:n], AF.Gelu_apprx_tanh, [Bpg], [Bgg])
            cw = lambda k: cf(g, "lcw", (l * 4 + k) * 8 + j)
            ACTV(S, xc[:], xr[:], AF.Identity, [Bxr, Bc], [Bxc], bias=cf(g, "lcb", l * 8 + j), scale=cw(2))
            for (o0, ln_, nl) in ((0, 256, 1), (CTX, Lh, SEQ // Lh)):
                xv = xr[:, o0:o0 + ln_ * nl].rearrange("p (a b) -> p a b", b=ln_)
                ov = xc[:, o0:o0 + ln_ * nl].rearrange("p (a b) -> p a b", b=ln_)
                STT(S, ov[:, :, 2:ln_], xv[:, :, 0:ln_ - 2], cw(0), ov[:, :, 2:ln_], ALU.mult, ALU.add, [Bxr, Bxc, Bc], [Bxc])
                STT(S, ov[:, :, 1:ln_], xv[:, :, 0:ln_ - 1], cw(1), ov[:, :, 1:ln_], ALU.mult, ALU.add, [Bxr, Bxc, Bc], [Bxc])
                STT(S, ov[:, :, 0:ln_ - 1], xv[:, :, 1:ln_], cw(3), ov[:, :, 0:ln_ - 1], ALU.mult, ALU.add, [Bxr, Bxc, Bc], [Bxc])
            COPY(S, ACT, xcb[:], xc[:], [Bxc], [Bxcb])
            def lru_dir(d):
                T = T_[d]
                BT = BT_[d]
                ci = (l * 2 + d) * 8 + j
                pk_ = 0
                for (s0, n) in SEGS:
                    pr, Bpr = ps[4 + 2 * d], Bps[4 + 2 * d]
                    pi_, Bpi = ps[5 + 2 * d], Bps[5 + 2 * d]
                    MM(S, pr[:, 0:n], wblk[:, j, d * 2 + 0, :], xcb[:, s0:s0 + n], True, True, [Bblk, Bxcb], [Bpr])
                    yield
                    MM(S, pi_[:, 0:n], wblk[:, j, d * 2 + 1, :], xcb[:, s0:s0 + n], True, True, [Bblk, Bxcb], [Bpi])
                    yield
                    ACTV(S, T[0][:, s0:s0 + n], pr[:, 0:n], AF.Sigmoid, [Bpr, Bc], [BT[0]], bias=cf(g, "lba", ci))
                    yield
                    ACTV(S, T[1][:, s0:s0 + n], pi_[:, 0:n], AF.Sigmoid, [Bpi, Bc], [BT[1]], bias=cf(g, "lbx", ci))
                    yield
                TT(S, POOL, T[1][:], T[1][:], xc[:], ALU.mult, [BT[1], Bxc], [BT[1]])
                yield
                ACTV(S, T[2][:], T[0][:], AF.Exp, [BT[0], Bc], [BT[2]], scale=cf(g, "lsp8", ci))
                yield
                TSC(S, DVE, T[3][:], T[0][:], cf(g, "lsp16", ci), None, ALU.mult, None, [BT[0], Bc], [BT[3]])
                yield
                TSC(S, DVE, T[0][:], T[3][:], 1.0 / 24, 1.0 / 6, ALU.mult, ALU.add, [BT[3]], [BT[0]])
                yield
                TT(S, DVE, T[0][:], T[0][:], T[3][:], ALU.mult, [BT[0], BT[3]], [BT[0]])
                yield
                for cst in (0.5, 1.0):
                    STT(S, T[0][:], T[0][:], cst, T[3][:], ALU.add, ALU.mult, [BT[0], BT[3]], [BT[0]])
                    yield
                ACTV(S, T[0][:], T[0][:], AF.Sqrt, [BT[0]], [BT[0]], scale=-1.0)
                yield
                TT(S, DVE, T[1][:], T[1][:], T[0][:], ALU.mult, [BT[1], BT[0]], [BT[1]])
                yield
                if d == 0:
                    S.dve(lambda e: e.tensor_tensor_scan(out=hf[:], data0=T[2][:], data1=T[1][:], initial=0.0,
                                                         op0=ALU.mult, op1=ALU.add), [BT[2], BT[1]], [Bhf])
                    yield
                else:
                    S.dve(lambda e: e.tensor_tensor_scan(out=hbk[:, 0:CTX][:, ::-1], data0=T[2][:, 0:CTX][:, ::-1],
                                                         data1=T[1][:, 0:CTX][:, ::-1], initial=0.0,
                                                         op0=ALU.mult, op1=ALU.add), [BT[2], BT[1]], [Bhb])
                    yield
                    S.dve(lambda e: e.tensor_tensor_scan(out=hbk[:, CTX:TB][:, ::-1], data0=T[2][:, CTX:TB][:, ::-1],
                                                         data1=T[1][:, CTX:TB][:, ::-1], initial=hbk[:, 0:1],
                                                         op0=ALU.mult, op1=ALU.add), [BT[2], BT[1], Bhb], [Bhb])
                    yield

            run_interleaved([lru_dir(0), lru_dir(1)])
            yi = j % 2
            TT(S, DVE, hf[:], hf[:], hbk[:], ALU.add, [Bhf, Bhb], [Bhf])
            TT(S, DVE, yst[yi][:, 0:CTX], hf[:, 0:CTX], gg[:, 0:CTX], ALU.mult, [Bhf, Bgg], [Byst[yi]])
            if odd:
                ov = yst[yi][:, CTX:TB].rearrange("p (r c) -> p c r", c=64)
                i0 = hf[:, CTX:TB].rearrange("p (c r) -> p c r", r=32)
                i1 = gg[:, CTX:TB].rearrange("p (c r) -> p c r", r=32)
                TT(S, DVE, ov, i0, i1, ALU.mult, [Bhf, Bgg], [Byst[yi]])
            else:
                TT(S, DVE, yst[yi][:, CTX:TB], hf[:, CTX:TB], gg[:, CTX:TB], ALU.mult, [Bhf, Bgg], [Byst[yi]])
            S.dma(SP, g.Y[1][:, j, b * TB:(b + 1) * TB], yst[yi][:], [Byst[yi]], ())
        S.emit()
    nc.all_engine_barrier()


def phase_mix(nc, g, l, which):
    with contextlib.ExitStack() as st:
        U = st.enter_context(nc.sbuf_tensor(uname("Umix"), [128, 8, TB], BF16))
        for b in range(NB):
            BU = Buf("U")
            phase_mixpro(nc, g, l, b, U, BU)
            BU = Buf("U")
            if "ssd" in which:
                phase_ssd(nc, g, l, b, U, BU)
            if "lru" in which:
                phase_lru(nc, g, l, b, U, BU)
            if "gla" in which:
                phase_gla(nc, g, l, b, U, BU)


FWD_CHUNKS = list(range(18))
REV_CHUNKS = [1, 0] + list(range(17, 1, -1))


def run_interleaved(gens):
    gens = list(gens)
    while gens:
        for g_ in list(gens):
            try:
                next(g_)
            except StopIteration:
                gens.remove(g_)


def conv_block(S, g, raw, Braw, out, Bout, wfn, bias_ap, Lh):
    Bc = g.Bconst
    ACTV(S, out[:], raw[:], AF.Identity, [Braw, Bc], [Bout], bias=bias_ap, scale=wfn(2))
    for (o0, ln_, nl) in ((0, 256, 1), (CTX, Lh, SEQ // Lh)):
        xv = raw[:, o0:o0 + ln_ * nl].rearrange("p (a b) -> p a b", b=ln_)
        ov = out[:, o0:o0 + ln_ * nl].rearrange("p (a b) -> p a b", b=ln_)
        STT(S, ov[:, :, 2:ln_], xv[:, :, 0:ln_ - 2], wfn(0), ov[:, :, 2:ln_], ALU.mult, ALU.add, [Braw, Bout, Bc], [Bout])
        STT(S, ov[:, :, 1:ln_], xv[:, :, 0:ln_ - 1], wfn(1), ov[:, :, 1:ln_], ALU.mult, ALU.add, [Braw, Bout, Bc], [Bout])
        STT(S, ov[:, :, 0:ln_ - 1], xv[:, :, 1:ln_], wfn(3), ov[:, :, 0:ln_ - 1], ALU.mult, ALU.add, [Braw, Bout, Bc], [Bout])


def phase_ssd(nc, g, l, b, U, BU):
    odd = (l % 2 == 1)
    Lh = 32 if odd else 64
    M = g.masks
    with contextlib.ExitStack() as st:
        sb = lambda nm, s, d: st.enter_context(nc.sbuf_tensor(uname(nm), s, d))
        wg = [sb("sdw0", [128, 8, 784], BF16)] * 2
        szT = sb("sdsz", [128, 2, TB], BF16)
        craw = sb("sdcraw", [128, TB], F32)
        ctmp = sb("sdctmp", [128, TB], F32)
        xsT = sb("sdxsT", [128, 2, TB], F32)
        BTf = sb("sdBTf", [128, TB], F32)
        BT = sb("sdBT", [128, TB], BF16)
        CT = sb("sdCT", [128, TB], BF16)
        dtv = sb("sddt", [128, 144], F32)
        av = sb("sda", [128, 144], F32)
        acs = sb("sdacs", [128, 144], F32)
        tot = sb("sdtot", [128, 144], F32)
        tm = sb("sdtm", [128, 144], F32)
        fs = sb("sdfs", [128, 144], F32)
        te = sb("sdte", [128, 144], F32)
        cd = sb("sdcd", [128, 144], F32)
        yacc = sb("sdyacc", [128, 18, 256], F32)
        Sst = sb("sdS", [128, 2, 256], F32)
        Sbf = sb("sdSb", [128, 2, 256], BF16)
        yst = [sb("sdyst0", [128, 2, TB], BF16)] * 2
        xs_all = sb("sdxsall", [128, 18, 256], F32)
        B_all = sb("sdBall", [128, 18, 128], BF16)
        xsd_ = [sb("sdxsd%d" % i, [128, 256], BF16) for i in range(2)]
        xw_ = [sb("sdxw%d" % i, [128, 256], BF16) for i in range(2)]
        scm_ = [sb("sdscm%d" % i, [128, 128], BF16) for i in range(2)]
        rhsA_ = [sb("sdrhsA%d" % i, [128, 512], F32) for i in range(2)]
        Eb_ = [sb("sdE%d" % i, [128, 512], BF16) for i in range(2)]
        MT_ = [sb("sdMT%d" % i, [128, 512], BF16) for i in range(2)]
        t1_ = [sb("sdt1%d" % i, [128, 256], F32) for i in range(2)]
        t2 = sb("sdt2", [128, 256], F32)
        ps = [st.enter_context(nc.psum_tensor(uname("sdps%d" % i), [128, 512], F32)) for i in range(8)]
        S = Sched(nc)
        Bc = g.Bconst
        Bwg = [Buf()] * 2
        (Bsz, Bcraw, Bctmp, BxsT, BBTf, BBT, BCT, Bdt, Ba, Bacs, Btot, Btm, Bfs, Bte, Bcd, Byacc, BS, BSb,
         Bxt, BBtok, Bxsd, Bxw, Bscm, BrhsA, BE, BMT, Bt1, Bt2) = [Buf() for _ in range(28)]
        Byst = [Buf()] * 2
        Bxall, BBall = Buf(), Buf()
        Byacc = [Buf() for _ in range(18)]
        Bxsd_, Bxw_, Bscm_, BrhsA_, BE_, BMT_, Bt1_, BS_, BSb_ = [[Buf(), Buf()] for _ in range(9)]
        Bps = [Buf() for _ in range(8)]
        wv = g.w_in[l].rearrange("(k p) n -> p k n", p=128)
        v4 = lambda t: t[:].rearrange("p (c d h) -> p c d h", d=2, h=4)
        pk = 0
        for gq in range(4):
            wi = gq % 2
            for (d0, c0, cn) in ((0, OFF_Z + 256 * gq, 256), (256, OFF_XBC + 256 * gq, 256),
                                 (512, OFF_XBC + 1024 + 128 * gq, 128), (640, OFF_XBC + 1536 + 128 * gq, 128),
                                 (768, OFF_DT + 4 * gq, 4), (772, OFF_DT + 16 + 4 * gq, 4)):
                S.dma(POOL, wg[wi][:, :, d0:d0 + cn], wv[:, :, c0:c0 + cn], (), [Bwg[wi]])

            def inproj(c0, evac):
                nonlocal pk
                for (s0, n) in SEGS:
                    p_, Bp = ps[pk % 2], Bps[pk % 2]
                    pk += 1
                    for kc in range(8):
                        MM(S, p_[:, 0:n], wg[wi][:, kc, c0:c0 + 128], U[:, kc, s0:s0 + n], kc == 0, kc == 7, [Bwg[wi], BU], [Bp])
                    evac(p_, Bp, s0, n)

            for i in range(2):
                inproj(i * 128, lambda p_, Bp, s0, n, i=i: ACTV(S, szT[:, i, s0:s0 + n], p_[:, 0:n], AF.Silu, [Bp], [Bsz]))
            for ci in range(4):
                inproj(256 + ci * 128, lambda p_, Bp, s0, n: COPY(S, DVE, craw[:, s0:s0 + n], p_[:, 0:n], [Bp], [Bcraw]))
                ch16 = (2 * gq + ci) if ci < 2 else (8 + gq if ci == 2 else 12 + gq)
                conv_block(S, g, craw, Bcraw, ctmp, Bctmp, lambda k, ch16=ch16: cf(g, "scw", (l * 4 + k) * 16 + ch16),
                           cf(g, "scb", l * 16 + ch16), Lh)
                if ci < 2:
                    ACTV(S, xsT[:, ci, :], ctmp[:], AF.Silu, [Bctmp], [BxsT])
                elif ci == 2:
                    ACTV(S, BTf[:], ctmp[:], AF.Silu, [Bctmp], [BBTf])
                    COPY(S, DVE, BT[:], BTf[:], [BBTf], [BBT])
                else:
                    ACTV(S, CT[:], ctmp[:], AF.Silu, [Bctmp], [BCT])
            pdt, Bpdt = ps[2], Bps[2]
            for c in range(18):
                for kc in range(8):
                    MM(S, pdt[:, c * 8:(c + 1) * 8], U[:, kc, c * 128:(c + 1) * 128], wg[wi][:, kc, 768:776], kc == 0, kc == 7,
                       [Bwg[wi], BU], [Bpdt])
            rb = lambda t: bc(t[:, l * 32:(l + 1) * 32].rearrange("p (d h) -> p d h", d=2)[:, :, 4 * gq:4 * gq + 4].unsqueeze(1), [128, 18, 2, 4])
            TT(S, DVE, v4(dtv), pdt[:, 0:144].rearrange("p (c d h) -> p c d h", d=2, h=4), rb(g.rb_dtb), ALU.add, [Bpdt, Bc], [Bdt])
            ACTV(S, dtv[:], dtv[:], AF.Exp, [Bdt], [Bdt])
            ACTV(S, dtv[:], dtv[:], AF.Ln, [Bdt], [Bdt], bias=1.0)
            TT(S, DVE, v4(av), v4(dtv), rb(g.rb_A), ALU.mult, [Bdt, Bc], [Ba])
            MM(S, ps[3][:, 0:144], M["LE"][:], av[:], True, True, [Ba, Bc], [Bps[3]])
            MM(S, ps[4][:, 0:144], M["ONES"][:], av[:], True, True, [Ba, Bc], [Bps[4]])
            COPY(S, DVE, acs[:], ps[3][:, 0:144], [Bps[3]], [Bacs])
            COPY(S, DVE, tot[:], ps[4][:, 0:144], [Bps[4]], [Btot])
            ACTV(S, cd[:], tot[:], AF.Exp, [Btot], [Bcd])
            ACTV(S, v4(fs)[:, :, 0, :], v4(acs)[:, :, 0, :], AF.Exp, [Bacs], [Bfs])
            TT(S, DVE, v4(tm)[:, :, 0, :], v4(tot)[:, :, 0, :], v4(acs)[:, :, 0, :], ALU.subtract, [Btot, Bacs], [Btm])
            TT(S, DVE, v4(tm)[:, :, 1, :], v4(acs)[:, :, 1, :], v4(av)[:, :, 1, :], ALU.subtract, [Bacs, Ba], [Btm])
            ACTV(S, te[:], tm[:], AF.Exp, [Btm], [Bte])
            TT(S, DVE, v4(tm)[:, :, 1, :], v4(tot)[:, :, 1, :], v4(tm)[:, :, 1, :], ALU.subtract, [Btot, Btm], [Btm])
            ACTV(S, v4(fs)[:, :, 1, :], v4(tm)[:, :, 1, :], AF.Exp, [Btm], [Bfs])
            yi = 0
            h3 = lambda t: t.rearrange("p (h q) -> p h q", h=4)
            for c in range(18):
                cs = slice(c * 128, (c + 1) * 128)
                ptr, Bptr = ps[2 + c % 2], Bps[2 + c % 2]
                for i in range(2):
                    TR(S, ptr[:, i * 128:(i + 1) * 128], xsT[:, i, cs], M["ID"][:], [BxsT, Bc], [Bptr])
                TR(S, ptr[:, 256:384], BTf[:, cs], M["ID"][:], [BBTf, Bc], [Bptr])
                COPY(S, ACT, xs_all[:, c, :], ptr[:, 0:256], [Bptr], [Bxall])
                COPY(S, ACT, B_all[:, c, :], ptr[:, 256:384], [Bptr], [BBall])
            seen = set()

            def ssd_iter(d, c):
                M1 = M["GT"] if d == 0 else M["LT"]
                M2 = M["LE"] if d == 0 else M["GE"]
                cs = slice(c * 128, (c + 1) * 128)
                xsd, xw, scm, rhsA, Eb, MT, t1 = xsd_[d], xw_[d], scm_[d], rhsA_[d], Eb_[d], MT_[d], t1_[d]
                Bxsd, Bxw, Bscm, BrhsA, BE, BMT, Bt1 = Bxsd_[d], Bxw_[d], Bscm_[d], BrhsA_[d], BE_[d], BMT_[d], Bt1_[d]
                pA, BpA = ps[2 + 3 * d], Bps[2 + 3 * d]
                pS, BpS = ps[3 + 3 * d], Bps[3 + 3 * d]
                pY, BpY = ps[4 + 3 * d], Bps[4 + 3 * d]
                xs_tok = xs_all[:, c, :]
                dt_c = bc(v4(dtv)[:, c, d, :].unsqueeze(2), [128, 4, 64])
                te_c = bc(v4(te)[:, c, d, :].unsqueeze(2), [128, 4, 64])
                fs_c = bc(v4(fs)[:, c, d, :].unsqueeze(2), [128, 4, 64])
                cd_c = bc(v4(cd)[:, c, d, :].unsqueeze(2), [128, 4, 64])
                TT(S, DVE, h3(xsd[:]), h3(xs_tok), dt_c, ALU.mult, [Bxall, Bdt], [Bxsd])
                yield
                TT(S, DVE, h3(xw[:]), h3(xsd[:]), te_c, ALU.mult, [Bxsd, Bte], [Bxw])
                yield
                MM(S, pA[:, 0:128], BT[:, cs], CT[:, cs], True, True, [BBT, BCT], [BpA])
                yield
                TT(S, DVE, scm[:], pA[:, 0:128], (M["LE"] if d == 0 else M["GE"])[:], ALU.mult, [BpA, Bc], [Bscm])
                yield
                TT(S, POOL, h3(rhsA[:]), bc(M2[:].unsqueeze(1), [128, 4, 128]), bc(v4(av)[:, c, d, :].unsqueeze(2), [128, 4, 128]),
                   ALU.mult, [Ba, Bc], [BrhsA])
                yield
                MM(S, pS[:], M1[:], rhsA[:], True, True, [BrhsA, Bc], [BpS])
                yield
                ACTV(S, Eb[:], pS[:], AF.Exp, [BpS], [BE])
                yield
                TT(S, DVE, h3(MT[:]), h3(Eb[:]), bc(scm[:].unsqueeze(1), [128, 4, 128]), ALU.mult, [BE, Bscm], [BMT])
                yield
                for hh in range(4):
                    MM(S, pY[:, hh * 64:(hh + 1) * 64], MT[:, hh * 128:(hh + 1) * 128], xsd[:, hh * 64:(hh + 1) * 64], True, True,
                       [BMT, Bxsd], [BpY])
                MM(S, pY[:, 256:512], CT[:, cs], Sbf[:, d, :], True, True, [BCT, BSb_[d]], [BpY])
                yield
                TT(S, DVE, h3(t1[:]), h3(pY[:, 256:512]), fs_c, ALU.mult, [BpY, Bfs], [Bt1])
                yield
                if c not in seen:
                    seen.add(c)
                    TT(S, DVE, yacc[:, c, :], pY[:, 0:256], t1[:], ALU.add, [BpY, Bt1], [Byacc[c]])
                else:
                    TT(S, DVE, t1[:], pY[:, 0:256], t1[:], ALU.add, [BpY, Bt1], [Bt1])
                    TT(S, POOL, h3(t2[:]), h3(xs_tok),
                       bc(g.rb_Ds[:, l * 16 + 4 * gq:l * 16 + 4 * gq + 4].unsqueeze(2), [128, 4, 64]), ALU.mult, [Bxall, Bc], [Bt2])
                    TT(S, POOL, t1[:], t1[:], t2[:], ALU.add, [Bt1, Bt2], [Bt1])
                    TT(S, DVE, yacc[:, c, :], yacc[:, c, :], t1[:], ALU.add, [Byacc[c], Bt1], [Byacc[c]])
                    pf, Bpf = ps[1], Bps[1]
                    for i in range(2):
                        TR(S, pf[:, i * 128:(i + 1) * 128], yacc[:, c, i * 128:(i + 1) * 128], M["ID"][:], [Byacc[c], Bc], [Bpf])
                    for i in range(2):
                        if odd and c >= 2:
                            cc0 = (c - 2) * 4
                            ov = yst[yi][:, i, CTX:TB].rearrange("p (r c) -> p c r", c=64)[:, cc0:cc0 + 4, :]
                            i0 = pf[:, i * 128:(i + 1) * 128].rearrange("p (c r) -> p c r", r=32)
                            i1 = szT[:, i, cs].rearrange("p (c r) -> p c r", r=32)
                        else:
                            ov, i0, i1 = yst[yi][:, i, cs], pf[:, i * 128:(i + 1) * 128], szT[:, i, cs]
                        TT(S, DVE, ov, i0, i1, ALU.mult, [Bpf, Bsz], [Byst[yi]])
                MM(S, pA[:, 128:384], B_all[:, c, :], xw[:], True, True, [BBall, Bxw], [BpA])
                yield
                TT(S, POOL, h3(Sst[:, d, :]), h3(Sst[:, d, :]), cd_c, ALU.mult, [BS_[d], Bcd], [BS_[d]])
                yield
                TT(S, DVE, Sst[:, d, :], Sst[:, d, :], pA[:, 128:384], ALU.add, [BS_[d], BpA], [BS_[d]])
                yield
                COPY(S, ACT, Sbf[:, d, :], Sst[:, d, :], [BS_[d]], [BSb_[d]])
                yield

            for d in range(2):
                S.pool(lambda e, d=d: e.memset(Sst[:, d, :], 0.0), (), [BS_[d]])
                S.pool(lambda e, d=d: e.memset(Sbf[:, d, :], 0.0), (), [BSb_[d]])
            def ssd_dir(d):
                for c in (FWD_CHUNKS if d == 0 else REV_CHUNKS):
                    yield from ssd_iter(d, c)

            run_interleaved([ssd_dir(0), ssd_dir(1)])
            S.dma(SP, g.Y[0][:, 2 * gq:2 * gq + 2, b * TB:(b + 1) * TB], yst[yi][:], [Byst[yi]], ())
        S.emit()
    nc.all_engine_barrier()


def phase_gla(nc, g, l, b, U, BU):
    odd = (l % 2 == 1)
    M = g.masks
    QS = 128.0 ** -0.5
    with contextlib.ExitStack() as st:
        sb = lambda nm, s, d: st.enter_context(nc.sbuf_tensor(uname(nm), s, d))
        wq = sb("glw", [128, 8, 896], BF16)
        WG = sb("glWG", [128, 256], BF16)
        bgb = sb("glbg", [128, 256], F32)
        gng = sb("glgng", [128, 256], F32)
        qT = sb("glqT", [128, TB], F32)
        kT = sb("glkT", [128, TB], F32)
        sgT = sb("glsg", [128, 2, TB], BF16)
        alrT = sb("glalr", [128, TB], BF16)
        v_tok = sb("glv", [128, 18, 256], BF16)
        k_tok = sb("glk", [128, 18, 128], F32)
        lsp = sb("gllsp", [128, 18, 256], F32)
        oacc = sb("gloacc", [128, 18, 256], F32)
        yst = [sb("glyst0", [128, 2, TB], BF16)] * 2
        eq_ = [sb("gleq%d" % i, [128, 128], F32) for i in range(2)]
        ek_ = [sb("glek%d" % i, [128, 128], F32) for i in range(2)]
        qin_ = [sb("glqin%d" % i, [128, 128], BF16) for i in range(2)]
        kin_ = [sb("glkin%d" % i, [128, 128], BF16) for i in range(2)]
        er_ = [sb("gler%d" % i, [128, 128], F32) for i in range(2)]
        kst_ = [[sb("glkst%d_%d" % (d_, i), [128, 128], BF16) for i in range(2)] for d_ in range(2)]
        qinh_ = [[sb("glqinh%d_%d" % (d_, i), [128, 128], BF16) for i in range(2)] for d_ in range(2)]
        attT_ = [sb("glatt%d" % i, [128, 128], BF16) for i in range(2)]
        Sst_ = [sb("glS%d" % i, [128, 256], F32) for i in range(2)]
        Sb0_ = [sb("glSb0%d" % i, [128, 256], BF16) for i in range(2)]
        Sb1_ = [sb("glSb1%d" % i, [128, 256], BF16) for i in range(2)]
        osq = sb("glosq", [128, 256], F32)
        ssq = sb("glssq", [128, 1], F32)
        on = sb("glon", [128, 256], F32)
        ps = [st.enter_context(nc.psum_tensor(uname("glps%d" % i), [128, 512], F32)) for i in range(8)]
        S = Sched(nc)
        Bc = g.Bconst
        (Bwq, BWG, Bbg, BqT, BkT, Bsg, Balr, Bv, Bk, Blsp, Boacc, Beq, Bek, Bqin, Bkin, Ber, Bkst, Batt, BS, BSb0, BSb1,
         Bosq, Bssq, Bon) = [Buf() for _ in range(24)]
        Byst = [Buf()] * 2
        Boacc = [Buf() for _ in range(18)]
        Beq_, Bek_, Bqin_, Bkin_, Ber_, Bkst_, Bqinh_, Batt_, BS_, BSb0_, BSb1_ = [[Buf(), Buf()] for _ in range(11)]
        Bps = [Buf() for _ in range(8)]
        wv = g.w_in[l].rearrange("(k p) n -> p k n", p=128)
        pk = 0
        Bgng = Buf()
        S.dma(SP, gng[:], g.gla_norm_g[l].partition_broadcast(128), (), [Bgng])
        for d_ in range(2):
            for i_ in range(2):
                S.pool(lambda e, i_=i_, d_=d_: e.memset(qinh_[d_][i_][:], 0.0), (), [Bqinh_[d_]])
        for hd in range(4):
            S.pool(lambda e: e.memset(wq[:, :, 768:896], 0.0), (), [Bwq])
            for (d0, c0, cn) in ((0, OFF_Q + 128 * hd, 128), (128, OFF_K + 128 * hd, 128), (256, OFF_V + 256 * hd, 256),
                                 (512, OFF_G + 256 * hd, 256), (768, OFF_ALR, 16), (800, OFF_ALR + 16, 16)):
                S.dma(POOL, wq[:, :, d0:d0 + cn], wv[:, :, c0:c0 + cn], (), [Bwq])
            import os as _os
            S.pool(lambda e: e.memset(WG[:], 0.0), (), [BWG])
            for d in range(0 if _os.environ.get('GLA_NOWG') else 2):
                S.dma(POOL, WG[32 * d:32 * d + 16, d * 128:(d + 1) * 128], g.gla_w_gate[l, d, :, hd * 128:(hd + 1) * 128], (), [BWG])
                S.dma(SP, bgb[:, d * 128:(d + 1) * 128], g.gla_b_gate[l, d, hd * 128:(hd + 1) * 128].partition_broadcast(128), (), [Bbg])

            _ninp = [0]

            def inproj(c0, m, evac):
                nonlocal pk
                _ninp[0] += 1
                if _ninp[0] > int(_os.environ.get('GLA_INP', '9')):
                    return
                for (s0, n) in SEGS:
                    p_, Bp = ps[pk % 2], Bps[pk % 2]
                    pk += 1
                    for kc in range(8):
                        MM(S, p_[0:m, 0:n], wq[:, kc, c0:c0 + m], U[:, kc, s0:s0 + n], kc == 0, kc == 7, [Bwq, BU], [Bp])
                    evac(p_, Bp, s0, n)

            if float(_os.environ.get('GLA_DBG', '9')) == 0:
                continue
            inproj(0, 128, lambda p_, Bp, s0, n: COPY(S, DVE, qT[:, s0:s0 + n], p_[:, 0:n], [Bp], [BqT]))
            inproj(128, 128, lambda p_, Bp, s0, n: COPY(S, DVE, kT[:, s0:s0 + n], p_[:, 0:n], [Bp], [BkT]))
            for i in range(2):
                inproj(512 + i * 128, 128, lambda p_, Bp, s0, n, i=i: ACTV(S, sgT[:, i, s0:s0 + n], p_[:, 0:n], AF.Silu, [Bp], [Bsg]))
            inproj(768, 128, lambda p_, Bp, s0, n: COPY(S, DVE, alrT[:, s0:s0 + n], p_[:, 0:n], [Bp], [Balr]))
            _l2 = float(_os.environ.get('GLA_DBG', '9'))
            for c in range(18 if _l2 > 0.5 else 0):
                cs = slice(c * 128, (c + 1) * 128)
                p_, Bp = ps[pk % 2], Bps[pk % 2]
                pk += 1
                for kc in range(8):
                    MM(S, p_[:, 0:256], U[:, kc, cs], wq[:, kc, 256:512], kc == 0, kc == 7, [Bwq, BU], [Bp])
                for kc in range(8):
                    MM(S, ps[7][:, 0:128], U[:, kc, cs], wq[:, kc, 128:256], kc == 0, kc == 7, [Bwq, BU], [Bps[7]])
                COPY(S, ACT, v_tok[:, c, :], p_[:, 0:256], [Bp], [Bv])
                COPY(S, DVE, k_tok[:, c, :], ps[7][:, 0:128], [Bps[7]], [Bk])
                if _l2 > 0.7:
                    MM(S, ps[2][:, 0:256], alrT[:, cs], WG[:, :], True, True, [Balr, BWG], [Bps[2]])
                    TT(S, DVE, lsp[:, c, :], ps[2][:, 0:256], bgb[:], ALU.add, [Bps[2], Bbg], [Blsp])
            if _l2 > 0.8:
                ACTV(S, lsp[:], lsp[:], AF.Exp, [Blsp], [Blsp], scale=-1.0)
                ACTV(S, lsp[:], lsp[:], AF.Ln, [Blsp], [Blsp], bias=1.0)
            yi = 0
            _lvl = 9
            seen = set()

            def gla_iter(d, c):
                CM = M["LE64"] if d == 0 else M["GE64"]
                RM = M["GT64"] if d == 0 else M["LT64"]
                blocks = (0, 1) if d == 0 else (1, 0)
                eq, ek, er, qin, kin, kst, qinh, attT = eq_[d], ek_[d], er_[d], qin_[d], kin_[d], kst_[d], qinh_[d], attT_[d]
                Beq, Bek, Ber, Bqin, Bkin, Bkst, Bqinh, Batt = Beq_[d], Bek_[d], Ber_[d], Bqin_[d], Bkin_[d], Bkst_[d], Bqinh_[d], Batt_[d]
                Sst, Sb0, Sb1, BS, BSb0, BSb1 = Sst_[d], Sb0_[d], Sb1_[d], BS_[d], BSb0_[d], BSb1_[d]
                pP, BpP = ps[2 + 3 * d], Bps[2 + 3 * d]
                pA, BpA = ps[3 + 3 * d], Bps[3 + 3 * d]
                pO, BpO = ps[4 + 3 * d], Bps[4 + 3 * d]
                cs = slice(c * 128, (c + 1) * 128)
                ld = lsp[:, c, d * 128:(d + 1) * 128]
                MM(S, pP[:, 0:128], ld, CM[:], True, True, [Blsp, Bc], [BpP])
                yield
                MM(S, pP[:, 128:256], RM[:], ld, True, True, [Blsp, Bc], [BpP])
                yield
                ACTV(S, eq[:], pP[:, 0:128], AF.Exp, [BpP], [Beq], scale=-1.0 / 16)
                yield
                ACTV(S, ek[:], pP[:, 0:128], AF.Exp, [BpP], [Bek], scale=1.0 / 16)
                yield
                ACTV(S, er[:], pP[:, 128:256], AF.Exp, [BpP], [Ber], scale=-1.0 / 16)
                yield
                STT(S, qin[:], qT[:, cs], QS, eq[:], ALU.mult, ALU.mult, [BqT, Beq], [Bqin])
                yield
                TT(S, DVE, kin[:], kT[:, cs], ek[:], ALU.mult, [BkT, Bek], [Bkin])
                yield
                for bi_ in range(2):
                    STT(S, kst[bi_][:], k_tok[:, c, :], M["BD"][:, 64 * bi_:64 * bi_ + 1], er[:], ALU.mult, ALU.mult, [Bk, Ber, Bc], [Bkst])
                    hs_ = slice(64 * bi_, 64 * bi_ + 64)
                    COPY(S, POOL, qinh[bi_][:, hs_], qin[:, hs_], [Bqin], [Bqinh])
                MM(S, pA[:, 0:128], kin[:], qin[:], True, True, [Bkin, Bqin], [BpA])
                yield
                TT(S, DVE, attT[:], pA[:, 0:128], CM[:], ALU.mult, [BpA, Bc], [Batt])
                yield
                MM(S, pO[:, 0:256], attT[:], v_tok[:, c, :], True, False, [Batt, Bv], [BpO])
                yield
                for bi, blk in enumerate(blocks):
                    Sb, BSb = (Sb0, BSb0) if bi == 0 else (Sb1, BSb1)
                    MM(S, pO[:, 0:256], qinh[blk][:], Sb[:], False, bi == 1, [Bqinh, BSb], [BpO])
                    MM(S, pA[:, 128:384], kst[blk][:], v_tok[:, c, :], True, True, [Bkst, Bv], [BpA])
                    ecol = (blk * 64 + 63) if d == 0 else (blk * 64)
                    STT(S, Sst[:], Sst[:], eq[:, ecol:ecol + 1], pA[:, 128:384], ALU.mult, ALU.add, [BS, Beq, BpA], [BS])
                    if bi == 0:
                        COPY(S, ACT, Sb1[:], Sst[:], [BS], [BSb1])
                    else:
                        COPY(S, ACT, Sb0[:], Sst[:], [BS], [BSb0])
                if c not in seen:
                    seen.add(c)
                    COPY(S, ACT if d == 0 else DVE, oacc[:, c, :], pO[:, 0:256], [BpO], [Boacc[c]])
                else:
                    TT(S, DVE, oacc[:, c, :], oacc[:, c, :], pO[:, 0:256], ALU.add, [Boacc[c], BpO], [Boacc[c]])
                    ACTV(S, osq[:], oacc[:, c, :], AF.Square, [Boacc[c]], [Bosq])
                    S.dve(lambda e: e.reduce_sum(out=ssq[:], in_=osq[:], axis=AX.X), [Bosq], [Bssq])
                    TSC(S, DVE, ssq[:], ssq[:], 1.0 / 256, EPS, ALU.mult, ALU.add, [Bssq], [Bssq])
                    ACTV(S, ssq[:], ssq[:], AF.Sqrt, [Bssq], [Bssq])
                    S.dve(lambda e: e.reciprocal(out=ssq[:], in_=ssq[:]), [Bssq], [Bssq])
                    STT(S, on[:], oacc[:, c, :], ssq[:, 0:1], gng[:], ALU.mult, ALU.mult, [Boacc[c], Bssq, Bgng], [Bon])
                    pf, Bpf = ps[1], Bps[1]
                    for i in range(2):
                        TR(S, pf[:, i * 128:(i + 1) * 128], on[:, i * 128:(i + 1) * 128], M["ID"][:], [Bon, Bc], [Bpf])
                    for i in range(2):
                        if odd and c >= 2:
                            cc0 = (c - 2) * 4
                            ov = yst[yi][:, i, CTX:TB].rearrange("p (r c) -> p c r", c=64)[:, cc0:cc0 + 4, :]
                            i0 = pf[:, i * 128:(i + 1) * 128].rearrange("p (c r) -> p c r", r=32)
                            i1 = sgT[:, i, cs].rearrange("p (c r) -> p c r", r=32)
                        else:
                            ov, i0, i1 = yst[yi][:, i, cs], pf[:, i * 128:(i + 1) * 128], sgT[:, i, cs]
                        TT(S, DVE, ov, i0, i1, ALU.mult, [Bpf, Bsg], [Byst[yi]])

            for d in range(2):
                S.pool(lambda e, d=d: e.memset(Sst_[d][:], 0.0), (), [BS_[d]])
                S.pool(lambda e, d=d: e.memset(Sb0_[d][:], 0.0), (), [BSb0_[d]])
            def gla_dir(d):
                for c in (FWD_CHUNKS if d == 0 else REV_CHUNKS):
                    yield from gla_iter(d, c)

            run_interleaved([gla_dir(0), gla_dir(1)])
            if _lvl >= 3:
                S.dma(SP, g.Y[2][:, 2 * hd:2 * hd + 2, b * TB:(b + 1) * TB], yst[yi][:], [Byst[yi]], ())
        S.emit()
    nc.all_engine_barrier()


def build_program(stages=None):
    nc = bass.Bass("TRN2", target_bir_lowering=False)
    g = G()
    declare_io(nc, g)
    with contextlib.ExitStack() as st:
        init_gsync(nc, st)
        sb = lambda nm, s, d: st.enter_context(nc.sbuf_tensor(uname(nm), s, d))
        g.constf = sb("constf", [128, NCF], F32)
        g.modT = sb("modT", [128, NL * 9 * 8 * 4], F32)
        g.masks = {nm: sb("mask_" + nm, [128, 128], F32) for nm in
                   ("ONES", "LE", "GT", "LT", "GE", "ID", "BD", "LE64", "GT64", "LT64", "GE64")}
        g.onesb = sb("onesb", [128, 128], BF16)
        g.rb_dtb = sb("rb_dtb", [128, NL * 32], F32)
        g.rb_A = sb("rb_A", [128, NL * 32], F32)
        g.rb_D = sb("rb_D", [128, NL * 32], F32)
        g.rb_Ds = sb("rb_Ds", [128, NL * 16], F32)
        if stages is None:
            stages = ["const", "p0"]
            for l in range(NL):
                stages += [("ffn", l, 0), ("mix", l), ("merge", l), ("ffn", l, 1)]
            stages += ["final"]
        for sg in stages:
            if sg == "const":
                phase_const(nc, g)
            elif sg == "p0":
                phase_p0(nc, g)
            elif sg == "final":
                phase_final(nc, g)
            elif sg[0] == "ffn":
                phase_ffn(nc, g, sg[1], sg[2], skip_ctx=(sg[1] == NL - 1 and sg[2] == 1))
            elif sg[0] == "mix":
                phase_mix(nc, g, sg[1], sg[2] if len(sg) > 2 else ("ssd", "lru", "gla"))
            elif sg[0] == "merge":
                phase_merge(nc, g, sg[1], skip_ctx=(sg[1] == NL - 1))
    return nc


_NC_CACHE = {}


def kernel(**inputs):
    from concourse.bass_utils import run_bass_kernel_spmd
    if "nc" not in _NC_CACHE:
        _NC_CACHE["nc"] = build_program()
    nc = _NC_CACHE["nc"]
    ncores = 8
    in_maps = []
    for i in range(ncores):
        m = {}
        for nm in INPUT_NAMES:
            a = np.asarray(inputs[nm], dtype=np.float32)
            if nm in ("x", "c", "ctx"):
                a = a[i * NB:(i + 1) * NB]
            elif nm == "c_ctx":
                a = a.reshape(1, D)
            m[nm] = np.ascontiguousarray(a)
        in_maps.append(m)
    res = run_bass_kernel_spmd(nc, in_maps, core_ids=list(range(ncores)))
    return np.concatenate([np.asarray(r["out"]) for r in res.results], axis=0).astype(np.float32)
```

```python
import numpy as np
import concourse.bass as bass
import concourse.mybir as mybir
from concourse.ap import AP

F32 = mybir.dt.float32
BF16 = mybir.dt.bfloat16
AF = mybir.ActivationFunctionType
ALU = mybir.AluOpType
AX = mybir.AxisListType

PE, ACT, DVE, POOL, SP = "pe", "act", "dve", "pool", "sp"
COMPUTE = (PE, ACT, DVE, POOL)
NDMASEM = 6


class Buf:
    __slots__ = ("name", "last_w", "readers")

    def __init__(self, name=""):
        self.name = name
        self.last_w = None
        self.readers = {}


class Op:
    __slots__ = ("eng", "fn", "deps", "idx", "dma", "signal", "sigval", "sem", "semval", "tag")


class Sched:
    def __init__(self, nc):
        self.nc = nc
        self.ops = {e: [] for e in (PE, ACT, DVE, POOL, SP)}
        self.n_dma = {SP: 0, POOL: 0, ACT: 0}

    def add(self, eng, fn, reads=(), writes=(), dma=False, tag=None):
        op = Op()
        op.eng, op.fn, op.dma, op.signal, op.tag = eng, fn, dma, False, tag
        op.idx = len(self.ops[eng])
        deps = {}

        def dep(d, kind):
            if d is None or d is op:
                return
            if d.eng == eng and not d.dma:
                if eng == PE or eng == SP:
                    return
                if kind == "WAR":
                    return
            key = id(d) if d.dma else d.eng
            cur = deps.get(key)
            if cur is None or (not d.dma and d.idx > cur.idx):
                deps[key] = d

        for b in reads:
            dep(b.last_w, "RAW")
        for b in writes:
            dep(b.last_w, "WAW")
            for r in b.readers.values():
                if isinstance(r, list):
                    for rr in r:
                        dep(rr, "WAR")
                else:
                    dep(r, "WAR")
        for b in reads:
            if dma:
                b.readers.setdefault("dma", []).append(op)
            else:
                b.readers[eng] = op
        for b in writes:
            b.last_w = op
            b.readers = {}
        op.deps = list(deps.values())
        for d in op.deps:
            d.signal = True
        self.ops[eng].append(op)
        return op

    def pe(self, fn, reads=(), writes=()):
        return self.add(PE, fn, reads, writes)

    def act(self, fn, reads=(), writes=()):
        return self.add(ACT, fn, reads, writes)

    def dve(self, fn, reads=(), writes=()):
        return self.add(DVE, fn, reads, writes)

    def pool(self, fn, reads=(), writes=()):
        return self.add(POOL, fn, reads, writes)

    def dma(self, q, out, in_, reads=(), writes=(), **kw):
        return self.add(q, lambda e: e.dma_start(out=out, in_=in_, **kw), reads, writes, dma=True)

    def emit(self, final_wait_all_dma=True):
        nc = self.nc
        gs = GSYNC[0]
        esem, dsem = gs["esem"], gs["dsem"]
        for e in COMPUTE:
            c = gs["ebase"][e]
            for op in self.ops[e]:
                if op.dma:
                    continue
                if op.signal:
                    c += 1
                    op.sigval = c
            gs["ebase"][e] = c
        for q in (SP, POOL, ACT):
            k = gs["dk"][q]
            vals = gs["dvals"][q]
            for op in self.ops[q]:
                if op.dma:
                    s = k % NDMASEM
                    k += 1
                    vals[s] += 16
                    op.sem = dsem[q][s]
                    op.semval = vals[s]
            gs["dk"][q] = k
        engobj = {PE: "tensor", ACT: "scalar", DVE: "vector", POOL: "gpsimd", SP: "sync"}
        with nc.Block() as block:

            def run(e, eng):
                waited = {}

                def wait(sem, val):
                    k = id(sem)
                    if waited.get(k, 0) >= val:
                        return
                    waited[k] = val
                    eng.wait_ge(sem, val)

                for op in self.ops[e]:
                    for d in op.deps:
                        if d.dma:
                            wait(d.sem, d.semval)
                        else:
                            wait(esem[d.eng], d.sigval)
                    if op.dma:
                        if op.semval > 16:
                            wait(op.sem, op.semval - 16)
                        ins = op.fn(eng)
                        ins.then_inc(op.sem, 16)
                    else:
                        ins = op.fn(eng)
                        if op.signal:
                            ins.then_inc(esem[e], 1)
                if final_wait_all_dma:
                    last = {}
                    for op in self.ops[e]:
                        if op.dma:
                            last[id(op.sem)] = (op.sem, op.semval)
                    for sem, val in last.values():
                        wait(sem, val)

            for e in (PE, ACT, DVE, POOL, SP):
                if not self.ops[e]:
                    continue
                getattr(block, engobj[e])(lambda eng, e=e: run(e, eng))


GSYNC = [None]


def init_gsync(nc, st):
    gs = {"esem": {e: st.enter_context(nc.semaphore("s_" + e)) for e in COMPUTE}, "dsem": {},
          "ebase": {e: 0 for e in COMPUTE}, "dk": {}, "dvals": {}}
    for q in (SP, POOL, ACT):
        gs["dsem"][q] = [st.enter_context(nc.semaphore("d_%s%d" % (q, i))) for i in range(NDMASEM)]
        gs["dk"][q] = 0
        gs["dvals"][q] = [0] * NDMASEM
    GSYNC[0] = gs

import contextlib

NL, D = 4, 1024
NB = 2
CTX, SEQ = 256, 2048
TB = CTX + SEQ
TT_ = NB * TB
DFF = 2816
ALPHA = 8.0 ** 0.25
EPS = 1e-5
EPSP = EPS / (ALPHA * ALPHA)
IN_TOTAL = 11328
OFF_Z, OFF_XBC, OFF_DT, OFF_LX, OFF_LG = 0, 1024, 3072, 3104, 4128
OFF_Q, OFF_K, OFF_V, OFF_G, OFF_ALR, OFF_GATE = 5152, 5664, 6176, 7200, 8224, 8256
TS_ = 256
TILES = []
for _b in range(NB):
    TILES.append((_b, 0, 256, 2))
    for _i in range(SEQ // TS_):
        TILES.append((_b, CTX + _i * TS_, TS_, _b))


def MM(S, out, lhsT, rhs, start, stop, R, W):
    return S.pe(lambda e: e.matmul(out, lhsT, rhs, start=start, stop=stop), R, W)


def TR(S, out, in_, ident, R, W):
    return S.pe(lambda e: e.transpose(out, in_, ident), R, W)


def ACTV(S, out, in_, func, R, W, bias=None, scale=None):
    kw = {}
    if bias is not None:
        kw["bias"] = bias
    if scale is not None:
        kw["scale"] = scale
    return S.act(lambda e: e.activation(out=out, in_=in_, func=func, **kw), R, W)


def TT(S, eng, out, in0, in1, op, R, W):
    return S.add(eng, lambda e: e.tensor_tensor(out=out, in0=in0, in1=in1, op=op), R, W)


def TSC(S, eng, out, in0, s1, s2, op0, op1, R, W):
    if s2 is None:
        return S.add(eng, lambda e: e.tensor_scalar(out=out, in0=in0, scalar1=s1, scalar2=None, op0=op0), R, W)
    return S.add(eng, lambda e: e.tensor_scalar(out=out, in0=in0, scalar1=s1, scalar2=s2, op0=op0, op1=op1), R, W)


def STT(S, out, in0, scalar, in1, op0, op1, R, W):
    return S.dve(lambda e: e.scalar_tensor_tensor(out=out, in0=in0, scalar=scalar, in1=in1, op0=op0, op1=op1), R, W)


def COPY(S, eng, out, in_, R, W):
    if eng == ACT:
        return S.act(lambda e: e.activation(out=out, in_=in_, func=AF.Identity), R, W)
    return S.add(eng, lambda e: e.tensor_copy(out=out, in_=in_), R, W)


def bc(ap, shape):
    return ap.to_broadcast(list(shape))


DEBUG_OUT = [False]
_UID = [0]


def uname(n):
    _UID[0] += 1
    return '%s_u%d' % (n, _UID[0])


class G:
    pass


def declare_io(nc, g):
    def din(name, shape):
        return nc.dram_tensor(name, list(shape), F32, kind="ExternalInput").ap()
    g.x = din("x", [NB, SEQ, D])
    g.c = din("c", [NB, D])
    g.ctx = din("ctx", [NB, CTX, D])
    g.c_ctx = din("c_ctx", [1, D])
    g.w_ada = din("w_ada", [NL, D, 9 * D])
    g.b_ada = din("b_ada", [NL, 9 * D])
    g.ln_g = din("ln_g", [NL, 3, D])
    g.ln_b = din("ln_b", [NL, 3, D])
    g.ffn_w_up = din("ffn_w_up", [NL, 2, D, 2 * DFF])
    g.ffn_w_down = din("ffn_w_down", [NL, 2, DFF, D])
    g.w_in = din("w_in", [NL, D, IN_TOTAL])
    g.ssd_conv_w = din("ssd_conv_w", [NL, 4, 2048])
    g.ssd_conv_b = din("ssd_conv_b", [NL, 2048])
    g.ssd_dt_bias = din("ssd_dt_bias", [NL, 2, 16])
    g.ssd_a_log = din("ssd_a_log", [NL, 2, 16])
    g.ssd_d = din("ssd_d", [NL, 2, 16])
    g.ssd_norm_g = din("ssd_norm_g", [NL, 1024])
    g.lru_conv_w = din("lru_conv_w", [NL, 4, 1024])
    g.lru_conv_b = din("lru_conv_b", [NL, 1024])
    g.lru_w_a = din("lru_w_a", [NL, 2, 16, 64, 64])
    g.lru_b_a = din("lru_b_a", [NL, 2, 1024])
    g.lru_w_x = din("lru_w_x", [NL, 2, 16, 64, 64])
    g.lru_b_x = din("lru_b_x", [NL, 2, 1024])
    g.lru_lam = din("lru_lam", [NL, 2, 1024])
    g.gla_w_gate = din("gla_w_gate", [NL, 2, 16, 512])
    g.gla_b_gate = din("gla_b_gate", [NL, 2, 512])
    g.gla_norm_g = din("gla_norm_g", [NL, 256])
    g.w_branch = din("w_branch", [NL, 3, 1024, 1024])
    g.w_out = din("w_out", [NL, 1024, 1024])
    g.out = nc.dram_tensor("out", [NB, SEQ, D], F32, kind="ExternalOutput").ap()
    kd = "ExternalOutput" if DEBUG_OUT[0] else "Internal"
    g.HT = nc.dram_tensor("HT", [128, 8, TT_], F32, kind=kd).ap()
    g.Y = [nc.dram_tensor("Y%d" % i, [128, 8, TT_], BF16, kind=kd).ap() for i in range(3)]


INPUT_NAMES = ["x", "c", "ctx", "c_ctx", "w_ada", "b_ada", "ln_g", "ln_b", "ffn_w_up", "ffn_w_down", "w_in",
               "ssd_conv_w", "ssd_conv_b", "ssd_dt_bias", "ssd_a_log", "ssd_d", "ssd_norm_g", "lru_conv_w",
               "lru_conv_b", "lru_w_a", "lru_b_a", "lru_w_x", "lru_b_x", "lru_lam", "gla_w_gate", "gla_b_gate",
               "gla_norm_g", "w_branch", "w_out"]

CF = {}
_o = 0
for _n, _sz in [("ln_g", NL * 3 * 8), ("ln_b", NL * 3 * 8), ("bada", NL * 9 * 8), ("scw", NL * 4 * 16),
                ("scb", NL * 16), ("sng", NL * 8), ("lcw", NL * 4 * 8), ("lcb", NL * 8), ("lba", NL * 16),
                ("lbx", NL * 16), ("llam", NL * 16), ("c", 16), ("cctx", 8), ("lsp8", NL * 16), ("lsp16", NL * 16),
                ("lsp24", NL * 16)]:
    CF[_n] = _o
    _o += _sz
NCF = _o


def mod_ap(g, l, j, kc, cls):
    i = (((l * 9 + j) * 8) + kc) * 4 + cls
    return g.modT[:, i:i + 1]


def cf(g, name, idx):
    o = CF[name] + idx
    return g.constf[:, o:o + 1]


def phase_const(nc, g):
    with contextlib.ExitStack() as st:
        sb = lambda n, s, d: st.enter_context(nc.sbuf_tensor(uname(n), s, d))
        rowbuf = [sb("rowbuf%d" % i, [128, 128], F32) for i in range(2)]
        wbuf = [sb("wadab%d" % i, [128, 8, 1024], BF16) for i in range(2)]
        sT = sb("sT", [128, 8, 4], BF16)
        tmpm = sb("tmpm", [128, 128], F32)
        pst = [st.enter_context(nc.psum_tensor(uname("pst%d" % i), [128, 512], F32)) for i in range(4)]
        S = Sched(nc)
        Bm = Buf("masks")
        Brow = [Buf(), Buf()]
        Bw = [Buf(), Buf()]
        Bps = [Buf() for _ in range(4)]
        Bcf, BsT, Bmod, Btm = Buf("cf"), Buf(), Buf("mod"), Buf()
        g.Bconst = Buf("constall")
        M = g.masks

        def amask(dst, cm, step, base, op):
            S.pool(lambda e: e.memset(dst, 1.0), (), [Bm])
            S.pool(lambda e: e.affine_select(out=dst, in_=dst, compare_op=op, fill=0.0, base=base,
                                             pattern=[[step, 128]], channel_multiplier=cm), [Bm], [Bm])

        S.pool(lambda e: e.memset(M["ONES"][:], 1.0), (), [Bm])
        S.pool(lambda e: e.memset(g.onesb[:], 1.0), (), [Bm])
        amask(M["LE"][:], -1, 1, 0, ALU.is_ge)
        amask(M["GT"][:], 1, -1, 0, ALU.is_gt)
        amask(M["LT"][:], -1, 1, 0, ALU.is_gt)
        amask(M["GE"][:], 1, -1, 0, ALU.is_ge)
        amask(M["ID"][:], 1, -1, 0, ALU.is_equal)
        S.pool(lambda e: e.memset(M["BD"][:], 0.0), (), [Bm])
        S.pool(lambda e: e.memset(M["BD"][0:64, 0:64], 1.0), [Bm], [Bm])
        S.pool(lambda e: e.memset(M["BD"][64:128, 64:128], 1.0), [Bm], [Bm])
        for nm in ("LE", "GT", "LT", "GE"):
            TT(S, POOL, M[nm + "64"][:], M[nm][:], M["BD"][:], ALU.mult, [Bm], [Bm])

        items = [
            ("ln_g", g.ln_g.rearrange("l i (k p) -> (l i k) p", p=128)),
            ("ln_b", g.ln_b.rearrange("l i (k p) -> (l i k) p", p=128)),
            ("bada", g.b_ada.rearrange("l (j p) -> (l j) p", p=128)),
            ("scw", g.ssd_conv_w.rearrange("l k (c p) -> (l k c) p", p=128)),
            ("scb", g.ssd_conv_b.rearrange("l (c p) -> (l c) p", p=128)),
            ("sng", g.ssd_norm_g.rearrange("l (c p) -> (l c) p", p=128)),
            ("lcw", g.lru_conv_w.rearrange("l k (c p) -> (l k c) p", p=128)),
            ("lcb", g.lru_conv_b.rearrange("l (c p) -> (l c) p", p=128)),
            ("lba", g.lru_b_a.rearrange("l d (c p) -> (l d c) p", p=128)),
            ("lbx", g.lru_b_x.rearrange("l d (c p) -> (l d c) p", p=128)),
            ("llam", g.lru_lam.rearrange("l d (c p) -> (l d c) p", p=128)),
            ("c", g.c.rearrange("b (c p) -> (b c) p", p=128)),
            ("cctx", g.c_ctx.rearrange("b (c p) -> (b c) p", p=128)),
        ]
        k = 0
        for nm, ap in items:
            R = ap.shape[0]
            r0 = 0
            while r0 < R:
                nr = min(128, R - r0)
                i = k % 2
                k += 1
                S.dma(SP, rowbuf[i][0:nr, :], ap[r0:r0 + nr, :], (), [Brow[i]])
                TR(S, pst[i][:, 0:nr], rowbuf[i][0:nr, :], M["ID"][0:nr, 0:nr], [Brow[i], Bm], [Bps[i]])
                o = CF[nm] + r0
                COPY(S, DVE, g.constf[:, o:o + nr], pst[i][:, 0:nr], [Bps[i]], [Bcf])
                r0 += nr
        n16 = NL * 16
        lam = g.constf[:, CF["llam"]:CF["llam"] + n16]
        ACTV(S, tmpm[:, 0:n16], lam, AF.Exp, [Bcf], [Btm], scale=-1.0)
        ACTV(S, tmpm[:, 0:n16], tmpm[:, 0:n16], AF.Ln, [Btm], [Btm], bias=1.0)
        for nm, sc in (("lsp8", -8.0), ("lsp16", -16.0), ("lsp24", -16.0 / 24.0)):
            TSC(S, DVE, g.constf[:, CF[nm]:CF[nm] + n16], tmpm[:, 0:n16], sc, None, ALU.mult, None, [Btm], [Bcf])
        S.dma(SP, g.rb_dtb[:], g.ssd_dt_bias.rearrange("l d h -> (l d h)").partition_broadcast(128), (), [Bcf])
        S.dma(SP, g.rb_A[:], g.ssd_a_log.rearrange("l d h -> (l d h)").partition_broadcast(128), (), [Bcf])
        S.dma(SP, g.rb_D[:], g.ssd_d.rearrange("l d h -> (l d h)").partition_broadcast(128), (), [Bcf])
        ACTV(S, g.rb_A[:], g.rb_A[:], AF.Exp, [Bcf], [Bcf])
        TSC(S, DVE, g.rb_A[:], g.rb_A[:], -1.0, None, ALU.mult, None, [Bcf], [Bcf])
        rbD = g.rb_D[:].rearrange("p (l d h) -> p l d h", l=NL, d=2)
        TT(S, DVE, g.rb_Ds[:].rearrange("p (l h) -> p l h", l=NL), rbD[:, :, 0, :], rbD[:, :, 1, :], ALU.add, [Bcf], [Bcf])
        S.pool(lambda e: e.memset(sT[:], 0.0), (), [BsT])
        for b in range(NB):
            o = CF["c"] + b * 8
            ACTV(S, sT[:, :, b], g.constf[:, o:o + 8], AF.Silu, [Bcf, BsT], [BsT])
        o = CF["cctx"]
        ACTV(S, sT[:, :, 2], g.constf[:, o:o + 8], AF.Silu, [Bcf, BsT], [BsT])
        k = 0
        for l in range(NL):
            wv = g.w_ada[l].rearrange("(k p) n -> p k n", p=128)
            for j in range(9):
                i = k % 2
                pi = 2 + (k % 2)
                k += 1
                S.dma(POOL, wbuf[i][:], wv[:, :, j * 1024:(j + 1) * 1024], (), [Bw[i]])
                for oc in range(8):
                    for kc in range(8):
                        MM(S, pst[pi][:, oc * 4:oc * 4 + 4], wbuf[i][:, kc, oc * 128:(oc + 1) * 128], sT[:, kc, :],
                           kc == 0, kc == 7, [Bw[i], BsT], [Bps[pi]])
                mo = ((l * 9 + j) * 8) * 4
                bo = CF["bada"] + (l * 9 + j) * 8
                TT(S, DVE, g.modT[:, mo:mo + 32].rearrange("p (k c) -> p k c", c=4),
                   pst[pi][:, 0:32].rearrange("p (k c) -> p k c", c=4),
                   bc(g.constf[:, bo:bo + 8].unsqueeze(2), [128, 8, 4]), ALU.add, [Bps[pi], Bcf], [Bmod])
        mv = g.modT[:].rearrange("p (l j r) -> p l j r", l=NL, j=9)
        for j in (1, 4, 7):
            TSC(S, DVE, mv[:, :, j, :], mv[:, :, j, :], 1.0, None, ALU.add, None, [Bmod], [Bmod])
        for j, sc in ((2, 0.5 / ALPHA), (8, 0.5 / ALPHA), (5, 1.0 / ALPHA)):
            TSC(S, DVE, mv[:, :, j, :], mv[:, :, j, :], sc, None, ALU.mult, None, [Bmod], [Bmod])
        S.emit()
    nc.all_engine_barrier()


def phase_p0(nc, g):
    with contextlib.ExitStack() as st:
        sb = lambda n, s, d: st.enter_context(nc.sbuf_tensor(uname(n), s, d))
        tin = [sb("p0in%d" % i, [128, 1024], F32) for i in range(3)]
        stg = [sb("p0st%d" % i, [128, 8, 512], F32) for i in range(2)]
        ps = [st.enter_context(nc.psum_tensor(uname("p0ps%d" % i), [128, 512], F32)) for i in range(4)]
        S = Sched(nc)
        Bin = [Buf() for _ in range(3)]
        Bst = [Buf(), Buf()]
        Bps = [Buf() for _ in range(4)]
        Bc = g.Bconst
        k = 0
        gi = 0
        for b in range(NB):
            groups = [(g.ctx[b], 0, 256)] + [(g.x[b, i * 512:(i + 1) * 512, :], CTX + i * 512, 512) for i in range(4)]
            for src, s0, n in groups:
                sg = gi % 2
                gi += 1
                for t in range(n // 128):
                    i = k % 3
                    S.dma(SP, tin[i][:], src[t * 128:(t + 1) * 128, :], (), [Bin[i]])
                    for half in range(2):
                        pi = (2 * k + half) % 4
                        for q in range(4):
                            kc = half * 4 + q
                            TR(S, ps[pi][:, q * 128:(q + 1) * 128], tin[i][:, kc * 128:(kc + 1) * 128], g.masks["ID"][:],
                               [Bin[i], Bc], [Bps[pi]])
                        COPY(S, ACT if half == 0 else DVE, stg[sg][:, half * 4:half * 4 + 4, t * 128:(t + 1) * 128],
                             ps[pi][:].rearrange("p (q t) -> p q t", q=4), [Bps[pi]], [Bst[sg]])
                    k += 1
                col = b * TB + s0
                S.dma(SP, g.HT[:, :, col:col + n], stg[sg][:, :, 0:n], [Bst[sg]], ())
        S.emit()
    nc.all_engine_barrier()


def phase_final(nc, g):
    with contextlib.ExitStack() as st:
        sb = lambda n, s, d: st.enter_context(nc.sbuf_tensor(uname(n), s, d))
        hin = [sb("pfin%d" % i, [128, 8, 512], F32) for i in range(2)]
        to = [sb("pfo%d" % i, [128, 1024], F32) for i in range(3)]
        ps = [st.enter_context(nc.psum_tensor(uname("pfps%d" % i), [128, 512], F32)) for i in range(4)]
        S = Sched(nc)
        Bin = [Buf(), Buf()]
        Bo = [Buf() for _ in range(3)]
        Bps = [Buf() for _ in range(4)]
        Bc = g.Bconst
        k = 0
        gi = 0
        for b in range(NB):
            for i4 in range(4):
                sg = gi % 2
                gi += 1
                col = b * TB + CTX + i4 * 512
                S.dma(SP, hin[sg][:], g.HT[:, :, col:col + 512], (), [Bin[sg]])
                for t in range(4):
                    oi = k % 3
                    for half in range(2):
                        pi = (2 * k + half) % 4
                        for q in range(4):
                            kc = half * 4 + q
                            TR(S, ps[pi][:, q * 128:(q + 1) * 128], hin[sg][:, kc, t * 128:(t + 1) * 128], g.masks["ID"][:],
                               [Bin[sg], Bc], [Bps[pi]])
                        COPY(S, ACT if half == 0 else DVE, to[oi][:, half * 512:(half + 1) * 512], ps[pi][:], [Bps[pi]], [Bo[oi]])
                    r0 = i4 * 512 + t * 128
                    S.dma(SP, g.out[b, r0:r0 + 128, :], to[oi][:], [Bo[oi]], ())
                    k += 1
        S.emit()
    nc.all_engine_barrier()


def load_weight_cast(S, dst3, src2, nk, ncols, Bw, piece=1024):
    sv = src2.rearrange("(k p) n -> p k n", p=128)
    for kc in range(nk):
        c0 = 0
        while c0 < ncols:
            cn = min(piece, ncols - c0)
            S.dma(POOL, dst3[:, kc, c0:c0 + cn], sv[:, kc, c0:c0 + cn], (), [Bw[kc]])
            c0 += cn


def ln_part1(S, g, zt, Bz, n, W):
    zbf, sq, Bzs = W["zbf"], W["sq"], W["Bzs"]
    COPY(S, ACT, zbf[:, :, 0:n], zt[:, :, 0:n], [Bz], [Bzs])
    ACTV(S, sq[:, :, 0:n], zt[:, :, 0:n], AF.Square, [Bz], [Bzs])


def ln_part2(S, g, l, i, zt, Bz, n, W):
    zbf, sq, Bzs = W["zbf"], W["sq"], W["Bzs"]
    psm, psq, Bpm, Bpq = W["psm"], W["psq"], W["Bpm"], W["Bpq"]
    mean, msq, var, Bsm = W["mean"], W["msq"], W["var"], W["Bsm"]
    Bc = g.Bconst
    for kc in range(8):
        MM(S, psm[:, 0:n], g.onesb[:], zbf[:, kc, 0:n], kc == 0, kc == 7, [Bzs, Bc], [Bpm])
    for kc in range(8):
        MM(S, psq[:, 0:n], g.onesb[:], sq[:, kc, 0:n], kc == 0, kc == 7, [Bzs, Bc], [Bpq])
    ACTV(S, mean[:, 0:n], psm[:, 0:n], AF.Identity, [Bpm], [Bsm], scale=1.0 / 1024)
    ACTV(S, msq[:, 0:n], psm[:, 0:n], AF.Square, [Bpm], [Bsm], scale=1.0 / 1024)
    STT(S, var[:, 0:n], psq[:, 0:n], 1.0 / 1024, msq[:, 0:n], ALU.mult, ALU.subtract, [Bpq, Bsm], [Bsm])
    TSC(S, DVE, var[:, 0:n], var[:, 0:n], EPSP, None, ALU.add, None, [Bsm], [Bsm])
    ACTV(S, var[:, 0:n], var[:, 0:n], AF.Sqrt, [Bsm], [Bsm])
    S.dve(lambda e: e.reciprocal(out=var[:, 0:n], in_=var[:, 0:n]), [Bsm], [Bsm])
    TT(S, DVE, zt[:, :, 0:n], zt[:, :, 0:n], bc(mean[:, 0:n].unsqueeze(1), [128, 8, n]), ALU.subtract, [Bz, Bsm], [Bz])
    TT(S, DVE, zt[:, :, 0:n], zt[:, :, 0:n], bc(var[:, 0:n].unsqueeze(1), [128, 8, n]), ALU.mult, [Bz, Bsm], [Bz])
    for kc in range(8):
        ACTV(S, zt[:, kc, 0:n], zt[:, kc, 0:n], AF.Identity, [Bz, Bc], [Bz],
             bias=cf(g, "ln_b", (l * 3 + i) * 8 + kc), scale=cf(g, "ln_g", (l * 3 + i) * 8 + kc))


def phase_ffn(nc, g, l, j, skip_ctx):
    n = TS_
    with contextlib.ExitStack() as st:
        sb = lambda nm, s, d: st.enter_context(nc.sbuf_tensor(uname(nm), s, d))
        wup = sb("wup", [128, 8, 2 * DFF], BF16)
        wdn = sb("wdn", [128, 22, D], BF16)
        hb = [sb("ffh%d" % i, [128, 8, n], F32) for i in range(3)]
        ub = [sb("ffu%d" % i, [128, 8, n], BF16) for i in range(2)]
        hid = sb("ffhid", [128, 22, n], BF16)
        sil = [sb("ffsil%d" % i, [128, n], F32) for i in range(2)]
        zbf = sb("ffzbf", [128, 8, n], BF16)
        sq = sb("ffsq", [128, 8, n], BF16)
        mean = sb("ffmean", [128, n], F32)
        msq = sb("ffmsq", [128, n], F32)
        var = sb("ffvar", [128, n], F32)
        ps = [st.enter_context(nc.psum_tensor(uname("ffps%d" % i), [128, 512], F32)) for i in range(8)]
        S = Sched(nc)
        Bc = g.Bconst
        Bwu = [Buf() for _ in range(8)]
        Bwd = [Buf() for _ in range(22)]
        Bh = [Buf(), Buf(), Buf()]
        Bu = [Buf(), Buf()]
        Bhid, Bzs, Bsm = Buf(), Buf(), Buf()
        Bsil = [Buf(), Buf()]
        Bps = [Buf() for _ in range(8)]
        load_weight_cast(S, wup, g.ffn_w_up[l, j], 8, 2 * DFF, Bwu, piece=1408)
        load_weight_cast(S, wdn, g.ffn_w_down[l, j], 22, D, Bwd)
        W = dict(zbf=zbf, sq=sq, Bzs=Bzs, psm=ps[6], psq=ps[7], Bpm=Bps[6], Bpq=Bps[7], mean=mean, msq=msq, var=var, Bsm=Bsm)
        tiles = [t for t in TILES if not (skip_ctx and t[3] == 2)]
        NT = len(tiles)
        pkc = [0]

        def stA(i):
            b, s0, nn, cls = tiles[i]
            col = b * TB + s0
            S.dma(SP, hb[i % 3][:], g.HT[:, :, col:col + n], (), [Bh[i % 3]])
            for kc in range(8):
                ACTV(S, ub[i % 2][:, kc, :], hb[i % 3][:, kc, :], AF.Identity, [Bh[i % 3], Bc], [Bu[i % 2]],
                     bias=mod_ap(g, l, 3 * (2 * j) + 0, kc, cls), scale=mod_ap(g, l, 3 * (2 * j) + 1, kc, cls))

        def stB(i):
            u_, Bu_ = ub[i % 2], Bu[i % 2]
            for fc in range(22):
                pk = pkc[0]
                pa, pv = ps[(pk % 2) * 2], ps[(pk % 2) * 2 + 1]
                Bpa, Bpv = Bps[(pk % 2) * 2], Bps[(pk % 2) * 2 + 1]
                si = pk % 2
                pkc[0] += 1
                for kc in range(8):
                    MM(S, pa[:, 0:n], wup[:, kc, fc * 128:(fc + 1) * 128], u_[:, kc, :], kc == 0, kc == 7, [Bwu[kc], Bu_], [Bpa])
                for kc in range(8):
                    MM(S, pv[:, 0:n], wup[:, kc, DFF + fc * 128:DFF + (fc + 1) * 128], u_[:, kc, :], kc == 0, kc == 7, [Bwu[kc], Bu_], [Bpv])
                ACTV(S, sil[si][:], pa[:, 0:n], AF.Silu, [Bpa], [Bsil[si]])
                TT(S, DVE, hid[:, fc, :], sil[si][:], pv[:, 0:n], ALU.mult, [Bsil[si], Bpv], [Bhid])

        def stC(i):
            b, s0, nn, cls = tiles[i]
            h_, Bh_ = hb[i % 3], Bh[i % 3]
            for oc in range(8):
                py, Bpy = ps[4 + oc % 2], Bps[4 + oc % 2]
                for fc in range(22):
                    MM(S, py[:, 0:n], wdn[:, fc, oc * 128:(oc + 1) * 128], hid[:, fc, :], fc == 0, fc == 21, [Bwd[fc], Bhid], [Bpy])
                STT(S, h_[:, oc, :], py[:, 0:n], mod_ap(g, l, 3 * (2 * j) + 2, oc, cls), h_[:, oc, :], ALU.mult, ALU.add,
                    [Bpy, Bh_, Bc], [Bh_])
            ln_part1(S, g, h_, Bh_, n, W)

        def stD(i):
            b, s0, nn, cls = tiles[i]
            col = b * TB + s0
            ln_part2(S, g, l, 2 * j, hb[i % 3], Bh[i % 3], n, W)
            S.dma(ACT, g.HT[:, :, col:col + n], hb[i % 3][:], [Bh[i % 3]], ())

        stA(0)
        stB(0)
        for i in range(NT):
            if i + 1 < NT:
                stA(i + 1)
            stC(i)
            if i + 1 < NT:
                stB(i + 1)
            stD(i)
        S.emit()
    nc.all_engine_barrier()


def phase_mixpro(nc, g, l, b, U, BU):
    n = TS_
    odd = (l % 2 == 1)
    with contextlib.ExitStack() as st:
        sb = lambda nm, s, d: st.enter_context(nc.sbuf_tensor(uname(nm), s, d))
        hb = [sb("mph%d" % i, [128, 8, n], F32) for i in range(3)]
        S = Sched(nc)
        Bh = [Buf() for _ in range(3)]
        Bc = g.Bconst
        ti = 0
        for (bb, s0, nn, cls) in TILES:
            if bb != b:
                continue
            hi = ti % 3
            ti += 1
            col = b * TB + s0
            S.dma(SP, hb[hi][:], g.HT[:, :, col:col + n], (), [Bh[hi]])
            for kc in range(8):
                if cls == 2 or not odd:
                    dst = U[:, kc, s0:s0 + n]
                    src = hb[hi][:, kc, :]
                else:
                    r0 = (s0 - CTX) // 64
                    nr = n // 64
                    dst = U[:, kc, CTX:TB].rearrange("p (c r) -> p r c", r=32)[:, r0:r0 + nr, :]
                    src = hb[hi][:, kc, :].rearrange("p (r c) -> p r c", c=64)
                ACTV(S, dst, src, AF.Identity, [Bh[hi], Bc], [BU],
                     bias=mod_ap(g, l, 3, kc, cls), scale=mod_ap(g, l, 4, kc, cls))
        S.emit()
    nc.all_engine_barrier()


def phase_merge(nc, g, l, skip_ctx):
    n = TS_
    with contextlib.ExitStack() as st:
        sb = lambda nm, s, d: st.enter_context(nc.sbuf_tensor(uname(nm), s, d))
        wgt = sb("mgwg", [128, 8, 3072], BF16)
        wbr = sb("mgwb", [128, 24, 1024], BF16)
        wo = sb("mgwo", [128, 8, 1024], BF16)
        hb = [sb("mgh%d" % i, [128, 8, n], F32) for i in range(3)]
        ub = [sb("mgu0", [128, 8, n], BF16)] * 2
        sq0 = sb("mgsq0", [128, 8, n], BF16)
        yb = [[sb("mgy%d_%d" % (i, k), [128, 8, n], BF16) for k in range(3)] for i in range(2)]
        mb = sb("mgm", [128, 8, n], BF16)
        zbf = sb("mgzbf", [128, 8, n], BF16)
        sq = sb("mgsq", [128, 8, n], BF16)
        mean = sb("mgmean", [128, n], F32)
        msq = sb("mgmsq", [128, n], F32)
        var = sb("mgvar", [128, n], F32)
        rstd0 = [sb("mgrstd0%d" % i, [128, n], F32) for i in range(2)]
        sig = [sb("mgsig%d" % i, [128, n], F32) for i in range(3)]
        acc = sb("mgacc", [128, n], F32)
        tmp = sb("mgtmp", [128, n], F32)
        ps = [st.enter_context(nc.psum_tensor(uname("mgps%d" % i), [128, 512], F32)) for i in range(8)]
        S = Sched(nc)
        Bc = g.Bconst
        Bwg = [Buf() for _ in range(8)]
        Bwb = [Buf() for _ in range(24)]
        Bwo = [Buf() for _ in range(8)]
        Bh = [Buf(), Buf(), Buf()]
        By = [[Buf() for _ in range(3)] for _ in range(2)]
        Bm, Bzs, Bsm, Bacc, Btmp, Bsq0 = Buf(), Buf(), Buf(), Buf(), Buf(), Buf()
        Bu = [Buf()] * 2
        Br0 = [Buf(), Buf()]
        Bsig = [Buf(), Buf(), Buf()]
        Bps = [Buf() for _ in range(8)]
        load_weight_cast(S, wgt, g.w_in[l][:, OFF_GATE:OFF_GATE + 3072], 8, 3072, Bwg)
        load_weight_cast(S, wbr, g.w_branch[l].rearrange("n k m -> (n k) m"), 24, 1024, Bwb)
        load_weight_cast(S, wo, g.w_out[l], 8, 1024, Bwo)
        import os as _os
        PL = DVE
        for kc in range(0 if _os.environ.get('MG_NOFOLD') else 8):
            TSC(S, DVE, wbr[:, kc, :], wbr[:, kc, :], cf(g, "sng", l * 8 + kc), None, ALU.mult, None, [Bwb[kc], Bc], [Bwb[kc]])
        W = dict(zbf=zbf, sq=sq, Bzs=Bzs, psm=ps[6], psq=ps[7], Bpm=Bps[6], Bpq=Bps[7], mean=mean, msq=msq, var=var, Bsm=Bsm)
        tiles = [t for t in TILES if not (skip_ctx and t[3] == 2)]
        NT = len(tiles)
        pkc = [0]

        def stA(i):
            b, s0, nn, cls = tiles[i]
            col = b * TB + s0
            hi = i % 2
            S.dma(SP, hb[i % 3][:], g.HT[:, :, col:col + n], (), [Bh[i % 3]])
            for k in range(3):
                S.dma(SP, yb[hi][k][:], g.Y[k][:, :, col:col + n], (), [By[hi][k]])
            for kc in range(8):
                ACTV(S, ub[hi][:, kc, :], hb[i % 3][:, kc, :], AF.Identity, [Bh[i % 3], Bc], [Bu[hi]],
                     bias=mod_ap(g, l, 3, kc, cls), scale=mod_ap(g, l, 4, kc, cls))

        def stA2(i):
            hi = i % 2
            ACTV(S, sq0[:], yb[hi][0][:], AF.Square, [By[hi][0]], [Bsq0])
            for kc in range(8):
                MM(S, ps[7][:, 0:n], g.onesb[:], sq0[:, kc, :], kc == 0, kc == 7, [Bsq0, Bc], [Bps[7]])
            TSC(S, DVE, rstd0[hi][:], ps[7][:, 0:n], 1.0 / 1024, EPS, ALU.mult, ALU.add, [Bps[7]], [Br0[hi]])
            ACTV(S, rstd0[hi][:], rstd0[hi][:], AF.Sqrt, [Br0[hi]], [Br0[hi]])
            S.dve(lambda e: e.reciprocal(out=rstd0[hi][:], in_=rstd0[hi][:]), [Br0[hi]], [Br0[hi]])

        def stB(i):
            hi = i % 2
            for oc in range(8):
                for k in range(3):
                    pk = pkc[0]
                    pg, pb = ps[(pk % 2) * 2], ps[(pk % 2) * 2 + 1]
                    Bpg, Bpb = Bps[(pk % 2) * 2], Bps[(pk % 2) * 2 + 1]
                    si = pk % 3
                    pkc[0] += 1
                    for kc in range(8):
                        MM(S, pg[:, 0:n], wgt[:, kc, k * 1024 + oc * 128:k * 1024 + (oc + 1) * 128], ub[hi][:, kc, :], kc == 0, kc == 7,
                           [Bwg[kc], Bu[hi]], [Bpg])
                    for kc in range(8):
                        MM(S, pb[:, 0:n], wbr[:, k * 8 + kc, oc * 128:(oc + 1) * 128], yb[hi][k][:, kc, :], kc == 0, kc == 7,
                           [Bwb[k * 8 + kc], By[hi][k]], [Bpb])
                    ACTV(S, sig[si][:], pg[:, 0:n], AF.Sigmoid, [Bpg], [Bsig[si]])
                    if k == 0:
                        TT(S, DVE, acc[:], sig[si][:], pb[:, 0:n], ALU.mult, [Bsig[si], Bpb], [Bacc])
                        TT(S, PL, acc[:], acc[:], rstd0[hi][:], ALU.mult, [Bacc, Br0[hi]], [Bacc])
                    elif k == 1:
                        TT(S, DVE, tmp[:], sig[si][:], pb[:, 0:n], ALU.mult, [Bsig[si], Bpb], [Btmp])
                        TT(S, PL, acc[:], acc[:], tmp[:], ALU.add, [Bacc, Btmp], [Bacc])
                    else:
                        TT(S, DVE, tmp[:], sig[si][:], pb[:, 0:n], ALU.mult, [Bsig[si], Bpb], [Btmp])
                        TT(S, PL, mb[:, oc, :], acc[:], tmp[:], ALU.add, [Bacc, Btmp], [Bm])

        def stC(i):
            b, s0, nn, cls = tiles[i]
            h_, Bh_ = hb[i % 3], Bh[i % 3]
            for oc in range(8):
                py, Bpy = ps[4 + oc % 2], Bps[4 + oc % 2]
                for kc in range(8):
                    MM(S, py[:, 0:n], wo[:, kc, oc * 128:(oc + 1) * 128], mb[:, kc, :], kc == 0, kc == 7, [Bwo[kc], Bm], [Bpy])
                STT(S, h_[:, oc, :], py[:, 0:n], mod_ap(g, l, 5, oc, cls), h_[:, oc, :], ALU.mult, ALU.add,
                    [Bpy, Bh_, Bc], [Bh_])
            ln_part1(S, g, h_, Bh_, n, W)

        def stD(i):
            b, s0, nn, cls = tiles[i]
            col = b * TB + s0
            ln_part2(S, g, l, 1, hb[i % 3], Bh[i % 3], n, W)
            S.dma(ACT, g.HT[:, :, col:col + n], hb[i % 3][:], [Bh[i % 3]], ())

        stA(0)
        stA2(0)
        stB(0)
        for i in range(NT):
            if i + 1 < NT:
                stA(i + 1)
            stC(i)
            if i + 1 < NT:
                stA2(i + 1)
                stB(i + 1)
            stD(i)
        S.emit()
    nc.all_engine_barrier()


SEGS = [(0, 256)] + [(CTX + i * 512, 512) for i in range(4)]


def phase_lru(nc, g, l, b, U, BU):
    odd = (l % 2 == 1)
    Lh = 32 if odd else 64
    with contextlib.ExitStack() as st:
        sb = lambda nm, s, d: st.enter_context(nc.sbuf_tensor(uname(nm), s, d))
        wl = [sb("lrw%d" % i, [128, 8, 256], BF16) for i in range(2)]
        wblk = sb("lrblk", [128, 8, 4, 128], BF16)
        xr = sb("lrxr", [128, TB], F32)
        xc = sb("lrxc", [128, TB], F32)
        xcb = sb("lrxcb", [128, TB], BF16)
        gg = sb("lrgg", [128, TB], F32)
        T_ = [[sb("lrT%d_%d" % (d_, i), [128, TB], F32) for i in range(4)] for d_ in range(2)]
        hf = sb("lrhf", [128, TB], F32)
        hbk = sb("lrhb", [128, TB], F32)
        yst = [sb("lryst%d" % i, [128, TB], BF16) for i in range(2)]
        ps = [st.enter_context(nc.psum_tensor(uname("lrps%d" % i), [128, 512], F32)) for i in range(8)]
        S = Sched(nc)
        Bc = g.Bconst
        Bwl = [Buf(), Buf()]
        Bblk, Bxr, Bxc, Bxcb, Bgg, Bhf, Bhb = Buf(), Buf(), Buf(), Buf(), Buf(), Buf(), Buf()
        BT_ = [[Buf() for _ in range(4)] for _ in range(2)]
        Byst = [Buf(), Buf()]
        Bps = [Buf() for _ in range(8)]
        S.pool(lambda e: e.memset(wblk[:], 0.0), (), [Bblk])
        for d in range(2):
            for t, wsrc in enumerate((g.lru_w_a, g.lru_w_x)):
                for h in range(2):
                    src = wsrc[l, d].rearrange("(j h) k c -> h k j c", h=2)[h]
                    S.dma(POOL, wblk[h * 64:(h + 1) * 64, :, d * 2 + t, h * 64:(h + 1) * 64], src, (), [Bblk])
        wv = g.w_in[l].rearrange("(k p) n -> p k n", p=128)
        pk = 0
        for j in range(8):
            wi = j % 2
            S.dma(POOL, wl[wi][:, :, 0:128], wv[:, :, OFF_LX + j * 128:OFF_LX + (j + 1) * 128], (), [Bwl[wi]])
            S.dma(POOL, wl[wi][:, :, 128:256], wv[:, :, OFF_LG + j * 128:OFF_LG + (j + 1) * 128], (), [Bwl[wi]])
            for (s0, n) in SEGS:
                px, Bpx = ps[pk % 4], Bps[pk % 4]
                pg, Bpg = ps[(pk + 1) % 4], Bps[(pk + 1) % 4]
                pk += 2
                for kc in range(8):
                    MM(S, px[:, 0:n], wl[wi][:, kc, 0:128], U[:, kc, s0:s0 + n], kc == 0, kc == 7, [Bwl[wi], BU], [Bpx])
                for kc in range(8):
                    MM(S, pg[:, 0:n], wl[wi][:, kc, 128:256], U[:, kc, s0:s0 + n], kc == 0, kc == 7, [Bwl[wi], BU], [Bpg])
                COPY(S, ACT, xr[:, s0:s0 + n], px[:, 0:n], [Bpx], [Bxr])
                ACTV(S, gg[:, s0:s0 + n], pg[:, 0:n], AF.Gelu_apprx_tanh, [Bpg], [Bgg])
            cw = lambda k: cf(g, "lcw", (l * 4 + k) * 8 + j)
            ACTV(S, xc[:], xr[:], AF.Identity, [Bxr, Bc], [Bxc], bias=cf(g, "lcb", l * 8 + j), scale=cw(2))
            for (o0, ln_, nl) in ((0, 256, 1), (CTX, Lh, SEQ // Lh)):
                xv = xr[:, o0:o0 + ln_ * nl].rearrange("p (a b) -> p a b", b=ln_)
                ov = xc[:, o0:o0 + ln_ * nl].rearrange("p (a b) -> p a b", b=ln_)
                STT(S, ov[:, :, 2:ln_], xv[:, :, 0:ln_ - 2], cw(0), ov[:, :, 2:ln_], ALU.mult, ALU.add, [Bxr, Bxc, Bc], [Bxc])
                STT(S, ov[:, :, 1:ln_], xv[:, :, 0:ln_ - 1], cw(1), ov[:, :, 1:ln_], ALU.mult, ALU.add, [Bxr, Bxc, Bc], [Bxc])
                STT(S, ov[:, :, 0:ln_ - 1], xv[:, :, 1:ln_], cw(3), ov[:, :, 0:ln_ - 1], ALU.mult, ALU.add, [Bxr, Bxc, Bc], [Bxc])
            COPY(S, ACT, xcb[:], xc[:], [Bxc], [Bxcb])
            def lru_dir(d):
                T = T_[d]
                BT = BT_[d]
                ci = (l * 2 + d) * 8 + j
                pk_ = 0
                for (s0, n) in SEGS:
                    pr, Bpr = ps[4 + 2 * d], Bps[4 + 2 * d]
                    pi_, Bpi = ps[5 + 2 * d], Bps[5 + 2 * d]
                    MM(S, pr[:, 0:n], wblk[:, j, d * 2 + 0, :], xcb[:, s0:s0 + n], True, True, [Bblk, Bxcb], [Bpr])
                    yield
                    MM(S, pi_[:, 0:n], wblk[:, j, d * 2 + 1, :], xcb[:, s0:s0 + n], True, True, [Bblk, Bxcb], [Bpi])
                    yield
                    ACTV(S, T[0][:, s0:s0 + n], pr[:, 0:n], AF.Sigmoid, [Bpr, Bc], [BT[0]], bias=cf(g, "lba", ci))
                    yield
                    ACTV(S, T[1][:, s0:s0 + n], pi_[:, 0:n], AF.Sigmoid, [Bpi, Bc], [BT[1]], bias=cf(g, "lbx", ci))
                    yield
                TT(S, POOL, T[1][:], T[1][:], xc[:], ALU.mult, [BT[1], Bxc], [BT[1]])
                yield
                ACTV(S, T[2][:], T[0][:], AF.Exp, [BT[0], Bc], [BT[2]], scale=cf(g, "lsp8", ci))
                yield
                TSC(S, DVE, T[3][:], T[0][:], cf(g, "lsp16", ci), None, ALU.mult, None, [BT[0], Bc], [BT[3]])
                yield
                TSC(S, DVE, T[0][:], T[3][:], 1.0 / 24, 1.0 / 6, ALU.mult, ALU.add, [BT[3]], [BT[0]])
                yield
                TT(S, DVE, T[0][:], T[0][:], T[3][:], ALU.mult, [BT[0], BT[3]], [BT[0]])
                yield
                for cst in (0.5, 1.0):
                    STT(S, T[0][:], T[0][:], cst, T[3][:], ALU.add, ALU.mult, [BT[0], BT[3]], [BT[0]])
                    yield
                ACTV(S, T[0][:], T[0][:], AF.Sqrt, [BT[0]], [BT[0]], scale=-1.0)
                yield
                TT(S, DVE, T[1][:], T[1][:], T[0][:], ALU.mult, [BT[1], BT[0]], [BT[1]])
                yield
                if d == 0:
                    S.dve(lambda e: e.tensor_tensor_scan(out=hf[:], data0=T[2][:], data1=T[1][:], initial=0.0,
                                                         op0=ALU.mult, op1=ALU.add), [BT[2], BT[1]], [Bhf])
                    yield
                else:
                    S.dve(lambda e: e.tensor_tensor_scan(out=hbk[:, 0:CTX][:, ::-1], data0=T[2][:, 0:CTX][:, ::-1],
                                                         data1=T[1][:, 0:CTX][:, ::-1], initial=0.0,
                                                         op0=ALU.mult, op1=ALU.add), [BT[2], BT[1]], [Bhb])
                    yield
                    S.dve(lambda e: e.tensor_tensor_scan(out=hbk[:, CTX:TB][:, ::-1], data0=T[2][:, CTX:TB][:, ::-1],
                                                         data1=T[1][:, CTX:TB][:, ::-1], initial=hbk[:, 0:1],
                                                         op0=ALU.mult, op1=ALU.add), [BT[2], BT[1], Bhb], [Bhb])
                    yield

            run_interleaved([lru_dir(0), lru_dir(1)])
            yi = j % 2
            TT(S, DVE, hf[:], hf[:], hbk[:], ALU.add, [Bhf, Bhb], [Bhf])
            TT(S, DVE, yst[yi][:, 0:CTX], hf[:, 0:CTX], gg[:, 0:CTX], ALU.mult, [Bhf, Bgg], [Byst[yi]])
            if odd:
                ov = yst[yi][:, CTX:TB].rearrange("p (r c) -> p c r", c=64)
                i0 = hf[:, CTX:TB].rearrange("p (c r) -> p c r", r=32)
                i1 = gg[:, CTX:TB].rearrange("p (c r) -> p c r", r=32)
                TT(S, DVE, ov, i0, i1, ALU.mult, [Bhf, Bgg], [Byst[yi]])
            else:
                TT(S, DVE, yst[yi][:, CTX:TB], hf[:, CTX:TB], gg[:, CTX:TB], ALU.mult, [Bhf, Bgg], [Byst[yi]])
            S.dma(SP, g.Y[1][:, j, b * TB:(b + 1) * TB], yst[yi][:], [Byst[yi]], ())
        S.emit()
    nc.all_engine_barrier()


def phase_mix(nc, g, l, which):
    with contextlib.ExitStack() as st:
        U = st.enter_context(nc.sbuf_tensor(uname("Umix"), [128, 8, TB], BF16))
        for b in range(NB):
            BU = Buf("U")
            phase_mixpro(nc, g, l, b, U, BU)
            BU = Buf("U")
            if "ssd" in which:
                phase_ssd(nc, g, l, b, U, BU)
            if "lru" in which:
                phase_lru(nc, g, l, b, U, BU)
            if "gla" in which:
                phase_gla(nc, g, l, b, U, BU)


FWD_CHUNKS = list(range(18))
REV_CHUNKS = [1, 0] + list(range(17, 1, -1))


def run_interleaved(gens):
    gens = list(gens)
    while gens:
        for g_ in list(gens):
            try:
                next(g_)
            except StopIteration:
                gens.remove(g_)


def conv_block(S, g, raw, Braw, out, Bout, wfn, bias_ap, Lh):
    Bc = g.Bconst
    ACTV(S, out[:], raw[:], AF.Identity, [Braw, Bc], [Bout], bias=bias_ap, scale=wfn(2))
    for (o0, ln_, nl) in ((0, 256, 1), (CTX, Lh, SEQ // Lh)):
        xv = raw[:, o0:o0 + ln_ * nl].rearrange("p (a b) -> p a b", b=ln_)
        ov = out[:, o0:o0 + ln_ * nl].rearrange("p (a b) -> p a b", b=ln_)
        STT(S, ov[:, :, 2:ln_], xv[:, :, 0:ln_ - 2], wfn(0), ov[:, :, 2:ln_], ALU.mult, ALU.add, [Braw, Bout, Bc], [Bout])
        STT(S, ov[:, :, 1:ln_], xv[:, :, 0:ln_ - 1], wfn(1), ov[:, :, 1:ln_], ALU.mult, ALU.add, [Braw, Bout, Bc], [Bout])
        STT(S, ov[:, :, 0:ln_ - 1], xv[:, :, 1:ln_], wfn(3), ov[:, :, 0:ln_ - 1], ALU.mult, ALU.add, [Braw, Bout, Bc], [Bout])


def phase_ssd(nc, g, l, b, U, BU):
    odd = (l % 2 == 1)
    Lh = 32 if odd else 64
    M = g.masks
    with contextlib.ExitStack() as st:
        sb = lambda nm, s, d: st.enter_context(nc.sbuf_tensor(uname(nm), s, d))
        wg = [sb("sdw0", [128, 8, 784], BF16)] * 2
        szT = sb("sdsz", [128, 2, TB], BF16)
        craw = sb("sdcraw", [128, TB], F32)
        ctmp = sb("sdctmp", [128, TB], F32)
        xsT = sb("sdxsT", [128, 2, TB], F32)
        BTf = sb("sdBTf", [128, TB], F32)
        BT = sb("sdBT", [128, TB], BF16)
        CT = sb("sdCT", [128, TB], BF16)
        dtv = sb("sddt", [128, 144], F32)
        av = sb("sda", [128, 144], F32)
        acs = sb("sdacs", [128, 144], F32)
        tot = sb("sdtot", [128, 144], F32)
        tm = sb("sdtm", [128, 144], F32)
        fs = sb("sdfs", [128, 144], F32)
        te = sb("sdte", [128, 144], F32)
        cd = sb("sdcd", [128, 144], F32)
        yacc = sb("sdyacc", [128, 18, 256], F32)
        Sst = sb("sdS", [128, 2, 256], F32)
        Sbf = sb("sdSb", [128, 2, 256], BF16)
        yst = [sb("sdyst0", [128, 2, TB], BF16)] * 2
        xs_all = sb("sdxsall", [128, 18, 256], F32)
        B_all = sb("sdBall", [128, 18, 128], BF16)
        xsd_ = [sb("sdxsd%d" % i, [128, 256], BF16) for i in range(2)]
        xw_ = [sb("sdxw%d" % i, [128, 256], BF16) for i in range(2)]
        scm_ = [sb("sdscm%d" % i, [128, 128], BF16) for i in range(2)]
        rhsA_ = [sb("sdrhsA%d" % i, [128, 512], F32) for i in range(2)]
        Eb_ = [sb("sdE%d" % i, [128, 512], BF16) for i in range(2)]
        MT_ = [sb("sdMT%d" % i, [128, 512], BF16) for i in range(2)]
        t1_ = [sb("sdt1%d" % i, [128, 256], F32) for i in range(2)]
        t2 = sb("sdt2", [128, 256], F32)
        ps = [st.enter_context(nc.psum_tensor(uname("sdps%d" % i), [128, 512], F32)) for i in range(8)]
        S = Sched(nc)
        Bc = g.Bconst
        Bwg = [Buf()] * 2
        (Bsz, Bcraw, Bctmp, BxsT, BBTf, BBT, BCT, Bdt, Ba, Bacs, Btot, Btm, Bfs, Bte, Bcd, Byacc, BS, BSb,
         Bxt, BBtok, Bxsd, Bxw, Bscm, BrhsA, BE, BMT, Bt1, Bt2) = [Buf() for _ in range(28)]
        Byst = [Buf()] * 2
        Bxall, BBall = Buf(), Buf()
        Byacc = [Buf() for _ in range(18)]
        Bxsd_, Bxw_, Bscm_, BrhsA_, BE_, BMT_, Bt1_, BS_, BSb_ = [[Buf(), Buf()] for _ in range(9)]
        Bps = [Buf() for _ in range(8)]
        wv = g.w_in[l].rearrange("(k p) n -> p k n", p=128)
        v4 = lambda t: t[:].rearrange("p (c d h) -> p c d h", d=2, h=4)
        pk = 0
        for gq in range(4):
            wi = gq % 2
            for (d0, c0, cn) in ((0, OFF_Z + 256 * gq, 256), (256, OFF_XBC + 256 * gq, 256),
                                 (512, OFF_XBC + 1024 + 128 * gq, 128), (640, OFF_XBC + 1536 + 128 * gq, 128),
                                 (768, OFF_DT + 4 * gq, 4), (772, OFF_DT + 16 + 4 * gq, 4)):
                S.dma(POOL, wg[wi][:, :, d0:d0 + cn], wv[:, :, c0:c0 + cn], (), [Bwg[wi]])

            def inproj(c0, evac):
                nonlocal pk
                for (s0, n) in SEGS:
                    p_, Bp = ps[pk % 2], Bps[pk % 2]
                    pk += 1
                    for kc in range(8):
                        MM(S, p_[:, 0:n], wg[wi][:, kc, c0:c0 + 128], U[:, kc, s0:s0 + n], kc == 0, kc == 7, [Bwg[wi], BU], [Bp])
                    evac(p_, Bp, s0, n)

            for i in range(2):
                inproj(i * 128, lambda p_, Bp, s0, n, i=i: ACTV(S, szT[:, i, s0:s0 + n], p_[:, 0:n], AF.Silu, [Bp], [Bsz]))
            for ci in range(4):
                inproj(256 + ci * 128, lambda p_, Bp, s0, n: COPY(S, ACT, craw[:, s0:s0 + n], p_[:, 0:n], [Bp], [Bcraw]))
                ch16 = (2 * gq + ci) if ci < 2 else (8 + gq if ci == 2 else 12 + gq)
                conv_block(S, g, craw, Bcraw, ctmp, Bctmp, lambda k, ch16=ch16: cf(g, "scw", (l * 4 + k) * 16 + ch16),
                           cf(g, "scb", l * 16 + ch16), Lh)
                if ci < 2:
                    ACTV(S, xsT[:, ci, :], ctmp[:], AF.Silu, [Bctmp], [BxsT])
                elif ci == 2:
                    ACTV(S, BTf[:], ctmp[:], AF.Silu, [Bctmp], [BBTf])
                    COPY(S, DVE, BT[:], BTf[:], [BBTf], [BBT])
                else:
                    ACTV(S, CT[:], ctmp[:], AF.Silu, [Bctmp], [BCT])
            pdt, Bpdt = ps[2], Bps[2]
            for c in range(18):
                for kc in range(8):
                    MM(S, pdt[:, c * 8:(c + 1) * 8], U[:, kc, c * 128:(c + 1) * 128], wg[wi][:, kc, 768:776], kc == 0, kc == 7,
                       [Bwg[wi], BU], [Bpdt])
            rb = lambda t: bc(t[:, l * 32:(l + 1) * 32].rearrange("p (d h) -> p d h", d=2)[:, :, 4 * gq:4 * gq + 4].unsqueeze(1), [128, 18, 2, 4])
            TT(S, DVE, v4(dtv), pdt[:, 0:144].rearrange("p (c d h) -> p c d h", d=2, h=4), rb(g.rb_dtb), ALU.add, [Bpdt, Bc], [Bdt])
            ACTV(S, dtv[:], dtv[:], AF.Exp, [Bdt], [Bdt])
            ACTV(S, dtv[:], dtv[:], AF.Ln, [Bdt], [Bdt], bias=1.0)
            TT(S, DVE, v4(av), v4(dtv), rb(g.rb_A), ALU.mult, [Bdt, Bc], [Ba])
            MM(S, ps[3][:, 0:144], M["LE"][:], av[:], True, True, [Ba, Bc], [Bps[3]])
            MM(S, ps[4][:, 0:144], M["ONES"][:], av[:], True, True, [Ba, Bc], [Bps[4]])
            COPY(S, DVE, acs[:], ps[3][:, 0:144], [Bps[3]], [Bacs])
            COPY(S, DVE, tot[:], ps[4][:, 0:144], [Bps[4]], [Btot])
            ACTV(S, cd[:], tot[:], AF.Exp, [Btot], [Bcd])
            ACTV(S, v4(fs)[:, :, 0, :], v4(acs)[:, :, 0, :], AF.Exp, [Bacs], [Bfs])
            TT(S, DVE, v4(tm)[:, :, 0, :], v4(tot)[:, :, 0, :], v4(acs)[:, :, 0, :], ALU.subtract, [Btot, Bacs], [Btm])
            TT(S, DVE, v4(tm)[:, :, 1, :], v4(acs)[:, :, 1, :], v4(av)[:, :, 1, :], ALU.subtract, [Bacs, Ba], [Btm])
            ACTV(S, te[:], tm[:], AF.Exp, [Btm], [Bte])
            TT(S, DVE, v4(tm)[:, :, 1, :], v4(tot)[:, :, 1, :], v4(tm)[:, :, 1, :], ALU.subtract, [Btot, Btm], [Btm])
            ACTV(S, v4(fs)[:, :, 1, :], v4(tm)[:, :, 1, :], AF.Exp, [Btm], [Bfs])
            yi = 0
            h3 = lambda t: t.rearrange("p (h q) -> p h q", h=4)
            for c in range(18):
                cs = slice(c * 128, (c + 1) * 128)
                ptr, Bptr = ps[2 + c % 2], Bps[2 + c % 2]
                for i in range(2):
                    TR(S, ptr[:, i * 128:(i + 1) * 128], xsT[:, i, cs], M["ID"][:], [BxsT, Bc], [Bptr])
                TR(S, ptr[:, 256:384], BTf[:, cs], M["ID"][:], [BBTf, Bc], [Bptr])
                COPY(S, ACT, xs_all[:, c, :], ptr[:, 0:256], [Bptr], [Bxall])
                COPY(S, ACT, B_all[:, c, :], ptr[:, 256:384], [Bptr], [BBall])
            seen = set()

            def ssd_iter(d, c):
                M1 = M["GT"] if d == 0 else M["LT"]
                M2 = M["LE"] if d == 0 else M["GE"]
                cs = slice(c * 128, (c + 1) * 128)
                xsd, xw, scm, rhsA, Eb, MT, t1 = xsd_[d], xw_[d], scm_[d], rhsA_[d], Eb_[d], MT_[d], t1_[d]
                Bxsd, Bxw, Bscm, BrhsA, BE, BMT, Bt1 = Bxsd_[d], Bxw_[d], Bscm_[d], BrhsA_[d], BE_[d], BMT_[d], Bt1_[d]
                pA, BpA = ps[2 + 3 * d], Bps[2 + 3 * d]
                pS, BpS = ps[3 + 3 * d], Bps[3 + 3 * d]
                pY, BpY = ps[4 + 3 * d], Bps[4 + 3 * d]
                xs_tok = xs_all[:, c, :]
                dt_c = bc(v4(dtv)[:, c, d, :].unsqueeze(2), [128, 4, 64])
                te_c = bc(v4(te)[:, c, d, :].unsqueeze(2), [128, 4, 64])
                fs_c = bc(v4(fs)[:, c, d, :].unsqueeze(2), [128, 4, 64])
                cd_c = bc(v4(cd)[:, c, d, :].unsqueeze(2), [128, 4, 64])
                TT(S, DVE, h3(xsd[:]), h3(xs_tok), dt_c, ALU.mult, [Bxall, Bdt], [Bxsd])
                yield
                TT(S, DVE, h3(xw[:]), h3(xsd[:]), te_c, ALU.mult, [Bxsd, Bte], [Bxw])
                yield
                MM(S, pA[:, 0:128], BT[:, cs], CT[:, cs], True, True, [BBT, BCT], [BpA])
                yield
                TT(S, DVE, scm[:], pA[:, 0:128], (M["LE"] if d == 0 else M["GE"])[:], ALU.mult, [BpA, Bc], [Bscm])
                yield
                TT(S, POOL, h3(rhsA[:]), bc(M2[:].unsqueeze(1), [128, 4, 128]), bc(v4(av)[:, c, d, :].unsqueeze(2), [128, 4, 128]),
                   ALU.mult, [Ba, Bc], [BrhsA])
                yield
                MM(S, pS[:], M1[:], rhsA[:], True, True, [BrhsA, Bc], [BpS])
                yield
                ACTV(S, Eb[:], pS[:], AF.Exp, [BpS], [BE])
                yield
                TT(S, DVE, h3(MT[:]), h3(Eb[:]), bc(scm[:].unsqueeze(1), [128, 4, 128]), ALU.mult, [BE, Bscm], [BMT])
                yield
                for hh in range(4):
                    MM(S, pY[:, hh * 64:(hh + 1) * 64], MT[:, hh * 128:(hh + 1) * 128], xsd[:, hh * 64:(hh + 1) * 64], True, True,
                       [BMT, Bxsd], [BpY])
                MM(S, pY[:, 256:512], CT[:, cs], Sbf[:, d, :], True, True, [BCT, BSb_[d]], [BpY])
                yield
                TT(S, DVE, h3(t1[:]), h3(pY[:, 256:512]), fs_c, ALU.mult, [BpY, Bfs], [Bt1])
                yield
                if c not in seen:
                    seen.add(c)
                    TT(S, DVE, yacc[:, c, :], pY[:, 0:256], t1[:], ALU.add, [BpY, Bt1], [Byacc[c]])
                else:
                    TT(S, DVE, t1[:], pY[:, 0:256], t1[:], ALU.add, [BpY, Bt1], [Bt1])
                    TT(S, POOL, h3(t2[:]), h3(xs_tok),
                       bc(g.rb_Ds[:, l * 16 + 4 * gq:l * 16 + 4 * gq + 4].unsqueeze(2), [128, 4, 64]), ALU.mult, [Bxall, Bc], [Bt2])
                    TT(S, POOL, t1[:], t1[:], t2[:], ALU.add, [Bt1, Bt2], [Bt1])
                    TT(S, DVE, yacc[:, c, :], yacc[:, c, :], t1[:], ALU.add, [Byacc[c], Bt1], [Byacc[c]])
                    pf, Bpf = ps[1], Bps[1]
                    for i in range(2):
                        TR(S, pf[:, i * 128:(i + 1) * 128], yacc[:, c, i * 128:(i + 1) * 128], M["ID"][:], [Byacc[c], Bc], [Bpf])
                    for i in range(2):
                        if odd and c >= 2:
                            cc0 = (c - 2) * 4
                            ov = yst[yi][:, i, CTX:TB].rearrange("p (r c) -> p c r", c=64)[:, cc0:cc0 + 4, :]
                            i0 = pf[:, i * 128:(i + 1) * 128].rearrange("p (c r) -> p c r", r=32)
                            i1 = szT[:, i, cs].rearrange("p (c r) -> p c r", r=32)
                        else:
                            ov, i0, i1 = yst[yi][:, i, cs], pf[:, i * 128:(i + 1) * 128], szT[:, i, cs]
                        TT(S, DVE, ov, i0, i1, ALU.mult, [Bpf, Bsz], [Byst[yi]])
                MM(S, pA[:, 128:384], B_all[:, c, :], xw[:], True, True, [BBall, Bxw], [BpA])
                yield
                TT(S, POOL, h3(Sst[:, d, :]), h3(Sst[:, d, :]), cd_c, ALU.mult, [BS_[d], Bcd], [BS_[d]])
                yield
                TT(S, DVE, Sst[:, d, :], Sst[:, d, :], pA[:, 128:384], ALU.add, [BS_[d], BpA], [BS_[d]])
                yield
                COPY(S, ACT, Sbf[:, d, :], Sst[:, d, :], [BS_[d]], [BSb_[d]])
                yield

            for d in range(2):
                S.pool(lambda e, d=d: e.memset(Sst[:, d, :], 0.0), (), [BS_[d]])
                S.pool(lambda e, d=d: e.memset(Sbf[:, d, :], 0.0), (), [BSb_[d]])
            def ssd_dir(d):
                for c in (FWD_CHUNKS if d == 0 else REV_CHUNKS):
                    yield from ssd_iter(d, c)

            run_interleaved([ssd_dir(0), ssd_dir(1)])
            S.dma(SP, g.Y[0][:, 2 * gq:2 * gq + 2, b * TB:(b + 1) * TB], yst[yi][:], [Byst[yi]], ())
        S.emit()
    nc.all_engine_barrier()


def phase_gla(nc, g, l, b, U, BU):
    odd = (l % 2 == 1)
    M = g.masks
    QS = 128.0 ** -0.5
    with contextlib.ExitStack() as st:
        sb = lambda nm, s, d: st.enter_context(nc.sbuf_tensor(uname(nm), s, d))
        wq = sb("glw", [128, 8, 896], BF16)
        WG = sb("glWG", [128, 256], BF16)
        bgb = sb("glbg", [128, 256], F32)
        gng = sb("glgng", [128, 256], F32)
        qT = sb("glqT", [128, TB], F32)
        kT = sb("glkT", [128, TB], F32)
        sgT = sb("glsg", [128, 2, TB], BF16)
        alrT = sb("glalr", [128, TB], BF16)
        v_tok = sb("glv", [128, 18, 256], BF16)
        k_tok = sb("glk", [128, 18, 128], F32)
        lsp = sb("gllsp", [128, 18, 256], F32)
        oacc = sb("gloacc", [128, 18, 256], F32)
        yst = [sb("glyst0", [128, 2, TB], BF16)] * 2
        eq_ = [sb("gleq%d" % i, [128, 128], F32) for i in range(2)]
        ek_ = [sb("glek%d" % i, [128, 128], F32) for i in range(2)]
        qin_ = [sb("glqin%d" % i, [128, 128], BF16) for i in range(2)]
        kin_ = [sb("glkin%d" % i, [128, 128], BF16) for i in range(2)]
        er_ = [sb("gler%d" % i, [128, 128], F32) for i in range(2)]
        kst_ = [[sb("glkst%d_%d" % (d_, i), [128, 128], BF16) for i in range(2)] for d_ in range(2)]
        qinh_ = [[sb("glqinh%d_%d" % (d_, i), [128, 128], BF16) for i in range(2)] for d_ in range(2)]
        attT_ = [sb("glatt%d" % i, [128, 128], BF16) for i in range(2)]
        Sst_ = [sb("glS%d" % i, [128, 256], F32) for i in range(2)]
        Sb0_ = [sb("glSb0%d" % i, [128, 256], BF16) for i in range(2)]
        Sb1_ = [sb("glSb1%d" % i, [128, 256], BF16) for i in range(2)]
        osq = sb("glosq", [128, 256], F32)
        ssq = sb("glssq", [128, 1], F32)
        on = sb("glon", [128, 256], F32)
        ps = [st.enter_context(nc.psum_tensor(uname("glps%d" % i), [128, 512], F32)) for i in range(8)]
        S = Sched(nc)
        Bc = g.Bconst
        (Bwq, BWG, Bbg, BqT, BkT, Bsg, Balr, Bv, Bk, Blsp, Boacc, Beq, Bek, Bqin, Bkin, Ber, Bkst, Batt, BS, BSb0, BSb1,
         Bosq, Bssq, Bon) = [Buf() for _ in range(24)]
        Byst = [Buf()] * 2
        Boacc = [Buf() for _ in range(18)]
        Beq_, Bek_, Bqin_, Bkin_, Ber_, Bkst_, Bqinh_, Batt_, BS_, BSb0_, BSb1_ = [[Buf(), Buf()] for _ in range(11)]
        Bps = [Buf() for _ in range(8)]
        wv = g.w_in[l].rearrange("(k p) n -> p k n", p=128)
        pk = 0
        Bgng = Buf()
        S.dma(SP, gng[:], g.gla_norm_g[l].partition_broadcast(128), (), [Bgng])
        for d_ in range(2):
            for i_ in range(2):
                S.pool(lambda e, i_=i_, d_=d_: e.memset(qinh_[d_][i_][:], 0.0), (), [Bqinh_[d_]])
        for hd in range(4):
            S.pool(lambda e: e.memset(wq[:, :, 768:896], 0.0), (), [Bwq])
            for (d0, c0, cn) in ((0, OFF_Q + 128 * hd, 128), (128, OFF_K + 128 * hd, 128), (256, OFF_V + 256 * hd, 256),
                                 (512, OFF_G + 256 * hd, 256), (768, OFF_ALR, 16), (800, OFF_ALR + 16, 16)):
                S.dma(POOL, wq[:, :, d0:d0 + cn], wv[:, :, c0:c0 + cn], (), [Bwq])
            import os as _os
            S.pool(lambda e: e.memset(WG[:], 0.0), (), [BWG])
            for d in range(0 if _os.environ.get('GLA_NOWG') else 2):
                S.dma(POOL, WG[32 * d:32 * d + 16, d * 128:(d + 1) * 128], g.gla_w_gate[l, d, :, hd * 128:(hd + 1) * 128], (), [BWG])
                S.dma(SP, bgb[:, d * 128:(d + 1) * 128], g.gla_b_gate[l, d, hd * 128:(hd + 1) * 128].partition_broadcast(128), (), [Bbg])

            _ninp = [0]

            def inproj(c0, m, evac):
                nonlocal pk
                _ninp[0] += 1
                if _ninp[0] > int(_os.environ.get('GLA_INP', '9')):
                    return
                for (s0, n) in SEGS:
                    p_, Bp = ps[pk % 2], Bps[pk % 2]
                    pk += 1
                    for kc in range(8):
                        MM(S, p_[0:m, 0:n], wq[:, kc, c0:c0 + m], U[:, kc, s0:s0 + n], kc == 0, kc == 7, [Bwq, BU], [Bp])
                    evac(p_, Bp, s0, n)

            if float(_os.environ.get('GLA_DBG', '9')) == 0:
                continue
            inproj(0, 128, lambda p_, Bp, s0, n: COPY(S, ACT, qT[:, s0:s0 + n], p_[:, 0:n], [Bp], [BqT]))
            inproj(128, 128, lambda p_, Bp, s0, n: COPY(S, ACT, kT[:, s0:s0 + n], p_[:, 0:n], [Bp], [BkT]))
            for i in range(2):
                inproj(512 + i * 128, 128, lambda p_, Bp, s0, n, i=i: ACTV(S, sgT[:, i, s0:s0 + n], p_[:, 0:n], AF.Silu, [Bp], [Bsg]))
            inproj(768, 128, lambda p_, Bp, s0, n: COPY(S, ACT, alrT[:, s0:s0 + n], p_[:, 0:n], [Bp], [Balr]))
            _l2 = float(_os.environ.get('GLA_DBG', '9'))
            for c in range(18 if _l2 > 0.5 else 0):
                cs = slice(c * 128, (c + 1) * 128)
                p_, Bp = ps[pk % 2], Bps[pk % 2]
                pk += 1
                for kc in range(8):
                    MM(S, p_[:, 0:256], U[:, kc, cs], wq[:, kc, 256:512], kc == 0, kc == 7, [Bwq, BU], [Bp])
                for kc in range(8):
                    MM(S, ps[7][:, 0:128], U[:, kc, cs], wq[:, kc, 128:256], kc == 0, kc == 7, [Bwq, BU], [Bps[7]])
                COPY(S, ACT, v_tok[:, c, :], p_[:, 0:256], [Bp], [Bv])
                COPY(S, DVE, k_tok[:, c, :], ps[7][:, 0:128], [Bps[7]], [Bk])
                if _l2 > 0.7:
                    MM(S, ps[2][:, 0:256], alrT[:, cs], WG[:, :], True, True, [Balr, BWG], [Bps[2]])
                    TT(S, DVE, lsp[:, c, :], ps[2][:, 0:256], bgb[:], ALU.add, [Bps[2], Bbg], [Blsp])
            if _l2 > 0.8:
                ACTV(S, lsp[:], lsp[:], AF.Exp, [Blsp], [Blsp], scale=-1.0)
                ACTV(S, lsp[:], lsp[:], AF.Ln, [Blsp], [Blsp], bias=1.0)
            yi = 0
            _lvl = 9
            seen = set()

            def gla_iter(d, c):
                CM = M["LE64"] if d == 0 else M["GE64"]
                RM = M["GT64"] if d == 0 else M["LT64"]
                blocks = (0, 1) if d == 0 else (1, 0)
                eq, ek, er, qin, kin, kst, qinh, attT = eq_[d], ek_[d], er_[d], qin_[d], kin_[d], kst_[d], qinh_[d], attT_[d]
                Beq, Bek, Ber, Bqin, Bkin, Bkst, Bqinh, Batt = Beq_[d], Bek_[d], Ber_[d], Bqin_[d], Bkin_[d], Bkst_[d], Bqinh_[d], Batt_[d]
                Sst, Sb0, Sb1, BS, BSb0, BSb1 = Sst_[d], Sb0_[d], Sb1_[d], BS_[d], BSb0_[d], BSb1_[d]
                pP, BpP = ps[2 + 3 * d], Bps[2 + 3 * d]
                pA, BpA = ps[3 + 3 * d], Bps[3 + 3 * d]
                pO, BpO = ps[4 + 3 * d], Bps[4 + 3 * d]
                cs = slice(c * 128, (c + 1) * 128)
                ld = lsp[:, c, d * 128:(d + 1) * 128]
                MM(S, pP[:, 0:128], ld, CM[:], True, True, [Blsp, Bc], [BpP])
                yield
                MM(S, pP[:, 128:256], RM[:], ld, True, True, [Blsp, Bc], [BpP])
                yield
                ACTV(S, eq[:], pP[:, 0:128], AF.Exp, [BpP], [Beq], scale=-1.0 / 16)
                yield
                ACTV(S, ek[:], pP[:, 0:128], AF.Exp, [BpP], [Bek], scale=1.0 / 16)
                yield
                ACTV(S, er[:], pP[:, 128:256], AF.Exp, [BpP], [Ber], scale=-1.0 / 16)
                yield
                STT(S, qin[:], qT[:, cs], QS, eq[:], ALU.mult, ALU.mult, [BqT, Beq], [Bqin])
                yield
                TT(S, DVE, kin[:], kT[:, cs], ek[:], ALU.mult, [BkT, Bek], [Bkin])
                yield
                for bi_ in range(2):
                    STT(S, kst[bi_][:], k_tok[:, c, :], M["BD"][:, 64 * bi_:64 * bi_ + 1], er[:], ALU.mult, ALU.mult, [Bk, Ber, Bc], [Bkst])
                    hs_ = slice(64 * bi_, 64 * bi_ + 64)
                    COPY(S, POOL, qinh[bi_][:, hs_], qin[:, hs_], [Bqin], [Bqinh])
                MM(S, pA[:, 0:128], kin[:], qin[:], True, True, [Bkin, Bqin], [BpA])
                yield
                TT(S, DVE, attT[:], pA[:, 0:128], CM[:], ALU.mult, [BpA, Bc], [Batt])
                yield
                MM(S, pO[:, 0:256], attT[:], v_tok[:, c, :], True, False, [Batt, Bv], [BpO])
                yield
                for bi, blk in enumerate(blocks):
                    Sb, BSb = (Sb0, BSb0) if bi == 0 else (Sb1, BSb1)
                    MM(S, pO[:, 0:256], qinh[blk][:], Sb[:], False, bi == 1, [Bqinh, BSb], [BpO])
                    MM(S, pA[:, 128:384], kst[blk][:], v_tok[:, c, :], True, True, [Bkst, Bv], [BpA])
                    ecol = (blk * 64 + 63) if d == 0 else (blk * 64)
                    STT(S, Sst[:], Sst[:], eq[:, ecol:ecol + 1], pA[:, 128:384], ALU.mult, ALU.add, [BS, Beq, BpA], [BS])
                    if bi == 0:
                        COPY(S, ACT, Sb1[:], Sst[:], [BS], [BSb1])
                    else:
                        COPY(S, ACT, Sb0[:], Sst[:], [BS], [BSb0])
                if c not in seen:
                    seen.add(c)
                    COPY(S, ACT if d == 0 else DVE, oacc[:, c, :], pO[:, 0:256], [BpO], [Boacc[c]])
                else:
                    TT(S, DVE, oacc[:, c, :], oacc[:, c, :], pO[:, 0:256], ALU.add, [Boacc[c], BpO], [Boacc[c]])
                    ACTV(S, osq[:], oacc[:, c, :], AF.Square, [Boacc[c]], [Bosq])
                    S.dve(lambda e: e.reduce_sum(out=ssq[:], in_=osq[:], axis=AX.X), [Bosq], [Bssq])
                    TSC(S, DVE, ssq[:], ssq[:], 1.0 / 256, EPS, ALU.mult, ALU.add, [Bssq], [Bssq])
                    ACTV(S, ssq[:], ssq[:], AF.Sqrt, [Bssq], [Bssq])
                    S.dve(lambda e: e.reciprocal(out=ssq[:], in_=ssq[:]), [Bssq], [Bssq])
                    STT(S, on[:], oacc[:, c, :], ssq[:, 0:1], gng[:], ALU.mult, ALU.mult, [Boacc[c], Bssq, Bgng], [Bon])
                    pf, Bpf = ps[1], Bps[1]
                    for i in range(2):
                        TR(S, pf[:, i * 128:(i + 1) * 128], on[:, i * 128:(i + 1) * 128], M["ID"][:], [Bon, Bc], [Bpf])
                    for i in range(2):
                        if odd and c >= 2:
                            cc0 = (c - 2) * 4
                            ov = yst[yi][:, i, CTX:TB].rearrange("p (r c) -> p c r", c=64)[:, cc0:cc0 + 4, :]
                            i0 = pf[:, i * 128:(i + 1) * 128].rearrange("p (c r) -> p c r", r=32)
                            i1 = sgT[:, i, cs].rearrange("p (c r) -> p c r", r=32)
                        else:
                            ov, i0, i1 = yst[yi][:, i, cs], pf[:, i * 128:(i + 1) * 128], sgT[:, i, cs]
                        TT(S, DVE, ov, i0, i1, ALU.mult, [Bpf, Bsg], [Byst[yi]])

            for d in range(2):
                S.pool(lambda e, d=d: e.memset(Sst_[d][:], 0.0), (), [BS_[d]])
                S.pool(lambda e, d=d: e.memset(Sb0_[d][:], 0.0), (), [BSb0_[d]])
            def gla_dir(d):
                for c in (FWD_CHUNKS if d == 0 else REV_CHUNKS):
                    yield from gla_iter(d, c)

            run_interleaved([gla_dir(0), gla_dir(1)])
            if _lvl >= 3:
                S.dma(SP, g.Y[2][:, 2 * hd:2 * hd + 2, b * TB:(b + 1) * TB], yst[yi][:], [Byst[yi]], ())
        S.emit()
    nc.all_engine_barrier()


def build_program(stages=None):
    nc = bass.Bass("TRN2", target_bir_lowering=False)
    g = G()
    declare_io(nc, g)
    with contextlib.ExitStack() as st:
        init_gsync(nc, st)
        sb = lambda nm, s, d: st.enter_context(nc.sbuf_tensor(uname(nm), s, d))
        g.constf = sb("constf", [128, NCF], F32)
        g.modT = sb("modT", [128, NL * 9 * 8 * 4], F32)
        g.masks = {nm: sb("mask_" + nm, [128, 128], F32) for nm in
                   ("ONES", "LE", "GT", "LT", "GE", "ID", "BD", "LE64", "GT64", "LT64", "GE64")}
        g.onesb = sb("onesb", [128, 128], BF16)
        g.rb_dtb = sb("rb_dtb", [128, NL * 32], F32)
        g.rb_A = sb("rb_A", [128, NL * 32], F32)
        g.rb_D = sb("rb_D", [128, NL * 32], F32)
        g.rb_Ds = sb("rb_Ds", [128, NL * 16], F32)
        if stages is None:
            stages = ["const", "p0"]
            for l in range(NL):
                stages += [("ffn", l, 0), ("mix", l), ("merge", l), ("ffn", l, 1)]
            stages += ["final"]
        for sg in stages:
            if sg == "const":
                phase_const(nc, g)
            elif sg == "p0":
                phase_p0(nc, g)
            elif sg == "final":
                phase_final(nc, g)
            elif sg[0] == "ffn":
                phase_ffn(nc, g, sg[1], sg[2], skip_ctx=(sg[1] == NL - 1 and sg[2] == 1))
            elif sg[0] == "mix":
                phase_mix(nc, g, sg[1], sg[2] if len(sg) > 2 else ("ssd", "lru", "gla"))
            elif sg[0] == "merge":
                phase_merge(nc, g, sg[1], skip_ctx=(sg[1] == NL - 1))
    return nc


_NC_CACHE = {}


def kernel(**inputs):
    from concourse.bass_utils import run_bass_kernel_spmd
    if "nc" not in _NC_CACHE:
        _NC_CACHE["nc"] = build_program()
    nc = _NC_CACHE["nc"]
    ncores = 8
    in_maps = []
    for i in range(ncores):
        m = {}
        for nm in INPUT_NAMES:
            a = np.asarray(inputs[nm], dtype=np.float32)
            if nm in ("x", "c", "ctx"):
                a = a[i * NB:(i + 1) * NB]
            elif nm == "c_ctx":
                a = a.reshape(1, D)
            m[nm] = np.ascontiguousarray(a)
        in_maps.append(m)
    res = run_bass_kernel_spmd(nc, in_maps, core_ids=list(range(ncores)))
    return np.concatenate([np.asarray(r["out"]) for r in res.results], axis=0).astype(np.float32)
```

```python
import numpy as np
import concourse.bass as bass
import concourse.mybir as mybir
from concourse.ap import AP

F32 = mybir.dt.float32
BF16 = mybir.dt.bfloat16
AF = mybir.ActivationFunctionType
ALU = mybir.AluOpType
AX = mybir.AxisListType

PE, ACT, DVE, POOL, SP = "pe", "act", "dve", "pool", "sp"
COMPUTE = (PE, ACT, DVE, POOL)
NDMASEM = 6


class Buf:
    __slots__ = ("name", "last_w", "readers")

    def __init__(self, name=""):
        self.name = name
        self.last_w = None
        self.readers = {}


class Op:
    __slots__ = ("eng", "fn", "deps", "idx", "dma", "signal", "sigval", "sem", "semval", "tag")


class Sched:
    def __init__(self, nc):
        self.nc = nc
        self.ops = {e: [] for e in (PE, ACT, DVE, POOL, SP)}
        self.n_dma = {SP: 0, POOL: 0, ACT: 0}

    def add(self, eng, fn, reads=(), writes=(), dma=False, tag=None):
        op = Op()
        op.eng, op.fn, op.dma, op.signal, op.tag = eng, fn, dma, False, tag
        op.idx = len(self.ops[eng])
        deps = {}

        def dep(d, kind):
            if d is None or d is op:
                return
            if d.eng == eng and not d.dma:
                if eng == PE or eng == SP:
                    return
                if kind == "WAR":
                    return
            key = id(d) if d.dma else d.eng
            cur = deps.get(key)
            if cur is None or (not d.dma and d.idx > cur.idx):
                deps[key] = d

        for b in reads:
            dep(b.last_w, "RAW")
        for b in writes:
            dep(b.last_w, "WAW")
            for r in b.readers.values():
                if isinstance(r, list):
                    for rr in r:
                        dep(rr, "WAR")
                else:
                    dep(r, "WAR")
        for b in reads:
            if dma:
                b.readers.setdefault("dma", []).append(op)
            else:
                b.readers[eng] = op
        for b in writes:
            b.last_w = op
            b.readers = {}
        op.deps = list(deps.values())
        for d in op.deps:
            d.signal = True
        self.ops[eng].append(op)
        return op

    def pe(self, fn, reads=(), writes=()):
        return self.add(PE, fn, reads, writes)

    def act(self, fn, reads=(), writes=()):
        return self.add(ACT, fn, reads, writes)

    def dve(self, fn, reads=(), writes=()):
        return self.add(DVE, fn, reads, writes)

    def pool(self, fn, reads=(), writes=()):
        return self.add(POOL, fn, reads, writes)

    def dma(self, q, out, in_, reads=(), writes=(), **kw):
        return self.add(q, lambda e: e.dma_start(out=out, in_=in_, **kw), reads, writes, dma=True)

    def emit(self, final_wait_all_dma=True):
        nc = self.nc
        gs = GSYNC[0]
        esem, dsem = gs["esem"], gs["dsem"]
        for e in COMPUTE:
            c = gs["ebase"][e]
            for op in self.ops[e]:
                if op.dma:
                    continue
                if op.signal:
                    c += 1
                    op.sigval = c
            gs["ebase"][e] = c
        for q in (SP, POOL, ACT):
            k = gs["dk"][q]
            vals = gs["dvals"][q]
            for op in self.ops[q]:
                if op.dma:
                    s = k % NDMASEM
                    k += 1
                    vals[s] += 16
                    op.sem = dsem[q][s]
                    op.semval = vals[s]
            gs["dk"][q] = k
        engobj = {PE: "tensor", ACT: "scalar", DVE: "vector", POOL: "gpsimd", SP: "sync"}
        with nc.Block() as block:

            def run(e, eng):
                waited = {}

                def wait(sem, val):
                    k = id(sem)
                    if waited.get(k, 0) >= val:
                        return
                    waited[k] = val
                    eng.wait_ge(sem, val)

                for op in self.ops[e]:
                    for d in op.deps:
                        if d.dma:
                            wait(d.sem, d.semval)
                        else:
                            wait(esem[d.eng], d.sigval)
                    if op.dma:
                        if op.semval > 16:
                            wait(op.sem, op.semval - 16)
                        ins = op.fn(eng)
                        ins.then_inc(op.sem, 16)
                    else:
                        ins = op.fn(eng)
                        if op.signal:
                            ins.then_inc(esem[e], 1)
                if final_wait_all_dma:
                    last = {}
                    for op in self.ops[e]:
                        if op.dma:
                            last[id(op.sem)] = (op.sem, op.semval)
                    for sem, val in last.values():
                        wait(sem, val)

            for e in (PE, ACT, DVE, POOL, SP):
                if not self.ops[e]:
                    continue
                getattr(block, engobj[e])(lambda eng, e=e: run(e, eng))


GSYNC = [None]


def init_gsync(nc, st):
    gs = {"esem": {e: st.enter_context(nc.semaphore("s_" + e)) for e in COMPUTE}, "dsem": {},
          "ebase": {e: 0 for e in COMPUTE}, "dk": {}, "dvals": {}}
    for q in (SP, POOL, ACT):
        gs["dsem"][q] = [st.enter_context(nc.semaphore("d_%s%d" % (q, i))) for i in range(NDMASEM)]
        gs["dk"][q] = 0
        gs["dvals"][q] = [0] * NDMASEM
    GSYNC[0] = gs

import contextlib

NL, D = 4, 1024
NB = 2
CTX, SEQ = 256, 2048
TB = CTX + SEQ
TT_ = NB * TB
DFF = 2816
ALPHA = 8.0 ** 0.25
EPS = 1e-5
EPSP = EPS / (ALPHA * ALPHA)
IN_TOTAL = 11328
OFF_Z, OFF_XBC, OFF_DT, OFF_LX, OFF_LG = 0, 1024, 3072, 3104, 4128
OFF_Q, OFF_K, OFF_V, OFF_G, OFF_ALR, OFF_GATE = 5152, 5664, 6176, 7200, 8224, 8256
TS_ = 256
TILES = []
for _b in range(NB):
    TILES.append((_b, 0, 256, 2))
    for _i in range(SEQ // TS_):
        TILES.append((_b, CTX + _i * TS_, TS_, _b))


def MM(S, out, lhsT, rhs, start, stop, R, W):
    return S.pe(lambda e: e.matmul(out, lhsT, rhs, start=start, stop=stop), R, W)


def TR(S, out, in_, ident, R, W):
    return S.pe(lambda e: e.transpose(out, in_, ident), R, W)


def ACTV(S, out, in_, func, R, W, bias=None, scale=None):
    kw = {}
    if bias is not None:
        kw["bias"] = bias
    if scale is not None:
        kw["scale"] = scale
    return S.act(lambda e: e.activation(out=out, in_=in_, func=func, **kw), R, W)


def TT(S, eng, out, in0, in1, op, R, W):
    return S.add(eng, lambda e: e.tensor_tensor(out=out, in0=in0, in1=in1, op=op), R, W)


def TSC(S, eng, out, in0, s1, s2, op0, op1, R, W):
    if s2 is None:
        return S.add(eng, lambda e: e.tensor_scalar(out=out, in0=in0, scalar1=s1, scalar2=None, op0=op0), R, W)
    return S.add(eng, lambda e: e.tensor_scalar(out=out, in0=in0, scalar1=s1, scalar2=s2, op0=op0, op1=op1), R, W)


def STT(S, out, in0, scalar, in1, op0, op1, R, W):
    return S.dve(lambda e: e.scalar_tensor_tensor(out=out, in0=in0, scalar=scalar, in1=in1, op0=op0, op1=op1), R, W)


def COPY(S, eng, out, in_, R, W):
    if eng == ACT:
        return S.act(lambda e: e.activation(out=out, in_=in_, func=AF.Identity), R, W)
    return S.add(eng, lambda e: e.tensor_copy(out=out, in_=in_), R, W)


def bc(ap, shape):
    return ap.to_broadcast(list(shape))


DEBUG_OUT = [False]
_UID = [0]


def uname(n):
    _UID[0] += 1
    return '%s_u%d' % (n, _UID[0])


class G:
    pass


def declare_io(nc, g):
    def din(name, shape):
        return nc.dram_tensor(name, list(shape), F32, kind="ExternalInput").ap()
    g.x = din("x", [NB, SEQ, D])
    g.c = din("c", [NB, D])
    g.ctx = din("ctx", [NB, CTX, D])
    g.c_ctx = din("c_ctx", [1, D])
    g.w_ada = din("w_ada", [NL, D, 9 * D])
    g.b_ada = din("b_ada", [NL, 9 * D])
    g.ln_g = din("ln_g", [NL, 3, D])
    g.ln_b = din("ln_b", [NL, 3, D])
    g.ffn_w_up = din("ffn_w_up", [NL, 2, D, 2 * DFF])
    g.ffn_w_down = din("ffn_w_down", [NL, 2, DFF, D])
    g.w_in = din("w_in", [NL, D, IN_TOTAL])
    g.ssd_conv_w = din("ssd_conv_w", [NL, 4, 2048])
    g.ssd_conv_b = din("ssd_conv_b", [NL, 2048])
    g.ssd_dt_bias = din("ssd_dt_bias", [NL, 2, 16])
    g.ssd_a_log = din("ssd_a_log", [NL, 2, 16])
    g.ssd_d = din("ssd_d", [NL, 2, 16])
    g.ssd_norm_g = din("ssd_norm_g", [NL, 1024])
    g.lru_conv_w = din("lru_conv_w", [NL, 4, 1024])
    g.lru_conv_b = din("lru_conv_b", [NL, 1024])
    g.lru_w_a = din("lru_w_a", [NL, 2, 16, 64, 64])
    g.lru_b_a = din("lru_b_a", [NL, 2, 1024])
    g.lru_w_x = din("lru_w_x", [NL, 2, 16, 64, 64])
    g.lru_b_x = din("lru_b_x", [NL, 2, 1024])
    g.lru_lam = din("lru_lam", [NL, 2, 1024])
    g.gla_w_gate = din("gla_w_gate", [NL, 2, 16, 512])
    g.gla_b_gate = din("gla_b_gate", [NL, 2, 512])
    g.gla_norm_g = din("gla_norm_g", [NL, 256])
    g.w_branch = din("w_branch", [NL, 3, 1024, 1024])
    g.w_out = din("w_out", [NL, 1024, 1024])
    g.out = nc.dram_tensor("out", [NB, SEQ, D], F32, kind="ExternalOutput").ap()
    kd = "ExternalOutput" if DEBUG_OUT[0] else "Internal"
    g.HT = nc.dram_tensor("HT", [128, 8, TT_], F32, kind=kd).ap()
    g.Y = [nc.dram_tensor("Y%d" % i, [128, 8, TT_], BF16, kind=kd).ap() for i in range(3)]


INPUT_NAMES = ["x", "c", "ctx", "c_ctx", "w_ada", "b_ada", "ln_g", "ln_b", "ffn_w_up", "ffn_w_down", "w_in",
               "ssd_conv_w", "ssd_conv_b", "ssd_dt_bias", "ssd_a_log", "ssd_d", "ssd_norm_g", "lru_conv_w",
               "lru_conv_b", "lru_w_a", "lru_b_a", "lru_w_x", "lru_b_x", "lru_lam", "gla_w_gate", "gla_b_gate",
               "gla_norm_g", "w_branch", "w_out"]

CF = {}
_o = 0
for _n, _sz in [("ln_g", NL * 3 * 8), ("ln_b", NL * 3 * 8), ("bada", NL * 9 * 8), ("scw", NL * 4 * 16),
                ("scb", NL * 16), ("sng", NL * 8), ("lcw", NL * 4 * 8), ("lcb", NL * 8), ("lba", NL * 16),
                ("lbx", NL * 16), ("llam", NL * 16), ("c", 16), ("cctx", 8), ("lsp8", NL * 16), ("lsp16", NL * 16),
                ("lsp24", NL * 16)]:
    CF[_n] = _o
    _o += _sz
NCF = _o


def mod_ap(g, l, j, kc, cls):
    i = (((l * 9 + j) * 8) + kc) * 4 + cls
    return g.modT[:, i:i + 1]


def cf(g, name, idx):
    o = CF[name] + idx
    return g.constf[:, o:o + 1]


def phase_const(nc, g):
    with contextlib.ExitStack() as st:
        sb = lambda n, s, d: st.enter_context(nc.sbuf_tensor(uname(n), s, d))
        rowbuf = [sb("rowbuf%d" % i, [128, 128], F32) for i in range(2)]
        wbuf = [sb("wadab%d" % i, [128, 8, 1024], BF16) for i in range(2)]
        sT = sb("sT", [128, 8, 4], BF16)
        tmpm = sb("tmpm", [128, 128], F32)
        pst = [st.enter_context(nc.psum_tensor(uname("pst%d" % i), [128, 512], F32)) for i in range(4)]
        S = Sched(nc)
        Bm = Buf("masks")
        Brow = [Buf(), Buf()]
        Bw = [Buf(), Buf()]
        Bps = [Buf() for _ in range(4)]
        Bcf, BsT, Bmod, Btm = Buf("cf"), Buf(), Buf("mod"), Buf()
        g.Bconst = Buf("constall")
        M = g.masks

        def amask(dst, cm, step, base, op):
            S.pool(lambda e: e.memset(dst, 1.0), (), [Bm])
            S.pool(lambda e: e.affine_select(out=dst, in_=dst, compare_op=op, fill=0.0, base=base,
                                             pattern=[[step, 128]], channel_multiplier=cm), [Bm], [Bm])

        S.pool(lambda e: e.memset(M["ONES"][:], 1.0), (), [Bm])
        S.pool(lambda e: e.memset(g.onesb[:], 1.0), (), [Bm])
        amask(M["LE"][:], -1, 1, 0, ALU.is_ge)
        amask(M["GT"][:], 1, -1, 0, ALU.is_gt)
        amask(M["LT"][:], -1, 1, 0, ALU.is_gt)
        amask(M["GE"][:], 1, -1, 0, ALU.is_ge)
        amask(M["ID"][:], 1, -1, 0, ALU.is_equal)
        S.pool(lambda e: e.memset(M["BD"][:], 0.0), (), [Bm])
        S.pool(lambda e: e.memset(M["BD"][0:64, 0:64], 1.0), [Bm], [Bm])
        S.pool(lambda e: e.memset(M["BD"][64:128, 64:128], 1.0), [Bm], [Bm])
        for nm in ("LE", "GT", "LT", "GE"):
            TT(S, POOL, M[nm + "64"][:], M[nm][:], M["BD"][:], ALU.mult, [Bm], [Bm])

        items = [
            ("ln_g", g.ln_g.rearrange("l i (k p) -> (l i k) p", p=128)),
            ("ln_b", g.ln_b.rearrange("l i (k p) -> (l i k) p", p=128)),
            ("bada", g.b_ada.rearrange("l (j p) -> (l j) p", p=128)),
            ("scw", g.ssd_conv_w.rearrange("l k (c p) -> (l k c) p", p=128)),
            ("scb", g.ssd_conv_b.rearrange("l (c p) -> (l c) p", p=128)),
            ("sng", g.ssd_norm_g.rearrange("l (c p) -> (l c) p", p=128)),
            ("lcw", g.lru_conv_w.rearrange("l k (c p) -> (l k c) p", p=128)),
            ("lcb", g.lru_conv_b.rearrange("l (c p) -> (l c) p", p=128)),
            ("lba", g.lru_b_a.rearrange("l d (c p) -> (l d c) p", p=128)),
            ("lbx", g.lru_b_x.rearrange("l d (c p) -> (l d c) p", p=128)),
            ("llam", g.lru_lam.rearrange("l d (c p) -> (l d c) p", p=128)),
            ("c", g.c.rearrange("b (c p) -> (b c) p", p=128)),
            ("cctx", g.c_ctx.rearrange("b (c p) -> (b c) p", p=128)),
        ]
        k = 0
        for nm, ap in items:
            R = ap.shape[0]
            r0 = 0
            while r0 < R:
                nr = min(128, R - r0)
                i = k % 2
                k += 1
                S.dma(SP, rowbuf[i][0:nr, :], ap[r0:r0 + nr, :], (), [Brow[i]])
                TR(S, pst[i][:, 0:nr], rowbuf[i][0:nr, :], M["ID"][0:nr, 0:nr], [Brow[i], Bm], [Bps[i]])
                o = CF[nm] + r0
                COPY(S, DVE, g.constf[:, o:o + nr], pst[i][:, 0:nr], [Bps[i]], [Bcf])
                r0 += nr
        n16 = NL * 16
        lam = g.constf[:, CF["llam"]:CF["llam"] + n16]
        ACTV(S, tmpm[:, 0:n16], lam, AF.Exp, [Bcf], [Btm], scale=-1.0)
        ACTV(S, tmpm[:, 0:n16], tmpm[:, 0:n16], AF.Ln, [Btm], [Btm], bias=1.0)
        for nm, sc in (("lsp8", -8.0), ("lsp16", -16.0), ("lsp24", -16.0 / 24.0)):
            TSC(S, DVE, g.constf[:, CF[nm]:CF[nm] + n16], tmpm[:, 0:n16], sc, None, ALU.mult, None, [Btm], [Bcf])
        S.dma(SP, g.rb_dtb[:], g.ssd_dt_bias.rearrange("l d h -> (l d h)").partition_broadcast(128), (), [Bcf])
        S.dma(SP, g.rb_A[:], g.ssd_a_log.rearrange("l d h -> (l d h)").partition_broadcast(128), (), [Bcf])
        S.dma(SP, g.rb_D[:], g.ssd_d.rearrange("l d h -> (l d h)").partition_broadcast(128), (), [Bcf])
        ACTV(S, g.rb_A[:], g.rb_A[:], AF.Exp, [Bcf], [Bcf])
        TSC(S, DVE, g.rb_A[:], g.rb_A[:], -1.0, None, ALU.mult, None, [Bcf], [Bcf])
        rbD = g.rb_D[:].rearrange("p (l d h) -> p l d h", l=NL, d=2)
        TT(S, DVE, g.rb_Ds[:].rearrange("p (l h) -> p l h", l=NL), rbD[:, :, 0, :], rbD[:, :, 1, :], ALU.add, [Bcf], [Bcf])
        S.pool(lambda e: e.memset(sT[:], 0.0), (), [BsT])
        for b in range(NB):
            o = CF["c"] + b * 8
            ACTV(S, sT[:, :, b], g.constf[:, o:o + 8], AF.Silu, [Bcf, BsT], [BsT])
        o = CF["cctx"]
        ACTV(S, sT[:, :, 2], g.constf[:, o:o + 8], AF.Silu, [Bcf, BsT], [BsT])
        k = 0
        for l in range(NL):
            wv = g.w_ada[l].rearrange("(k p) n -> p k n", p=128)
            for j in range(9):
                i = k % 2
                pi = 2 + (k % 2)
                k += 1
                S.dma(POOL, wbuf[i][:], wv[:, :, j * 1024:(j + 1) * 1024], (), [Bw[i]])
                for oc in range(8):
                    for kc in range(8):
                        MM(S, pst[pi][:, oc * 4:oc * 4 + 4], wbuf[i][:, kc, oc * 128:(oc + 1) * 128], sT[:, kc, :],
                           kc == 0, kc == 7, [Bw[i], BsT], [Bps[pi]])
                mo = ((l * 9 + j) * 8) * 4
                bo = CF["bada"] + (l * 9 + j) * 8
                TT(S, DVE, g.modT[:, mo:mo + 32].rearrange("p (k c) -> p k c", c=4),
                   pst[pi][:, 0:32].rearrange("p (k c) -> p k c", c=4),
                   bc(g.constf[:, bo:bo + 8].unsqueeze(2), [128, 8, 4]), ALU.add, [Bps[pi], Bcf], [Bmod])
        mv = g.modT[:].rearrange("p (l j r) -> p l j r", l=NL, j=9)
        for j in (1, 4, 7):
            TSC(S, DVE, mv[:, :, j, :], mv[:, :, j, :], 1.0, None, ALU.add, None, [Bmod], [Bmod])
        for j, sc in ((2, 0.5 / ALPHA), (8, 0.5 / ALPHA), (5, 1.0 / ALPHA)):
            TSC(S, DVE, mv[:, :, j, :], mv[:, :, j, :], sc, None, ALU.mult, None, [Bmod], [Bmod])
        S.emit()
    nc.all_engine_barrier()


def phase_p0(nc, g):
    with contextlib.ExitStack() as st:
        sb = lambda n, s, d: st.enter_context(nc.sbuf_tensor(uname(n), s, d))
        tin = [sb("p0in%d" % i, [128, 1024], F32) for i in range(3)]
        stg = [sb("p0st%d" % i, [128, 8, 512], F32) for i in range(2)]
        ps = [st.enter_context(nc.psum_tensor(uname("p0ps%d" % i), [128, 512], F32)) for i in range(4)]
        S = Sched(nc)
        Bin = [Buf() for _ in range(3)]
        Bst = [Buf(), Buf()]
        Bps = [Buf() for _ in range(4)]
        Bc = g.Bconst
        k = 0
        gi = 0
        for b in range(NB):
            groups = [(g.ctx[b], 0, 256)] + [(g.x[b, i * 512:(i + 1) * 512, :], CTX + i * 512, 512) for i in range(4)]
            for src, s0, n in groups:
                sg = gi % 2
                gi += 1
                for t in range(n // 128):
                    i = k % 3
                    S.dma(SP, tin[i][:], src[t * 128:(t + 1) * 128, :], (), [Bin[i]])
                    for half in range(2):
                        pi = (2 * k + half) % 4
                        for q in range(4):
                            kc = half * 4 + q
                            TR(S, ps[pi][:, q * 128:(q + 1) * 128], tin[i][:, kc * 128:(kc + 1) * 128], g.masks["ID"][:],
                               [Bin[i], Bc], [Bps[pi]])
                        COPY(S, ACT if half == 0 else DVE, stg[sg][:, half * 4:half * 4 + 4, t * 128:(t + 1) * 128],
                             ps[pi][:].rearrange("p (q t) -> p q t", q=4), [Bps[pi]], [Bst[sg]])
                    k += 1
                col = b * TB + s0
                S.dma(SP, g.HT[:, :, col:col + n], stg[sg][:, :, 0:n], [Bst[sg]], ())
        S.emit()
    nc.all_engine_barrier()


def phase_final(nc, g):
    with contextlib.ExitStack() as st:
        sb = lambda n, s, d: st.enter_context(nc.sbuf_tensor(uname(n), s, d))
        hin = [sb("pfin%d" % i, [128, 8, 512], F32) for i in range(2)]
        to = [sb("pfo%d" % i, [128, 1024], F32) for i in range(3)]
        ps = [st.enter_context(nc.psum_tensor(uname("pfps%d" % i), [128, 512], F32)) for i in range(4)]
        S = Sched(nc)
        Bin = [Buf(), Buf()]
        Bo = [Buf() for _ in range(3)]
        Bps = [Buf() for _ in range(4)]
        Bc = g.Bconst
        k = 0
        gi = 0
        for b in range(NB):
            for i4 in range(4):
                sg = gi % 2
                gi += 1
                col = b * TB + CTX + i4 * 512
                S.dma(SP, hin[sg][:], g.HT[:, :, col:col + 512], (), [Bin[sg]])
                for t in range(4):
                    oi = k % 3
                    for half in range(2):
                        pi = (2 * k + half) % 4
                        for q in range(4):
                            kc = half * 4 + q
                            TR(S, ps[pi][:, q * 128:(q + 1) * 128], hin[sg][:, kc, t * 128:(t + 1) * 128], g.masks["ID"][:],
                               [Bin[sg], Bc], [Bps[pi]])
                        COPY(S, ACT if half == 0 else DVE, to[oi][:, half * 512:(half + 1) * 512], ps[pi][:], [Bps[pi]], [Bo[oi]])
                    r0 = i4 * 512 + t * 128
                    S.dma(SP, g.out[b, r0:r0 + 128, :], to[oi][:], [Bo[oi]], ())
                    k += 1
        S.emit()
    nc.all_engine_barrier()


def load_weight_cast(S, dst3, src2, nk, ncols, Bw, piece=1024):
    sv = src2.rearrange("(k p) n -> p k n", p=128)
    c0 = 0
    p = 0
    while c0 < ncols:
        cn = min(piece, ncols - c0)
        for kc in range(nk):
            S.dma(POOL, dst3[:, kc, c0:c0 + cn], sv[:, kc, c0:c0 + cn], (), [Bw[kc][p]])
        c0 += cn
        p += 1


def ln_part1(S, g, zt, Bz, n, W):
    zbf, sq, Bzs = W["zbf"], W["sq"], W["Bzs"]
    COPY(S, ACT, zbf[:, :, 0:n], zt[:, :, 0:n], [Bz], [Bzs])
    ACTV(S, sq[:, :, 0:n], zt[:, :, 0:n], AF.Square, [Bz], [Bzs])


def ln_part2(S, g, l, i, zt, Bz, n, W):
    zbf, sq, Bzs = W["zbf"], W["sq"], W["Bzs"]
    psm, psq, Bpm, Bpq = W["psm"], W["psq"], W["Bpm"], W["Bpq"]
    mean, msq, var, Bsm = W["mean"], W["msq"], W["var"], W["Bsm"]
    Bc = g.Bconst
    for kc in range(8):
        MM(S, psm[:, 0:n], g.onesb[:], zbf[:, kc, 0:n], kc == 0, kc == 7, [Bzs, Bc], [Bpm])
    for kc in range(8):
        MM(S, psq[:, 0:n], g.onesb[:], sq[:, kc, 0:n], kc == 0, kc == 7, [Bzs, Bc], [Bpq])
    ACTV(S, mean[:, 0:n], psm[:, 0:n], AF.Identity, [Bpm], [Bsm], scale=1.0 / 1024)
    ACTV(S, msq[:, 0:n], psm[:, 0:n], AF.Square, [Bpm], [Bsm], scale=1.0 / 1024)
    STT(S, var[:, 0:n], psq[:, 0:n], 1.0 / 1024, msq[:, 0:n], ALU.mult, ALU.subtract, [Bpq, Bsm], [Bsm])
    TSC(S, DVE, var[:, 0:n], var[:, 0:n], EPSP, None, ALU.add, None, [Bsm], [Bsm])
    ACTV(S, var[:, 0:n], var[:, 0:n], AF.Sqrt, [Bsm], [Bsm])
    S.dve(lambda e: e.reciprocal(out=var[:, 0:n], in_=var[:, 0:n]), [Bsm], [Bsm])
    TT(S, DVE, zt[:, :, 0:n], zt[:, :, 0:n], bc(mean[:, 0:n].unsqueeze(1), [128, 8, n]), ALU.subtract, [Bz, Bsm], [Bz])
    TT(S, DVE, zt[:, :, 0:n], zt[:, :, 0:n], bc(var[:, 0:n].unsqueeze(1), [128, 8, n]), ALU.mult, [Bz, Bsm], [Bz])
    for kc in range(8):
        ACTV(S, zt[:, kc, 0:n], zt[:, kc, 0:n], AF.Identity, [Bz, Bc], [Bz],
             bias=cf(g, "ln_b", (l * 3 + i) * 8 + kc), scale=cf(g, "ln_g", (l * 3 + i) * 8 + kc))


def phase_ffn(nc, g, l, j, skip_ctx):
    n = TS_
    with contextlib.ExitStack() as st:
        sb = lambda nm, s, d: st.enter_context(nc.sbuf_tensor(uname(nm), s, d))
        wup = sb("wup", [128, 8, 2 * DFF], BF16)
        wdn = sb("wdn", [128, 22, D], BF16)
        hb = [sb("ffh%d" % i, [128, 8, n], F32) for i in range(3)]
        ub = [sb("ffu%d" % i, [128, 8, n], BF16) for i in range(2)]
        hid = sb("ffhid", [128, 22, n], BF16)
        sil = [sb("ffsil%d" % i, [128, n], F32) for i in range(2)]
        zbf = sb("ffzbf", [128, 8, n], BF16)
        sq = sb("ffsq", [128, 8, n], BF16)
        mean = sb("ffmean", [128, n], F32)
        msq = sb("ffmsq", [128, n], F32)
        var = sb("ffvar", [128, n], F32)
        ps = [st.enter_context(nc.psum_tensor(uname("ffps%d" % i), [128, 512], F32)) for i in range(8)]
        S = Sched(nc)
        Bc = g.Bconst
        Bwu = [[Buf() for _ in range(4)] for _ in range(8)]
        Bwd = [[Buf()] for _ in range(22)]
        Bh = [Buf(), Buf(), Buf()]
        Bu = [Buf(), Buf()]
        Bhid, Bzs, Bsm = Buf(), Buf(), Buf()
        Bsil = [Buf(), Buf()]
        Bps = [Buf() for _ in range(8)]
        load_weight_cast(S, wup, g.ffn_w_up[l, j], 8, 2 * DFF, Bwu, piece=1408)
        load_weight_cast(S, wdn, g.ffn_w_down[l, j], 22, D, Bwd)
        W = dict(zbf=zbf, sq=sq, Bzs=Bzs, psm=ps[6], psq=ps[7], Bpm=Bps[6], Bpq=Bps[7], mean=mean, msq=msq, var=var, Bsm=Bsm)
        tiles = [t for t in TILES if not (skip_ctx and t[3] == 2)]
        NT = len(tiles)
        pkc = [0]

        def stA(i):
            b, s0, nn, cls = tiles[i]
            col = b * TB + s0
            S.dma(SP, hb[i % 3][:], g.HT[:, :, col:col + n], (), [Bh[i % 3]])
            for kc in range(8):
                ACTV(S, ub[i % 2][:, kc, :], hb[i % 3][:, kc, :], AF.Identity, [Bh[i % 3], Bc], [Bu[i % 2]],
                     bias=mod_ap(g, l, 3 * (2 * j) + 0, kc, cls), scale=mod_ap(g, l, 3 * (2 * j) + 1, kc, cls))

        def stB(i):
            u_, Bu_ = ub[i % 2], Bu[i % 2]
            for fc in range(22):
                pk = pkc[0]
                pa, pv = ps[(pk % 2) * 2], ps[(pk % 2) * 2 + 1]
                Bpa, Bpv = Bps[(pk % 2) * 2], Bps[(pk % 2) * 2 + 1]
                si = pk % 2
                pkc[0] += 1
                for kc in range(8):
                    MM(S, pa[:, 0:n], wup[:, kc, fc * 128:(fc + 1) * 128], u_[:, kc, :], kc == 0, kc == 7, [Bwu[kc][(fc * 128) // 1408], Bu_], [Bpa])
                for kc in range(8):
                    MM(S, pv[:, 0:n], wup[:, kc, DFF + fc * 128:DFF + (fc + 1) * 128], u_[:, kc, :], kc == 0, kc == 7, [Bwu[kc][(DFF + fc * 128) // 1408], Bu_], [Bpv])
                ACTV(S, sil[si][:], pa[:, 0:n], AF.Silu, [Bpa], [Bsil[si]])
                TT(S, DVE, hid[:, fc, :], sil[si][:], pv[:, 0:n], ALU.mult, [Bsil[si], Bpv], [Bhid])

        def stC(i):
            b, s0, nn, cls = tiles[i]
            h_, Bh_ = hb[i % 3], Bh[i % 3]
            for oc in range(8):
                py, Bpy = ps[4 + oc % 2], Bps[4 + oc % 2]
                for fc in range(22):
                    MM(S, py[:, 0:n], wdn[:, fc, oc * 128:(oc + 1) * 128], hid[:, fc, :], fc == 0, fc == 21, [Bwd[fc][0], Bhid], [Bpy])
                STT(S, h_[:, oc, :], py[:, 0:n], mod_ap(g, l, 3 * (2 * j) + 2, oc, cls), h_[:, oc, :], ALU.mult, ALU.add,
                    [Bpy, Bh_, Bc], [Bh_])
            ln_part1(S, g, h_, Bh_, n, W)

        def stD(i):
            b, s0, nn, cls = tiles[i]
            col = b * TB + s0
            ln_part2(S, g, l, 2 * j, hb[i % 3], Bh[i % 3], n, W)
            S.dma(ACT, g.HT[:, :, col:col + n], hb[i % 3][:], [Bh[i % 3]], ())

        stA(0)
        stB(0)
        for i in range(NT):
            if i + 1 < NT:
                stA(i + 1)
            stC(i)
            if i + 1 < NT:
                stB(i + 1)
            stD(i)
        S.emit()
    nc.all_engine_barrier()


def phase_mixpro(nc, g, l, b, U, BU):
    n = TS_
    odd = (l % 2 == 1)
    with contextlib.ExitStack() as st:
        sb = lambda nm, s, d: st.enter_context(nc.sbuf_tensor(uname(nm), s, d))
        hb = [sb("mph%d" % i, [128, 8, n], F32) for i in range(3)]
        S = Sched(nc)
        Bh = [Buf() for _ in range(3)]
        Bc = g.Bconst
        ti = 0
        for (bb, s0, nn, cls) in TILES:
            if bb != b:
                continue
            hi = ti % 3
            ti += 1
            col = b * TB + s0
            S.dma(SP, hb[hi][:], g.HT[:, :, col:col + n], (), [Bh[hi]])
            for kc in range(8):
                if cls == 2 or not odd:
                    dst = U[:, kc, s0:s0 + n]
                    src = hb[hi][:, kc, :]
                else:
                    r0 = (s0 - CTX) // 64
                    nr = n // 64
                    dst = U[:, kc, CTX:TB].rearrange("p (c r) -> p r c", r=32)[:, r0:r0 + nr, :]
                    src = hb[hi][:, kc, :].rearrange("p (r c) -> p r c", c=64)
                ACTV(S, dst, src, AF.Identity, [Bh[hi], Bc], [BU],
                     bias=mod_ap(g, l, 3, kc, cls), scale=mod_ap(g, l, 4, kc, cls))
        S.emit()
    nc.all_engine_barrier()


def phase_merge(nc, g, l, skip_ctx):
    n = TS_
    with contextlib.ExitStack() as st:
        sb = lambda nm, s, d: st.enter_context(nc.sbuf_tensor(uname(nm), s, d))
        wgt = sb("mgwg", [128, 8, 3072], BF16)
        wbr = sb("mgwb", [128, 24, 1024], BF16)
        wo = sb("mgwo", [128, 8, 1024], BF16)
        hb = [sb("mgh%d" % i, [128, 8, n], F32) for i in range(3)]
        ub = [sb("mgu0", [128, 8, n], BF16)] * 2
        sq0 = sb("mgsq0", [128, 8, n], BF16)
        yb = [[sb("mgy%d_%d" % (i, k), [128, 8, n], BF16) for k in range(3)] for i in range(2)]
        mb = sb("mgm", [128, 8, n], BF16)
        zbf = sb("mgzbf", [128, 8, n], BF16)
        sq = sb("mgsq", [128, 8, n], BF16)
        mean = sb("mgmean", [128, n], F32)
        msq = sb("mgmsq", [128, n], F32)
        var = sb("mgvar", [128, n], F32)
        rstd0 = [sb("mgrstd0%d" % i, [128, n], F32) for i in range(2)]
        sig = [sb("mgsig%d" % i, [128, n], F32) for i in range(3)]
        acc = sb("mgacc", [128, n], F32)
        tmp = sb("mgtmp", [128, n], F32)
        ps = [st.enter_context(nc.psum_tensor(uname("mgps%d" % i), [128, 512], F32)) for i in range(8)]
        S = Sched(nc)
        Bc = g.Bconst
        Bwg = [[Buf() for _ in range(3)] for _ in range(8)]
        Bwb = [[Buf()] for _ in range(24)]
        Bwo = [[Buf()] for _ in range(8)]
        Bh = [Buf(), Buf(), Buf()]
        By = [[Buf() for _ in range(3)] for _ in range(2)]
        Bm, Bzs, Bsm, Bacc, Btmp, Bsq0 = Buf(), Buf(), Buf(), Buf(), Buf(), Buf()
        Bu = [Buf()] * 2
        Br0 = [Buf(), Buf()]
        Bsig = [Buf(), Buf(), Buf()]
        Bps = [Buf() for _ in range(8)]
        load_weight_cast(S, wgt, g.w_in[l][:, OFF_GATE:OFF_GATE + 3072], 8, 3072, Bwg)
        load_weight_cast(S, wbr, g.w_branch[l].rearrange("n k m -> (n k) m"), 24, 1024, Bwb)
        load_weight_cast(S, wo, g.w_out[l], 8, 1024, Bwo)
        import os as _os
        PL = DVE
        for kc in range(0 if _os.environ.get('MG_NOFOLD') else 8):
            TSC(S, DVE, wbr[:, kc, :], wbr[:, kc, :], cf(g, "sng", l * 8 + kc), None, ALU.mult, None, [Bwb[kc][0], Bc], [Bwb[kc][0]])
        W = dict(zbf=zbf, sq=sq, Bzs=Bzs, psm=ps[6], psq=ps[7], Bpm=Bps[6], Bpq=Bps[7], mean=mean, msq=msq, var=var, Bsm=Bsm)
        tiles = [t for t in TILES if not (skip_ctx and t[3] == 2)]
        NT = len(tiles)
        pkc = [0]

        def stA(i):
            b, s0, nn, cls = tiles[i]
            col = b * TB + s0
            hi = i % 2
            S.dma(SP, hb[i % 3][:], g.HT[:, :, col:col + n], (), [Bh[i % 3]])
            for k in range(3):
                S.dma(SP, yb[hi][k][:], g.Y[k][:, :, col:col + n], (), [By[hi][k]])
            for kc in range(8):
                ACTV(S, ub[hi][:, kc, :], hb[i % 3][:, kc, :], AF.Identity, [Bh[i % 3], Bc], [Bu[hi]],
                     bias=mod_ap(g, l, 3, kc, cls), scale=mod_ap(g, l, 4, kc, cls))

        def stA2(i):
            hi = i % 2
            ACTV(S, sq0[:], yb[hi][0][:], AF.Square, [By[hi][0]], [Bsq0])
            for kc in range(8):
                MM(S, ps[7][:, 0:n], g.onesb[:], sq0[:, kc, :], kc == 0, kc == 7, [Bsq0, Bc], [Bps[7]])
            TSC(S, DVE, rstd0[hi][:], ps[7][:, 0:n], 1.0 / 1024, EPS, ALU.mult, ALU.add, [Bps[7]], [Br0[hi]])
            ACTV(S, rstd0[hi][:], rstd0[hi][:], AF.Sqrt, [Br0[hi]], [Br0[hi]])
            S.dve(lambda e: e.reciprocal(out=rstd0[hi][:], in_=rstd0[hi][:]), [Br0[hi]], [Br0[hi]])

        def stB(i):
            hi = i % 2
            for oc in range(8):
                for k in range(3):
                    pk = pkc[0]
                    pg, pb = ps[(pk % 2) * 2], ps[(pk % 2) * 2 + 1]
                    Bpg, Bpb = Bps[(pk % 2) * 2], Bps[(pk % 2) * 2 + 1]
                    si = pk % 3
                    pkc[0] += 1
                    for kc in range(8):
                        MM(S, pg[:, 0:n], wgt[:, kc, k * 1024 + oc * 128:k * 1024 + (oc + 1) * 128], ub[hi][:, kc, :], kc == 0, kc == 7,
                           [Bwg[kc][k], Bu[hi]], [Bpg])
                    for kc in range(8):
                        MM(S, pb[:, 0:n], wbr[:, k * 8 + kc, oc * 128:(oc + 1) * 128], yb[hi][k][:, kc, :], kc == 0, kc == 7,
                           [Bwb[k * 8 + kc][0], By[hi][k]], [Bpb])
                    ACTV(S, sig[si][:], pg[:, 0:n], AF.Sigmoid, [Bpg], [Bsig[si]])
                    if k == 0:
                        TT(S, DVE, acc[:], sig[si][:], pb[:, 0:n], ALU.mult, [Bsig[si], Bpb], [Bacc])
                        TT(S, PL, acc[:], acc[:], rstd0[hi][:], ALU.mult, [Bacc, Br0[hi]], [Bacc])
                    elif k == 1:
                        TT(S, DVE, tmp[:], sig[si][:], pb[:, 0:n], ALU.mult, [Bsig[si], Bpb], [Btmp])
                        TT(S, PL, acc[:], acc[:], tmp[:], ALU.add, [Bacc, Btmp], [Bacc])
                    else:
                        TT(S, DVE, tmp[:], sig[si][:], pb[:, 0:n], ALU.mult, [Bsig[si], Bpb], [Btmp])
                        TT(S, PL, mb[:, oc, :], acc[:], tmp[:], ALU.add, [Bacc, Btmp], [Bm])

        def stC(i):
            b, s0, nn, cls = tiles[i]
            h_, Bh_ = hb[i % 3], Bh[i % 3]
            for oc in range(8):
                py, Bpy = ps[4 + oc % 2], Bps[4 + oc % 2]
                for kc in range(8):
                    MM(S, py[:, 0:n], wo[:, kc, oc * 128:(oc + 1) * 128], mb[:, kc, :], kc == 0, kc == 7, [Bwo[kc][0], Bm], [Bpy])
                STT(S, h_[:, oc, :], py[:, 0:n], mod_ap(g, l, 5, oc, cls), h_[:, oc, :], ALU.mult, ALU.add,
                    [Bpy, Bh_, Bc], [Bh_])
            ln_part1(S, g, h_, Bh_, n, W)

        def stD(i):
            b, s0, nn, cls = tiles[i]
            col = b * TB + s0
            ln_part2(S, g, l, 1, hb[i % 3], Bh[i % 3], n, W)
            S.dma(ACT, g.HT[:, :, col:col + n], hb[i % 3][:], [Bh[i % 3]], ())

        stA(0)
        stA2(0)
        stB(0)
        for i in range(NT):
            if i + 1 < NT:
                stA(i + 1)
            stC(i)
            if i + 1 < NT:
                stA2(i + 1)
                stB(i + 1)
            stD(i)
        S.emit()
    nc.all_engine_barrier()


SEGS = [(0, 256)] + [(CTX + i * 512, 512) for i in range(4)]


def phase_lru(nc, g, l, b, U, BU):
    odd = (l % 2 == 1)
    Lh = 32 if odd else 64
    with contextlib.ExitStack() as st:
        sb = lambda nm, s, d: st.enter_context(nc.sbuf_tensor(uname(nm), s, d))
        wl = [sb("lrw%d" % i, [128, 8, 256], BF16) for i in range(2)]
        wblk = sb("lrblk", [128, 8, 4, 128], BF16)
        xr = sb("lrxr", [128, TB], F32)
        xc = sb("lrxc", [128, TB], F32)
        xcb = sb("lrxcb", [128, TB], BF16)
        gg = sb("lrgg", [128, TB], F32)
        T_ = [[sb("lrT%d_%d" % (d_, i), [128, TB], F32) for i in range(4)] for d_ in range(2)]
        hf = sb("lrhf", [128, TB], F32)
        hbk = sb("lrhb", [128, TB], F32)
        yst = [sb("lryst%d" % i, [128, TB], BF16) for i in range(2)]
        ps = [st.enter_context(nc.psum_tensor(uname("lrps%d" % i), [128, 512], F32)) for i in range(8)]
        S = Sched(nc)
        Bc = g.Bconst
        Bwl = [Buf(), Buf()]
        Bblk, Bxr, Bxc, Bxcb, Bgg, Bhf, Bhb = Buf(), Buf(), Buf(), Buf(), Buf(), Buf(), Buf()
        BT_ = [[Buf() for _ in range(4)] for _ in range(2)]
        Byst = [Buf(), Buf()]
        Bps = [Buf() for _ in range(8)]
        S.pool(lambda e: e.memset(wblk[:], 0.0), (), [Bblk])
        for d in range(2):
            for t, wsrc in enumerate((g.lru_w_a, g.lru_w_x)):
                for h in range(2):
                    src = wsrc[l, d].rearrange("(j h) k c -> h k j c", h=2)[h]
                    S.dma(POOL, wblk[h * 64:(h + 1) * 64, :, d * 2 + t, h * 64:(h + 1) * 64], src, (), [Bblk])
        wv = g.w_in[l].rearrange("(k p) n -> p k n", p=128)
        pk = 0
        for j in range(8):
            wi = j % 2
            S.dma(POOL, wl[wi][:, :, 0:128], wv[:, :, OFF_LX + j * 128:OFF_LX + (j + 1) * 128], (), [Bwl[wi]])
            S.dma(POOL, wl[wi][:, :, 128:256], wv[:, :, OFF_LG + j * 128:OFF_LG + (j + 1) * 128], (), [Bwl[wi]])
            for (s0, n) in SEGS:
                px, Bpx = ps[pk % 4], Bps[pk % 4]
                pg, Bpg = ps[(pk + 1) % 4], Bps[(pk + 1) % 4]
                pk += 2
                for kc in range(8):
                    MM(S, px[:, 0:n], wl[wi][:, kc, 0:128], U[:, kc, s0:s0 + n], kc == 0, kc == 7, [Bwl[wi], BU], [Bpx])
                for kc in range(8):
                    MM(S, pg[:, 0:n], wl[wi][:, kc, 128:256], U[:, kc, s0:s0 + n], kc == 0, kc == 7, [Bwl[wi], BU], [Bpg])
                COPY(S, ACT, xr[:, s0:s0 + n], px[:, 0:n], [Bpx], [Bxr])
                ACTV(S, gg[:, s0:s0 + n], pg[:, 0:n], AF.Gelu_apprx_tanh, [Bpg], [Bgg])
            cw = lambda k: cf(g, "lcw", (l * 4 + k) * 8 + j)
            ACTV(S, xc[:], xr[:], AF.Identity, [Bxr, Bc], [Bxc], bias=cf(g, "lcb", l * 8 + j), scale=cw(2))
            for (o0, ln_, nl) in ((0, 256, 1), (CTX, Lh, SEQ // Lh)):
                xv = xr[:, o0:o0 + ln_ * nl].rearrange("p (a b) -> p a b", b=ln_)
                ov = xc[:, o0:o0 + ln_ * nl].rearrange("p (a b) -> p a b", b=ln_)
                STT(S, ov[:, :, 2:ln_], xv[:, :, 0:ln_ - 2], cw(0), ov[:, :, 2:ln_], ALU.mult, ALU.add, [Bxr, Bxc, Bc], [Bxc])
                STT(S, ov[:, :, 1:ln_], xv[:, :, 0:ln_ - 1], cw(1), ov[:, :, 1:ln_], ALU.mult, ALU.add, [Bxr, Bxc, Bc], [Bxc])
                STT(S, ov[:, :, 0:ln_ - 1], xv[:, :, 1:ln_], cw(3), ov[:, :, 0:ln_ - 1], ALU.mult, ALU.add, [Bxr, Bxc, Bc], [Bxc])
            COPY(S, ACT, xcb[:], xc[:], [Bxc], [Bxcb])
            def lru_dir(d):
                T = T_[d]
                BT = BT_[d]
                ci = (l * 2 + d) * 8 + j
                pk_ = 0
                for (s0, n) in SEGS:
                    pr, Bpr = ps[4 + 2 * d], Bps[4 + 2 * d]
                    pi_, Bpi = ps[5 + 2 * d], Bps[5 + 2 * d]
                    MM(S, pr[:, 0:n], wblk[:, j, d * 2 + 0, :], xcb[:, s0:s0 + n], True, True, [Bblk, Bxcb], [Bpr])
                    yield
                    MM(S, pi_[:, 0:n], wblk[:, j, d * 2 + 1, :], xcb[:, s0:s0 + n], True, True, [Bblk, Bxcb], [Bpi])
                    yield
                    ACTV(S, T[0][:, s0:s0 + n], pr[:, 0:n], AF.Sigmoid, [Bpr, Bc], [BT[0]], bias=cf(g, "lba", ci))
                    yield
                    ACTV(S, T[1][:, s0:s0 + n], pi_[:, 0:n], AF.Sigmoid, [Bpi, Bc], [BT[1]], bias=cf(g, "lbx", ci))
                    yield
                TT(S, POOL, T[1][:], T[1][:], xc[:], ALU.mult, [BT[1], Bxc], [BT[1]])
                yield
                ACTV(S, T[2][:], T[0][:], AF.Exp, [BT[0], Bc], [BT[2]], scale=cf(g, "lsp8", ci))
                yield
                ACTV(S, T[3][:], T[0][:], AF.Identity, [BT[0], Bc], [BT[3]], scale=cf(g, "lsp16", ci))
                yield
                ACTV(S, T[0][:], T[0][:], AF.Identity, [BT[0], Bc], [BT[0]], bias=1.0 / 6, scale=cf(g, "lsp24", ci))
                yield
                TT(S, DVE, T[0][:], T[0][:], T[3][:], ALU.mult, [BT[0], BT[3]], [BT[0]])
                yield
                for cst in (0.5, 1.0):
                    STT(S, T[0][:], T[0][:], cst, T[3][:], ALU.add, ALU.mult, [BT[0], BT[3]], [BT[0]])
                    yield
                ACTV(S, T[0][:], T[0][:], AF.Sqrt, [BT[0]], [BT[0]], scale=-1.0)
                yield
                TT(S, DVE, T[1][:], T[1][:], T[0][:], ALU.mult, [BT[1], BT[0]], [BT[1]])
                yield
                if d == 0:
                    S.dve(lambda e: e.tensor_tensor_scan(out=hf[:], data0=T[2][:], data1=T[1][:], initial=0.0,
                                                         op0=ALU.mult, op1=ALU.add), [BT[2], BT[1]], [Bhf])
                    yield
                else:
                    S.dve(lambda e: e.tensor_tensor_scan(out=hbk[:, 0:CTX][:, ::-1], data0=T[2][:, 0:CTX][:, ::-1],
                                                         data1=T[1][:, 0:CTX][:, ::-1], initial=0.0,
                                                         op0=ALU.mult, op1=ALU.add), [BT[2], BT[1]], [Bhb])
                    yield
                    S.dve(lambda e: e.tensor_tensor_scan(out=hbk[:, CTX:TB][:, ::-1], data0=T[2][:, CTX:TB][:, ::-1],
                                                         data1=T[1][:, CTX:TB][:, ::-1], initial=hbk[:, 0:1],
                                                         op0=ALU.mult, op1=ALU.add), [BT[2], BT[1], Bhb], [Bhb])
                    yield

            run_interleaved([lru_dir(0), lru_dir(1)])
            yi = j % 2
            TT(S, DVE, hf[:], hf[:], hbk[:], ALU.add, [Bhf, Bhb], [Bhf])
            TT(S, DVE, yst[yi][:, 0:CTX], hf[:, 0:CTX], gg[:, 0:CTX], ALU.mult, [Bhf, Bgg], [Byst[yi]])
            if odd:
                ov = yst[yi][:, CTX:TB].rearrange("p (r c) -> p c r", c=64)
                i0 = hf[:, CTX:TB].rearrange("p (c r) -> p c r", r=32)
                i1 = gg[:, CTX:TB].rearrange("p (c r) -> p c r", r=32)
                TT(S, DVE, ov, i0, i1, ALU.mult, [Bhf, Bgg], [Byst[yi]])
            else:
                TT(S, DVE, yst[yi][:, CTX:TB], hf[:, CTX:TB], gg[:, CTX:TB], ALU.mult, [Bhf, Bgg], [Byst[yi]])
            S.dma(SP, g.Y[1][:, j, b * TB:(b + 1) * TB], yst[yi][:], [Byst[yi]], ())
        S.emit()
    nc.all_engine_barrier()


def phase_mix(nc, g, l, which):
    with contextlib.ExitStack() as st:
        U = st.enter_context(nc.sbuf_tensor(uname("Umix"), [128, 8, TB], BF16))
        for b in range(NB):
            BU = Buf("U")
            phase_mixpro(nc, g, l, b, U, BU)
            BU = Buf("U")
            if "ssd" in which:
                phase_ssd(nc, g, l, b, U, BU)
            if "lru" in which:
                phase_lru(nc, g, l, b, U, BU)
            if "gla" in which:
                phase_gla(nc, g, l, b, U, BU)


FWD_CHUNKS = list(range(18))
REV_CHUNKS = [1, 0] + list(range(17, 1, -1))


def run_interleaved(gens):
    gens = list(gens)
    while gens:
        for g_ in list(gens):
            try:
                next(g_)
            except StopIteration:
                gens.remove(g_)


def conv_block(S, g, raw, Braw, out, Bout, wfn, bias_ap, Lh):
    Bc = g.Bconst
    ACTV(S, out[:], raw[:], AF.Identity, [Braw, Bc], [Bout], bias=bias_ap, scale=wfn(2))
    for (o0, ln_, nl) in ((0, 256, 1), (CTX, Lh, SEQ // Lh)):
        xv = raw[:, o0:o0 + ln_ * nl].rearrange("p (a b) -> p a b", b=ln_)
        ov = out[:, o0:o0 + ln_ * nl].rearrange("p (a b) -> p a b", b=ln_)
        STT(S, ov[:, :, 2:ln_], xv[:, :, 0:ln_ - 2], wfn(0), ov[:, :, 2:ln_], ALU.mult, ALU.add, [Braw, Bout, Bc], [Bout])
        STT(S, ov[:, :, 1:ln_], xv[:, :, 0:ln_ - 1], wfn(1), ov[:, :, 1:ln_], ALU.mult, ALU.add, [Braw, Bout, Bc], [Bout])
        STT(S, ov[:, :, 0:ln_ - 1], xv[:, :, 1:ln_], wfn(3), ov[:, :, 0:ln_ - 1], ALU.mult, ALU.add, [Braw, Bout, Bc], [Bout])


def phase_ssd(nc, g, l, b, U, BU):
    odd = (l % 2 == 1)
    Lh = 32 if odd else 64
    M = g.masks
    with contextlib.ExitStack() as st:
        sb = lambda nm, s, d: st.enter_context(nc.sbuf_tensor(uname(nm), s, d))
        wg = [sb("sdw0", [128, 8, 784], BF16)] * 2
        szT = sb("sdsz", [128, 2, TB], BF16)
        craw = sb("sdcraw", [128, TB], F32)
        ctmp = sb("sdctmp", [128, TB], F32)
        xsT = sb("sdxsT", [128, 2, TB], F32)
        BTf = sb("sdBTf", [128, TB], F32)
        BT = sb("sdBT", [128, TB], BF16)
        CT = sb("sdCT", [128, TB], BF16)
        dtv = sb("sddt", [128, 144], F32)
        av = sb("sda", [128, 144], F32)
        acs = sb("sdacs", [128, 144], F32)
        tot = sb("sdtot", [128, 144], F32)
        tm = sb("sdtm", [128, 144], F32)
        fs = sb("sdfs", [128, 144], F32)
        te = sb("sdte", [128, 144], F32)
        cd = sb("sdcd", [128, 144], F32)
        yacc = sb("sdyacc", [128, 18, 256], F32)
        Sst = sb("sdS", [128, 2, 256], F32)
        Sbf = sb("sdSb", [128, 2, 256], BF16)
        yst = [sb("sdyst0", [128, 2, TB], BF16)] * 2
        xs_all = sb("sdxsall", [128, 18, 256], F32)
        B_all = sb("sdBall", [128, 18, 128], BF16)
        xsd_ = [sb("sdxsd%d" % i, [128, 256], BF16) for i in range(2)]
        xw_ = [sb("sdxw%d" % i, [128, 256], BF16) for i in range(2)]
        scm_ = [sb("sdscm%d" % i, [128, 128], BF16) for i in range(2)]
        rhsA_ = [sb("sdrhsA%d" % i, [128, 512], F32) for i in range(2)]
        Eb_ = [sb("sdE%d" % i, [128, 512], BF16) for i in range(2)]
        MT_ = [sb("sdMT%d" % i, [128, 512], BF16) for i in range(2)]
        t1_ = [sb("sdt1%d" % i, [128, 256], F32) for i in range(2)]
        t2 = sb("sdt2", [128, 256], F32)
        ps = [st.enter_context(nc.psum_tensor(uname("sdps%d" % i), [128, 512], F32)) for i in range(8)]
        S = Sched(nc)
        Bc = g.Bconst
        Bwg = [Buf()] * 2
        (Bsz, Bcraw, Bctmp, BxsT, BBTf, BBT, BCT, Bdt, Ba, Bacs, Btot, Btm, Bfs, Bte, Bcd, Byacc, BS, BSb,
         Bxt, BBtok, Bxsd, Bxw, Bscm, BrhsA, BE, BMT, Bt1, Bt2) = [Buf() for _ in range(28)]
        Byst = [Buf()] * 2
        Bxall, BBall = Buf(), Buf()
        Byacc = [Buf() for _ in range(18)]
        Bxsd_, Bxw_, Bscm_, BrhsA_, BE_, BMT_, Bt1_, BS_, BSb_ = [[Buf(), Buf()] for _ in range(9)]
        Bps = [Buf() for _ in range(8)]
        wv = g.w_in[l].rearrange("(k p) n -> p k n", p=128)
        v4 = lambda t: t[:].rearrange("p (c d h) -> p c d h", d=2, h=4)
        pk = 0
        for gq in range(4):
            wi = gq % 2
            for (d0, c0, cn) in ((0, OFF_Z + 256 * gq, 256), (256, OFF_XBC + 256 * gq, 256),
                                 (512, OFF_XBC + 1024 + 128 * gq, 128), (640, OFF_XBC + 1536 + 128 * gq, 128),
                                 (768, OFF_DT + 4 * gq, 4), (772, OFF_DT + 16 + 4 * gq, 4)):
                S.dma(POOL, wg[wi][:, :, d0:d0 + cn], wv[:, :, c0:c0 + cn], (), [Bwg[wi]])

            def inproj(c0, evac):
                nonlocal pk
                for (s0, n) in SEGS:
                    p_, Bp = ps[pk % 2], Bps[pk % 2]
                    pk += 1
                    for kc in range(8):
                        MM(S, p_[:, 0:n], wg[wi][:, kc, c0:c0 + 128], U[:, kc, s0:s0 + n], kc == 0, kc == 7, [Bwg[wi], BU], [Bp])
                    evac(p_, Bp, s0, n)

            for i in range(2):
                inproj(i * 128, lambda p_, Bp, s0, n, i=i: ACTV(S, szT[:, i, s0:s0 + n], p_[:, 0:n], AF.Silu, [Bp], [Bsz]))
            for ci in range(4):
                inproj(256 + ci * 128, lambda p_, Bp, s0, n: COPY(S, ACT, craw[:, s0:s0 + n], p_[:, 0:n], [Bp], [Bcraw]))
                ch16 = (2 * gq + ci) if ci < 2 else (8 + gq if ci == 2 else 12 + gq)
                conv_block(S, g, craw, Bcraw, ctmp, Bctmp, lambda k, ch16=ch16: cf(g, "scw", (l * 4 + k) * 16 + ch16),
                           cf(g, "scb", l * 16 + ch16), Lh)
                if ci < 2:
                    ACTV(S, xsT[:, ci, :], ctmp[:], AF.Silu, [Bctmp], [BxsT])
                elif ci == 2:
                    ACTV(S, BTf[:], ctmp[:], AF.Silu, [Bctmp], [BBTf])
                    COPY(S, DVE, BT[:], BTf[:], [BBTf], [BBT])
                else:
                    ACTV(S, CT[:], ctmp[:], AF.Silu, [Bctmp], [BCT])
            pdt, Bpdt = ps[2], Bps[2]
            for c in range(18):
                for kc in range(8):
                    MM(S, pdt[:, c * 8:(c + 1) * 8], U[:, kc, c * 128:(c + 1) * 128], wg[wi][:, kc, 768:776], kc == 0, kc == 7,
                       [Bwg[wi], BU], [Bpdt])
            rb = lambda t: bc(t[:, l * 32:(l + 1) * 32].rearrange("p (d h) -> p d h", d=2)[:, :, 4 * gq:4 * gq + 4].unsqueeze(1), [128, 18, 2, 4])
            TT(S, DVE, v4(dtv), pdt[:, 0:144].rearrange("p (c d h) -> p c d h", d=2, h=4), rb(g.rb_dtb), ALU.add, [Bpdt, Bc], [Bdt])
            ACTV(S, dtv[:], dtv[:], AF.Exp, [Bdt], [Bdt])
            ACTV(S, dtv[:], dtv[:], AF.Ln, [Bdt], [Bdt], bias=1.0)
            TT(S, DVE, v4(av), v4(dtv), rb(g.rb_A), ALU.mult, [Bdt, Bc], [Ba])
            MM(S, ps[3][:, 0:144], M["LE"][:], av[:], True, True, [Ba, Bc], [Bps[3]])
            MM(S, ps[4][:, 0:144], M["ONES"][:], av[:], True, True, [Ba, Bc], [Bps[4]])
            COPY(S, DVE, acs[:], ps[3][:, 0:144], [Bps[3]], [Bacs])
            COPY(S, DVE, tot[:], ps[4][:, 0:144], [Bps[4]], [Btot])
            ACTV(S, cd[:], tot[:], AF.Exp, [Btot], [Bcd])
            ACTV(S, v4(fs)[:, :, 0, :], v4(acs)[:, :, 0, :], AF.Exp, [Bacs], [Bfs])
            TT(S, DVE, v4(tm)[:, :, 0, :], v4(tot)[:, :, 0, :], v4(acs)[:, :, 0, :], ALU.subtract, [Btot, Bacs], [Btm])
            TT(S, DVE, v4(tm)[:, :, 1, :], v4(acs)[:, :, 1, :], v4(av)[:, :, 1, :], ALU.subtract, [Bacs, Ba], [Btm])
            ACTV(S, te[:], tm[:], AF.Exp, [Btm], [Bte])
            TT(S, DVE, v4(tm)[:, :, 1, :], v4(tot)[:, :, 1, :], v4(tm)[:, :, 1, :], ALU.subtract, [Btot, Btm], [Btm])
            ACTV(S, v4(fs)[:, :, 1, :], v4(tm)[:, :, 1, :], AF.Exp, [Btm], [Bfs])
            yi = 0
            h3 = lambda t: t.rearrange("p (h q) -> p h q", h=4)
            for c in range(18):
                cs = slice(c * 128, (c + 1) * 128)
                ptr, Bptr = ps[2 + c % 2], Bps[2 + c % 2]
                for i in range(2):
                    TR(S, ptr[:, i * 128:(i + 1) * 128], xsT[:, i, cs], M["ID"][:], [BxsT, Bc], [Bptr])
                TR(S, ptr[:, 256:384], BTf[:, cs], M["ID"][:], [BBTf, Bc], [Bptr])
                COPY(S, ACT, xs_all[:, c, :], ptr[:, 0:256], [Bptr], [Bxall])
                COPY(S, ACT, B_all[:, c, :], ptr[:, 256:384], [Bptr], [BBall])
            seen = set()

            def ssd_iter(d, c):
                M1 = M["GT"] if d == 0 else M["LT"]
                M2 = M["LE"] if d == 0 else M["GE"]
                cs = slice(c * 128, (c + 1) * 128)
                xsd, xw, scm, rhsA, Eb, MT, t1 = xsd_[d], xw_[d], scm_[d], rhsA_[d], Eb_[d], MT_[d], t1_[d]
                Bxsd, Bxw, Bscm, BrhsA, BE, BMT, Bt1 = Bxsd_[d], Bxw_[d], Bscm_[d], BrhsA_[d], BE_[d], BMT_[d], Bt1_[d]
                pA, BpA = ps[2 + 3 * d], Bps[2 + 3 * d]
                pS, BpS = ps[3 + 3 * d], Bps[3 + 3 * d]
                pY, BpY = ps[4 + 3 * d], Bps[4 + 3 * d]
                xs_tok = xs_all[:, c, :]
                dt_c = bc(v4(dtv)[:, c, d, :].unsqueeze(2), [128, 4, 64])
                te_c = bc(v4(te)[:, c, d, :].unsqueeze(2), [128, 4, 64])
                fs_c = bc(v4(fs)[:, c, d, :].unsqueeze(2), [128, 4, 64])
                cd_c = bc(v4(cd)[:, c, d, :].unsqueeze(2), [128, 4, 64])
                TT(S, DVE, h3(xsd[:]), h3(xs_tok), dt_c, ALU.mult, [Bxall, Bdt], [Bxsd])
                yield
                TT(S, DVE, h3(xw[:]), h3(xsd[:]), te_c, ALU.mult, [Bxsd, Bte], [Bxw])
                yield
                MM(S, pA[:, 0:128], BT[:, cs], CT[:, cs], True, True, [BBT, BCT], [BpA])
                yield
                TT(S, DVE, scm[:], pA[:, 0:128], (M["LE"] if d == 0 else M["GE"])[:], ALU.mult, [BpA, Bc], [Bscm])
                yield
                TT(S, POOL, h3(rhsA[:]), bc(M2[:].unsqueeze(1), [128, 4, 128]), bc(v4(av)[:, c, d, :].unsqueeze(2), [128, 4, 128]),
                   ALU.mult, [Ba, Bc], [BrhsA])
                yield
                MM(S, pS[:], M1[:], rhsA[:], True, True, [BrhsA, Bc], [BpS])
                yield
                ACTV(S, Eb[:], pS[:], AF.Exp, [BpS], [BE])
                yield
                TT(S, DVE, h3(MT[:]), h3(Eb[:]), bc(scm[:].unsqueeze(1), [128, 4, 128]), ALU.mult, [BE, Bscm], [BMT])
                yield
                for hh in range(4):
                    MM(S, pY[:, hh * 64:(hh + 1) * 64], MT[:, hh * 128:(hh + 1) * 128], xsd[:, hh * 64:(hh + 1) * 64], True, True,
                       [BMT, Bxsd], [BpY])
                MM(S, pY[:, 256:512], CT[:, cs], Sbf[:, d, :], True, True, [BCT, BSb_[d]], [BpY])
                yield
                TT(S, DVE, h3(t1[:]), h3(pY[:, 256:512]), fs_c, ALU.mult, [BpY, Bfs], [Bt1])
                yield
                if c not in seen:
                    seen.add(c)
                    TT(S, DVE, yacc[:, c, :], pY[:, 0:256], t1[:], ALU.add, [BpY, Bt1], [Byacc[c]])
                else:
                    TT(S, DVE, t1[:], pY[:, 0:256], t1[:], ALU.add, [BpY, Bt1], [Bt1])
                    TT(S, POOL, h3(t2[:]), h3(xs_tok),
                       bc(g.rb_Ds[:, l * 16 + 4 * gq:l * 16 + 4 * gq + 4].unsqueeze(2), [128, 4, 64]), ALU.mult, [Bxall, Bc], [Bt2])
                    TT(S, POOL, t1[:], t1[:], t2[:], ALU.add, [Bt1, Bt2], [Bt1])
                    TT(S, DVE, yacc[:, c, :], yacc[:, c, :], t1[:], ALU.add, [Byacc[c], Bt1], [Byacc[c]])
                    pf, Bpf = ps[1], Bps[1]
                    for i in range(2):
                        TR(S, pf[:, i * 128:(i + 1) * 128], yacc[:, c, i * 128:(i + 1) * 128], M["ID"][:], [Byacc[c], Bc], [Bpf])
                    for i in range(2):
                        if odd and c >= 2:
                            cc0 = (c - 2) * 4
                            ov = yst[yi][:, i, CTX:TB].rearrange("p (r c) -> p c r", c=64)[:, cc0:cc0 + 4, :]
                            i0 = pf[:, i * 128:(i + 1) * 128].rearrange("p (c r) -> p c r", r=32)
                            i1 = szT[:, i, cs].rearrange("p (c r) -> p c r", r=32)
                        else:
                            ov, i0, i1 = yst[yi][:, i, cs], pf[:, i * 128:(i + 1) * 128], szT[:, i, cs]
                        TT(S, DVE, ov, i0, i1, ALU.mult, [Bpf, Bsz], [Byst[yi]])
                MM(S, pA[:, 128:384], B_all[:, c, :], xw[:], True, True, [BBall, Bxw], [BpA])
                yield
                TT(S, POOL, h3(Sst[:, d, :]), h3(Sst[:, d, :]), cd_c, ALU.mult, [BS_[d], Bcd], [BS_[d]])
                yield
                TT(S, DVE, Sst[:, d, :], Sst[:, d, :], pA[:, 128:384], ALU.add, [BS_[d], BpA], [BS_[d]])
                yield
                COPY(S, ACT, Sbf[:, d, :], Sst[:, d, :], [BS_[d]], [BSb_[d]])
                yield

            for d in range(2):
                S.pool(lambda e, d=d: e.memset(Sst[:, d, :], 0.0), (), [BS_[d]])
                S.pool(lambda e, d=d: e.memset(Sbf[:, d, :], 0.0), (), [BSb_[d]])
            def ssd_dir(d):
                for c in (FWD_CHUNKS if d == 0 else REV_CHUNKS):
                    yield from ssd_iter(d, c)

            run_interleaved([ssd_dir(0), ssd_dir(1)])
            S.dma(SP, g.Y[0][:, 2 * gq:2 * gq + 2, b * TB:(b + 1) * TB], yst[yi][:], [Byst[yi]], ())
        S.emit()
    nc.all_engine_barrier()


def phase_gla(nc, g, l, b, U, BU):
    odd = (l % 2 == 1)
    M = g.masks
    QS = 128.0 ** -0.5
    with contextlib.ExitStack() as st:
        sb = lambda nm, s, d: st.enter_context(nc.sbuf_tensor(uname(nm), s, d))
        wq = sb("glw", [128, 8, 896], BF16)
        WG = sb("glWG", [128, 256], BF16)
        bgb = sb("glbg", [128, 256], F32)
        gng = sb("glgng", [128, 256], F32)
        qT = sb("glqT", [128, TB], F32)
        kT = sb("glkT", [128, TB], F32)
        sgT = sb("glsg", [128, 2, TB], BF16)
        alrT = sb("glalr", [128, TB], BF16)
        v_tok = sb("glv", [128, 18, 256], BF16)
        k_tok = sb("glk", [128, 18, 128], F32)
        lsp = sb("gllsp", [128, 18, 256], F32)
        oacc = sb("gloacc", [128, 18, 256], F32)
        yst = [sb("glyst0", [128, 2, TB], BF16)] * 2
        eq_ = [sb("gleq%d" % i, [128, 128], F32) for i in range(2)]
        ek_ = [sb("glek%d" % i, [128, 128], F32) for i in range(2)]
        qin_ = [sb("glqin%d" % i, [128, 128], BF16) for i in range(2)]
        kin_ = [sb("glkin%d" % i, [128, 128], BF16) for i in range(2)]
        er_ = [sb("gler%d" % i, [128, 128], F32) for i in range(2)]
        kst_ = [[sb("glkst%d_%d" % (d_, i), [128, 128], BF16) for i in range(2)] for d_ in range(2)]
        qinh_ = [[sb("glqinh%d_%d" % (d_, i), [128, 128], BF16) for i in range(2)] for d_ in range(2)]
        attT_ = [sb("glatt%d" % i, [128, 128], BF16) for i in range(2)]
        Sst_ = [sb("glS%d" % i, [128, 256], F32) for i in range(2)]
        Sb0_ = [sb("glSb0%d" % i, [128, 256], BF16) for i in range(2)]
        Sb1_ = [sb("glSb1%d" % i, [128, 256], BF16) for i in range(2)]
        osq = sb("glosq", [128, 256], F32)
        ssq = sb("glssq", [128, 1], F32)
        on = sb("glon", [128, 256], F32)
        ps = [st.enter_context(nc.psum_tensor(uname("glps%d" % i), [128, 512], F32)) for i in range(8)]
        S = Sched(nc)
        Bc = g.Bconst
        (Bwq, BWG, Bbg, BqT, BkT, Bsg, Balr, Bv, Bk, Blsp, Boacc, Beq, Bek, Bqin, Bkin, Ber, Bkst, Batt, BS, BSb0, BSb1,
         Bosq, Bssq, Bon) = [Buf() for _ in range(24)]
        Byst = [Buf()] * 2
        Boacc = [Buf() for _ in range(18)]
        Beq_, Bek_, Bqin_, Bkin_, Ber_, Bkst_, Bqinh_, Batt_, BS_, BSb0_, BSb1_ = [[Buf(), Buf()] for _ in range(11)]
        Bps = [Buf() for _ in range(8)]
        wv = g.w_in[l].rearrange("(k p) n -> p k n", p=128)
        pk = 0
        Bgng = Buf()
        S.dma(SP, gng[:], g.gla_norm_g[l].partition_broadcast(128), (), [Bgng])
        for d_ in range(2):
            for i_ in range(2):
                S.pool(lambda e, i_=i_, d_=d_: e.memset(qinh_[d_][i_][:], 0.0), (), [Bqinh_[d_]])
        for hd in range(4):
            S.pool(lambda e: e.memset(wq[:, :, 768:896], 0.0), (), [Bwq])
            for (d0, c0, cn) in ((0, OFF_Q + 128 * hd, 128), (128, OFF_K + 128 * hd, 128), (256, OFF_V + 256 * hd, 256),
                                 (512, OFF_G + 256 * hd, 256), (768, OFF_ALR, 16), (800, OFF_ALR + 16, 16)):
                S.dma(POOL, wq[:, :, d0:d0 + cn], wv[:, :, c0:c0 + cn], (), [Bwq])
            import os as _os
            S.pool(lambda e: e.memset(WG[:], 0.0), (), [BWG])
            for d in range(0 if _os.environ.get('GLA_NOWG') else 2):
                S.dma(POOL, WG[32 * d:32 * d + 16, d * 128:(d + 1) * 128], g.gla_w_gate[l, d, :, hd * 128:(hd + 1) * 128], (), [BWG])
                S.dma(SP, bgb[:, d * 128:(d + 1) * 128], g.gla_b_gate[l, d, hd * 128:(hd + 1) * 128].partition_broadcast(128), (), [Bbg])

            _ninp = [0]

            def inproj(c0, m, evac):
                nonlocal pk
                _ninp[0] += 1
                if _ninp[0] > int(_os.environ.get('GLA_INP', '9')):
                    return
                for (s0, n) in SEGS:
                    p_, Bp = ps[pk % 2], Bps[pk % 2]
                    pk += 1
                    for kc in range(8):
                        MM(S, p_[0:m, 0:n], wq[:, kc, c0:c0 + m], U[:, kc, s0:s0 + n], kc == 0, kc == 7, [Bwq, BU], [Bp])
                    evac(p_, Bp, s0, n)

            if float(_os.environ.get('GLA_DBG', '9')) == 0:
                continue
            inproj(0, 128, lambda p_, Bp, s0, n: COPY(S, ACT, qT[:, s0:s0 + n], p_[:, 0:n], [Bp], [BqT]))
            inproj(128, 128, lambda p_, Bp, s0, n: COPY(S, ACT, kT[:, s0:s0 + n], p_[:, 0:n], [Bp], [BkT]))
            for i in range(2):
                inproj(512 + i * 128, 128, lambda p_, Bp, s0, n, i=i: ACTV(S, sgT[:, i, s0:s0 + n], p_[:, 0:n], AF.Silu, [Bp], [Bsg]))
            inproj(768, 128, lambda p_, Bp, s0, n: COPY(S, ACT, alrT[:, s0:s0 + n], p_[:, 0:n], [Bp], [Balr]))
            _l2 = float(_os.environ.get('GLA_DBG', '9'))
            for c in range(18 if _l2 > 0.5 else 0):
                cs = slice(c * 128, (c + 1) * 128)
                p_, Bp = ps[pk % 2], Bps[pk % 2]
                pk += 1
                for kc in range(8):
                    MM(S, p_[:, 0:256], U[:, kc, cs], wq[:, kc, 256:512], kc == 0, kc == 7, [Bwq, BU], [Bp])
                for kc in range(8):
                    MM(S, ps[7][:, 0:128], U[:, kc, cs], wq[:, kc, 128:256], kc == 0, kc == 7, [Bwq, BU], [Bps[7]])
                COPY(S, ACT, v_tok[:, c, :], p_[:, 0:256], [Bp], [Bv])
                COPY(S, DVE, k_tok[:, c, :], ps[7][:, 0:128], [Bps[7]], [Bk])
                if _l2 > 0.7:
                    MM(S, ps[2][:, 0:256], alrT[:, cs], WG[:, :], True, True, [Balr, BWG], [Bps[2]])
                    TT(S, DVE, lsp[:, c, :], ps[2][:, 0:256], bgb[:], ALU.add, [Bps[2], Bbg], [Blsp])
            if _l2 > 0.8:
                ACTV(S, lsp[:], lsp[:], AF.Exp, [Blsp], [Blsp], scale=-1.0)
                ACTV(S, lsp[:], lsp[:], AF.Ln, [Blsp], [Blsp], bias=1.0)
            yi = 0
            _lvl = 9
            seen = set()

            def gla_iter(d, c):
                CM = M["LE64"] if d == 0 else M["GE64"]
                RM = M["GT64"] if d == 0 else M["LT64"]
                blocks = (0, 1) if d == 0 else (1, 0)
                eq, ek, er, qin, kin, kst, qinh, attT = eq_[d], ek_[d], er_[d], qin_[d], kin_[d], kst_[d], qinh_[d], attT_[d]
                Beq, Bek, Ber, Bqin, Bkin, Bkst, Bqinh, Batt = Beq_[d], Bek_[d], Ber_[d], Bqin_[d], Bkin_[d], Bkst_[d], Bqinh_[d], Batt_[d]
                Sst, Sb0, Sb1, BS, BSb0, BSb1 = Sst_[d], Sb0_[d], Sb1_[d], BS_[d], BSb0_[d], BSb1_[d]
                pP, BpP = ps[2 + 3 * d], Bps[2 + 3 * d]
                pA, BpA = ps[3 + 3 * d], Bps[3 + 3 * d]
                pO, BpO = ps[4 + 3 * d], Bps[4 + 3 * d]
                cs = slice(c * 128, (c + 1) * 128)
                ld = lsp[:, c, d * 128:(d + 1) * 128]
                MM(S, pP[:, 0:128], ld, CM[:], True, True, [Blsp, Bc], [BpP])
                yield
                MM(S, pP[:, 128:256], RM[:], ld, True, True, [Blsp, Bc], [BpP])
                yield
                ACTV(S, eq[:], pP[:, 0:128], AF.Exp, [BpP], [Beq], scale=-1.0 / 16)
                yield
                ACTV(S, ek[:], pP[:, 0:128], AF.Exp, [BpP], [Bek], scale=1.0 / 16)
                yield
                ACTV(S, er[:], pP[:, 128:256], AF.Exp, [BpP], [Ber], scale=-1.0 / 16)
                yield
                STT(S, qin[:], qT[:, cs], QS, eq[:], ALU.mult, ALU.mult, [BqT, Beq], [Bqin])
                yield
                TT(S, DVE, kin[:], kT[:, cs], ek[:], ALU.mult, [BkT, Bek], [Bkin])
                yield
                for bi_ in range(2):
                    STT(S, kst[bi_][:], k_tok[:, c, :], M["BD"][:, 64 * bi_:64 * bi_ + 1], er[:], ALU.mult, ALU.mult, [Bk, Ber, Bc], [Bkst])
                    hs_ = slice(64 * bi_, 64 * bi_ + 64)
                    COPY(S, POOL, qinh[bi_][:, hs_], qin[:, hs_], [Bqin], [Bqinh])
                MM(S, pA[:, 0:128], kin[:], qin[:], True, True, [Bkin, Bqin], [BpA])
                yield
                TT(S, DVE, attT[:], pA[:, 0:128], CM[:], ALU.mult, [BpA, Bc], [Batt])
                yield
                MM(S, pO[:, 0:256], attT[:], v_tok[:, c, :], True, False, [Batt, Bv], [BpO])
                yield
                for bi, blk in enumerate(blocks):
                    Sb, BSb = (Sb0, BSb0) if bi == 0 else (Sb1, BSb1)
                    MM(S, pO[:, 0:256], qinh[blk][:], Sb[:], False, bi == 1, [Bqinh, BSb], [BpO])
                    MM(S, pA[:, 128:384], kst[blk][:], v_tok[:, c, :], True, True, [Bkst, Bv], [BpA])
                    ecol = (blk * 64 + 63) if d == 0 else (blk * 64)
                    STT(S, Sst[:], Sst[:], eq[:, ecol:ecol + 1], pA[:, 128:384], ALU.mult, ALU.add, [BS, Beq, BpA], [BS])
                    if bi == 0:
                        COPY(S, ACT, Sb1[:], Sst[:], [BS], [BSb1])
                    else:
                        COPY(S, ACT, Sb0[:], Sst[:], [BS], [BSb0])
                if c not in seen:
                    seen.add(c)
                    COPY(S, ACT if d == 0 else DVE, oacc[:, c, :], pO[:, 0:256], [BpO], [Boacc[c]])
                else:
                    TT(S, DVE, oacc[:, c, :], oacc[:, c, :], pO[:, 0:256], ALU.add, [Boacc[c], BpO], [Boacc[c]])
                    ACTV(S, osq[:], oacc[:, c, :], AF.Square, [Boacc[c]], [Bosq])
                    S.dve(lambda e: e.reduce_sum(out=ssq[:], in_=osq[:], axis=AX.X), [Bosq], [Bssq])
                    TSC(S, DVE, ssq[:], ssq[:], 1.0 / 256, EPS, ALU.mult, ALU.add, [Bssq], [Bssq])
                    ACTV(S, ssq[:], ssq[:], AF.Sqrt, [Bssq], [Bssq])
                    S.dve(lambda e: e.reciprocal(out=ssq[:], in_=ssq[:]), [Bssq], [Bssq])
                    STT(S, on[:], oacc[:, c, :], ssq[:, 0:1], gng[:], ALU.mult, ALU.mult, [Boacc[c], Bssq, Bgng], [Bon])
                    pf, Bpf = ps[1], Bps[1]
                    for i in range(2):
                        TR(S, pf[:, i * 128:(i + 1) * 128], on[:, i * 128:(i + 1) * 128], M["ID"][:], [Bon, Bc], [Bpf])
                    for i in range(2):
                        if odd and c >= 2:
                            cc0 = (c - 2) * 4
                            ov = yst[yi][:, i, CTX:TB].rearrange("p (r c) -> p c r", c=64)[:, cc0:cc0 + 4, :]
                            i0 = pf[:, i * 128:(i + 1) * 128].rearrange("p (c r) -> p c r", r=32)
                            i1 = sgT[:, i, cs].rearrange("p (c r) -> p c r", r=32)
                        else:
                            ov, i0, i1 = yst[yi][:, i, cs], pf[:, i * 128:(i + 1) * 128], sgT[:, i, cs]
                        TT(S, DVE, ov, i0, i1, ALU.mult, [Bpf, Bsg], [Byst[yi]])

            for d in range(2):
                S.pool(lambda e, d=d: e.memset(Sst_[d][:], 0.0), (), [BS_[d]])
                S.pool(lambda e, d=d: e.memset(Sb0_[d][:], 0.0), (), [BSb0_[d]])
            def gla_dir(d):
                for c in (FWD_CHUNKS if d == 0 else REV_CHUNKS):
                    yield from gla_iter(d, c)

            run_interleaved([gla_dir(0), gla_dir(1)])
            if _lvl >= 3:
                S.dma(SP, g.Y[2][:, 2 * hd:2 * hd + 2, b * TB:(b + 1) * TB], yst[yi][:], [Byst[yi]], ())
        S.emit()
    nc.all_engine_barrier()


def build_program(stages=None):
    nc = bass.Bass("TRN2", target_bir_lowering=False)
    g = G()
    declare_io(nc, g)
    with contextlib.ExitStack() as st:
        init_gsync(nc, st)
        sb = lambda nm, s, d: st.enter_context(nc.sbuf_tensor(uname(nm), s, d))
        g.constf = sb("constf", [128, NCF], F32)
        g.modT = sb("modT", [128, NL * 9 * 8 * 4], F32)
        g.masks = {nm: sb("mask_" + nm, [128, 128], F32) for nm in
                   ("ONES", "LE", "GT", "LT", "GE", "ID", "BD", "LE64", "GT64", "LT64", "GE64")}
        g.onesb = sb("onesb", [128, 128], BF16)
        g.rb_dtb = sb("rb_dtb", [128, NL * 32], F32)
        g.rb_A = sb("rb_A", [128, NL * 32], F32)
        g.rb_D = sb("rb_D", [128, NL * 32], F32)
        g.rb_Ds = sb("rb_Ds", [128, NL * 16], F32)
        if stages is None:
            stages = ["const", "p0"]
            for l in range(NL):
                stages += [("ffn", l, 0), ("mix", l), ("merge", l), ("ffn", l, 1)]
            stages += ["final"]
        for sg in stages:
            if sg == "const":
                phase_const(nc, g)
            elif sg == "p0":
                phase_p0(nc, g)
            elif sg == "final":
                phase_final(nc, g)
            elif sg[0] == "ffn":
                phase_ffn(nc, g, sg[1], sg[2], skip_ctx=(sg[1] == NL - 1 and sg[2] == 1))
            elif sg[0] == "mix":
                phase_mix(nc, g, sg[1], sg[2] if len(sg) > 2 else ("ssd", "lru", "gla"))
            elif sg[0] == "merge":
                phase_merge(nc, g, sg[1], skip_ctx=(sg[1] == NL - 1))
    return nc


_NC_CACHE = {}


def kernel(**inputs):
    from concourse.bass_utils import run_bass_kernel_spmd
    if "nc" not in _NC_CACHE:
        _NC_CACHE["nc"] = build_program()
    nc = _NC_CACHE["nc"]
    ncores = 8
    in_maps = []
    for i in range(ncores):
        m = {}
        for nm in INPUT_NAMES:
            a = np.asarray(inputs[nm], dtype=np.float32)
            if nm in ("x", "c", "ctx"):
                a = a[i * NB:(i + 1) * NB]
            elif nm == "c_ctx":
                a = a.reshape(1, D)
            m[nm] = np.ascontiguousarray(a)
        in_maps.append(m)
    res = run_bass_kernel_spmd(nc, in_maps, core_ids=list(range(ncores)))
    return np.concatenate([np.asarray(r["out"]) for r in res.results], axis=0).astype(np.float32)
```
